# Optimizing a Trainium2 kernel written in Bass

```python
import math, functools
import jax, jax.numpy as jnp
from jax import lax
import numpy as np

D_MODEL = 1024
BATCH = 8
SEQ = 4096
DEPTH = 2

EPS = 1e-6
D_FF = 2816
FFN_RES = 0.5
SHORT_CONV = 4

POOL_WIDTH = 512
POOL_WINDOWS = (2, 4, 8, 16)
POOL_GROUPS = len(POOL_WINDOWS)
POOL_GROUP_DIM = POOL_WIDTH // POOL_GROUPS
MLSTM_HEADS = 4
MLSTM_HEAD_DIM = 128
MLSTM_WIDTH = MLSTM_HEADS * MLSTM_HEAD_DIM
MLSTM_CHUNK = 64
EVEN_IN = POOL_WIDTH + 4 * MLSTM_WIDTH + 2 * MLSTM_HEADS
EVEN_MIX = POOL_WIDTH + MLSTM_WIDTH

SSD_HEADS = 16
SSD_HEAD_DIM = 64
SSD_WIDTH = SSD_HEADS * SSD_HEAD_DIM
SSD_GROUPS = 4
SSD_STATE = 128
SSD_CHUNK = 128
SB_HEADS = 8
SB_HEAD_DIM = 64
SB_WIDTH = SB_HEADS * SB_HEAD_DIM
SB_BLOCK = 128
SSD_XBC = SSD_WIDTH + 2 * SSD_GROUPS * SSD_STATE
ODD_IN = SSD_WIDTH + SSD_XBC + SSD_HEADS + 3 * SB_WIDTH
ODD_MIX = SSD_WIDTH + SB_WIDTH

kernel_name = 'hybrid_pool_mlstm_ssd_stickbreak_macaron'

F32 = jnp.float32


def rmsnorm(x, w):
    xf = x.astype(F32)
    y = xf * lax.rsqrt(jnp.mean(xf * xf, axis=-1, keepdims=True) + EPS)
    return (y * w.astype(F32)).astype(x.dtype)


def swiglu(x, wg, wu, wd):
    return (jax.nn.silu(x @ wg) * (x @ wu)) @ wd


def half_ffn(x, norm_w, wg, wu, wd):
    return x + FFN_RES * swiglu(rmsnorm(x, norm_w), wg, wu, wd)


def causal_dwconv(x, w, b):
    K, C = w.shape
    y = lax.conv_general_dilated(x, w[:, None, :].astype(x.dtype), window_strides=(1,),
                                 padding=[(K - 1, 0)], dimension_numbers=('NWC', 'WIO', 'NWC'),
                                 feature_group_count=C)
    return y + b.astype(x.dtype)


def pool_mixer(u, w_grp, scale):
    Bsz, S, _ = u.shape
    uf = u.astype(F32)
    pad = POOL_WINDOWS[-1]
    csum = jnp.pad(jnp.cumsum(uf, axis=1), ((0, 0), (pad, 0), (0, 0)))
    pos = jnp.arange(1, S + 1, dtype=F32)[None, :, None]
    groups = []
    for g, win in enumerate(POOL_WINDOWS):
        lo, hi = g * POOL_GROUP_DIM, (g + 1) * POOL_GROUP_DIM
        win_sum = csum[:, pad:, lo:hi] - csum[:, pad - win:pad - win + S, lo:hi]
        groups.append(win_sum / jnp.minimum(pos, float(win)) - uf[:, :, lo:hi])
    pooled = jnp.stack(groups, axis=2)
    mixed = jnp.einsum('bsgc,gcd->bsgd', pooled, w_grp.astype(F32))
    return (mixed.reshape(Bsz, S, POOL_WIDTH) * scale.astype(F32)).astype(u.dtype)


def mlstm(q, k, v, log_i, log_f):
    Bsz, S, H, Dh = q.shape
    L = MLSTM_CHUNK
    nc = S // L

    def chunks(t):
        return t.reshape(Bsz, nc, L, H, Dh).transpose(1, 0, 3, 2, 4)

    def gchunks(t):
        return t.reshape(Bsz, nc, L, H).transpose(1, 0, 3, 2)

    k = k * Dh ** -0.5
    causal = jnp.tril(jnp.ones((L, L), dtype=bool))

    def step(carry, inp):
        C, n, m = carry
        qc, kc, vc, li, lf = inp
        b = jnp.cumsum(lf, axis=-1)
        dmat = jnp.where(causal, b[..., :, None] - b[..., None, :] + li[..., None, :], -jnp.inf)
        m_inter = b + m[..., None]
        m_t = jnp.maximum(m_inter, jnp.max(dmat, axis=-1))
        w_intra = jnp.exp(dmat - m_t[..., None])
        w_inter = jnp.exp(m_inter - m_t)
        qk = jnp.einsum('bhtd,bhsd->bhts', qc, kc) * w_intra
        num = (w_inter[..., None] * jnp.einsum('bhvk,bhtk->bhtv', C, qc)
               + jnp.einsum('bhts,bhsv->bhtv', qk, vc))
        den = w_inter * jnp.einsum('bhk,bhtk->bht', n, qc) + jnp.sum(qk, axis=-1)
        h = num / jnp.maximum(jnp.abs(den), jnp.exp(-m_t))[..., None]
        m_new = m_t[..., -1]
        decay_state = jnp.exp(b[..., -1] + m - m_new)
        w_s = jnp.exp(b[..., -1:] - b + li - m_new[..., None])
        kw = kc * w_s[..., None]
        C_new = decay_state[..., None, None] * C + jnp.einsum('bhsv,bhsk->bhvk', vc, kw)
        n_new = decay_state[..., None] * n + jnp.sum(kw, axis=2)
        return (C_new, n_new, m_new), h

    init = (jnp.zeros((Bsz, H, Dh, Dh), F32), jnp.zeros((Bsz, H, Dh), F32), jnp.zeros((Bsz, H), F32))
    _, h = lax.scan(step, init, (chunks(q), chunks(k), chunks(v), gchunks(log_i), gchunks(log_f)))
    return h.transpose(1, 0, 3, 2, 4).reshape(Bsz, S, H, Dh)


def even_mixer(h, w_in, pool_w, pool_scale, qk_conv_w, qk_conv_b, gate_bias, mlstm_norm, w_out):
    Bsz, S, _ = h.shape
    proj = h @ w_in
    c0 = POOL_WIDTH
    c1 = c0 + 2 * MLSTM_WIDTH
    c2 = c1 + MLSTM_WIDTH
    c3 = c2 + MLSTM_WIDTH
    u, qk, v, o, gates = jnp.split(proj, [c0, c1, c2, c3], axis=-1)
    a_out = pool_mixer(u, pool_w, pool_scale)
    qk = jax.nn.silu(causal_dwconv(qk, qk_conv_w, qk_conv_b))
    q, k = jnp.split(qk, 2, axis=-1)

    def heads(t):
        return t.astype(F32).reshape(Bsz, S, MLSTM_HEADS, MLSTM_HEAD_DIM)

    gates = gates.astype(F32) + gate_bias.astype(F32)
    log_i = gates[..., :MLSTM_HEADS]
    log_f = jax.nn.log_sigmoid(gates[..., MLSTM_HEADS:])
    hm = mlstm(heads(q), heads(k), heads(v), log_i, log_f)
    hm = rmsnorm(hm, mlstm_norm.reshape(MLSTM_HEADS, MLSTM_HEAD_DIM))
    b_out = (jax.nn.sigmoid(o.astype(F32)) * hm.reshape(Bsz, S, MLSTM_WIDTH)).astype(h.dtype)
    return jnp.concatenate([a_out, b_out], axis=-1) @ w_out


def ssd_chunked(x, a, Bm, Cm):
    Bsz, S, H, P = x.shape
    G, N = Bm.shape[2], Bm.shape[3]
    Hg = H // G
    L = SSD_CHUNK
    nc = S // L
    x = x.reshape(Bsz, nc, L, G, Hg, P)
    a = a.reshape(Bsz, nc, L, G, Hg).transpose(0, 1, 3, 4, 2)
    Bm = Bm.reshape(Bsz, nc, L, G, N)
    Cm = Cm.reshape(Bsz, nc, L, G, N)
    a_cum = jnp.cumsum(a, axis=-1)
    causal = jnp.tril(jnp.ones((L, L), dtype=bool))
    decay = jnp.exp(jnp.where(causal, a_cum[..., :, None] - a_cum[..., None, :], -jnp.inf))
    cb = jnp.einsum('bctgn,bcsgn->bcgts', Cm, Bm)
    y_diag = jnp.einsum('bcghts,bcsghp->bctghp', cb[:, :, :, None] * decay, x)
    to_end = jnp.exp(a_cum[..., -1:] - a_cum).transpose(0, 1, 4, 2, 3)
    states = jnp.einsum('bcsgn,bcsghp->bcghpn', Bm, x * to_end[..., None])
    chunk_decay = jnp.exp(a_cum[..., -1])

    def step(hstate, inp):
        st, dec = inp
        return dec[..., None, None] * hstate + st, hstate

    h0 = jnp.zeros((Bsz, G, Hg, P, N), F32)
    _, h_prev = lax.scan(step, h0, (states.swapaxes(0, 1), chunk_decay.swapaxes(0, 1)))
    h_prev = h_prev.swapaxes(0, 1)
    from_start = jnp.exp(a_cum).transpose(0, 1, 4, 2, 3)
    y_off = jnp.einsum('bctgn,bcghpn->bctghp', Cm, h_prev) * from_start[..., None]
    return (y_diag + y_off).reshape(Bsz, S, H, P)


def stick_breaking(q, k, v):
    Bsz, S, H, Dh = q.shape
    nb = S // SB_BLOCK
    scale = Dh ** -0.5
    qb = q.reshape(Bsz, nb, SB_BLOCK, H, Dh).transpose(1, 0, 3, 2, 4)
    kh = k.transpose(0, 2, 1, 3)
    vh = v.transpose(0, 2, 1, 3)
    key_pos = jnp.arange(S)

    def block(args):
        qblk, i = args
        q_pos = i * SB_BLOCK + jnp.arange(SB_BLOCK)
        z = jnp.einsum('bhtd,bhsd->bhts', qblk, kh).astype(F32) * scale
        mask = key_pos[None, :] < q_pos[:, None]
        log_beta = jax.nn.log_sigmoid(z)
        log_1m = jnp.where(mask, log_beta - z, 0.0)
        suffix = lax.cumsum(log_1m, axis=3, reverse=True)
        w = jnp.where(mask, jnp.exp(log_beta + suffix - log_1m), 0.0)
        return jnp.einsum('bhts,bhsd->bhtd', w, vh)

    out = lax.map(block, (qb, jnp.arange(nb)))
    return out.transpose(1, 0, 3, 2, 4).reshape(Bsz, S, H * Dh)


def odd_mixer(h, w_in, conv_w, conv_b, dt_bias, A_log, D_skip, ssd_norm, q_norm, k_norm, w_out):
    Bsz, S, _ = h.shape
    proj = h @ w_in
    c0 = SSD_WIDTH
    c1 = c0 + SSD_XBC
    c2 = c1 + SSD_HEADS
    c3 = c2 + SB_WIDTH
    c4 = c3 + SB_WIDTH
    z, xbc, dt, q, k, v = jnp.split(proj, [c0, c1, c2, c3, c4], axis=-1)
    xbc = jax.nn.silu(causal_dwconv(xbc, conv_w, conv_b)).astype(F32)
    xs, Bm, Cm = jnp.split(xbc, [SSD_WIDTH, SSD_WIDTH + SSD_GROUPS * SSD_STATE], axis=-1)
    xs = xs.reshape(Bsz, S, SSD_HEADS, SSD_HEAD_DIM)
    Bm = Bm.reshape(Bsz, S, SSD_GROUPS, SSD_STATE)
    Cm = Cm.reshape(Bsz, S, SSD_GROUPS, SSD_STATE)
    dt = jax.nn.softplus(dt.astype(F32) + dt_bias.astype(F32))
    A = -jnp.exp(A_log.astype(F32))
    y = ssd_chunked(xs * dt[..., None], dt * A, Bm, Cm) + D_skip.astype(F32)[:, None] * xs
    gsz = SSD_WIDTH // SSD_GROUPS
    y = y.reshape(Bsz, S, SSD_GROUPS, gsz) * jax.nn.silu(z.astype(F32)).reshape(Bsz, S, SSD_GROUPS, gsz)
    c_out = rmsnorm(y, ssd_norm.reshape(SSD_GROUPS, gsz)).reshape(Bsz, S, SSD_WIDTH).astype(h.dtype)
    def heads(t):
        return t.astype(F32).reshape(Bsz, S, SB_HEADS, SB_HEAD_DIM)

    qh = rmsnorm(heads(q), q_norm)
    kh = rmsnorm(heads(k), k_norm)
    d_out = stick_breaking(qh, kh, heads(v)).astype(h.dtype)
    return jnp.concatenate([c_out, d_out], axis=-1) @ w_out


def setup_inputs(seed: int = 0) -> dict:
    key = jax.random.key(seed)
    keys = iter(jax.random.split(key, 64))

    def nk():
        return next(keys)

    def dense(fan_in, fan_out):
        return jax.random.normal(nk(), (fan_in, fan_out), F32) * fan_in ** -0.5

    def gain(n):
        return 1.0 + 0.02 * jax.random.normal(nk(), (n,), F32)

    def small(shape):
        return 0.02 * jax.random.normal(nk(), shape, F32)

    inp = {}
    inp['x'] = jax.random.normal(nk(), (BATCH, SEQ, D_MODEL), F32)

    def ffn(prefix):
        inp[prefix + '_norm'] = gain(D_MODEL)
        inp[prefix + '_wg'] = dense(D_MODEL, D_FF)
        inp[prefix + '_wu'] = dense(D_MODEL, D_FF)
        inp[prefix + '_wd'] = dense(D_FF, D_MODEL)

    ffn('l0_ffn1')
    inp['l0_mix_norm'] = gain(D_MODEL)
    inp['l0_w_in'] = dense(D_MODEL, EVEN_IN)
    inp['l0_pool_w'] = jax.random.normal(nk(), (POOL_GROUPS, POOL_GROUP_DIM, POOL_GROUP_DIM), F32) * POOL_GROUP_DIM ** -0.5
    inp['l0_pool_scale'] = 1.0 + 0.1 * jax.random.normal(nk(), (POOL_WIDTH,), F32)
    inp['l0_qk_conv_w'] = jax.random.normal(nk(), (SHORT_CONV, 2 * MLSTM_WIDTH), F32) * SHORT_CONV ** -0.5
    inp['l0_qk_conv_b'] = small((2 * MLSTM_WIDTH,))
    i_bias = 0.1 * jax.random.normal(nk(), (MLSTM_HEADS,), F32)
    f_bias = jnp.linspace(3.0, 6.0, MLSTM_HEADS, dtype=F32) + 0.1 * jax.random.normal(nk(), (MLSTM_HEADS,), F32)
    inp['l0_gate_bias'] = jnp.concatenate([i_bias, f_bias])
    inp['l0_mlstm_norm'] = gain(MLSTM_WIDTH)
    inp['l0_w_out'] = dense(EVEN_MIX, D_MODEL)
    ffn('l0_ffn2')
    ffn('l1_ffn1')
    inp['l1_mix_norm'] = gain(D_MODEL)
    inp['l1_w_in'] = dense(D_MODEL, ODD_IN)
    inp['l1_ssd_conv_w'] = jax.random.normal(nk(), (SHORT_CONV, SSD_XBC), F32) * SHORT_CONV ** -0.5
    inp['l1_ssd_conv_b'] = small((SSD_XBC,))
    dt0 = jnp.exp(jax.random.uniform(nk(), (SSD_HEADS,), F32, minval=math.log(1e-3), maxval=math.log(1e-1)))
    inp['l1_ssd_dt_bias'] = dt0 + jnp.log(-jnp.expm1(-dt0))
    inp['l1_ssd_A_log'] = jnp.log(jax.random.uniform(nk(), (SSD_HEADS,), F32, minval=1.0, maxval=16.0))
    inp['l1_ssd_D'] = 1.0 + 0.1 * jax.random.normal(nk(), (SSD_HEADS,), F32)
    inp['l1_ssd_norm'] = gain(SSD_WIDTH)
    inp['l1_sb_q_norm'] = gain(SB_HEAD_DIM)
    inp['l1_sb_k_norm'] = gain(SB_HEAD_DIM)
    inp['l1_w_out'] = dense(ODD_MIX, D_MODEL)
    ffn('l1_ffn2')
    return inp


def reference(x,
              l0_ffn1_norm, l0_ffn1_wg, l0_ffn1_wu, l0_ffn1_wd,
              l0_mix_norm, l0_w_in, l0_pool_w, l0_pool_scale, l0_qk_conv_w, l0_qk_conv_b,
              l0_gate_bias, l0_mlstm_norm, l0_w_out,
              l0_ffn2_norm, l0_ffn2_wg, l0_ffn2_wu, l0_ffn2_wd,
              l1_ffn1_norm, l1_ffn1_wg, l1_ffn1_wu, l1_ffn1_wd,
              l1_mix_norm, l1_w_in, l1_ssd_conv_w, l1_ssd_conv_b, l1_ssd_dt_bias, l1_ssd_A_log,
              l1_ssd_D, l1_ssd_norm, l1_sb_q_norm, l1_sb_k_norm, l1_w_out,
              l1_ffn2_norm, l1_ffn2_wg, l1_ffn2_wu, l1_ffn2_wd):
    layers = (
        ((l0_ffn1_norm, l0_ffn1_wg, l0_ffn1_wu, l0_ffn1_wd),
         l0_mix_norm,
         functools.partial(even_mixer, w_in=l0_w_in, pool_w=l0_pool_w, pool_scale=l0_pool_scale,
                           qk_conv_w=l0_qk_conv_w, qk_conv_b=l0_qk_conv_b, gate_bias=l0_gate_bias,
                           mlstm_norm=l0_mlstm_norm, w_out=l0_w_out),
         (l0_ffn2_norm, l0_ffn2_wg, l0_ffn2_wu, l0_ffn2_wd)),
        ((l1_ffn1_norm, l1_ffn1_wg, l1_ffn1_wu, l1_ffn1_wd),
         l1_mix_norm,
         functools.partial(odd_mixer, w_in=l1_w_in, conv_w=l1_ssd_conv_w, conv_b=l1_ssd_conv_b,
                           dt_bias=l1_ssd_dt_bias, A_log=l1_ssd_A_log, D_skip=l1_ssd_D,
                           ssd_norm=l1_ssd_norm, q_norm=l1_sb_q_norm, k_norm=l1_sb_k_norm,
                           w_out=l1_w_out),
         (l1_ffn2_norm, l1_ffn2_wg, l1_ffn2_wu, l1_ffn2_wd)),
    )
    for layer in range(DEPTH):
        ffn1, mix_norm, mixer, ffn2 = layers[layer]
        x = half_ffn(x, *ffn1)
        x = x + mixer(rmsnorm(x, mix_norm))
        x = half_ffn(x, *ffn2)
    return x
```

```python
from contextlib import ExitStack

import numpy as np
import concourse.bass as bass
import concourse.mybir as mybir
from concourse.bass_utils import run_bass_kernel_spmd

F32 = mybir.dt.float32
BF16 = mybir.dt.bfloat16
ALU = mybir.AluOpType
AF = mybir.ActivationFunctionType
AX = mybir.AxisListType

S = 4096
D = 1024
DFF = 2816
NCORES = 8
EPS = 1e-6
CAST_DMA = True


class _Op:
    __slots__ = ("eng", "fn", "deps", "sig", "sem", "cnt", "is_dma", "pos")


def _bank_of(k):
    if isinstance(k, tuple):
        if k[0] == "ps":
            return k[1]
        if k[0] == "pso":
            return 3 + k[1] // 2
        if k[0] == "psd":
            return 5 + k[1] // 2
        if k[0] == "ps0":
            return 0
    return None


class Prog:
    ENGS = ("pe", "act", "dve", "pool", "sp")

    def __init__(self, nc, stack, n_dma_sems=8):
        self.nc = nc
        self.stack = stack
        self.streams = {e: [] for e in self.ENGS}
        self.lastw = {}
        self.readers = {}
        self.eng_sem = {e: stack.enter_context(nc.semaphore("s_" + e)) for e in ("pe", "act", "dve", "pool")}
        self.dma_pool = {}
        self.n_dma_sems = n_dma_sems
        self.dma_rr = {}
        self.nops = 0
        self.pending = {}
        self.bank_last = {}
        self.multi = {}

    def barrier(self):
        lasts = []
        for e in self.ENGS:
            for o in reversed(self.streams[e]):
                if not o.is_dma:
                    o.sig = True
                    lasts.append(o)
                    break
        for q, slots in self.dma_pool.items():
            for sl in slots:
                if sl[2] is not None:
                    lasts.append(sl[2])
        for e in self.ENGS:
            self.pending[e] = list(lasts) + self.pending.get(e, [])
        self.lastw.clear()
        self.readers.clear()
        self.multi.clear()

    def _dma_sem(self, q):
        if q not in self.dma_pool:
            self.dma_pool[q] = [[self.stack.enter_context(self.nc.semaphore("d_%s%d" % (q, i))), 0, None]
                                for i in range(self.n_dma_sems)]
            self.dma_rr[q] = 0
        i = self.dma_rr[q]
        self.dma_rr[q] = (i + 1) % self.n_dma_sems
        return self.dma_pool[q][i]

    def _deps(self, op, reads, writes):
        deps = set()
        for k in reads:
            w = self.lastw.get(k)
            if w is not None:
                deps.add(w)
            if k in self.multi:
                deps.update(self.multi[k])
        for k in writes:
            w = self.lastw.get(k)
            if w is not None:
                deps.add(w)
            for r in self.readers.get(k, ()):
                deps.add(r)
        deps.discard(op)
        for k in reads:
            self.readers.setdefault(k, []).append(op)
        for k in writes:
            self.lastw[k] = op
            self.readers[k] = []
        return deps

    def op(self, eng, fn, reads=(), writes=()):
        o = _Op()
        o.eng = eng
        o.fn = fn
        o.is_dma = False
        o.sig = False
        o.sem = None
        o.cnt = 0
        o.pos = self.nops
        self.nops += 1
        deps = self._deps(o, reads, writes)
        deps.update(self.pending.pop(eng, ()))
        for k in list(reads) + list(writes):
            bk = _bank_of(k)
            if bk is not None:
                prev = self.bank_last.get(bk)
                if prev is not None and prev is not o and prev.eng != eng:
                    deps.add(prev)
                self.bank_last[bk] = o
        keep = []
        for d in deps:
            if (not d.is_dma) and d.eng == eng and eng == "pe":
                continue
            keep.append(d)
            if not d.is_dma:
                d.sig = True
        o.deps = keep
        self.streams[eng].append(o)
        return o

    def dma(self, q, out, in_, reads=(), writes=(), multi=(), **kw):
        o = _Op()
        o.eng = q
        o.fn = lambda e: e.dma_start(out=out, in_=in_, **kw)
        o.is_dma = True
        o.sig = True
        o.pos = self.nops
        self.nops += 1
        slot = self._dma_sem(q)
        deps = self._deps(o, reads, writes)
        for k in multi:
            for r in self.readers.get(k, ()):
                deps.add(r)
            self.multi.setdefault(k, []).append(o)
        deps.update(self.pending.pop(q, ()))
        if slot[2] is not None:
            deps.add(slot[2])
        slot[1] += 16
        slot[2] = o
        o.sem = slot[0]
        o.cnt = slot[1]
        for d in deps:
            if not d.is_dma:
                d.sig = True
        o.deps = list(deps)
        self.streams[q].append(o)
        return o

    def emit(self):
        nc = self.nc
        for e in ("pe", "act", "dve", "pool"):
            c = 0
            for o in self.streams[e]:
                if o.is_dma:
                    continue
                o.sem = self.eng_sem[e]
                if o.sig:
                    c += 1
                    o.cnt = c
        streams = self.streams

        def run(e, h):
            known = {}
            for o in streams[e]:
                need = {}
                for d in o.deps:
                    key = id(d.sem)
                    if key not in need or need[key][1] < d.cnt:
                        need[key] = (d.sem, d.cnt)
                for key, (sem, v) in need.items():
                    if known.get(key, 0) < v:
                        h.wait_ge(sem, v)
                        known[key] = v
                ins = o.fn(h)
                if o.is_dma:
                    ins.then_inc(o.sem, 16)
                elif o.sig:
                    ins.then_inc(o.sem, 1)
            if e in self.dma_pool:
                for sem, cnt, _ in self.dma_pool[e]:
                    if cnt > 0:
                        h.wait_ge(sem, cnt)

        with nc.Block() as block:
            @block.tensor
            def _(h):
                run("pe", h)

            @block.scalar
            def _(h):
                run("act", h)

            @block.vector
            def _(h):
                run("dve", h)

            @block.gpsimd
            def _(h):
                run("pool", h)

            @block.sync
            def _(h):
                run("sp", h)


ARENA_WORDS = 53000


class Ctx:
    def __init__(self, nc, stack):
        self.nc = nc
        self.stack = stack
        self.P = Prog(nc, stack)
        self.psum_all = stack.enter_context(nc.psum_tensor("psall", [128, 4096], F32))
        self.psum = [self.psum_all[:, i * 512:(i + 1) * 512] for i in range(8)]
        self.arena = stack.enter_context(nc.sbuf_tensor("arena", [128, ARENA_WORDS], F32))
        self.top = 0
        self.scratch = {}

    def sb(self, name, shape, dt):
        esz = 4 if dt == F32 else 2
        n = 1
        for d_ in shape[1:]:
            n *= int(d_)
        words = (n * esz + 3) // 4
        words = (words + 15) // 16 * 16
        off = self.top
        self.top += words
        assert self.top <= ARENA_WORDS, "SBUF arena overflow at %s: %d words" % (name, self.top)
        ap = self.arena[0:shape[0], off:off + words]
        if dt != F32:
            ap = ap.bitcast(dt)
        ap = ap[:, 0:n]
        if len(shape) == 3:
            ap = ap.rearrange("p (a b) -> p a b", b=int(shape[2]))
        elif len(shape) == 4:
            ap = ap.rearrange("p (a b c) -> p a b c", b=int(shape[2]), c=int(shape[3]))
        return ap

    def scope(self):
        return _Scope(self)

    def dram(self, name, shape, dt):
        if name not in self.scratch:
            kind = "ExternalOutput" if name in getattr(self, "debug_out", ()) else "Internal"
            self.scratch[name] = self.nc.dram_tensor(name, list(shape), dt, kind=kind).ap()
        return self.scratch[name]


class _Scope:
    def __init__(self, C):
        self.C = C

    def __enter__(self):
        self.mark = self.C.top
        return self

    def __exit__(self, *a):
        self.C.P.barrier()
        self.C.top = self.mark
        return False


def ffn_phase(C, tag, xT_in, xT_out, nw, wg, wu, wd, bufs, T=512):
    P = C.P
    NT = S // T
    NF = DFF // 128
    wg_sb, wu_sb, wd_sb = bufs["wg"], bufs["wu"], bufs["wd"]
    nw_sb = bufs["nw"]
    ones = bufs["ones"]
    xin = xT_in.rearrange("(c p) t -> p c t", p=128)
    xout = xT_out.rearrange("(c p) t -> p c t", p=128)

    P.dma("sp", nw_sb[:, :], nw, writes=["nw"])
    wgv = wg.rearrange("(c p) f -> p c f", p=128)
    wuv = wu.rearrange("(c p) f -> p c f", p=128)
    wdv = wd.rearrange("(j p) d -> p j d", p=128)
    si = [0]

    def load_cast(dst, src, key):
        P.dma("pool", dst, src, writes=[key])

    wgk, wuk, wdk = ["wg"], ["wu"], ["wd"]
    if CAST_DMA:
        H = DFF // 2
        wgk, wuk, wdk = [], [], []
        for c in range(8):
            for hh in range(2):
                wgk.append(("wg", c, hh))
                load_cast(wg_sb[:, c, hh * H:(hh + 1) * H], wgv[:, c, hh * H:(hh + 1) * H], wgk[-1])
            for hh in range(2):
                wuk.append(("wu", c, hh))
                load_cast(wu_sb[:, c, hh * H:(hh + 1) * H], wuv[:, c, hh * H:(hh + 1) * H], wuk[-1])
        for j in range(0, NF, 2):
            wdk.append(("wd", j))
            load_cast(wd_sb[:, j:j + 2, :], wdv[:, j:j + 2, :], wdk[-1])
    else:
        st_ = Stager(C, tag + "_stg", cols=DFF // 8)
        for c in range(8):
            st_.load(wg_sb[:, c, :], wgv[:, c, :], "wg")
            st_.load(wu_sb[:, c, :], wuv[:, c, :], "wu")
        for j in range(NF):
            st_.load(wd_sb[:, j, :], wdv[:, j, :], "wd")

    xt, hT, aT, rs = bufs["xt"], bufs["hT"], bufs["aT"], bufs["rs"]
    sq2 = bufs["sq2"]
    sg = bufs["sg"]
    ps = C.psum

    def tsl_(it):
        return slice(it * T, (it + 1) * T)

    def load_x(it):
        b = it % 2
        P.dma("sp", xt[b][:, :, :], xin[:, :, tsl_(it)], writes=[("xt", b)])

    def sq_op(it, c):
        b = it % 2
        P.op("act", lambda e: e.activation(out=sq2[:, c % 2, :], in_=xt[b][:, c, :], func=AF.Square),
             reads=[("xt", b)], writes=[("sq2", c % 2)])

    def ones_mm(it, c):
        P.op("pe", lambda e: e.matmul(ps[0][:, :T], lhsT=ones[:, :], rhs=sq2[:, c % 2, :], start=(c == 0), stop=(c == 7)),
             reads=[("sq2", c % 2), "ones"], writes=[("ps", 0)])

    def norm_back(it):
        b = it % 2
        P.op("act", lambda e: e.activation(out=rs[:, :], in_=ps[0][:, :T], func=AF.Sqrt, scale=1.0 / D,
                                           bias=bufs["eps"][:, 0:1]), reads=[("ps", 0), "eps"], writes=["rs"])
        P.op("dve", lambda e: e.reciprocal(out=rs[:, :], in_=rs[:, :]), reads=["rs"], writes=["rs"])
        for c in range(8):
            P.op("dve", lambda e, c=c: e.scalar_tensor_tensor(
                out=hT[:, c, :], in0=xt[b][:, c, :], scalar=nw_sb[:, c:c + 1], in1=rs[:, :],
                op0=ALU.mult, op1=ALU.mult), reads=[("xt", b), "rs", "nw"], writes=["hT"])

    load_x(0)
    for c in range(8):
        sq_op(0, c)
        ones_mm(0, c)
    norm_back(0)
    for it in range(NT):
        b = it % 2
        nxt = it + 1 < NT
        if nxt:
            load_x(it + 1)
        for j in range(NF):
            pg = 1 + (j % 2) * 2
            pu = pg + 1
            fs = slice(j * 128, (j + 1) * 128)
            for c in range(8):
                P.op("pe", lambda e, c=c, fs=fs, pg=pg: e.matmul(ps[pg][:, :T], lhsT=wg_sb[:, c, fs], rhs=hT[:, c, :],
                                                                   start=(c == 0), stop=(c == 7)),
                     reads=wgk + ["hT"], writes=[("ps", pg)])
            for c in range(8):
                P.op("pe", lambda e, c=c, fs=fs, pu=pu: e.matmul(ps[pu][:, :T], lhsT=wu_sb[:, c, fs], rhs=hT[:, c, :],
                                                                   start=(c == 0), stop=(c == 7)),
                     reads=wuk + ["hT"], writes=[("ps", pu)])
            k = j % 2
            P.op("act", lambda e, pg=pg, k=k: e.activation(out=sg[k][:, :], in_=ps[pg][:, :T], func=AF.Silu),
                 reads=[("ps", pg)], writes=[("sg", k)])
            P.op("dve", lambda e, pu=pu, k=k, j=j: e.tensor_tensor(out=aT[:, j, :], in0=ps[pu][:, :T], in1=sg[k][:, :],
                                                                   op=ALU.mult),
                 reads=[("ps", pu), ("sg", k)], writes=[("aT", j)])
        if nxt:
            sq_op(it + 1, 0)
            sq_op(it + 1, 1)
        for i in range(8):
            py = 5 + (i % 2)
            ds = slice(i * 128, (i + 1) * 128)
            for j in range(NF):
                P.op("pe", lambda e, j=j, ds=ds, py=py: e.matmul(ps[py][:, :T], lhsT=wd_sb[:, j, ds], rhs=aT[:, j, :],
                                                                   start=(j == 0), stop=(j == NF - 1)),
                     reads=wdk + [("aT", j)], writes=[("ps", py)])
            P.op("dve", lambda e, i=i, py=py, b=b: e.scalar_tensor_tensor(
                out=xt[b][:, i, :], in0=ps[py][:, :T], scalar=0.5, in1=xt[b][:, i, :],
                op0=ALU.mult, op1=ALU.add),
                reads=[("ps", py), ("xt", b)], writes=[("xt", b)])
            if nxt and i < 4:
                ones_mm(it + 1, 2 * i)
                ones_mm(it + 1, 2 * i + 1)
                if i < 3:
                    sq_op(it + 1, 2 * i + 2)
                    sq_op(it + 1, 2 * i + 3)
                else:
                    norm_back(it + 1)
        P.dma("sp", xout[:, :, tsl_(it)], xt[b][:, :, :], reads=[("xt", b)], writes=[(tag, "out", it)])


def alloc_ffn_bufs(C, cst, T=512):
    b = {}
    b["wg"] = C.sb("wg_sb", [128, 8, DFF], BF16)
    b["wu"] = C.sb("wu_sb", [128, 8, DFF], BF16)
    b["wd"] = C.sb("wd_sb", [128, DFF // 128, D], BF16)
    b["nw"] = C.sb("nw_sb", [128, 8], F32)
    b["xt"] = [C.sb("xt%d" % i, [128, 8, T], F32) for i in range(2)]
    b["hT"] = C.sb("hT", [128, 8, T], BF16)
    b["aT"] = C.sb("aT", [128, DFF // 128, T], BF16)
    b["rs"] = C.sb("rs", [128, T], F32)
    b["sg"] = [C.sb("sg%d" % i, [128, T], F32) for i in range(2)]
    b["sq2"] = C.sb("sq2", [128, 2, T], BF16)
    b["ones"] = cst["ones"]
    b["eps"] = cst["eps"]
    return b


class Stager:
    def __init__(self, C, name, cols=704, n=2):
        self.C = C

    def load(self, dst, src, key, np_=128):
        P = self.C.P
        n = src.shape[-1]
        step = 1536
        for c0 in range(0, n, step):
            c1 = min(n, c0 + step)
            P.dma("pool", dst[:, c0:c1], src[:, c0:c1], multi=[key])


def emit_rmsnorm(C, xt, xkey, nw_sb, sq, sqkeys, hT, rs, cst, T, psb=0, hkey="hT"):
    P = C.P
    ps = C.psum
    P.op("act", lambda e: e.activation(out=sq[:, 0:8, :], in_=xt[:, :, :], func=AF.Square),
         reads=[xkey], writes=list(sqkeys))
    for c in range(8):
        P.op("pe", lambda e, c=c: e.matmul(ps[psb][:, :T], lhsT=cst["ones"][:, :], rhs=sq[:, c, :],
                                             start=(c == 0), stop=(c == 7)),
             reads=[sqkeys[c], "ones"], writes=[("ps", psb)])
    P.op("act", lambda e: e.activation(out=rs[:, :], in_=ps[psb][:, :T], func=AF.Sqrt,
                                       scale=1.0 / D, bias=cst["eps"][:, 0:1]),
         reads=[("ps", psb), "eps"], writes=["rs"])
    P.op("dve", lambda e: e.reciprocal(out=rs[:, :], in_=rs[:, :]), reads=["rs"], writes=["rs"])
    for c in range(8):
        P.op("dve", lambda e, c=c: e.scalar_tensor_tensor(
            out=hT[:, c, :], in0=xt[:, c, :], scalar=nw_sb[:, c:c + 1], in1=rs[:, :],
            op0=ALU.mult, op1=ALU.mult),
            reads=[xkey, "rs", "nw"], writes=[hkey])


def alloc_consts(C, cin):
    P = C.P
    cst = {}
    cst["ones"] = C.sb("ones", [128, 128], BF16)
    cst["eps"] = C.sb("epsc", [128, 1], F32)
    cst["one"] = C.sb("onec", [128, 1], F32)
    cst["identb"] = C.sb("identb", [128, 128], BF16)
    cst["identf"] = C.sb("identf", [128, 128], F32)
    cst["maskT"] = C.sb("maskT", [128, 128], F32)
    cst["maskTs"] = C.sb("maskTs", [128, 128], F32)
    cst["triu"] = C.sb("triu", [128, 128], BF16)
    P.op("pool", lambda e: e.memset(cst["ones"][:, :], 1.0), writes=["ones"])
    P.op("pool", lambda e: e.memset(cst["eps"][:, :], EPS), writes=["eps"])
    P.op("pool", lambda e: e.memset(cst["one"][:, :], 1.0), writes=["one"])
    P.dma("sp", cst["identf"][:, :], cin["identf"], writes=["identf"])
    P.dma("sp", cst["maskT"][:, :], cin["maskT"], writes=["maskT"])
    P.dma("sp", cst["maskTs"][:, :], cin["maskTs"], writes=["maskTs"])
    P.op("pool", lambda e: e.tensor_copy(out=cst["identb"][:, :], in_=cst["identf"][:, :]),
         reads=["identf"], writes=["identb"])
    cst["tmpf"] = C.sb("tmpf", [128, 128], F32)
    P.dma("sp", cst["tmpf"][:, :], cin["triu"], writes=["tmpf"])
    P.op("pool", lambda e: e.tensor_copy(out=cst["triu"][:, :], in_=cst["tmpf"][:, :]),
         reads=["tmpf"], writes=["triu"])
    return cst


def host_consts():
    i = np.arange(128)
    c = {}
    c["identf"] = np.eye(128, dtype=np.float32)
    c["maskT"] = (i[:, None] <= i[None, :]).astype(np.float32)
    c["maskTs"] = (i[:, None] < i[None, :]).astype(np.float32)
    c["triu"] = (i[:, None] > i[None, :]).astype(np.float32)
    c["invc"] = np.broadcast_to((1.0 / (np.arange(16) + 1.0)).astype(np.float32)[None, :], (128, 16)).copy()
    oh = np.zeros((4, 4, 128), np.float32)
    for h in range(4):
        oh[h, h, :] = 1.0
    c["onehot4"] = oh.reshape(4, 512)
    oh = np.zeros((16, 16, 128), np.float32)
    for h in range(16):
        oh[h, h, :] = 1.0
    c["onehot16"] = oh.reshape(16, 2048)
    c["blk1"] = ((i[:, None] // 64) == (i[None, :] // 64)).astype(np.float32)
    c["negm"] = np.where(i[:, None] > i[None, :], -30000.0, 0.0).astype(np.float32)
    return c


def conv_silu_pe(C, cst, tagp, src, nch, cw, cb, sink):
    P = C.P
    ps = C.psum
    xrow = [C.sb("%s_xrow%d" % (tagp, i), [128, 3 + S], BF16) for i in range(2)]
    dg = [C.sb("%s_dg%d" % (tagp, i), [128, 4, 128], BF16) for i in range(2)]
    for k in range(2):
        P.op("pool", lambda e, k=k: e.memset(xrow[k][:, 0:3], 0.0), writes=[(tagp, "xrow", k)])
    n = 0
    for m in range(nch):
        k = m % 2
        for h_ in range(4):
            P.dma("pool", xrow[k][:, 3 + h_ * 1024:3 + (h_ + 1) * 1024], src[m * 128:(m + 1) * 128, h_ * 1024:(h_ + 1) * 1024],
                  multi=[(tagp, "xrow", k)])
        for j in range(4):
            P.op("dve", lambda e, k=k, m=m, j=j: e.tensor_scalar(out=dg[k][:, j, :], in0=cst["identf"][:, :],
                                                                 scalar1=cw[:, m, j:j + 1], scalar2=None, op0=ALU.mult),
                 reads=["identf", "cw"], writes=[(tagp, "dg", k)])
        for it in range(S // 512):
            pb = 1 + n % 4
            n += 1
            for j in range(4):
                P.op("pe", lambda e, k=k, j=j, it=it, pb=pb: e.matmul(
                    ps[pb][:, :512], lhsT=dg[k][:, j, :], rhs=xrow[k][:, j + it * 512:j + it * 512 + 512],
                    start=(j == 0), stop=(j == 3)), reads=[(tagp, "dg", k), (tagp, "xrow", k)], writes=[("ps", pb)])
            sink(m, it, ps[pb][:, :512], cb[:, m:m + 1], ("ps", pb))


def evac(P, eng, out, in_, reads, writes):
    if eng == "act":
        return P.op("act", lambda e: e.activation(out=out, in_=in_, func=AF.Copy), reads=reads, writes=writes)
    return P.op(eng, lambda e: e.tensor_copy(out=out, in_=in_), reads=reads, writes=writes)


def proj_norm_tiles(C, cst, xT_in, nw_dram, T, body):
    P = C.P
    xin = xT_in.rearrange("(c p) t -> p c t", p=128)
    nw_sb = C.sb("pn_nw", [128, 8], F32)
    xt = [C.sb("pn_xt%d" % i, [128, 8, T], F32) for i in range(2)]
    sq = C.sb("pn_sq", [128, 8, T], BF16)
    hT = [C.sb("pn_hT%d" % i, [128, 8, T], BF16) for i in range(2)]
    rs = C.sb("pn_rs", [128, T], F32)
    P.dma("sp", nw_sb[:, :], nw_dram, writes=["nw"])
    NT = S // T

    def norm(it):
        b = it % 2
        P.dma("sp", xt[b][:, :, :], xin[:, :, it * T:(it + 1) * T], writes=[("pn_xt", b)])
        emit_rmsnorm(C, xt[b], ("pn_xt", b), nw_sb, sq, [("pn_sq", c) for c in range(8)], hT[b], rs, cst, T,
                     hkey=("hT", b))

    norm(0)
    for it in range(NT):
        if it + 1 < NT:
            norm(it + 1)
        body(it, hT[it % 2], ("hT", it % 2))


def even_mixer_phase(C, cst, cin, xT_in, xT_out, W, upto="E"):
    P = C.P
    ps = C.psum
    T = 512
    uT = C.dram("e_uT", [512, S], F32)
    qkT = C.dram("e_qkT", [1024, S], F32)
    v_tm = C.dram("e_v", [S, 512], BF16)
    o_tm = C.dram("e_o", [S, 512], F32)
    gT = C.dram("e_g", [8, S], F32)
    mixT = C.dram("e_mix", [1024, S], BF16)

    ws_tm = C.sb("e_ws", [128, 32, 4], F32)
    thr_tm = C.sb("e_thr", [128, 32, 4], F32)
    dcol = C.sb("e_dcol", [128, 4, 32], F32)

    with C.scope():
        w_sb = C.sb("e_win", [128, 8, 2568], BF16)
        stg = Stager(C, "e_stg")
        win = W["w_in"].rearrange("(c p) f -> p c f", p=128)
        for c in range(8):
            stg.load(w_sb[:, c, :], win[:, c, :], "win")
        fm = [C.sb("e_fm%d" % i, [128, T], F32) for i in range(6)]
        vst = [C.sb("e_vst%d" % i, [128, 512], BF16) for i in range(4)]
        ost = [C.sb("e_ost%d" % i, [128, 512], F32) for i in range(4)]
        gst = [C.sb("e_gst%d" % i, [4, T], F32) for i in range(2)]
        cnt = {"fm": 0, "tm": 0, "g": 0}

        def body(it, hT, hk):
            tsl = slice(it * T, (it + 1) * T)
            for m in range(12):
                pb = 1 + m % 4
                for c in range(8):
                    P.op("pe", lambda e, c=c, m=m, pb=pb: e.matmul(
                        ps[pb][:, :T], lhsT=w_sb[:, c, m * 128:(m + 1) * 128], rhs=hT[:, c, :],
                        start=(c == 0), stop=(c == 7)), reads=["win", hk], writes=[("ps", pb)])
                k = cnt["fm"] % 6
                cnt["fm"] += 1
                evac(P, "act" if m % 2 == 0 else "dve", fm[k][:, :], ps[pb][:, :T], [("ps", pb)], [("fm", k)])
                dst = uT[m * 128:(m + 1) * 128, tsl] if m < 4 else qkT[(m - 4) * 128:(m - 3) * 128, tsl]
                P.dma("sp", dst, fm[k][:, :], reads=[("fm", k)], writes=[("A_out", m, it)])
            for q in range(4):
                tok = slice(q * 128, (q + 1) * 128)
                r0 = it * T + q * 128
                for which, col0, pb, stb, dstT in (("v", 1536, 5, vst, v_tm), ("o", 2048, 6, ost, o_tm)):
                    for c in range(8):
                        P.op("pe", lambda e, c=c, tok=tok, col0=col0, pb=pb: e.matmul(
                            ps[pb][:, :512], lhsT=hT[:, c, tok], rhs=w_sb[:, c, col0:col0 + 512],
                            start=(c == 0), stop=(c == 7)), reads=["win", hk], writes=[("ps", pb)])
                    k = q % 4
                    evac(P, "act" if which == "v" else "dve", stb[k][:, :], ps[pb][:, :512],
                         [("ps", pb)], [(which + "st", k)])
                    P.dma("sp", dstT[r0:r0 + 128, :], stb[k][:, :], reads=[(which + "st", k)],
                          writes=[("A_out", which, r0)])
            for gi_ in range(2):
                col0 = 2560 + 4 * gi_
                for c in range(8):
                    P.op("pe", lambda e, c=c, col0=col0: e.matmul(
                        ps[7][0:4, :T], lhsT=w_sb[:, c, col0:col0 + 4], rhs=hT[:, c, :],
                        start=(c == 0), stop=(c == 7)), reads=["win", hk], writes=[("ps", 7)])
                evac(P, "dve", gst[gi_][:, :], ps[7][0:4, :T], [("ps", 7)], [("gst", gi_)])
                P.dma("sp", gT[4 * gi_:4 * gi_ + 4, tsl], gst[gi_][:, :], reads=[("gst", gi_)],
                      writes=[("A_out", "g", gi_, it)])

        proj_norm_tiles(C, cst, xT_in, W["mix_norm"], T, body)

    if upto == "A":
        return
    with C.scope():
        PADL = 16
        ub = C.sb("e_ub", [128, PADL + S], F32)
        sA = C.sb("e_sA", [128, PADL + S], F32)
        sB = C.sb("e_sB", [128, PADL + S], F32)
        pooled = C.sb("e_pooled", [128, S], BF16)
        aout = C.sb("e_aout", [128, S], BF16)
        pw_sb = C.sb("e_pw", [128, 4, 128], BF16)
        psc = C.sb("e_psc", [128, 4], F32)
        invc = C.sb("e_invc", [128, 16], F32)
        tmpc = C.sb("e_tmpc", [128, 16], F32)
        stg = Stager(C, "e_stgB", cols=512)
        stg.load(pw_sb.rearrange("p a b -> p (a b)"), W["pool_w"], "pw")
        P.dma("sp", psc[:, :], W["pool_scale"], writes=["psc"])
        P.dma("sp", invc[:, :], cin["invc"], writes=["invc"])
        for bname, buf in (("ub", ub), ("sA", sA), ("sB", sB)):
            P.op("pool", lambda e, buf=buf: e.memset(buf[:, 0:PADL], 0.0), writes=[bname])
        for g in range(4):
            win_ = 2 << g
            P.dma("sp", ub[:, PADL:], uT[g * 128:(g + 1) * 128, :], reads=["ub"], writes=["ub"])
            src, sname = ub, "ub"
            dsts = [(sA, "sA"), (sB, "sB")]
            for k in range(g + 1):
                sh = 1 << k
                dst, dname = dsts[k % 2]
                P.op("dve", lambda e, src=src, dst=dst, sh=sh: e.tensor_tensor(
                    out=dst[:, PADL:], in0=src[:, PADL:], in1=src[:, PADL - sh:PADL - sh + S], op=ALU.add),
                    reads=[sname], writes=[dname])
                src, sname = dst, dname
            P.op("dve", lambda e, src=src, win_=win_: e.scalar_tensor_tensor(
                out=pooled[:, :], in0=src[:, PADL:], scalar=1.0 / win_, in1=ub[:, PADL:],
                op0=ALU.mult, op1=ALU.subtract), reads=[sname, "ub"], writes=["pooled"])
            nfix = win_ - 1
            P.op("dve", lambda e, src=src, nfix=nfix: e.tensor_tensor(
                out=tmpc[:, 0:nfix], in0=src[:, PADL:PADL + nfix], in1=invc[:, 0:nfix], op=ALU.mult),
                reads=[sname, "invc"], writes=["tmpc"])
            P.op("dve", lambda e, nfix=nfix: e.tensor_tensor(
                out=pooled[:, 0:nfix], in0=tmpc[:, 0:nfix], in1=ub[:, PADL:PADL + nfix], op=ALU.subtract),
                reads=["tmpc", "ub", "pooled"], writes=["pooled"])
            for it in range(S // T):
                pb = 1 + it % 2
                P.op("pe", lambda e, g=g, it=it, pb=pb: e.matmul(
                    ps[pb][:, :T], lhsT=pw_sb[:, g, :], rhs=pooled[:, it * T:(it + 1) * T], start=True, stop=True),
                    reads=["pw", "pooled"], writes=[("ps", pb)])
                P.op("dve", lambda e, g=g, it=it, pb=pb: e.tensor_scalar(
                    out=aout[:, it * T:(it + 1) * T], in0=ps[pb][:, :T], scalar1=psc[:, g:g + 1], scalar2=None,
                    op0=ALU.mult), reads=[("ps", pb), "psc"], writes=["aout"])
            P.dma("sp", mixT[g * 128:(g + 1) * 128, :], aout[:, :], reads=["aout"], writes=[("mixA", g)])

    if upto == "B":
        return
    with C.scope():
        gi = C.sb("e_gi", [4, S], F32)
        gf = C.sb("e_gf", [4, S], F32)
        Bc = C.sb("e_Bc", [4, S], F32)
        Ac = C.sb("e_Ac", [4, S], F32)
        Gm = C.sb("e_Gm", [4, S], F32)
        gb = C.sb("e_gb", [4, 2], F32)
        nbf = C.sb("e_nbf", [4, 1], F32)
        mucol = C.sb("e_mucol", [4, 33], F32)
        dd = C.sb("e_dd", [4, 32], F32)
        oh4 = C.sb("e_oh4", [4, 4, 128], F32)
        P.dma("sp", gi[:, :], gT[0:4, :], writes=["gi"])
        P.dma("sp", gf[:, :], gT[4:8, :], writes=["gf"])
        P.dma("sp", gb[:, :], W["gate_bias"], writes=["gb"])
        P.dma("sp", oh4.rearrange("p a b -> p (a b)"), cin["onehot4"], writes=["oh4"])
        one4 = cst["one"][0:4, 0:1]
        P.op("dve", lambda e: e.tensor_scalar(out=gi[:, :], in0=gi[:, :], scalar1=gb[:, 0:1], scalar2=None, op0=ALU.add),
             reads=["gi", "gb"], writes=["gi"])
        P.op("dve", lambda e: e.tensor_scalar(out=nbf[:, :], in0=gb[:, 1:2], scalar1=-1.0, scalar2=None, op0=ALU.mult),
             reads=["gb"], writes=["nbf"])
        P.op("act", lambda e: e.activation(out=gf[:, :], in_=gf[:, :], func=AF.Exp, scale=-1.0, bias=nbf[:, 0:1]),
             reads=["gf", "nbf"], writes=["gf"])
        P.op("act", lambda e: e.activation(out=gf[:, :], in_=gf[:, :], func=AF.Ln, scale=1.0, bias=one4),
             reads=["gf", "one"], writes=["gf"])
        P.op("dve", lambda e: e.tensor_tensor_scan(out=Bc[:, :], data0=one4.to_broadcast([4, S]), data1=gf[:, :],
                                                   initial=0.0, op0=ALU.mult, op1=ALU.subtract),
             reads=["gf", "one"], writes=["Bc"])
        P.op("dve", lambda e: e.tensor_tensor(out=Ac[:, :], in0=gi[:, :], in1=Bc[:, :], op=ALU.subtract),
             reads=["gi", "Bc"], writes=["Ac"])
        P.op("dve", lambda e: e.tensor_tensor_scan(out=Gm[:, :], data0=Ac[:, :], data1=Ac[:, :],
                                                   initial=0.0, op0=ALU.max, op1=ALU.max),
             reads=["Ac"], writes=["Gm"])
        Gend = Gm.rearrange("h (c l) -> h c l", l=128)[:, :, 127:128]
        P.op("dve", lambda e: e.memset(mucol[:, 0:1], 0.0), writes=["mucol"])
        P.op("dve", lambda e: e.tensor_copy(out=mucol[:, 1:33].unsqueeze(2), in_=Gend), reads=["Gm", "mucol"],
             writes=["mucol"])
        P.op("dve", lambda e: e.tensor_tensor(out=dd[:, :], in0=mucol[:, 0:32], in1=mucol[:, 1:33], op=ALU.subtract),
             reads=["mucol"], writes=["dd"])
        P.op("act", lambda e: e.activation(out=dd[:, :], in_=dd[:, :], func=AF.Exp), reads=["dd"], writes=["dd"])
        A3 = Ac.rearrange("h (c l) -> h c l", l=128)
        B3 = Bc.rearrange("h (c l) -> h c l", l=128)
        P.op("dve", lambda e: e.tensor_tensor(out=A3, in0=A3, in1=Gend.to_broadcast([4, 32, 128]), op=ALU.subtract),
             reads=["Ac", "Gm"], writes=["Ac"])
        P.op("act", lambda e: e.activation(out=Ac[:, :], in_=Ac[:, :], func=AF.Exp), reads=["Ac"], writes=["Ac"])
        P.op("dve", lambda e: e.tensor_tensor(out=B3, in0=B3, in1=Gend.to_broadcast([4, 32, 128]), op=ALU.add),
             reads=["Bc", "Gm"], writes=["Bc"])
        P.op("act", lambda e: e.activation(out=Bc[:, :], in_=Bc[:, :], func=AF.Exp, scale=-1.0), reads=["Bc"],
             writes=["Bc"])
        for h in range(4):
            P.op("pe", lambda e, h=h: e.matmul(ps[1][:, h * 32:(h + 1) * 32], lhsT=oh4[:, h, :], rhs=dd[:, :],
                                                start=True, stop=True), reads=["oh4", "dd"], writes=[("ps", 1)])
        P.op("dve", lambda e: e.tensor_copy(out=dcol.rearrange("p a b -> p (a b)"), in_=ps[1][:, 0:128]),
             reads=[("ps", 1)], writes=["dcol"])
        for src, sname, dst, dname, pb in ((Ac, "Ac", ws_tm, "ws_tm", 2), (Bc, "Bc", thr_tm, "thr_tm", 3)):
            for c in range(32):
                P.op("pe", lambda e, src=src, c=c, pb=pb: e.transpose(
                    out=ps[pb][:, c * 4:(c + 1) * 4], in_=src[:, c * 128:(c + 1) * 128], identity=cst["identf"][0:4, 0:4]),
                    reads=[sname, "identf"], writes=[("ps", pb)])
            P.op("dve", lambda e, dst=dst, pb=pb: e.tensor_copy(out=dst.rearrange("p a b -> p (a b)"),
                                                               in_=ps[pb][:, 0:128]),
                 reads=[("ps", pb)], writes=[dname])

    if upto == "C":
        return
    with C.scope():
        qkb = C.sb("e_qkb", [128, 8, S], BF16)
        cw = C.sb("e_cw", [128, 8, 4], F32)
        cb = C.sb("e_cb", [128, 8], F32)
        qtmp = [C.sb("e_qtmp%d" % i, [128, 512], F32) for i in range(2)]
        P.dma("sp", cw.rearrange("p a b -> p (a b)"), W["qk_conv_w"], writes=["cw"])
        P.dma("sp", cb[:, :], W["qk_conv_b"], writes=["cb"])
        qcnt = [0]

        def sink_e(m, it, psap, bias, pkey):
            tsl = slice(it * 512, (it + 1) * 512)
            if m < 4:
                k = qcnt[0] % 2
                qcnt[0] += 1
                P.op("act", lambda e: e.activation(out=qtmp[k][:, :], in_=psap, func=AF.Silu, bias=bias),
                     reads=[pkey, "cb"], writes=[("qtmp", k)])
                P.op("dve", lambda e: e.tensor_scalar(out=qkb[:, m, tsl], in0=qtmp[k][:, :], scalar1=128.0 ** -0.5,
                                                       scalar2=None, op0=ALU.mult),
                     reads=[("qtmp", k)], writes=[("qkb", m)])
            else:
                P.op("act", lambda e: e.activation(out=qkb[:, m, tsl], in_=psap, func=AF.Silu, bias=bias),
                     reads=[pkey, "cb"], writes=[("qkb", m)])

        conv_silu_pe(C, cst, "ecv", qkT, 8, cw, cb, sink_e)

        if upto == "D0":
            return
        vch = [C.sb("e_vch%d" % i, [128, 4, 128], BF16) for i in range(2)]
        och = [C.sb("e_och%d" % i, [128, 512], F32) for i in range(2)]
        vw = [C.sb("e_vw%d" % i, [128, 4, 136], BF16) for i in range(2)]
        ktm = [C.sb("e_ktm%d" % i, [128, 4, 128], BF16) for i in range(2)]
        PT = C.sb("e_PT", [128, 4, 128], BF16)
        Sst = C.sb("e_S", [128, 4, 129], F32)
        Sbf = C.sb("e_Sbf", [128, 4, 136], BF16)
        nwb = C.sb("e_nwb", [128, 512], F32)
        nwo = C.sb("e_nwo", [128, 512], F32)
        den = C.sb("e_den", [128, 4], F32)
        ss = C.sb("e_ss", [128, 4], F32)
        junk = C.sb("e_junk", [128, 128], F32)
        bout = C.sb("e_bout", [128, 4, 128], BF16)
        boutT = [C.sb("e_boutT%d" % i, [128, 4, 128], BF16) for i in range(2)]
        P.dma("sp", nwb[:, :], W["mlstm_norm"].partition_broadcast(128), writes=["nwb"])
        P.op("pool", lambda e: e.memset(Sst.rearrange("p a b -> p (a b)"), 0.0), writes=["S0", "S1", "S2", "S3"])
        ps0b = ps[0][:, :].bitcast(BF16)
        pending_tail = []
        for c in range(1 if upto in ("D1", "D2", "D3") else 32):
            b = c % 2
            blk = slice(c * 128, (c + 1) * 128)
            P.dma("sp", vch[b].rearrange("p a b -> p (a b)"), v_tm[blk, :], writes=[("vch", b)])
            P.dma("sp", och[b][:, :], o_tm[blk, :], writes=[("och", b)])
            P.op("dve", lambda e, b=b, c=c: e.tensor_tensor(
                out=vw[b][:, :, 0:128], in0=vch[b][:, :, :],
                in1=ws_tm[:, c, :].unsqueeze(2).to_broadcast([128, 4, 128]), op=ALU.mult),
                reads=[("vch", b), "ws_tm"], writes=[("vw", b)])
            P.op("dve", lambda e, b=b, c=c: e.tensor_copy(out=vw[b][:, :, 128:129], in_=ws_tm[:, c, :].unsqueeze(2)),
                 reads=["ws_tm", ("vw", b)], writes=[("vw", b)])
            for h in range(4):
                P.op("pe", lambda e, h=h, blk=blk: e.transpose(out=ps0b[:, h * 128:(h + 1) * 128], in_=qkb[:, 4 + h, blk],
                                                                identity=cst["identb"][:, :]),
                     reads=[("qkb", 4 + h), "identb"], writes=[("ps0", "k")])
            P.op("act", lambda e, b=b: e.activation(out=ktm[b].rearrange("p a b -> p (a b)"), in_=ps0b[:, 0:512],
                                                    func=AF.Copy), reads=[("ps0", "k")], writes=[("ktm", b)])
            sb_ = 1 + b
            for h in range(4):
                P.op("pe", lambda e, h=h, blk=blk, sb_=sb_: e.matmul(
                    ps[sb_][:, h * 128:(h + 1) * 128], lhsT=qkb[:, 4 + h, blk], rhs=qkb[:, h, blk],
                    start=True, stop=True), reads=[("qkb", 4 + h), ("qkb", h)], writes=[("ps", sb_)])
            P.op("dve", lambda e, sb_=sb_: e.tensor_tensor(
                out=PT[:, :, :], in0=ps[sb_][:, :].rearrange("p (a b) -> p a b", b=128),
                in1=cst["maskT"][:, :].unsqueeze(1).to_broadcast([128, 4, 128]), op=ALU.mult),
                reads=[("ps", sb_), "maskT"], writes=["PT"])
            if upto == "D1":
                continue
            for h in range(4):
                ob = 3 + h // 2
                oc = (h % 2) * 129
                P.op("dve", lambda e, h=h, c=c: e.tensor_scalar(out=Sbf[:, h, 0:129], in0=Sst[:, h, :],
                                                                 scalar1=dcol[:, h, c:c + 1], scalar2=None, op0=ALU.mult),
                     reads=["S%d" % h, "dcol"], writes=["Sbf%d" % h])
                P.op("pe", lambda e, h=h, blk=blk, ob=ob, oc=oc: e.matmul(
                    ps[ob][:, oc:oc + 129], lhsT=qkb[:, h, blk], rhs=Sbf[:, h, 0:129], start=True, stop=False),
                    reads=[("qkb", h), "Sbf%d" % h], writes=[("pso", h)])
                P.op("pe", lambda e, h=h, b=b, ob=ob, oc=oc: e.matmul(
                    ps[ob][:, oc:oc + 129], lhsT=PT[:, h, :], rhs=vw[b][:, h, 0:129], start=False, stop=True),
                    reads=["PT", ("vw", b)], writes=[("pso", h)])
                db = 5 + h // 2
                P.op("pe", lambda e, h=h, b=b, db=db, oc=oc: e.matmul(
                    ps[db][:, oc:oc + 129], lhsT=ktm[b][:, h, :], rhs=vw[b][:, h, 0:129], start=True, stop=True),
                    reads=[("ktm", b), ("vw", b)], writes=[("psd", h)])
                P.op("dve", lambda e, h=h, c=c, db=db, oc=oc: e.scalar_tensor_tensor(
                    out=Sst[:, h, :], in0=Sst[:, h, :], scalar=dcol[:, h, c:c + 1], in1=ps[db][:, oc:oc + 129],
                    op0=ALU.mult, op1=ALU.add), reads=["S%d" % h, "dcol", ("psd", h), "Sbf%d" % h], writes=["S%d" % h])
            if upto == "D2":
                continue
            while pending_tail:
                pending_tail.pop(0)()
            P.op("act", lambda e, b=b: e.activation(out=nwo[:, :], in_=och[b][:, :], func=AF.Sigmoid),
                 reads=[("och", b)], writes=["nwo"])
            P.op("dve", lambda e: e.tensor_tensor(out=nwo[:, :], in0=nwo[:, :], in1=nwb[:, :], op=ALU.mult),
                 reads=["nwo", "nwb"], writes=["nwo"])
            for hp in range(2):
                ob = 3 + hp
                Dv = ps[ob][:, 0:258].rearrange("p (a b) -> p a b", b=129)[:, :, 128:129]
                P.op("act", lambda e, hp=hp, Dv=Dv: e.activation(
                    out=den[:, 2 * hp:2 * hp + 2].unsqueeze(2), in_=Dv, func=AF.Abs),
                    reads=[("pso", 2 * hp), ("pso", 2 * hp + 1)], writes=["den"])
            P.op("dve", lambda e, c=c: e.tensor_tensor(out=den[:, :], in0=den[:, :], in1=thr_tm[:, c, :], op=ALU.max),
                 reads=["den", "thr_tm"], writes=["den"])
            P.op("dve", lambda e: e.reciprocal(out=den[:, :], in_=den[:, :]), reads=["den"], writes=["den"])
            for h in range(4):
                ob = 3 + h // 2
                oc = (h % 2) * 129
                P.op("act", lambda e, h=h, ob=ob, oc=oc: e.activation(
                    out=junk[:, :], in_=ps[ob][:, oc:oc + 128], func=AF.Square, scale=den[:, h:h + 1],
                    accum_out=ss[:, h:h + 1]), reads=[("pso", h), "den"], writes=["junk", ("ss", h)])
            P.op("act", lambda e: e.activation(out=ss[:, :], in_=ss[:, :], func=AF.Sqrt, scale=1.0 / 128, bias=cst["eps"][:, 0:1]),
                 reads=[("ss", h) for h in range(4)] + ["eps"], writes=[("ss", h) for h in range(4)])
            P.op("dve", lambda e: e.reciprocal(out=ss[:, :], in_=ss[:, :]), reads=[("ss", h) for h in range(4)],
                 writes=[("ss", h) for h in range(4)])
            P.op("dve", lambda e: e.tensor_tensor(out=ss[:, :], in0=ss[:, :], in1=den[:, :], op=ALU.mult),
                 reads=[("ss", h) for h in range(4)] + ["den"], writes=[("ss", h) for h in range(4)])
            for h in range(4):
                ob = 3 + h // 2
                oc = (h % 2) * 129
                P.op("dve", lambda e, h=h, ob=ob, oc=oc: e.scalar_tensor_tensor(
                    out=bout[:, h, :], in0=ps[ob][:, oc:oc + 128], scalar=ss[:, h:h + 1], in1=nwo[:, h * 128:(h + 1) * 128],
                    op0=ALU.mult, op1=ALU.mult), reads=[("pso", h), ("ss", h), "nwo"], writes=[("bout", h)])
            def emit_tail(c=c, b=b, blk=blk):
                for h in range(4):
                    P.op("pe", lambda e, h=h: e.transpose(out=ps0b[:, 512 + h * 128:512 + (h + 1) * 128], in_=bout[:, h, :],
                                                          identity=cst["identb"][:, :]),
                         reads=[("bout", h), "identb"], writes=[("ps0", "b")])
                P.op("act", lambda e, b=b: e.activation(out=boutT[b].rearrange("p a b -> p (a b)"), in_=ps0b[:, 512:1024],
                                                        func=AF.Copy), reads=[("ps0", "b")], writes=[("boutT", b)])
                P.dma("sp", mixT[512:1024, blk].rearrange("(h p) t -> p h t", p=128), boutT[b][:, :, :],
                      reads=[("boutT", b)], writes=[("mixB", c)])
            pending_tail.append(emit_tail)
        for f_ in pending_tail:
            f_()

    if upto[0] == "D":
        return
    with C.scope():
        wo = C.sb("e_wo", [128, 8, D], BF16)
        stg = Stager(C, "e_stgE")
        wov = W["w_out"].rearrange("(c p) f -> p c f", p=128)
        for c in range(8):
            stg.load(wo[:, c, :], wov[:, c, :], "wo")
        out_proj(C, wo, 8, mixT, xT_in, xT_out, T)


def out_proj(C, wo, nk, mixT, xT_in, xT_out, T):
    P = C.P
    ps = C.psum
    xin = xT_in.rearrange("(c p) t -> p c t", p=128)
    xout = xT_out.rearrange("(c p) t -> p c t", p=128)
    mixv = mixT.rearrange("(c p) t -> p c t", p=128)
    xt = [C.sb("op_xt%d" % i, [128, 8, T], F32) for i in range(2)]
    mt = [C.sb("op_mt%d" % i, [128, nk, T], BF16) for i in range(2)]
    for it in range(S // T):
        b = it % 2
        tsl = slice(it * T, (it + 1) * T)
        P.dma("sp", xt[b][:, :, :], xin[:, :, tsl], writes=[("op_xt", b)])
        P.dma("sp", mt[b][:, :, :], mixv[:, :, tsl], writes=[("op_mt", b)])
        for i in range(8):
            pb = 1 + i % 4
            for k in range(nk):
                P.op("pe", lambda e, i=i, k=k, pb=pb, b=b: e.matmul(
                    ps[pb][:, :T], lhsT=wo[:, k, i * 128:(i + 1) * 128], rhs=mt[b][:, k, :],
                    start=(k == 0), stop=(k == nk - 1)), reads=["wo", ("op_mt", b)], writes=[("ps", pb)])
            P.op("dve", lambda e, i=i, pb=pb, b=b: e.tensor_tensor(
                out=xt[b][:, i, :], in0=ps[pb][:, :T], in1=xt[b][:, i, :], op=ALU.add),
                reads=[("ps", pb), ("op_xt", b)], writes=[("op_xt", b)])
        P.dma("sp", xout[:, :, tsl], xt[b][:, :, :], reads=[("op_xt", b)], writes=[("op_out", it)])


def _pc(v, nchunk):
    return np.ascontiguousarray(np.asarray(v, np.float32).reshape(nchunk, 128).T)


def host_params(inp):
    f = lambda k: np.ascontiguousarray(np.asarray(inp[k], np.float32))
    out = {}
    for pre in ("l0_ffn1", "l0_ffn2", "l1_ffn1", "l1_ffn2"):
        out[pre + "_norm"] = _pc(inp[pre + "_norm"], 8)
        for w in ("wg", "wu", "wd"):
            out[pre + "_" + w] = f(pre + "_" + w)
    out["l0_mix_norm"] = _pc(inp["l0_mix_norm"], 8)
    out["l0_w_in"] = f("l0_w_in")
    out["l0_pool_w"] = np.ascontiguousarray(np.transpose(f("l0_pool_w"), (1, 0, 2)).reshape(128, 512))
    out["l0_pool_scale"] = _pc(inp["l0_pool_scale"], 4)
    out["l0_qk_conv_w"] = np.ascontiguousarray(f("l0_qk_conv_w").reshape(4, 8, 128).transpose(2, 1, 0).reshape(128, 32))
    out["l0_qk_conv_b"] = _pc(inp["l0_qk_conv_b"], 8)
    out["l0_gate_bias"] = np.ascontiguousarray(f("l0_gate_bias").reshape(2, 4).T)
    out["l0_mlstm_norm"] = f("l0_mlstm_norm")
    out["l0_w_out"] = f("l0_w_out")
    out["l1_mix_norm"] = _pc(inp["l1_mix_norm"], 8)
    out["l1_w_in"] = f("l1_w_in")
    out["l1_ssd_conv_w"] = np.ascontiguousarray(f("l1_ssd_conv_w").reshape(4, 16, 128).transpose(2, 1, 0).reshape(128, 64))
    out["l1_ssd_conv_b"] = _pc(inp["l1_ssd_conv_b"], 16)
    out["l1_ssd_dt_bias"] = f("l1_ssd_dt_bias").reshape(16, 1)
    out["l1_ssd_A_log"] = f("l1_ssd_A_log").reshape(16, 1)
    out["l1_ssd_D"] = f("l1_ssd_D")
    out["l1_ssd_norm"] = f("l1_ssd_norm")
    out["l1_sb_q_norm"] = np.ascontiguousarray(np.tile(f("l1_sb_q_norm"), 2).reshape(128, 1))
    out["l1_sb_k_norm"] = np.ascontiguousarray(np.tile(f("l1_sb_k_norm"), 2).reshape(128, 1))
    out["l1_w_out"] = f("l1_w_out")
    return out


PARAM_SHAPES = {
    "l0_mix_norm": [128, 8], "l0_w_in": [1024, 2568], "l0_pool_w": [128, 512], "l0_pool_scale": [128, 4],
    "l0_qk_conv_w": [128, 32], "l0_qk_conv_b": [128, 8], "l0_gate_bias": [4, 2], "l0_mlstm_norm": [512],
    "l0_w_out": [1024, 1024],
    "l1_mix_norm": [128, 8], "l1_w_in": [1024, 4624], "l1_ssd_conv_w": [128, 64], "l1_ssd_conv_b": [128, 16],
    "l1_ssd_dt_bias": [16, 1], "l1_ssd_A_log": [16, 1], "l1_ssd_D": [16], "l1_ssd_norm": [1024],
    "l1_sb_q_norm": [128, 1], "l1_sb_k_norm": [128, 1], "l1_w_out": [1536, 1024],
}
for _pre in ("l0_ffn1", "l0_ffn2", "l1_ffn1", "l1_ffn2"):
    PARAM_SHAPES[_pre + "_norm"] = [128, 8]
    PARAM_SHAPES[_pre + "_wg"] = [D, DFF]
    PARAM_SHAPES[_pre + "_wu"] = [D, DFF]
    PARAM_SHAPES[_pre + "_wd"] = [DFF, D]
CONST_SHAPES = {"identf": [128, 128], "maskT": [128, 128], "maskTs": [128, 128], "triu": [128, 128],
                "invc": [128, 16], "onehot4": [4, 512], "onehot16": [16, 2048], "blk1": [128, 128], "negm": [128, 128]}


def odd_mixer_phase(C, cst, cin, xT_in, xT_out, W, upto="E"):
    P = C.P
    ps = C.psum
    T = 512
    o_z = C.dram("o_z", [S, 1024], F32)
    o_xbcT = C.dram("o_xbcT", [2048, S], F32)
    o_dtT = C.dram("o_dtT", [16, S], F32)
    o_qT = C.dram("o_qT", [512, S], BF16)
    o_kT = C.dram("o_kT", [512, S], BF16)
    o_v = C.dram("o_v", [S, 512], BF16)
    o_xc = C.dram("o_xc", [2048, S], BF16)
    mixT = C.dram("o_mix", [1536, S], BF16)

    bias_tm = C.sb("o_bias_tm", [128, 32, 16], F32)
    est_tm = C.sb("o_est_tm", [128, 32, 16], F32)
    dtte_tm = C.sb("o_dtte_tm", [128, 32, 16], F32)
    cdb = C.sb("o_cdb", [128, 32, 16], F32)
    blk1 = C.sb("o_blk1", [128, 128], BF16)
    negm = C.sb("o_negm", [128, 128], BF16)
    eps64 = C.sb("o_eps64", [128, 1], F32)
    P.op("pool", lambda e: e.memset(eps64[:, :], 64.0 * EPS), writes=["eps64"])
    P.dma("sp", cst["tmpf"][:, :], cin["blk1"], reads=["tmpf"], writes=["tmpf"])
    P.op("pool", lambda e: e.tensor_copy(out=blk1[:, :], in_=cst["tmpf"][:, :]), reads=["tmpf"], writes=["blk1"])
    P.dma("sp", cst["tmpf"][:, :], cin["negm"], reads=["tmpf"], writes=["tmpf"])
    P.op("pool", lambda e: e.tensor_copy(out=negm[:, :], in_=cst["tmpf"][:, :]), reads=["tmpf"], writes=["negm"])

    with C.scope():
        w_sb = C.sb("o_win", [128, 8, 4624], BF16)
        stg = Stager(C, "o_stg", cols=1156)
        win = W["w_in"].rearrange("(c p) f -> p c f", p=128)
        for c in range(8):
            stg.load(w_sb[:, c, :], win[:, c, :], "win")
        fm = [C.sb("o_fm%d" % i, [128, T], F32) for i in range(6)]
        zst = [C.sb("o_zst%d" % i, [128, 512], F32) for i in range(6)]
        vst = [C.sb("o_vst%d" % i, [128, 512], BF16) for i in range(4)]
        dst_ = C.sb("o_dst", [16, T], F32)
        sqb = [C.sb("o_sqb%d" % i, [128, T], BF16) for i in range(4)]
        rr = [C.sb("o_rr%d" % i, [128, T], F32) for i in range(2)]
        qn = [C.sb("o_qn%d" % i, [128, T], BF16) for i in range(4)]
        qw = C.sb("o_qw", [128, 2], F32)
        P.dma("sp", qw[:, 0:1], W["sb_q_norm"], writes=["qw"])
        P.dma("sp", qw[:, 1:2], W["sb_k_norm"], reads=["qw"], writes=["qw"])
        cnt = {"fm": 0, "qn": 0, "z": 0}

        def body(it, hT, hk):
            tsl = slice(it * T, (it + 1) * T)
            for m in range(16):
                pb = 1 + m % 2
                for c in range(8):
                    P.op("pe", lambda e, c=c, m=m, pb=pb: e.matmul(
                        ps[pb][:, :T], lhsT=w_sb[:, c, 1024 + m * 128:1024 + (m + 1) * 128], rhs=hT[:, c, :],
                        start=(c == 0), stop=(c == 7)), reads=["win", hk], writes=[("ps", pb)])
                k = cnt["fm"] % 6
                cnt["fm"] += 1
                evac(P, "act" if m % 2 == 0 else "dve", fm[k][:, :], ps[pb][:, :T], [("ps", pb)], [("fm", k)])
                P.dma("sp", o_xbcT[m * 128:(m + 1) * 128, tsl], fm[k][:, :], reads=[("fm", k)],
                      writes=[("A_out", "xbc", m, it)])
            def qk_proj(m):
                isq = m < 4
                col0 = (3088 if isq else 3600) + (m % 4) * 128
                pbq = 3 + m % 4
                kk = m % 4
                for c in range(8):
                    P.op("pe", lambda e, c=c, col0=col0, pbq=pbq: e.matmul(
                        ps[pbq][:, :T], lhsT=w_sb[:, c, col0:col0 + 128], rhs=hT[:, c, :],
                        start=(c == 0), stop=(c == 7)), reads=["win", hk], writes=[("ps", pbq)])
                P.op("act", lambda e, pbq=pbq, kk=kk: e.activation(out=sqb[kk][:, :], in_=ps[pbq][:, :T], func=AF.Square),
                     reads=[("ps", pbq)], writes=[("sqb", kk)])

            qk_proj(0)
            qk_proj(1)
            for m in range(8):
                isq = m < 4
                pbq = 3 + m % 4
                pbs = 1 + m % 2
                kk = m % 4
                k2 = m % 2
                P.op("pe", lambda e, pbs=pbs, kk=kk: e.matmul(ps[pbs][:, :T], lhsT=blk1[:, :], rhs=sqb[kk][:, :],
                                                              start=True, stop=True),
                     reads=["blk1", ("sqb", kk)], writes=[("ps", pbs)])
                if isq:
                    P.op("act", lambda e, pbs=pbs, k2=k2: e.activation(out=rr[k2][:, :], in_=ps[pbs][:, :T], func=AF.Sqrt,
                                                                       scale=1.0, bias=eps64[:, 0:1]),
                         reads=[("ps", pbs), "eps64"], writes=[("rr", k2)])
                else:
                    P.op("act", lambda e, pbs=pbs, k2=k2: e.activation(out=rr[k2][:, :], in_=ps[pbs][:, :T], func=AF.Sqrt,
                                                                       scale=1.0 / 64, bias=cst["eps"][:, 0:1]),
                         reads=[("ps", pbs), "eps"], writes=[("rr", k2)])
                P.op("dve", lambda e, k2=k2: e.reciprocal(out=rr[k2][:, :], in_=rr[k2][:, :]), reads=[("rr", k2)],
                     writes=[("rr", k2)])
                k = cnt["qn"] % 4
                cnt["qn"] += 1
                wi = 0 if isq else 1
                P.op("dve", lambda e, k=k, wi=wi, pbq=pbq, k2=k2: e.scalar_tensor_tensor(
                    out=qn[k][:, :], in0=ps[pbq][:, :T], scalar=qw[:, wi:wi + 1], in1=rr[k2][:, :], op0=ALU.mult, op1=ALU.mult),
                    reads=[("ps", pbq), "qw", ("rr", k2)], writes=[("qn", k)])
                dd_ = (o_qT if isq else o_kT)[(m % 4) * 128:(m % 4 + 1) * 128, tsl]
                P.dma("sp", dd_, qn[k][:, :], reads=[("qn", k)], writes=[("A_out", "qk", m, it)])
                if m + 2 < 8:
                    qk_proj(m + 2)
            for q in range(4):
                tok = slice(q * 128, (q + 1) * 128)
                r0 = it * T + q * 128
                for half in range(2):
                    pb = 5 + half
                    col0 = half * 512
                    for c in range(8):
                        P.op("pe", lambda e, c=c, tok=tok, col0=col0, pb=pb: e.matmul(
                            ps[pb][:, :512], lhsT=hT[:, c, tok], rhs=w_sb[:, c, col0:col0 + 512],
                            start=(c == 0), stop=(c == 7)), reads=["win", hk], writes=[("ps", pb)])
                    k = cnt["z"] % 6
                    cnt["z"] += 1
                    evac(P, "act" if half == 0 else "dve", zst[k][:, :], ps[pb][:, :512], [("ps", pb)], [("zst", k)])
                    P.dma("sp", o_z[r0:r0 + 128, col0:col0 + 512], zst[k][:, :], reads=[("zst", k)],
                          writes=[("A_out", "z", r0, half)])
                for c in range(8):
                    P.op("pe", lambda e, c=c, tok=tok: e.matmul(
                        ps[7][:, :512], lhsT=hT[:, c, tok], rhs=w_sb[:, c, 4112:4624],
                        start=(c == 0), stop=(c == 7)), reads=["win", hk], writes=[("ps", 7)])
                k = q % 4
                evac(P, "act", vst[k][:, :], ps[7][:, :512], [("ps", 7)], [("vst", k)])
                P.dma("sp", o_v[r0:r0 + 128, :], vst[k][:, :], reads=[("vst", k)], writes=[("A_out", "v", r0)])
            for c in range(8):
                P.op("pe", lambda e, c=c: e.matmul(ps[7][0:16, :T], lhsT=w_sb[:, c, 3072:3088], rhs=hT[:, c, :],
                                                    start=(c == 0), stop=(c == 7)), reads=["win", hk], writes=[("ps", 7)])
            evac(P, "dve", dst_[:, :], ps[7][0:16, :T], [("ps", 7)], ["dst"])
            P.dma("sp", o_dtT[:, tsl], dst_[:, :], reads=["dst"], writes=[("A_out", "dt", it)])

        proj_norm_tiles(C, cst, xT_in, W["mix_norm"], T, body)
    if upto == "A":
        return

    with C.scope():
        xcb = [C.sb("o_xcb%d" % i, [128, S], BF16) for i in range(2)]
        cw = C.sb("o_cw", [128, 16, 4], F32)
        cb = C.sb("o_cb", [128, 16], F32)
        P.dma("sp", cw.rearrange("p a b -> p (a b)"), W["ssd_conv_w"], writes=["cw"])
        P.dma("sp", cb[:, :], W["ssd_conv_b"], writes=["cb"])

        def sink_o(m, it, psap, bias, pkey):
            k = m % 2
            P.op("act", lambda e: e.activation(out=xcb[k][:, it * 512:(it + 1) * 512], in_=psap, func=AF.Silu, bias=bias),
                 reads=[pkey, "cb"], writes=[("xcb", k)])
            if it == S // 512 - 1:
                P.dma("sp", o_xc[m * 128:(m + 1) * 128, :], xcb[k][:, :], reads=[("xcb", k)], writes=[("xc", m)])

        conv_silu_pe(C, cst, "ocv", o_xbcT, 16, cw, cb, sink_o)
    if upto == "S0":
        return

    with C.scope():
        dtr = C.sb("o_dtr", [16, S], F32)
        ldt = C.sb("o_ldt", [16, S], F32)
        aa = C.sb("o_aa", [16, S], F32)
        te = C.sb("o_te", [16, S], F32)
        es = C.sb("o_es", [16, S], F32)
        dtb = C.sb("o_dtb", [16, 1], F32)
        Aneg = C.sb("o_Aneg", [16, 1], F32)
        cde = C.sb("o_cde", [16, 32], F32)
        oh16 = C.sb("o_oh16", [16, 16, 128], F32)
        one16 = cst["one"][0:16, 0:1]
        P.dma("sp", dtr[:, :], o_dtT[:, :], writes=["dtr"])
        P.dma("sp", dtb[:, :], W["ssd_dt_bias"], writes=["dtb"])
        P.dma("sp", Aneg[:, :], W["ssd_A_log"], writes=["Aneg"])
        P.dma("sp", oh16.rearrange("p a b -> p (a b)"), cin["onehot16"], writes=["oh16"])
        P.op("act", lambda e: e.activation(out=Aneg[:, :], in_=Aneg[:, :], func=AF.Exp), reads=["Aneg"], writes=["Aneg"])
        P.op("dve", lambda e: e.tensor_scalar(out=Aneg[:, :], in0=Aneg[:, :], scalar1=-1.0, scalar2=None, op0=ALU.mult),
             reads=["Aneg"], writes=["Aneg"])
        P.op("act", lambda e: e.activation(out=dtr[:, :], in_=dtr[:, :], func=AF.Exp, bias=dtb[:, 0:1]),
             reads=["dtr", "dtb"], writes=["dtr"])
        P.op("act", lambda e: e.activation(out=dtr[:, :], in_=dtr[:, :], func=AF.Ln, bias=one16),
             reads=["dtr", "one"], writes=["dtr"])
        P.op("act", lambda e: e.activation(out=ldt[:, :], in_=dtr[:, :], func=AF.Ln), reads=["dtr"], writes=["ldt"])
        P.op("dve", lambda e: e.tensor_scalar(out=aa[:, :], in0=dtr[:, :], scalar1=Aneg[:, 0:1], scalar2=None, op0=ALU.mult),
             reads=["dtr", "Aneg"], writes=["aa"])
        for c in range(32):
            blk = slice(c * 128, (c + 1) * 128)
            P.op("dve", lambda e, blk=blk: e.tensor_tensor_scan(
                out=aa[:, blk], data0=one16.to_broadcast([16, 128]), data1=aa[:, blk], initial=0.0,
                op0=ALU.mult, op1=ALU.add), reads=["aa", "one"], writes=["aa"])
        aa3 = aa.rearrange("h (c l) -> h c l", l=128)
        aend = aa3[:, :, 127:128]
        P.op("dve", lambda e: e.tensor_copy(out=cde[:, :].unsqueeze(2), in_=aend), reads=["aa"], writes=["cde"])
        P.op("act", lambda e: e.activation(out=cde[:, :], in_=cde[:, :], func=AF.Exp), reads=["cde"], writes=["cde"])
        P.op("act", lambda e: e.activation(out=es[:, :], in_=aa[:, :], func=AF.Exp), reads=["aa"], writes=["es"])
        te3 = te.rearrange("h (c l) -> h c l", l=128)
        P.op("dve", lambda e: e.tensor_tensor(out=te3, in0=aend.to_broadcast([16, 32, 128]), in1=aa3, op=ALU.subtract),
             reads=["aa"], writes=["te"])
        P.op("act", lambda e: e.activation(out=te[:, :], in_=te[:, :], func=AF.Exp), reads=["te"], writes=["te"])
        P.op("dve", lambda e: e.tensor_tensor(out=te[:, :], in0=te[:, :], in1=dtr[:, :], op=ALU.mult),
             reads=["te", "dtr"], writes=["te"])
        P.op("dve", lambda e: e.tensor_tensor(out=ldt[:, :], in0=ldt[:, :], in1=aa[:, :], op=ALU.subtract),
             reads=["ldt", "aa"], writes=["ldt"])
        for h in range(16):
            P.op("pe", lambda e, h=h: e.matmul(ps[1][:, h * 32:(h + 1) * 32], lhsT=oh16[:, h, :], rhs=cde[:, :],
                                                start=True, stop=True), reads=["oh16", "cde"], writes=[("ps", 1)])
        P.op("dve", lambda e: e.tensor_copy(out=cdb.rearrange("p c h -> p h c"),
                                            in_=ps[1][:, :].rearrange("p (h c) -> p h c", c=32)),
             reads=[("ps", 1)], writes=["cdb"])
        for src, sname, dst, dname, pb in ((ldt, "ldt", bias_tm, "bias_tm", 2), (es, "es", est_tm, "est_tm", 3),
                                           (te, "te", dtte_tm, "dtte_tm", 4)):
            for c in range(32):
                P.op("pe", lambda e, src=src, c=c, pb=pb: e.transpose(
                    out=ps[pb][:, c * 16:(c + 1) * 16], in_=src[:, c * 128:(c + 1) * 128],
                    identity=cst["identf"][0:16, 0:16]), reads=[sname, "identf"], writes=[("ps", pb)])
            P.op("dve", lambda e, dst=dst, pb=pb: e.tensor_copy(out=dst.rearrange("p a b -> p (a b)"), in_=ps[pb][:, :]),
                 reads=[("ps", pb)], writes=[dname])
        o_acum = C.dram("o_acum", [16, S], F32)
        P.dma("sp", o_acum[:, :], aa[:, :], reads=["aa"], writes=["o_acum"])
    if upto == "S1":
        return
    ssd_main(C, cst, cin, W, o_z, o_xc, mixT, bias_tm, est_tm, dtte_tm, cdb, negm, upto)
    if upto[0] == "S":
        return
    stick_breaking_phase(C, cst, o_qT, o_kT, o_v, mixT, upto)
    if upto[0] == "T":
        return
    with C.scope():
        wo = C.sb("o_wo", [128, 12, D], BF16)
        stg = Stager(C, "o_stgE")
        wov = W["w_out"].rearrange("(c p) f -> p c f", p=128)
        for c in range(12):
            stg.load(wo[:, c, :], wov[:, c, :], "wo")
        out_proj(C, wo, 12, mixT, xT_in, xT_out, T)


def ssd_main(C, cst, cin, W, o_z, o_xc, mixT, bias_tm, est_tm, dtte_tm, cdb, negm, upto):
    P = C.P
    ps = C.psum
    o_acum = C.dram("o_acum", [16, S], F32)
    with C.scope():
        acf = C.sb("s_acf", [16, S], F32)
        oh16 = C.sb("s_oh16", [16, 16, 128], F32)
        Dbc = C.sb("s_Dbc", [128, 16], F32)
        Did = C.sb("s_Did", [128, 16, 128], BF16)
        nwb = C.sb("s_nwb", [128, 1024], F32)
        xsup = [C.sb("s_xsup%d" % i, [128, 16, 512], BF16) for i in range(2)]
        xtm = C.sb("s_xtm", [128, 16, 64], BF16)
        xw = C.sb("s_xw", [128, 16, 64], BF16)
        Btm = C.sb("s_Btm", [128, 4, 128], BF16)
        dec = [C.sb("s_dec%d" % i, [128, 4, 128], F32) for i in range(2)]
        PTs = [C.sb("s_PT%d" % i, [128, 4, 128], BF16) for i in range(2)]
        Hst = C.sb("s_H", [128, 16, 64], F32)
        Hbf = C.sb("s_Hbf", [128, 16, 64], BF16)
        zch = [C.sb("s_z%d" % i, [128, 1024], F32) for i in range(2)]
        yoff = C.sb("s_yoff", [128, 16, 64], F32)
        ysb = C.sb("s_y", [128, 1024], F32)
        ssg = C.sb("s_ss", [128, 4], F32)
        junk = C.sb("s_junk", [128, 256], F32)
        cout = C.sb("s_cout", [128, 1024], BF16)
        coutT = [C.sb("s_coutT%d" % i, [128, 8, 128], BF16) for i in range(2)]
        P.dma("sp", acf[:, :], o_acum[:, :], writes=["acf"])
        P.dma("sp", oh16.rearrange("p a b -> p (a b)"), cin["onehot16"], writes=["oh16"])
        P.dma("sp", Dbc[:, :], W["ssd_D"].partition_broadcast(128), writes=["Dbc"])
        P.dma("sp", nwb[:, :], W["ssd_norm"].partition_broadcast(128), writes=["nwb"])
        for h in range(16):
            P.op("dve", lambda e, h=h: e.tensor_scalar(out=Did[:, h, :], in0=cst["identf"][:, :], scalar1=Dbc[:, h:h + 1],
                                                       scalar2=None, op0=ALU.mult), reads=["identf", "Dbc"], writes=["Did"])
        P.op("pool", lambda e: e.memset(Hst.rearrange("p a b -> p (a b)"), 0.0), writes=["H"])
        P.op("pool", lambda e: e.memset(Hbf.rearrange("p a b -> p (a b)"), 0.0), writes=["Hbf"])
        ps0b = ps[0][:, :].bitcast(BF16)
        ps1b = ps[1][:, :].bitcast(BF16)
        xcv = o_xc.rearrange("(m p) t -> p m t", p=128)
        nch = 1 if upto == "S2" else 32
        pending_tail = []
        for c in range(nch):
            b = c % 2
            sc, lc = c // 4, c % 4
            blk = slice(c * 128, (c + 1) * 128)
            tl = slice(lc * 128, (lc + 1) * 128)
            if lc == 0:
                P.dma("sp", xsup[sc % 2][:, :, :], xcv[:, :, sc * 512:(sc + 1) * 512], writes=[("xsup", sc % 2)])
            xs_ = xsup[sc % 2]
            xk = ("xsup", sc % 2)
            P.dma("sp", zch[b][:, :], o_z[blk, :], writes=[("zch", b)])
            for m in range(8):
                P.op("pe", lambda e, m=m, xs_=xs_, tl=tl: e.transpose(out=ps0b[:, m * 128:(m + 1) * 128], in_=xs_[:, m, tl],
                                                                      identity=cst["identb"][:, :]),
                     reads=[xk, "identb"], writes=[("ps", 0)])
            P.op("act", lambda e: e.activation(out=xtm.rearrange("p a b -> p (a b)"), in_=ps0b[:, :], func=AF.Copy),
                 reads=[("ps", 0)], writes=["xtm"])
            P.op("dve", lambda e, c=c: e.tensor_tensor(
                out=xw[:, :, :], in0=ps0b[:, :].rearrange("p (a b) -> p a b", b=64),
                in1=dtte_tm[:, c, :].unsqueeze(2).to_broadcast([128, 16, 64]), op=ALU.mult),
                reads=[("ps", 0), "dtte_tm"], writes=["xw"])
            for g in range(4):
                P.op("pe", lambda e, g=g, xs_=xs_, tl=tl: e.transpose(out=ps1b[:, g * 128:(g + 1) * 128], in_=xs_[:, 8 + g, tl],
                                                                      identity=cst["identb"][:, :]),
                     reads=[xk, "identb"], writes=[("ps", 1)])
            P.op("act", lambda e: e.activation(out=Btm.rearrange("p a b -> p (a b)"), in_=ps1b[:, 0:512], func=AF.Copy),
                 reads=[("ps", 1)], writes=["Btm"])
            for g in range(4):
                P.op("pe", lambda e, g=g, xs_=xs_, tl=tl: e.matmul(ps[2][:, g * 128:(g + 1) * 128], lhsT=xs_[:, 8 + g, tl],
                                                                   rhs=xs_[:, 12 + g, tl], start=True, stop=True),
                     reads=[xk], writes=[("ps", 2)])
            def emit_yoff(c=c, xs_=xs_, tl=tl, xk=xk):
                for g in range(4):
                    P.op("pe", lambda e, g=g, xs_=xs_, tl=tl: e.matmul(
                        ps[6 + g // 2][:, (g % 2) * 256:(g % 2 + 1) * 256], lhsT=xs_[:, 12 + g, tl],
                        rhs=Hbf[:, 4 * g:4 * g + 4, :], start=True, stop=True), reads=[xk, "Hbf"], writes=[("ps", 6 + g // 2)])
                for hb in range(2):
                    P.op("dve", lambda e, hb=hb, c=c: e.tensor_tensor(
                        out=yoff[:, 8 * hb:8 * hb + 8, :], in0=ps[6 + hb][:, :].rearrange("p (a b) -> p a b", b=64),
                        in1=est_tm[:, c, 8 * hb:8 * hb + 8].unsqueeze(2).to_broadcast([128, 8, 64]), op=ALU.mult),
                        reads=[("ps", 6 + hb), "est_tm"], writes=[("yoff", hb)])

            for g in range(4):
                k = g % 2
                if g == 2:
                    emit_yoff()
                for hh in range(4):
                    h = 4 * g + hh
                    P.op("pe", lambda e, hh=hh, h=h, blk=blk: e.matmul(
                        ps[3][:, hh * 128:(hh + 1) * 128], lhsT=oh16[:, h, :], rhs=acf[:, blk], start=True, stop=False),
                        reads=["oh16", "acf"], writes=[("ps", 3)])
                    P.op("pe", lambda e, hh=hh: e.matmul(
                        ps[3][:, hh * 128:(hh + 1) * 128], lhsT=cst["identb"][:, :], rhs=negm[:, :], start=False, stop=True),
                        reads=["identb", "negm"], writes=[("ps", 3)])
                for hh in range(4):
                    h = 4 * g + hh
                    P.op("act", lambda e, hh=hh, h=h, k=k, c=c: e.activation(
                        out=dec[k][:, hh, :], in_=ps[3][:, hh * 128:(hh + 1) * 128], func=AF.Exp,
                        bias=bias_tm[:, c, h:h + 1]), reads=[("ps", 3), "bias_tm"], writes=[("dec", k)])
                P.op("dve", lambda e, g=g, k=k: e.tensor_tensor(
                    out=PTs[k][:, :, :], in0=dec[k][:, :, :],
                    in1=ps[2][:, g * 128:(g + 1) * 128].unsqueeze(1).to_broadcast([128, 4, 128]), op=ALU.mult),
                    reads=[("dec", k), ("ps", 2)], writes=[("PTs", k)])
                for hh in range(4):
                    h = 4 * g + hh
                    yb = 4 + h // 8
                    yc = (h % 8) * 64
                    P.op("pe", lambda e, hh=hh, h=h, k=k, yb=yb, yc=yc: e.matmul(
                        ps[yb][:, yc:yc + 64], lhsT=PTs[k][:, hh, :], rhs=xtm[:, h, :], start=True, stop=False),
                        reads=[("PTs", k), "xtm"], writes=[("ps", yb)])
                    P.op("pe", lambda e, h=h, yb=yb, yc=yc: e.matmul(
                        ps[yb][:, yc:yc + 64], lhsT=Did[:, h, :], rhs=xtm[:, h, :], start=False, stop=True),
                        reads=["Did", "xtm"], writes=[("ps", yb)])
            while pending_tail:
                pending_tail.pop(0)()
            for hb in range(2):
                P.op("dve", lambda e, hb=hb: e.tensor_tensor(
                    out=ysb[:, hb * 512:(hb + 1) * 512], in0=ps[4 + hb][:, :],
                    in1=yoff[:, 8 * hb:8 * hb + 8, :].rearrange("p a b -> p (a b)"), op=ALU.add),
                    reads=[("ps", 4 + hb), ("yoff", hb)], writes=[("ysb", hb)])
            for g in range(4):
                P.op("pe", lambda e, g=g: e.matmul(
                    ps[6 + g // 2][:, (g % 2) * 256:(g % 2 + 1) * 256], lhsT=Btm[:, g, :],
                    rhs=xw[:, 4 * g:4 * g + 4, :], start=True, stop=True), reads=["Btm", "xw"], writes=[("ps", 6 + g // 2)])
            P.op("dve", lambda e, c=c: e.tensor_tensor(
                out=Hst[:, :, :], in0=Hst[:, :, :], in1=cdb[:, c, :].unsqueeze(2).to_broadcast([128, 16, 64]), op=ALU.mult),
                reads=["H", "cdb"], writes=["H"])
            for hb in range(2):
                P.op("dve", lambda e, hb=hb: e.tensor_tensor(
                    out=Hst[:, 8 * hb:8 * hb + 8, :], in0=Hst[:, 8 * hb:8 * hb + 8, :],
                    in1=ps[6 + hb][:, :].rearrange("p (a b) -> p a b", b=64), op=ALU.add),
                    reads=["H", ("ps", 6 + hb)], writes=["H"])
            P.op("act", lambda e: e.activation(out=Hbf.rearrange("p a b -> p (a b)"),
                                               in_=Hst.rearrange("p a b -> p (a b)"), func=AF.Copy),
                 reads=["H"], writes=["Hbf"])
            P.op("act", lambda e, b=b: e.activation(out=zch[b][:, :], in_=zch[b][:, :], func=AF.Silu),
                 reads=[("zch", b)], writes=[("zch", b)])
            P.op("dve", lambda e, b=b: e.tensor_tensor(out=ysb[:, :], in0=ysb[:, :], in1=zch[b][:, :], op=ALU.mult),
                 reads=[("ysb", 0), ("ysb", 1), ("zch", b)], writes=[("ysb", 0), ("ysb", 1)])
            for g in range(4):
                P.op("act", lambda e, g=g: e.activation(out=junk[:, :], in_=ysb[:, g * 256:(g + 1) * 256], func=AF.Square,
                                                        accum_out=ssg[:, g:g + 1]),
                     reads=[("ysb", 0), ("ysb", 1)], writes=["junk", ("ssg", g)])
            P.op("act", lambda e: e.activation(out=ssg[:, :], in_=ssg[:, :], func=AF.Sqrt, scale=1.0 / 256,
                                               bias=cst["eps"][:, 0:1]),
                 reads=[("ssg", g) for g in range(4)] + ["eps"], writes=[("ssg", g) for g in range(4)])
            P.op("dve", lambda e: e.reciprocal(out=ssg[:, :], in_=ssg[:, :]), reads=[("ssg", g) for g in range(4)],
                 writes=[("ssg", g) for g in range(4)])
            for g in range(4):
                P.op("dve", lambda e, g=g: e.scalar_tensor_tensor(
                    out=cout[:, g * 256:(g + 1) * 256], in0=ysb[:, g * 256:(g + 1) * 256], scalar=ssg[:, g:g + 1],
                    in1=nwb[:, g * 256:(g + 1) * 256], op0=ALU.mult, op1=ALU.mult),
                    reads=[("ysb", 0), ("ysb", 1), ("ssg", g), "nwb"], writes=["cout"])
            def emit_tail(c=c, b=b, blk=blk):
                for m in range(8):
                    P.op("pe", lambda e, m=m: e.transpose(out=ps0b[:, m * 128:(m + 1) * 128], in_=cout[:, m * 128:(m + 1) * 128],
                                                          identity=cst["identb"][:, :]),
                         reads=["cout", "identb"], writes=[("ps", 0)])
                P.op("act", lambda e, b=b: e.activation(out=coutT[b].rearrange("p a b -> p (a b)"), in_=ps0b[:, :], func=AF.Copy),
                     reads=[("ps", 0)], writes=[("coutT", b)])
                P.dma("sp", mixT[0:1024, blk].rearrange("(m p) t -> p m t", p=128), coutT[b][:, :, :],
                      reads=[("coutT", b)], writes=[("mixC", c)])
            pending_tail.append(emit_tail)
        for f_ in pending_tail:
            f_()


def stick_breaking_phase(C, cst, o_qT, o_kT, o_v, mixT, upto):
    P = C.P
    ps = C.psum
    with C.scope():
        kT = C.sb("t_kT", [128, 4, S], BF16)
        qT = C.sb("t_qT", [128, 4, S], BF16)
        vtm = C.sb("t_v", [128, 32, 512], BF16)
        tril = C.sb("t_tril", [128, 128], BF16)
        P.op("pool", lambda e: e.tensor_tensor(out=tril[:, :], in0=cst["ones"][:, :], in1=cst["triu"][:, :], op=ALU.subtract),
             reads=["ones", "triu"], writes=["tril"])
        P.dma("sp", kT[:, :, :], o_kT.rearrange("(m p) t -> p m t", p=128), writes=["kT"])
        P.dma("sp", qT[:, :, :], o_qT.rearrange("(m p) t -> p m t", p=128), writes=["qT"])
        ovv = o_v.rearrange("(b p) f -> p b f", p=128)
        for i in range(8):
            P.dma("sp", vtm[:, 4 * i:4 * i + 4, :], ovv[:, 4 * i:4 * i + 4, :], reads=["vtm"] if i else [], writes=["vtm"])
        NS = 2
        ee = [[C.sb("t_e%d_%d" % (s_, i), [128, 512], F32) for i in range(2)] for s_ in range(NS)]
        sp = [[C.sb("t_sp%d_%d" % (s_, i), [128, 512], F32) for i in range(2)] for s_ in range(NS)]
        l1m = [[C.sb("t_l1m%d_%d" % (s_, i), [128, 512], BF16) for i in range(2)] for s_ in range(NS)]
        E1 = [[C.sb("t_E1%d_%d" % (s_, i), [128, 512], F32) for i in range(2)] for s_ in range(NS)]
        PTb = [[C.sb("t_PT%d_%d" % (s_, i), [128, 512], BF16) for i in range(2)] for s_ in range(NS)]
        dout = C.sb("t_dout", [128, 4, 512], BF16)
        doutT = [C.sb("t_doutT%d" % i, [128, 4, 512], BF16) for i in range(2)]
        ps7b = ps[6][:, :].bitcast(BF16)
        nQ = {"T1": 1, "T2": 2}.get(upto, 8)
        for Q in range(nQ):
            for h0 in range(0, 8, NS):
                kbs = list(range(4 * Q + 3, -1, -1))
                n = len(kbs)

                def geo(i):
                    kb = kbs[i]
                    tb0 = max(0, kb - 4 * Q)
                    c0 = tb0 * 128
                    return kb, tb0, c0, slice(c0, 512), kb >= 4 * Q

                hs = [(h0 + s_, (h0 + s_) // 2, 64 * ((h0 + s_) % 2), 2 * s_, 4 + s_, 6 + s_) for s_ in range(NS)]
                def emit_z(i):
                    kb, tb0, c0, cs_, diag = geo(i)
                    kblk = slice(kb * 128, (kb + 1) * 128)
                    qsl = slice(Q * 512 + c0, (Q + 1) * 512)
                    for s_, (h, m, pb0, zb0, xb, ob) in enumerate(hs):
                        zb = zb0 + i % 2
                        P.op("pe", lambda e, m=m, pb0=pb0, kblk=kblk, qsl=qsl, cs_=cs_, zb=zb: e.matmul(
                            ps[zb][:, cs_], lhsT=kT[pb0:pb0 + 64, m, kblk], rhs=qT[pb0:pb0 + 64, m, qsl],
                            start=True, stop=True), reads=["kT", "qT"], writes=[("ps", zb)])

                for i in range(n + 1):
                    if i >= 1:
                        kb, tb0, c0, cs_, diag = geo(i - 1)
                        k = (i - 1) % 2
                        for s_, (h, m, pb0, zb, xb, ob) in enumerate(hs):
                            P.op("dve", lambda e, s_=s_, k=k, cs_=cs_, xb=xb: e.tensor_tensor(
                                out=E1[s_][k][:, cs_], in0=ps[xb][:, cs_], in1=sp[s_][k][:, cs_], op=ALU.subtract),
                                reads=[("ps", xb), ("sp", s_, k)], writes=[("E1", s_, k)])
                    if i < n:
                        kb, tb0, c0, cs_, diag = geo(i)
                        k = i % 2
                        if i == 0:
                            emit_z(0)
                        for s_, (h, m, pb0, zb0, xb, ob) in enumerate(hs):
                            zb = zb0 + k
                            P.op("act", lambda e, s_=s_, k=k, cs_=cs_, zb=zb: e.activation(
                                out=ee[s_][k][:, cs_], in_=ps[zb][:, cs_], func=AF.Exp, scale=-1.0),
                                reads=[("ps", zb)], writes=[("ee", s_, k)])
                        for s_, (h, m, pb0, zb, xb, ob) in enumerate(hs):
                            P.op("act", lambda e, s_=s_, k=k, cs_=cs_: e.activation(
                                out=sp[s_][k][:, cs_], in_=ee[s_][k][:, cs_], func=AF.Ln, bias=cst["one"][:, 0:1]),
                                reads=[("ee", s_, k), "one"], writes=[("sp", s_, k)])
                        for s_, (h, m, pb0, zb0, xb, ob) in enumerate(hs):
                            zb = zb0 + k
                            P.op("dve", lambda e, s_=s_, k=k, cs_=cs_, zb=zb: e.scalar_tensor_tensor(
                                out=l1m[s_][k][:, cs_], in0=ps[zb][:, cs_], scalar=-1.0, in1=sp[s_][k][:, cs_],
                                op0=ALU.mult, op1=ALU.subtract), reads=[("ps", zb), ("sp", s_, k)], writes=[("l1m", s_, k)])
                            if diag:
                                dsl = slice(c0, c0 + 128)
                                P.op("pool", lambda e, s_=s_, k=k, dsl=dsl: e.tensor_tensor(
                                    out=l1m[s_][k][:, dsl], in0=l1m[s_][k][:, dsl], in1=cst["maskTs"][:, :], op=ALU.mult),
                                    reads=[("l1m", s_, k), "maskTs"], writes=[("l1m", s_, k)])
                    if i + 1 < n:
                        emit_z(i + 1)
                    if i >= 1:
                        kb, tb0, c0, cs_, diag = geo(i - 1)
                        k = (i - 1) % 2
                        for s_, (h, m, pb0, zb, xb, ob) in enumerate(hs):
                            P.op("act", lambda e, s_=s_, k=k, cs_=cs_: e.activation(
                                out=PTb[s_][k][:, cs_], in_=E1[s_][k][:, cs_], func=AF.Exp),
                                reads=[("E1", s_, k)], writes=[("PTb", s_, k)])
                            if diag:
                                dsl = slice(c0, c0 + 128)
                                P.op("pool", lambda e, s_=s_, k=k, dsl=dsl: e.tensor_tensor(
                                    out=PTb[s_][k][:, dsl], in0=PTb[s_][k][:, dsl], in1=cst["maskTs"][:, :], op=ALU.mult),
                                    reads=[("PTb", s_, k), "maskTs"], writes=[("PTb", s_, k)])
                        for s_, (h, m, pb0, zb, xb, ob) in enumerate(hs):
                            for tb in range(tb0, 4):
                                P.op("pe", lambda e, s_=s_, k=k, tb=tb, kb=kb, h=h, ob=ob, st=(i == 1 and tb == tb0): e.matmul(
                                    ps[ob][:, tb * 64:(tb + 1) * 64], lhsT=PTb[s_][k][:, tb * 128:(tb + 1) * 128],
                                    rhs=vtm[:, kb, h * 64:(h + 1) * 64], start=st, stop=(kb == 0), skip_group_check=True),
                                    reads=[("PTb", s_, k), "vtm"], writes=[("ps", ob)])
                    if i < n:
                        kb, tb0, c0, cs_, diag = geo(i)
                        k = i % 2
                        for s_, (h, m, pb0, zb, xb, ob) in enumerate(hs):
                            if i >= 1:
                                pcs = geo(i - 1)[3]
                                pk = (i - 1) % 2
                                P.op("pe", lambda e, s_=s_, pk=pk, pcs=pcs, xb=xb: e.matmul(
                                    ps[xb][:, pcs], lhsT=tril[:, :], rhs=l1m[s_][pk][:, pcs], start=False, stop=False,
                                    skip_group_check=True), reads=["tril", ("l1m", s_, pk)], writes=[("ps", xb)])
                            P.op("pe", lambda e, s_=s_, k=k, cs_=cs_, xb=xb, st=(i == 0): e.matmul(
                                ps[xb][:, cs_], lhsT=cst["triu"][:, :], rhs=l1m[s_][k][:, cs_], start=st, stop=False,
                                skip_group_check=True), reads=["triu", ("l1m", s_, k)], writes=[("ps", xb)])
                for s_ in range(NS):
                    h = h0 + s_
                    ob = 6 + s_
                    P.op("act", lambda e, h=h, ob=ob: e.activation(
                        out=dout[:, :, h * 64:(h + 1) * 64], in_=ps[ob][:, 0:256].rearrange("p (a b) -> p a b", b=64),
                        func=AF.Copy), reads=[("ps", ob)], writes=["dout"])
            qb = Q % 2
            for mm in range(4):
                for tb in range(4):
                    P.op("pe", lambda e, mm=mm, tb=tb: e.transpose(
                        out=ps7b[:, tb * 128:(tb + 1) * 128], in_=dout[:, tb, mm * 128:(mm + 1) * 128],
                        identity=cst["identb"][:, :]), reads=["dout", "identb"], writes=[("ps", 6)])
                P.op("dve", lambda e, mm=mm, qb=qb: e.tensor_copy(out=doutT[qb][:, mm, :], in_=ps7b[:, 0:512]),
                     reads=[("ps", 6)], writes=[("doutT", qb)])
            P.dma("sp", mixT[1024:1536, Q * 512:(Q + 1) * 512].rearrange("(m p) t -> p m t", p=128), doutT[qb][:, :, :],
                  reads=[("doutT", qb)], writes=[("mixD", Q)])


def build_program():
    nc = bass.Bass("TRN2", target_bir_lowering=False)
    xT = nc.dram_tensor("xT", [D, S], F32, kind="ExternalInput").ap()
    yT = nc.dram_tensor("yT", [D, S], F32, kind="ExternalOutput").ap()
    Wd = {k: nc.dram_tensor(k, sh, F32, kind="ExternalInput").ap() for k, sh in PARAM_SHAPES.items()}
    cin = {k: nc.dram_tensor("c_" + k, sh, F32, kind="ExternalInput").ap() for k, sh in CONST_SHAPES.items()}
    with ExitStack() as stack:
        C = Ctx(nc, stack)
        cst = alloc_consts(C, cin)
        res = [C.dram("res%d" % i, [D, S], F32) for i in range(5)]

        def ffn(pre, src, dst):
            with C.scope():
                bufs = alloc_ffn_bufs(C, cst)
                ffn_phase(C, pre, src, dst, Wd[pre + "_norm"], Wd[pre + "_wg"], Wd[pre + "_wu"], Wd[pre + "_wd"], bufs)

        ffn("l0_ffn1", xT, res[0])
        with C.scope():
            even_mixer_phase(C, cst, cin, res[0], res[1], {k[3:]: v for k, v in Wd.items() if k.startswith("l0_")})
        ffn("l0_ffn2", res[1], res[2])
        ffn("l1_ffn1", res[2], res[3])
        with C.scope():
            odd_mixer_phase(C, cst, cin, res[3], res[4], {k[3:]: v for k, v in Wd.items() if k.startswith("l1_")})
        ffn("l1_ffn2", res[4], yT)
        C.P.emit()
    return nc


_NC_CACHE = {}


def kernel(**inputs):
    x = np.asarray(inputs["x"], np.float32)
    hp = host_params(inputs)
    hc = host_consts()
    shared = {k: hp[k] for k in PARAM_SHAPES}
    for k in CONST_SHAPES:
        shared["c_" + k] = hc[k]
    in_maps = []
    for b in range(NCORES):
        m = dict(shared)
        m["xT"] = np.ascontiguousarray(x[b].T)
        in_maps.append(m)
    if "nc" not in _NC_CACHE:
        _NC_CACHE["nc"] = build_program()
    res = run_bass_kernel_spmd(_NC_CACHE["nc"], in_maps, core_ids=list(range(NCORES)))
    out = np.stack([np.asarray(r["yT"], np.float32).T for r in res.results], axis=0)
    return np.ascontiguousarray(out)
```

```python
from contextlib import ExitStack

import numpy as np
import concourse.bass as bass
import concourse.mybir as mybir
from concourse.bass_utils import run_bass_kernel_spmd

F32 = mybir.dt.float32
BF16 = mybir.dt.bfloat16
ALU = mybir.AluOpType
AF = mybir.ActivationFunctionType
AX = mybir.AxisListType

S = 4096
D = 1024
DFF = 2816
NCORES = 8
EPS = 1e-6
CAST_DMA = True


class _Op:
    __slots__ = ("eng", "fn", "deps", "sig", "sem", "cnt", "is_dma", "pos")


def _bank_of(k):
    if isinstance(k, tuple):
        if k[0] == "ps":
            return k[1]
        if k[0] == "pso":
            return 3 + k[1] // 2
        if k[0] == "psd":
            return 5 + k[1] // 2
        if k[0] == "ps0":
            return 0
    return None


class Prog:
    ENGS = ("pe", "act", "dve", "pool", "sp")

    def __init__(self, nc, stack, n_dma_sems=8):
        self.nc = nc
        self.stack = stack
        self.streams = {e: [] for e in self.ENGS}
        self.lastw = {}
        self.readers = {}
        self.eng_sem = {e: stack.enter_context(nc.semaphore("s_" + e)) for e in ("pe", "act", "dve", "pool")}
        self.dma_pool = {}
        self.n_dma_sems = n_dma_sems
        self.dma_rr = {}
        self.nops = 0
        self.pending = {}
        self.bank_last = {}
        self.multi = {}

    def barrier(self):
        lasts = []
        for e in self.ENGS:
            for o in reversed(self.streams[e]):
                if not o.is_dma:
                    o.sig = True
                    lasts.append(o)
                    break
        for q, slots in self.dma_pool.items():
            for sl in slots:
                if sl[2] is not None:
                    lasts.append(sl[2])
        for e in self.ENGS:
            self.pending[e] = list(lasts) + self.pending.get(e, [])
        self.lastw.clear()
        self.readers.clear()
        self.multi.clear()

    def _dma_sem(self, q):
        if q not in self.dma_pool:
            self.dma_pool[q] = [[self.stack.enter_context(self.nc.semaphore("d_%s%d" % (q, i))), 0, None]
                                for i in range(self.n_dma_sems)]
            self.dma_rr[q] = 0
        i = self.dma_rr[q]
        self.dma_rr[q] = (i + 1) % self.n_dma_sems
        return self.dma_pool[q][i]

    def _deps(self, op, reads, writes):
        deps = set()
        for k in reads:
            w = self.lastw.get(k)
            if w is not None:
                deps.add(w)
            if k in self.multi:
                deps.update(self.multi[k])
        for k in writes:
            w = self.lastw.get(k)
            if w is not None:
                deps.add(w)
            for r in self.readers.get(k, ()):
                deps.add(r)
        deps.discard(op)
        for k in reads:
            self.readers.setdefault(k, []).append(op)
        for k in writes:
            self.lastw[k] = op
            self.readers[k] = []
        return deps

    def op(self, eng, fn, reads=(), writes=()):
        o = _Op()
        o.eng = eng
        o.fn = fn
        o.is_dma = False
        o.sig = False
        o.sem = None
        o.cnt = 0
        o.pos = self.nops
        self.nops += 1
        deps = self._deps(o, reads, writes)
        deps.update(self.pending.pop(eng, ()))
        for k in list(reads) + list(writes):
            bk = _bank_of(k)
            if bk is not None:
                prev = self.bank_last.get(bk)
                if prev is not None and prev is not o and prev.eng != eng:
                    deps.add(prev)
                self.bank_last[bk] = o
        keep = []
        for d in deps:
            if (not d.is_dma) and d.eng == eng and eng == "pe":
                continue
            keep.append(d)
            if not d.is_dma:
                d.sig = True
        o.deps = keep
        self.streams[eng].append(o)
        return o

    def dma(self, q, out, in_, reads=(), writes=(), multi=(), **kw):
        o = _Op()
        o.eng = q
        o.fn = lambda e: e.dma_start(out=out, in_=in_, **kw)
        o.is_dma = True
        o.sig = True
        o.pos = self.nops
        self.nops += 1
        slot = self._dma_sem(q)
        deps = self._deps(o, reads, writes)
        for k in multi:
            for r in self.readers.get(k, ()):
                deps.add(r)
            self.multi.setdefault(k, []).append(o)
        deps.update(self.pending.pop(q, ()))
        if slot[2] is not None:
            deps.add(slot[2])
        slot[1] += 16
        slot[2] = o
        o.sem = slot[0]
        o.cnt = slot[1]
        for d in deps:
            if not d.is_dma:
                d.sig = True
        o.deps = list(deps)
        self.streams[q].append(o)
        return o

    def emit(self):
        nc = self.nc
        for e in ("pe", "act", "dve", "pool"):
            c = 0
            for o in self.streams[e]:
                if o.is_dma:
                    continue
                o.sem = self.eng_sem[e]
                if o.sig:
                    c += 1
                    o.cnt = c
        streams = self.streams

        def run(e, h):
            known = {}
            for o in streams[e]:
                need = {}
                for d in o.deps:
                    key = id(d.sem)
                    if key not in need or need[key][1] < d.cnt:
                        need[key] = (d.sem, d.cnt)
                for key, (sem, v) in need.items():
                    if known.get(key, 0) < v:
                        h.wait_ge(sem, v)
                        known[key] = v
                ins = o.fn(h)
                if o.is_dma:
                    ins.then_inc(o.sem, 16)
                elif o.sig:
                    ins.then_inc(o.sem, 1)
            if e in self.dma_pool:
                for sem, cnt, _ in self.dma_pool[e]:
                    if cnt > 0:
                        h.wait_ge(sem, cnt)

        with nc.Block() as block:
            @block.tensor
            def _(h):
                run("pe", h)

            @block.scalar
            def _(h):
                run("act", h)

            @block.vector
            def _(h):
                run("dve", h)

            @block.gpsimd
            def _(h):
                run("pool", h)

            @block.sync
            def _(h):
                run("sp", h)


ARENA_WORDS = 53000


class Ctx:
    def __init__(self, nc, stack):
        self.nc = nc
        self.stack = stack
        self.P = Prog(nc, stack)
        self.psum_all = stack.enter_context(nc.psum_tensor("psall", [128, 4096], F32))
        self.psum = [self.psum_all[:, i * 512:(i + 1) * 512] for i in range(8)]
        self.arena = stack.enter_context(nc.sbuf_tensor("arena", [128, ARENA_WORDS], F32))
        self.top = 0
        self.scratch = {}

    def sb(self, name, shape, dt):
        esz = 4 if dt == F32 else 2
        n = 1
        for d_ in shape[1:]:
            n *= int(d_)
        words = (n * esz + 3) // 4
        words = (words + 15) // 16 * 16
        off = self.top
        self.top += words
        assert self.top <= ARENA_WORDS, "SBUF arena overflow at %s: %d words" % (name, self.top)
        ap = self.arena[0:shape[0], off:off + words]
        if dt != F32:
            ap = ap.bitcast(dt)
        ap = ap[:, 0:n]
        if len(shape) == 3:
            ap = ap.rearrange("p (a b) -> p a b", b=int(shape[2]))
        elif len(shape) == 4:
            ap = ap.rearrange("p (a b c) -> p a b c", b=int(shape[2]), c=int(shape[3]))
        return ap

    def scope(self):
        return _Scope(self)

    def dram(self, name, shape, dt):
        if name not in self.scratch:
            kind = "ExternalOutput" if name in getattr(self, "debug_out", ()) else "Internal"
            self.scratch[name] = self.nc.dram_tensor(name, list(shape), dt, kind=kind).ap()
        return self.scratch[name]


class _Scope:
    def __init__(self, C):
        self.C = C

    def __enter__(self):
        self.mark = self.C.top
        return self

    def __exit__(self, *a):
        self.C.P.barrier()
        self.C.top = self.mark
        return False


def ffn_phase(C, tag, xT_in, xT_out, nw, wg, wu, wd, bufs, T=512):
    P = C.P
    NT = S // T
    NF = DFF // 128
    wg_sb, wu_sb, wd_sb = bufs["wg"], bufs["wu"], bufs["wd"]
    nw_sb = bufs["nw"]
    ones = bufs["ones"]
    xin = xT_in.rearrange("(c p) t -> p c t", p=128)
    xout = xT_out.rearrange("(c p) t -> p c t", p=128)

    P.dma("sp", nw_sb[:, :], nw, writes=["nw"])
    wgv = wg.rearrange("(c p) f -> p c f", p=128)
    wuv = wu.rearrange("(c p) f -> p c f", p=128)
    wdv = wd.rearrange("(j p) d -> p j d", p=128)
    si = [0]

    def load_cast(dst, src, key):
        P.dma("pool", dst, src, writes=[key])

    wgk, wuk, wdk = ["wg"], ["wu"], ["wd"]
    if CAST_DMA:
        H = DFF // 2
        wgk, wuk, wdk = [], [], []
        for c in range(8):
            for hh in range(2):
                wgk.append(("wg", c, hh))
                load_cast(wg_sb[:, c, hh * H:(hh + 1) * H], wgv[:, c, hh * H:(hh + 1) * H], wgk[-1])
            for hh in range(2):
                wuk.append(("wu", c, hh))
                load_cast(wu_sb[:, c, hh * H:(hh + 1) * H], wuv[:, c, hh * H:(hh + 1) * H], wuk[-1])
        for j in range(0, NF, 2):
            wdk.append(("wd", j))
            load_cast(wd_sb[:, j:j + 2, :], wdv[:, j:j + 2, :], wdk[-1])
    else:
        st_ = Stager(C, tag + "_stg", cols=DFF // 8)
        for c in range(8):
            st_.load(wg_sb[:, c, :], wgv[:, c, :], "wg")
            st_.load(wu_sb[:, c, :], wuv[:, c, :], "wu")
        for j in range(NF):
            st_.load(wd_sb[:, j, :], wdv[:, j, :], "wd")

    xt, hT, aT, rs = bufs["xt"], bufs["hT"], bufs["aT"], bufs["rs"]
    sq2 = bufs["sq2"]
    sg = bufs["sg"]
    ps = C.psum

    def tsl_(it):
        return slice(it * T, (it + 1) * T)

    def load_x(it):
        b = it % 2
        P.dma("sp", xt[b][:, :, :], xin[:, :, tsl_(it)], writes=[("xt", b)])

    def sq_op(it, c):
        b = it % 2
        P.op("act", lambda e: e.activation(out=sq2[:, c % 2, :], in_=xt[b][:, c, :], func=AF.Square),
             reads=[("xt", b)], writes=[("sq2", c % 2)])

    def ones_mm(it, c):
        P.op("pe", lambda e: e.matmul(ps[0][:, :T], lhsT=ones[:, :], rhs=sq2[:, c % 2, :], start=(c == 0), stop=(c == 7)),
             reads=[("sq2", c % 2), "ones"], writes=[("ps", 0)])

    def norm_back(it):
        b = it % 2
        P.op("act", lambda e: e.activation(out=rs[:, :], in_=ps[0][:, :T], func=AF.Sqrt, scale=1.0 / D,
                                           bias=bufs["eps"][:, 0:1]), reads=[("ps", 0), "eps"], writes=["rs"])
        P.op("dve", lambda e: e.reciprocal(out=rs[:, :], in_=rs[:, :]), reads=["rs"], writes=["rs"])
        for c in range(8):
            P.op("dve", lambda e, c=c: e.scalar_tensor_tensor(
                out=hT[:, c, :], in0=xt[b][:, c, :], scalar=nw_sb[:, c:c + 1], in1=rs[:, :],
                op0=ALU.mult, op1=ALU.mult), reads=[("xt", b), "rs", "nw"], writes=["hT"])

    load_x(0)
    for c in range(8):
        sq_op(0, c)
        ones_mm(0, c)
    norm_back(0)
    for it in range(NT):
        b = it % 2
        nxt = it + 1 < NT
        if nxt:
            load_x(it + 1)
        for j in range(NF):
            pg = 1 + (j % 2) * 2
            pu = pg + 1
            fs = slice(j * 128, (j + 1) * 128)
            for c in range(8):
                P.op("pe", lambda e, c=c, fs=fs, pg=pg: e.matmul(ps[pg][:, :T], lhsT=wg_sb[:, c, fs], rhs=hT[:, c, :],
                                                                   start=(c == 0), stop=(c == 7)),
                     reads=wgk + ["hT"], writes=[("ps", pg)])
            for c in range(8):
                P.op("pe", lambda e, c=c, fs=fs, pu=pu: e.matmul(ps[pu][:, :T], lhsT=wu_sb[:, c, fs], rhs=hT[:, c, :],
                                                                   start=(c == 0), stop=(c == 7)),
                     reads=wuk + ["hT"], writes=[("ps", pu)])
            k = j % 2
            P.op("act", lambda e, pg=pg, k=k: e.activation(out=sg[k][:, :], in_=ps[pg][:, :T], func=AF.Silu),
                 reads=[("ps", pg)], writes=[("sg", k)])
            P.op("dve", lambda e, pu=pu, k=k, j=j: e.tensor_tensor(out=aT[:, j, :], in0=ps[pu][:, :T], in1=sg[k][:, :],
                                                                   op=ALU.mult),
                 reads=[("ps", pu), ("sg", k)], writes=[("aT", j)])
        if nxt:
            sq_op(it + 1, 0)
            sq_op(it + 1, 1)
        for i in range(8):
            py = 5 + (i % 2)
            ds = slice(i * 128, (i + 1) * 128)
            for j in range(NF):
                P.op("pe", lambda e, j=j, ds=ds, py=py: e.matmul(ps[py][:, :T], lhsT=wd_sb[:, j, ds], rhs=aT[:, j, :],
                                                                   start=(j == 0), stop=(j == NF - 1)),
                     reads=wdk + [("aT", j)], writes=[("ps", py)])
            P.op("dve", lambda e, i=i, py=py, b=b: e.scalar_tensor_tensor(
                out=xt[b][:, i, :], in0=ps[py][:, :T], scalar=0.5, in1=xt[b][:, i, :],
                op0=ALU.mult, op1=ALU.add),
                reads=[("ps", py), ("xt", b)], writes=[("xt", b)])
            if nxt and i < 4:
                ones_mm(it + 1, 2 * i)
                ones_mm(it + 1, 2 * i + 1)
                if i < 3:
                    sq_op(it + 1, 2 * i + 2)
                    sq_op(it + 1, 2 * i + 3)
                else:
                    norm_back(it + 1)
        P.dma("sp", xout[:, :, tsl_(it)], xt[b][:, :, :], reads=[("xt", b)], writes=[(tag, "out", it)])


def alloc_ffn_bufs(C, cst, T=512):
    b = {}
    b["wg"] = C.sb("wg_sb", [128, 8, DFF], BF16)
    b["wu"] = C.sb("wu_sb", [128, 8, DFF], BF16)
    b["wd"] = C.sb("wd_sb", [128, DFF // 128, D], BF16)
    b["nw"] = C.sb("nw_sb", [128, 8], F32)
    b["xt"] = [C.sb("xt%d" % i, [128, 8, T], F32) for i in range(2)]
    b["hT"] = C.sb("hT", [128, 8, T], BF16)
    b["aT"] = C.sb("aT", [128, DFF // 128, T], BF16)
    b["rs"] = C.sb("rs", [128, T], F32)
    b["sg"] = [C.sb("sg%d" % i, [128, T], F32) for i in range(2)]
    b["sq2"] = C.sb("sq2", [128, 2, T], BF16)
    b["ones"] = cst["ones"]
    b["eps"] = cst["eps"]
    return b


class Stager:
    def __init__(self, C, name, cols=704, n=2):
        self.C = C

    def load(self, dst, src, key, np_=128):
        P = self.C.P
        n = src.shape[-1]
        step = 1536
        for c0 in range(0, n, step):
            c1 = min(n, c0 + step)
            P.dma("pool", dst[:, c0:c1], src[:, c0:c1], multi=[key])


def emit_rmsnorm(C, xt, xkey, nw_sb, sq, sqkeys, hT, rs, cst, T, psb=0, hkey="hT"):
    P = C.P
    ps = C.psum
    P.op("act", lambda e: e.activation(out=sq[:, 0:8, :], in_=xt[:, :, :], func=AF.Square),
         reads=[xkey], writes=list(sqkeys))
    for c in range(8):
        P.op("pe", lambda e, c=c: e.matmul(ps[psb][:, :T], lhsT=cst["ones"][:, :], rhs=sq[:, c, :],
                                             start=(c == 0), stop=(c == 7)),
             reads=[sqkeys[c], "ones"], writes=[("ps", psb)])
    P.op("act", lambda e: e.activation(out=rs[:, :], in_=ps[psb][:, :T], func=AF.Sqrt,
                                       scale=1.0 / D, bias=cst["eps"][:, 0:1]),
         reads=[("ps", psb), "eps"], writes=["rs"])
    P.op("dve", lambda e: e.reciprocal(out=rs[:, :], in_=rs[:, :]), reads=["rs"], writes=["rs"])
    for c in range(8):
        P.op("dve", lambda e, c=c: e.scalar_tensor_tensor(
            out=hT[:, c, :], in0=xt[:, c, :], scalar=nw_sb[:, c:c + 1], in1=rs[:, :],
            op0=ALU.mult, op1=ALU.mult),
            reads=[xkey, "rs", "nw"], writes=[hkey])


def alloc_consts(C, cin):
    P = C.P
    cst = {}
    cst["ones"] = C.sb("ones", [128, 128], BF16)
    cst["eps"] = C.sb("epsc", [128, 1], F32)
    cst["one"] = C.sb("onec", [128, 1], F32)
    cst["identb"] = C.sb("identb", [128, 128], BF16)
    cst["identf"] = C.sb("identf", [128, 128], F32)
    cst["maskT"] = C.sb("maskT", [128, 128], F32)
    cst["maskTs"] = C.sb("maskTs", [128, 128], F32)
    cst["triu"] = C.sb("triu", [128, 128], BF16)
    P.op("pool", lambda e: e.memset(cst["ones"][:, :], 1.0), writes=["ones"])
    P.op("pool", lambda e: e.memset(cst["eps"][:, :], EPS), writes=["eps"])
    P.op("pool", lambda e: e.memset(cst["one"][:, :], 1.0), writes=["one"])
    P.dma("sp", cst["identf"][:, :], cin["identf"], writes=["identf"])
    P.dma("sp", cst["maskT"][:, :], cin["maskT"], writes=["maskT"])
    P.dma("sp", cst["maskTs"][:, :], cin["maskTs"], writes=["maskTs"])
    P.op("pool", lambda e: e.tensor_copy(out=cst["identb"][:, :], in_=cst["identf"][:, :]),
         reads=["identf"], writes=["identb"])
    cst["tmpf"] = C.sb("tmpf", [128, 128], F32)
    P.dma("sp", cst["tmpf"][:, :], cin["triu"], writes=["tmpf"])
    P.op("pool", lambda e: e.tensor_copy(out=cst["triu"][:, :], in_=cst["tmpf"][:, :]),
         reads=["tmpf"], writes=["triu"])
    return cst


def host_consts():
    i = np.arange(128)
    c = {}
    c["identf"] = np.eye(128, dtype=np.float32)
    c["maskT"] = (i[:, None] <= i[None, :]).astype(np.float32)
    c["maskTs"] = (i[:, None] < i[None, :]).astype(np.float32)
    c["triu"] = (i[:, None] > i[None, :]).astype(np.float32)
    c["invc"] = np.broadcast_to((1.0 / (np.arange(16) + 1.0)).astype(np.float32)[None, :], (128, 16)).copy()
    oh = np.zeros((4, 4, 128), np.float32)
    for h in range(4):
        oh[h, h, :] = 1.0
    c["onehot4"] = oh.reshape(4, 512)
    oh = np.zeros((16, 16, 128), np.float32)
    for h in range(16):
        oh[h, h, :] = 1.0
    c["onehot16"] = oh.reshape(16, 2048)
    c["blk1"] = ((i[:, None] // 64) == (i[None, :] // 64)).astype(np.float32)
    c["negm"] = np.where(i[:, None] > i[None, :], -30000.0, 0.0).astype(np.float32)
    return c


def conv_silu_pe(C, cst, tagp, src, nch, cw, cb, sink):
    P = C.P
    ps = C.psum
    xrow = [C.sb("%s_xrow%d" % (tagp, i), [128, 3 + S], BF16) for i in range(2)]
    dg = [C.sb("%s_dg%d" % (tagp, i), [128, 4, 128], BF16) for i in range(2)]
    for k in range(2):
        P.op("pool", lambda e, k=k: e.memset(xrow[k][:, 0:3], 0.0), writes=[(tagp, "xrow", k)])
    n = 0
    for m in range(nch):
        k = m % 2
        for h_ in range(4):
            P.dma("pool", xrow[k][:, 3 + h_ * 1024:3 + (h_ + 1) * 1024], src[m * 128:(m + 1) * 128, h_ * 1024:(h_ + 1) * 1024],
                  multi=[(tagp, "xrow", k)])
        for j in range(4):
            P.op("dve", lambda e, k=k, m=m, j=j: e.tensor_scalar(out=dg[k][:, j, :], in0=cst["identf"][:, :],
                                                                 scalar1=cw[:, m, j:j + 1], scalar2=None, op0=ALU.mult),
                 reads=["identf", "cw"], writes=[(tagp, "dg", k)])
        for it in range(S // 512):
            pb = 1 + n % 4
            n += 1
            for j in range(4):
                P.op("pe", lambda e, k=k, j=j, it=it, pb=pb: e.matmul(
                    ps[pb][:, :512], lhsT=dg[k][:, j, :], rhs=xrow[k][:, j + it * 512:j + it * 512 + 512],
                    start=(j == 0), stop=(j == 3)), reads=[(tagp, "dg", k), (tagp, "xrow", k)], writes=[("ps", pb)])
            sink(m, it, ps[pb][:, :512], cb[:, m:m + 1], ("ps", pb))


def evac(P, eng, out, in_, reads, writes):
    if eng == "act":
        return P.op("act", lambda e: e.activation(out=out, in_=in_, func=AF.Copy), reads=reads, writes=writes)
    return P.op(eng, lambda e: e.tensor_copy(out=out, in_=in_), reads=reads, writes=writes)


def proj_norm_tiles(C, cst, xT_in, nw_dram, T, body):
    P = C.P
    xin = xT_in.rearrange("(c p) t -> p c t", p=128)
    nw_sb = C.sb("pn_nw", [128, 8], F32)
    xt = [C.sb("pn_xt%d" % i, [128, 8, T], F32) for i in range(2)]
    sq = C.sb("pn_sq", [128, 8, T], BF16)
    hT = [C.sb("pn_hT%d" % i, [128, 8, T], BF16) for i in range(2)]
    rs = C.sb("pn_rs", [128, T], F32)
    P.dma("sp", nw_sb[:, :], nw_dram, writes=["nw"])
    NT = S // T

    def norm(it):
        b = it % 2
        P.dma("sp", xt[b][:, :, :], xin[:, :, it * T:(it + 1) * T], writes=[("pn_xt", b)])
        emit_rmsnorm(C, xt[b], ("pn_xt", b), nw_sb, sq, [("pn_sq", c) for c in range(8)], hT[b], rs, cst, T,
                     hkey=("hT", b))

    norm(0)
    for it in range(NT):
        if it + 1 < NT:
            norm(it + 1)
        body(it, hT[it % 2], ("hT", it % 2))


def even_mixer_phase(C, cst, cin, xT_in, xT_out, W, upto="E"):
    P = C.P
    ps = C.psum
    T = 512
    uT = C.dram("e_uT", [512, S], F32)
    qkT = C.dram("e_qkT", [1024, S], F32)
    v_tm = C.dram("e_v", [S, 512], BF16)
    o_tm = C.dram("e_o", [S, 512], F32)
    gT = C.dram("e_g", [8, S], F32)
    mixT = C.dram("e_mix", [1024, S], BF16)

    ws_tm = C.sb("e_ws", [128, 32, 4], F32)
    thr_tm = C.sb("e_thr", [128, 32, 4], F32)
    dcol = C.sb("e_dcol", [128, 4, 32], F32)

    with C.scope():
        w_sb = C.sb("e_win", [128, 8, 2568], BF16)
        stg = Stager(C, "e_stg")
        win = W["w_in"].rearrange("(c p) f -> p c f", p=128)
        for c in range(8):
            stg.load(w_sb[:, c, :], win[:, c, :], "win")
        fm = [C.sb("e_fm%d" % i, [128, T], F32) for i in range(6)]
        vst = [C.sb("e_vst%d" % i, [128, 512], BF16) for i in range(4)]
        ost = [C.sb("e_ost%d" % i, [128, 512], F32) for i in range(4)]
        gst = [C.sb("e_gst%d" % i, [4, T], F32) for i in range(2)]
        cnt = {"fm": 0, "tm": 0, "g": 0}

        def body(it, hT, hk):
            tsl = slice(it * T, (it + 1) * T)
            for m in range(12):
                pb = 1 + m % 4
                for c in range(8):
                    P.op("pe", lambda e, c=c, m=m, pb=pb: e.matmul(
                        ps[pb][:, :T], lhsT=w_sb[:, c, m * 128:(m + 1) * 128], rhs=hT[:, c, :],
                        start=(c == 0), stop=(c == 7)), reads=["win", hk], writes=[("ps", pb)])
                k = cnt["fm"] % 6
                cnt["fm"] += 1
                evac(P, "act" if m % 2 == 0 else "dve", fm[k][:, :], ps[pb][:, :T], [("ps", pb)], [("fm", k)])
                dst = uT[m * 128:(m + 1) * 128, tsl] if m < 4 else qkT[(m - 4) * 128:(m - 3) * 128, tsl]
                P.dma("sp", dst, fm[k][:, :], reads=[("fm", k)], writes=[("A_out", m, it)])
            for q in range(4):
                tok = slice(q * 128, (q + 1) * 128)
                r0 = it * T + q * 128
                for which, col0, pb, stb, dstT in (("v", 1536, 5, vst, v_tm), ("o", 2048, 6, ost, o_tm)):
                    for c in range(8):
                        P.op("pe", lambda e, c=c, tok=tok, col0=col0, pb=pb: e.matmul(
                            ps[pb][:, :512], lhsT=hT[:, c, tok], rhs=w_sb[:, c, col0:col0 + 512],
                            start=(c == 0), stop=(c == 7)), reads=["win", hk], writes=[("ps", pb)])
                    k = q % 4
                    evac(P, "act" if which == "v" else "dve", stb[k][:, :], ps[pb][:, :512],
                         [("ps", pb)], [(which + "st", k)])
                    P.dma("sp", dstT[r0:r0 + 128, :], stb[k][:, :], reads=[(which + "st", k)],
                          writes=[("A_out", which, r0)])
            for gi_ in range(2):
                col0 = 2560 + 4 * gi_
                for c in range(8):
                    P.op("pe", lambda e, c=c, col0=col0: e.matmul(
                        ps[7][0:4, :T], lhsT=w_sb[:, c, col0:col0 + 4], rhs=hT[:, c, :],
                        start=(c == 0), stop=(c == 7)), reads=["win", hk], writes=[("ps", 7)])
                evac(P, "dve", gst[gi_][:, :], ps[7][0:4, :T], [("ps", 7)], [("gst", gi_)])
                P.dma("sp", gT[4 * gi_:4 * gi_ + 4, tsl], gst[gi_][:, :], reads=[("gst", gi_)],
                      writes=[("A_out", "g", gi_, it)])

        proj_norm_tiles(C, cst, xT_in, W["mix_norm"], T, body)

    if upto == "A":
        return
    with C.scope():
        PADL = 16
        ub = C.sb("e_ub", [128, PADL + S], F32)
        sA = C.sb("e_sA", [128, PADL + S], F32)
        sB = C.sb("e_sB", [128, PADL + S], F32)
        pooled = C.sb("e_pooled", [128, S], BF16)
        aout = C.sb("e_aout", [128, S], BF16)
        pw_sb = C.sb("e_pw", [128, 4, 128], BF16)
        psc = C.sb("e_psc", [128, 4], F32)
        invc = C.sb("e_invc", [128, 16], F32)
        tmpc = C.sb("e_tmpc", [128, 16], F32)
        stg = Stager(C, "e_stgB", cols=512)
        stg.load(pw_sb.rearrange("p a b -> p (a b)"), W["pool_w"], "pw")
        P.dma("sp", psc[:, :], W["pool_scale"], writes=["psc"])
        P.dma("sp", invc[:, :], cin["invc"], writes=["invc"])
        for bname, buf in (("ub", ub), ("sA", sA), ("sB", sB)):
            P.op("pool", lambda e, buf=buf: e.memset(buf[:, 0:PADL], 0.0), writes=[bname])
        for g in range(4):
            win_ = 2 << g
            P.dma("sp", ub[:, PADL:], uT[g * 128:(g + 1) * 128, :], reads=["ub"], writes=["ub"])
            src, sname = ub, "ub"
            dsts = [(sA, "sA"), (sB, "sB")]
            for k in range(g + 1):
                sh = 1 << k
                dst, dname = dsts[k % 2]
                P.op("dve", lambda e, src=src, dst=dst, sh=sh: e.tensor_tensor(
                    out=dst[:, PADL:], in0=src[:, PADL:], in1=src[:, PADL - sh:PADL - sh + S], op=ALU.add),
                    reads=[sname], writes=[dname])
                src, sname = dst, dname
            P.op("dve", lambda e, src=src, win_=win_: e.scalar_tensor_tensor(
                out=pooled[:, :], in0=src[:, PADL:], scalar=1.0 / win_, in1=ub[:, PADL:],
                op0=ALU.mult, op1=ALU.subtract), reads=[sname, "ub"], writes=["pooled"])
            nfix = win_ - 1
            P.op("dve", lambda e, src=src, nfix=nfix: e.tensor_tensor(
                out=tmpc[:, 0:nfix], in0=src[:, PADL:PADL + nfix], in1=invc[:, 0:nfix], op=ALU.mult),
                reads=[sname, "invc"], writes=["tmpc"])
            P.op("dve", lambda e, nfix=nfix: e.tensor_tensor(
                out=pooled[:, 0:nfix], in0=tmpc[:, 0:nfix], in1=ub[:, PADL:PADL + nfix], op=ALU.subtract),
                reads=["tmpc", "ub", "pooled"], writes=["pooled"])
            for it in range(S // T):
                pb = 1 + it % 2
                P.op("pe", lambda e, g=g, it=it, pb=pb: e.matmul(
                    ps[pb][:, :T], lhsT=pw_sb[:, g, :], rhs=pooled[:, it * T:(it + 1) * T], start=True, stop=True),
                    reads=["pw", "pooled"], writes=[("ps", pb)])
                P.op("dve", lambda e, g=g, it=it, pb=pb: e.tensor_scalar(
                    out=aout[:, it * T:(it + 1) * T], in0=ps[pb][:, :T], scalar1=psc[:, g:g + 1], scalar2=None,
                    op0=ALU.mult), reads=[("ps", pb), "psc"], writes=["aout"])
            P.dma("sp", mixT[g * 128:(g + 1) * 128, :], aout[:, :], reads=["aout"], writes=[("mixA", g)])

    if upto == "B":
        return
    with C.scope():
        gi = C.sb("e_gi", [4, S], F32)
        gf = C.sb("e_gf", [4, S], F32)
        Bc = C.sb("e_Bc", [4, S], F32)
        Ac = C.sb("e_Ac", [4, S], F32)
        Gm = C.sb("e_Gm", [4, S], F32)
        gb = C.sb("e_gb", [4, 2], F32)
        nbf = C.sb("e_nbf", [4, 1], F32)
        mucol = C.sb("e_mucol", [4, 33], F32)
        dd = C.sb("e_dd", [4, 32], F32)
        oh4 = C.sb("e_oh4", [4, 4, 128], F32)
        P.dma("sp", gi[:, :], gT[0:4, :], writes=["gi"])
        P.dma("sp", gf[:, :], gT[4:8, :], writes=["gf"])
        P.dma("sp", gb[:, :], W["gate_bias"], writes=["gb"])
        P.dma("sp", oh4.rearrange("p a b -> p (a b)"), cin["onehot4"], writes=["oh4"])
        one4 = cst["one"][0:4, 0:1]
        P.op("dve", lambda e: e.tensor_scalar(out=gi[:, :], in0=gi[:, :], scalar1=gb[:, 0:1], scalar2=None, op0=ALU.add),
             reads=["gi", "gb"], writes=["gi"])
        P.op("dve", lambda e: e.tensor_scalar(out=nbf[:, :], in0=gb[:, 1:2], scalar1=-1.0, scalar2=None, op0=ALU.mult),
             reads=["gb"], writes=["nbf"])
        P.op("act", lambda e: e.activation(out=gf[:, :], in_=gf[:, :], func=AF.Exp, scale=-1.0, bias=nbf[:, 0:1]),
             reads=["gf", "nbf"], writes=["gf"])
        P.op("act", lambda e: e.activation(out=gf[:, :], in_=gf[:, :], func=AF.Ln, scale=1.0, bias=one4),
             reads=["gf", "one"], writes=["gf"])
        P.op("dve", lambda e: e.tensor_tensor_scan(out=Bc[:, :], data0=one4.to_broadcast([4, S]), data1=gf[:, :],
                                                   initial=0.0, op0=ALU.mult, op1=ALU.subtract),
             reads=["gf", "one"], writes=["Bc"])
        P.op("dve", lambda e: e.tensor_tensor(out=Ac[:, :], in0=gi[:, :], in1=Bc[:, :], op=ALU.subtract),
             reads=["gi", "Bc"], writes=["Ac"])
        P.op("dve", lambda e: e.tensor_tensor_scan(out=Gm[:, :], data0=Ac[:, :], data1=Ac[:, :],
                                                   initial=0.0, op0=ALU.max, op1=ALU.max),
             reads=["Ac"], writes=["Gm"])
        Gend = Gm.rearrange("h (c l) -> h c l", l=128)[:, :, 127:128]
        P.op("dve", lambda e: e.memset(mucol[:, 0:1], 0.0), writes=["mucol"])
        P.op("dve", lambda e: e.tensor_copy(out=mucol[:, 1:33].unsqueeze(2), in_=Gend), reads=["Gm", "mucol"],
             writes=["mucol"])
        P.op("dve", lambda e: e.tensor_tensor(out=dd[:, :], in0=mucol[:, 0:32], in1=mucol[:, 1:33], op=ALU.subtract),
             reads=["mucol"], writes=["dd"])
        P.op("act", lambda e: e.activation(out=dd[:, :], in_=dd[:, :], func=AF.Exp), reads=["dd"], writes=["dd"])
        A3 = Ac.rearrange("h (c l) -> h c l", l=128)
        B3 = Bc.rearrange("h (c l) -> h c l", l=128)
        P.op("dve", lambda e: e.tensor_tensor(out=A3, in0=A3, in1=Gend.to_broadcast([4, 32, 128]), op=ALU.subtract),
             reads=["Ac", "Gm"], writes=["Ac"])
        P.op("act", lambda e: e.activation(out=Ac[:, :], in_=Ac[:, :], func=AF.Exp), reads=["Ac"], writes=["Ac"])
        P.op("dve", lambda e: e.tensor_tensor(out=B3, in0=B3, in1=Gend.to_broadcast([4, 32, 128]), op=ALU.add),
             reads=["Bc", "Gm"], writes=["Bc"])
        P.op("act", lambda e: e.activation(out=Bc[:, :], in_=Bc[:, :], func=AF.Exp, scale=-1.0), reads=["Bc"],
             writes=["Bc"])
        for h in range(4):
            P.op("pe", lambda e, h=h: e.matmul(ps[1][:, h * 32:(h + 1) * 32], lhsT=oh4[:, h, :], rhs=dd[:, :],
                                                start=True, stop=True), reads=["oh4", "dd"], writes=[("ps", 1)])
        P.op("dve", lambda e: e.tensor_copy(out=dcol.rearrange("p a b -> p (a b)"), in_=ps[1][:, 0:128]),
             reads=[("ps", 1)], writes=["dcol"])
        for src, sname, dst, dname, pb in ((Ac, "Ac", ws_tm, "ws_tm", 2), (Bc, "Bc", thr_tm, "thr_tm", 3)):
            for c in range(32):
                P.op("pe", lambda e, src=src, c=c, pb=pb: e.transpose(
                    out=ps[pb][:, c * 4:(c + 1) * 4], in_=src[:, c * 128:(c + 1) * 128], identity=cst["identf"][0:4, 0:4]),
                    reads=[sname, "identf"], writes=[("ps", pb)])
            P.op("dve", lambda e, dst=dst, pb=pb: e.tensor_copy(out=dst.rearrange("p a b -> p (a b)"),
                                                               in_=ps[pb][:, 0:128]),
                 reads=[("ps", pb)], writes=[dname])

    if upto == "C":
        return
    with C.scope():
        qkb = C.sb("e_qkb", [128, 8, S], BF16)
        cw = C.sb("e_cw", [128, 8, 4], F32)
        cb = C.sb("e_cb", [128, 8], F32)
        qtmp = [C.sb("e_qtmp%d" % i, [128, 512], F32) for i in range(2)]
        P.dma("sp", cw.rearrange("p a b -> p (a b)"), W["qk_conv_w"], writes=["cw"])
        P.dma("sp", cb[:, :], W["qk_conv_b"], writes=["cb"])
        qcnt = [0]

        def sink_e(m, it, psap, bias, pkey):
            tsl = slice(it * 512, (it + 1) * 512)
            if m < 4:
                k = qcnt[0] % 2
                qcnt[0] += 1
                P.op("act", lambda e: e.activation(out=qtmp[k][:, :], in_=psap, func=AF.Silu, bias=bias),
                     reads=[pkey, "cb"], writes=[("qtmp", k)])
                P.op("dve", lambda e: e.tensor_scalar(out=qkb[:, m, tsl], in0=qtmp[k][:, :], scalar1=128.0 ** -0.5,
                                                       scalar2=None, op0=ALU.mult),
                     reads=[("qtmp", k)], writes=[("qkb", m)])
            else:
                P.op("act", lambda e: e.activation(out=qkb[:, m, tsl], in_=psap, func=AF.Silu, bias=bias),
                     reads=[pkey, "cb"], writes=[("qkb", m)])

        conv_silu_pe(C, cst, "ecv", qkT, 8, cw, cb, sink_e)

        if upto == "D0":
            return
        vch = [C.sb("e_vch%d" % i, [128, 4, 128], BF16) for i in range(2)]
        och = [C.sb("e_och%d" % i, [128, 512], F32) for i in range(2)]
        vw = [C.sb("e_vw%d" % i, [128, 4, 136], BF16) for i in range(2)]
        ktm = [C.sb("e_ktm%d" % i, [128, 4, 128], BF16) for i in range(2)]
        PT = C.sb("e_PT", [128, 4, 128], BF16)
        Sst = C.sb("e_S", [128, 4, 129], F32)
        Sbf = C.sb("e_Sbf", [128, 4, 136], BF16)
        nwb = C.sb("e_nwb", [128, 512], F32)
        nwo = C.sb("e_nwo", [128, 512], F32)
        den = C.sb("e_den", [128, 4], F32)
        ss = C.sb("e_ss", [128, 4], F32)
        junk = C.sb("e_junk", [128, 128], F32)
        bout = C.sb("e_bout", [128, 4, 128], BF16)
        boutT = [C.sb("e_boutT%d" % i, [128, 4, 128], BF16) for i in range(2)]
        P.dma("sp", nwb[:, :], W["mlstm_norm"].partition_broadcast(128), writes=["nwb"])
        P.op("pool", lambda e: e.memset(Sst.rearrange("p a b -> p (a b)"), 0.0), writes=["S0", "S1", "S2", "S3"])
        ps0b = ps[0][:, :].bitcast(BF16)
        pending_tail = []
        for c in range(1 if upto in ("D1", "D2", "D3") else 32):
            b = c % 2
            blk = slice(c * 128, (c + 1) * 128)
            P.dma("sp", vch[b].rearrange("p a b -> p (a b)"), v_tm[blk, :], writes=[("vch", b)])
            P.dma("sp", och[b][:, :], o_tm[blk, :], writes=[("och", b)])
            P.op("dve", lambda e, b=b, c=c: e.tensor_tensor(
                out=vw[b][:, :, 0:128], in0=vch[b][:, :, :],
                in1=ws_tm[:, c, :].unsqueeze(2).to_broadcast([128, 4, 128]), op=ALU.mult),
                reads=[("vch", b), "ws_tm"], writes=[("vw", b)])
            P.op("dve", lambda e, b=b, c=c: e.tensor_copy(out=vw[b][:, :, 128:129], in_=ws_tm[:, c, :].unsqueeze(2)),
                 reads=["ws_tm", ("vw", b)], writes=[("vw", b)])
            for h in range(4):
                P.op("pe", lambda e, h=h, blk=blk: e.transpose(out=ps0b[:, h * 128:(h + 1) * 128], in_=qkb[:, 4 + h, blk],
                                                                identity=cst["identb"][:, :]),
                     reads=[("qkb", 4 + h), "identb"], writes=[("ps0", "k")])
            P.op("act", lambda e, b=b: e.activation(out=ktm[b].rearrange("p a b -> p (a b)"), in_=ps0b[:, 0:512],
                                                    func=AF.Copy), reads=[("ps0", "k")], writes=[("ktm", b)])
            sb_ = 1 + b
            for h in range(4):
                P.op("pe", lambda e, h=h, blk=blk, sb_=sb_: e.matmul(
                    ps[sb_][:, h * 128:(h + 1) * 128], lhsT=qkb[:, 4 + h, blk], rhs=qkb[:, h, blk],
                    start=True, stop=True), reads=[("qkb", 4 + h), ("qkb", h)], writes=[("ps", sb_)])
            P.op("dve", lambda e, sb_=sb_: e.tensor_tensor(
                out=PT[:, :, :], in0=ps[sb_][:, :].rearrange("p (a b) -> p a b", b=128),
                in1=cst["maskT"][:, :].unsqueeze(1).to_broadcast([128, 4, 128]), op=ALU.mult),
                reads=[("ps", sb_), "maskT"], writes=["PT"])
            if upto == "D1":
                continue
            for h in range(4):
                ob = 3 + h // 2
                oc = (h % 2) * 129
                P.op("dve", lambda e, h=h, c=c: e.tensor_scalar(out=Sbf[:, h, 0:129], in0=Sst[:, h, :],
                                                                 scalar1=dcol[:, h, c:c + 1], scalar2=None, op0=ALU.mult),
                     reads=["S%d" % h, "dcol"], writes=["Sbf%d" % h])
                P.op("pe", lambda e, h=h, blk=blk, ob=ob, oc=oc: e.matmul(
                    ps[ob][:, oc:oc + 129], lhsT=qkb[:, h, blk], rhs=Sbf[:, h, 0:129], start=True, stop=False),
                    reads=[("qkb", h), "Sbf%d" % h], writes=[("pso", h)])
                P.op("pe", lambda e, h=h, b=b, ob=ob, oc=oc: e.matmul(
                    ps[ob][:, oc:oc + 129], lhsT=PT[:, h, :], rhs=vw[b][:, h, 0:129], start=False, stop=True),
                    reads=["PT", ("vw", b)], writes=[("pso", h)])
                db = 5 + h // 2
                P.op("pe", lambda e, h=h, b=b, db=db, oc=oc: e.matmul(
                    ps[db][:, oc:oc + 129], lhsT=ktm[b][:, h, :], rhs=vw[b][:, h, 0:129], start=True, stop=True),
                    reads=[("ktm", b), ("vw", b)], writes=[("psd", h)])
                P.op("dve", lambda e, h=h, c=c, db=db, oc=oc: e.scalar_tensor_tensor(
                    out=Sst[:, h, :], in0=Sst[:, h, :], scalar=dcol[:, h, c:c + 1], in1=ps[db][:, oc:oc + 129],
                    op0=ALU.mult, op1=ALU.add), reads=["S%d" % h, "dcol", ("psd", h), "Sbf%d" % h], writes=["S%d" % h])
            if upto == "D2":
                continue
            while pending_tail:
                pending_tail.pop(0)()
            P.op("act", lambda e, b=b: e.activation(out=nwo[:, :], in_=och[b][:, :], func=AF.Sigmoid),
                 reads=[("och", b)], writes=["nwo"])
            P.op("dve", lambda e: e.tensor_tensor(out=nwo[:, :], in0=nwo[:, :], in1=nwb[:, :], op=ALU.mult),
                 reads=["nwo", "nwb"], writes=["nwo"])
            for hp in range(2):
                ob = 3 + hp
                Dv = ps[ob][:, 0:258].rearrange("p (a b) -> p a b", b=129)[:, :, 128:129]
                P.op("act", lambda e, hp=hp, Dv=Dv: e.activation(
                    out=den[:, 2 * hp:2 * hp + 2].unsqueeze(2), in_=Dv, func=AF.Abs),
                    reads=[("pso", 2 * hp), ("pso", 2 * hp + 1)], writes=["den"])
            P.op("dve", lambda e, c=c: e.tensor_tensor(out=den[:, :], in0=den[:, :], in1=thr_tm[:, c, :], op=ALU.max),
                 reads=["den", "thr_tm"], writes=["den"])
            P.op("dve", lambda e: e.reciprocal(out=den[:, :], in_=den[:, :]), reads=["den"], writes=["den"])
            for h in range(4):
                ob = 3 + h // 2
                oc = (h % 2) * 129
                P.op("act", lambda e, h=h, ob=ob, oc=oc: e.activation(
                    out=junk[:, :], in_=ps[ob][:, oc:oc + 128], func=AF.Square, scale=den[:, h:h + 1],
                    accum_out=ss[:, h:h + 1]), reads=[("pso", h), "den"], writes=["junk", ("ss", h)])
            P.op("act", lambda e: e.activation(out=ss[:, :], in_=ss[:, :], func=AF.Sqrt, scale=1.0 / 128, bias=cst["eps"][:, 0:1]),
                 reads=[("ss", h) for h in range(4)] + ["eps"], writes=[("ss", h) for h in range(4)])
            P.op("dve", lambda e: e.reciprocal(out=ss[:, :], in_=ss[:, :]), reads=[("ss", h) for h in range(4)],
                 writes=[("ss", h) for h in range(4)])
            P.op("dve", lambda e: e.tensor_tensor(out=ss[:, :], in0=ss[:, :], in1=den[:, :], op=ALU.mult),
                 reads=[("ss", h) for h in range(4)] + ["den"], writes=[("ss", h) for h in range(4)])
            for h in range(4):
                ob = 3 + h // 2
                oc = (h % 2) * 129
                P.op("dve", lambda e, h=h, ob=ob, oc=oc: e.scalar_tensor_tensor(
                    out=bout[:, h, :], in0=ps[ob][:, oc:oc + 128], scalar=ss[:, h:h + 1], in1=nwo[:, h * 128:(h + 1) * 128],
                    op0=ALU.mult, op1=ALU.mult), reads=[("pso", h), ("ss", h), "nwo"], writes=[("bout", h)])
            def emit_tail(c=c, b=b, blk=blk):
                for h in range(4):
                    P.op("pe", lambda e, h=h: e.transpose(out=ps0b[:, 512 + h * 128:512 + (h + 1) * 128], in_=bout[:, h, :],
                                                          identity=cst["identb"][:, :]),
                         reads=[("bout", h), "identb"], writes=[("ps0", "b")])
                P.op("act", lambda e, b=b: e.activation(out=boutT[b].rearrange("p a b -> p (a b)"), in_=ps0b[:, 512:1024],
                                                        func=AF.Copy), reads=[("ps0", "b")], writes=[("boutT", b)])
                P.dma("sp", mixT[512:1024, blk].rearrange("(h p) t -> p h t", p=128), boutT[b][:, :, :],
                      reads=[("boutT", b)], writes=[("mixB", c)])
            pending_tail.append(emit_tail)
        for f_ in pending_tail:
            f_()

    if upto[0] == "D":
        return
    with C.scope():
        wo = C.sb("e_wo", [128, 8, D], BF16)
        stg = Stager(C, "e_stgE")
        wov = W["w_out"].rearrange("(c p) f -> p c f", p=128)
        for c in range(8):
            stg.load(wo[:, c, :], wov[:, c, :], "wo")
        out_proj(C, wo, 8, mixT, xT_in, xT_out, T)


def out_proj(C, wo, nk, mixT, xT_in, xT_out, T):
    P = C.P
    ps = C.psum
    xin = xT_in.rearrange("(c p) t -> p c t", p=128)
    xout = xT_out.rearrange("(c p) t -> p c t", p=128)
    mixv = mixT.rearrange("(c p) t -> p c t", p=128)
    xt = [C.sb("op_xt%d" % i, [128, 8, T], F32) for i in range(2)]
    mt = [C.sb("op_mt%d" % i, [128, nk, T], BF16) for i in range(2)]
    for it in range(S // T):
        b = it % 2
        tsl = slice(it * T, (it + 1) * T)
        P.dma("sp", xt[b][:, :, :], xin[:, :, tsl], writes=[("op_xt", b)])
        P.dma("act", mt[b][:, :, :], mixv[:, :, tsl], writes=[("op_mt", b)])
        for i in range(8):
            pb = 1 + i % 4
            for k in range(nk):
                P.op("pe", lambda e, i=i, k=k, pb=pb, b=b: e.matmul(
                    ps[pb][:, :T], lhsT=wo[:, k, i * 128:(i + 1) * 128], rhs=mt[b][:, k, :],
                    start=(k == 0), stop=(k == nk - 1)), reads=["wo", ("op_mt", b)], writes=[("ps", pb)])
            P.op("dve", lambda e, i=i, pb=pb, b=b: e.tensor_tensor(
                out=xt[b][:, i, :], in0=ps[pb][:, :T], in1=xt[b][:, i, :], op=ALU.add),
                reads=[("ps", pb), ("op_xt", b)], writes=[("op_xt", b)])
        P.dma("sp", xout[:, :, tsl], xt[b][:, :, :], reads=[("op_xt", b)], writes=[("op_out", it)])


def _pc(v, nchunk):
    return np.ascontiguousarray(np.asarray(v, np.float32).reshape(nchunk, 128).T)


def host_params(inp):
    f = lambda k: np.ascontiguousarray(np.asarray(inp[k], np.float32))
    out = {}
    for pre in ("l0_ffn1", "l0_ffn2", "l1_ffn1", "l1_ffn2"):
        out[pre + "_norm"] = _pc(inp[pre + "_norm"], 8)
        for w in ("wg", "wu", "wd"):
            out[pre + "_" + w] = f(pre + "_" + w)
    out["l0_mix_norm"] = _pc(inp["l0_mix_norm"], 8)
    out["l0_w_in"] = f("l0_w_in")
    out["l0_pool_w"] = np.ascontiguousarray(np.transpose(f("l0_pool_w"), (1, 0, 2)).reshape(128, 512))
    out["l0_pool_scale"] = _pc(inp["l0_pool_scale"], 4)
    out["l0_qk_conv_w"] = np.ascontiguousarray(f("l0_qk_conv_w").reshape(4, 8, 128).transpose(2, 1, 0).reshape(128, 32))
    out["l0_qk_conv_b"] = _pc(inp["l0_qk_conv_b"], 8)
    out["l0_gate_bias"] = np.ascontiguousarray(f("l0_gate_bias").reshape(2, 4).T)
    out["l0_mlstm_norm"] = f("l0_mlstm_norm")
    out["l0_w_out"] = f("l0_w_out")
    out["l1_mix_norm"] = _pc(inp["l1_mix_norm"], 8)
    out["l1_w_in"] = f("l1_w_in")
    out["l1_ssd_conv_w"] = np.ascontiguousarray(f("l1_ssd_conv_w").reshape(4, 16, 128).transpose(2, 1, 0).reshape(128, 64))
    out["l1_ssd_conv_b"] = _pc(inp["l1_ssd_conv_b"], 16)
    out["l1_ssd_dt_bias"] = f("l1_ssd_dt_bias").reshape(16, 1)
    out["l1_ssd_A_log"] = f("l1_ssd_A_log").reshape(16, 1)
    out["l1_ssd_D"] = f("l1_ssd_D")
    out["l1_ssd_norm"] = f("l1_ssd_norm")
    out["l1_sb_q_norm"] = np.ascontiguousarray(np.tile(f("l1_sb_q_norm"), 2).reshape(128, 1))
    out["l1_sb_k_norm"] = np.ascontiguousarray(np.tile(f("l1_sb_k_norm"), 2).reshape(128, 1))
    out["l1_w_out"] = f("l1_w_out")
    return out


PARAM_SHAPES = {
    "l0_mix_norm": [128, 8], "l0_w_in": [1024, 2568], "l0_pool_w": [128, 512], "l0_pool_scale": [128, 4],
    "l0_qk_conv_w": [128, 32], "l0_qk_conv_b": [128, 8], "l0_gate_bias": [4, 2], "l0_mlstm_norm": [512],
    "l0_w_out": [1024, 1024],
    "l1_mix_norm": [128, 8], "l1_w_in": [1024, 4624], "l1_ssd_conv_w": [128, 64], "l1_ssd_conv_b": [128, 16],
    "l1_ssd_dt_bias": [16, 1], "l1_ssd_A_log": [16, 1], "l1_ssd_D": [16], "l1_ssd_norm": [1024],
    "l1_sb_q_norm": [128, 1], "l1_sb_k_norm": [128, 1], "l1_w_out": [1536, 1024],
}
for _pre in ("l0_ffn1", "l0_ffn2", "l1_ffn1", "l1_ffn2"):
    PARAM_SHAPES[_pre + "_norm"] = [128, 8]
    PARAM_SHAPES[_pre + "_wg"] = [D, DFF]
    PARAM_SHAPES[_pre + "_wu"] = [D, DFF]
    PARAM_SHAPES[_pre + "_wd"] = [DFF, D]
CONST_SHAPES = {"identf": [128, 128], "maskT": [128, 128], "maskTs": [128, 128], "triu": [128, 128],
                "invc": [128, 16], "onehot4": [4, 512], "onehot16": [16, 2048], "blk1": [128, 128], "negm": [128, 128]}


def odd_mixer_phase(C, cst, cin, xT_in, xT_out, W, upto="E"):
    P = C.P
    ps = C.psum
    T = 512
    o_z = C.dram("o_z", [S, 1024], F32)
    o_xbcT = C.dram("o_xbcT", [2048, S], F32)
    o_dtT = C.dram("o_dtT", [16, S], F32)
    o_qT = C.dram("o_qT", [512, S], BF16)
    o_kT = C.dram("o_kT", [512, S], BF16)
    o_v = C.dram("o_v", [S, 512], BF16)
    o_xc = C.dram("o_xc", [2048, S], BF16)
    mixT = C.dram("o_mix", [1536, S], BF16)

    bias_tm = C.sb("o_bias_tm", [128, 32, 16], F32)
    est_tm = C.sb("o_est_tm", [128, 32, 16], F32)
    dtte_tm = C.sb("o_dtte_tm", [128, 32, 16], F32)
    cdb = C.sb("o_cdb", [128, 32, 16], F32)
    blk1 = C.sb("o_blk1", [128, 128], BF16)
    negm = C.sb("o_negm", [128, 128], BF16)
    eps64 = C.sb("o_eps64", [128, 1], F32)
    P.op("pool", lambda e: e.memset(eps64[:, :], 64.0 * EPS), writes=["eps64"])
    P.dma("sp", cst["tmpf"][:, :], cin["blk1"], reads=["tmpf"], writes=["tmpf"])
    P.op("pool", lambda e: e.tensor_copy(out=blk1[:, :], in_=cst["tmpf"][:, :]), reads=["tmpf"], writes=["blk1"])
    P.dma("sp", cst["tmpf"][:, :], cin["negm"], reads=["tmpf"], writes=["tmpf"])
    P.op("pool", lambda e: e.tensor_copy(out=negm[:, :], in_=cst["tmpf"][:, :]), reads=["tmpf"], writes=["negm"])

    with C.scope():
        w_sb = C.sb("o_win", [128, 8, 4624], BF16)
        stg = Stager(C, "o_stg", cols=1156)
        win = W["w_in"].rearrange("(c p) f -> p c f", p=128)
        for c in range(8):
            stg.load(w_sb[:, c, :], win[:, c, :], "win")
        fm = [C.sb("o_fm%d" % i, [128, T], F32) for i in range(6)]
        zst = [C.sb("o_zst%d" % i, [128, 512], F32) for i in range(6)]
        vst = [C.sb("o_vst%d" % i, [128, 512], BF16) for i in range(4)]
        dst_ = C.sb("o_dst", [16, T], F32)
        sqb = [C.sb("o_sqb%d" % i, [128, T], BF16) for i in range(4)]
        rr = [C.sb("o_rr%d" % i, [128, T], F32) for i in range(2)]
        qn = [C.sb("o_qn%d" % i, [128, T], BF16) for i in range(4)]
        qw = C.sb("o_qw", [128, 2], F32)
        P.dma("sp", qw[:, 0:1], W["sb_q_norm"], writes=["qw"])
        P.dma("sp", qw[:, 1:2], W["sb_k_norm"], reads=["qw"], writes=["qw"])
        cnt = {"fm": 0, "qn": 0, "z": 0}

        def body(it, hT, hk):
            tsl = slice(it * T, (it + 1) * T)
            for m in range(16):
                pb = 1 + m % 2
                for c in range(8):
                    P.op("pe", lambda e, c=c, m=m, pb=pb: e.matmul(
                        ps[pb][:, :T], lhsT=w_sb[:, c, 1024 + m * 128:1024 + (m + 1) * 128], rhs=hT[:, c, :],
                        start=(c == 0), stop=(c == 7)), reads=["win", hk], writes=[("ps", pb)])
                k = cnt["fm"] % 6
                cnt["fm"] += 1
                evac(P, "act" if m % 2 == 0 else "dve", fm[k][:, :], ps[pb][:, :T], [("ps", pb)], [("fm", k)])
                P.dma("sp", o_xbcT[m * 128:(m + 1) * 128, tsl], fm[k][:, :], reads=[("fm", k)],
                      writes=[("A_out", "xbc", m, it)])
            def qk_proj(m):
                isq = m < 4
                col0 = (3088 if isq else 3600) + (m % 4) * 128
                pbq = 3 + m % 4
                kk = m % 4
                for c in range(8):
                    P.op("pe", lambda e, c=c, col0=col0, pbq=pbq: e.matmul(
                        ps[pbq][:, :T], lhsT=w_sb[:, c, col0:col0 + 128], rhs=hT[:, c, :],
                        start=(c == 0), stop=(c == 7)), reads=["win", hk], writes=[("ps", pbq)])
                P.op("act", lambda e, pbq=pbq, kk=kk: e.activation(out=sqb[kk][:, :], in_=ps[pbq][:, :T], func=AF.Square),
                     reads=[("ps", pbq)], writes=[("sqb", kk)])

            qk_proj(0)
            qk_proj(1)
            for m in range(8):
                isq = m < 4
                pbq = 3 + m % 4
                pbs = 1 + m % 2
                kk = m % 4
                k2 = m % 2
                P.op("pe", lambda e, pbs=pbs, kk=kk: e.matmul(ps[pbs][:, :T], lhsT=blk1[:, :], rhs=sqb[kk][:, :],
                                                              start=True, stop=True),
                     reads=["blk1", ("sqb", kk)], writes=[("ps", pbs)])
                if isq:
                    P.op("act", lambda e, pbs=pbs, k2=k2: e.activation(out=rr[k2][:, :], in_=ps[pbs][:, :T], func=AF.Sqrt,
                                                                       scale=1.0, bias=eps64[:, 0:1]),
                         reads=[("ps", pbs), "eps64"], writes=[("rr", k2)])
                else:
                    P.op("act", lambda e, pbs=pbs, k2=k2: e.activation(out=rr[k2][:, :], in_=ps[pbs][:, :T], func=AF.Sqrt,
                                                                       scale=1.0 / 64, bias=cst["eps"][:, 0:1]),
                         reads=[("ps", pbs), "eps"], writes=[("rr", k2)])
                P.op("dve", lambda e, k2=k2: e.reciprocal(out=rr[k2][:, :], in_=rr[k2][:, :]), reads=[("rr", k2)],
                     writes=[("rr", k2)])
                k = cnt["qn"] % 4
                cnt["qn"] += 1
                wi = 0 if isq else 1
                P.op("dve", lambda e, k=k, wi=wi, pbq=pbq, k2=k2: e.scalar_tensor_tensor(
                    out=qn[k][:, :], in0=ps[pbq][:, :T], scalar=qw[:, wi:wi + 1], in1=rr[k2][:, :], op0=ALU.mult, op1=ALU.mult),
                    reads=[("ps", pbq), "qw", ("rr", k2)], writes=[("qn", k)])
                dd_ = (o_qT if isq else o_kT)[(m % 4) * 128:(m % 4 + 1) * 128, tsl]
                P.dma("sp", dd_, qn[k][:, :], reads=[("qn", k)], writes=[("A_out", "qk", m, it)])
                if m + 2 < 8:
                    qk_proj(m + 2)
            for q in range(4):
                tok = slice(q * 128, (q + 1) * 128)
                r0 = it * T + q * 128
                for half in range(2):
                    pb = 5 + half
                    col0 = half * 512
                    for c in range(8):
                        P.op("pe", lambda e, c=c, tok=tok, col0=col0, pb=pb: e.matmul(
                            ps[pb][:, :512], lhsT=hT[:, c, tok], rhs=w_sb[:, c, col0:col0 + 512],
                            start=(c == 0), stop=(c == 7)), reads=["win", hk], writes=[("ps", pb)])
                    k = cnt["z"] % 6
                    cnt["z"] += 1
                    evac(P, "act" if half == 0 else "dve", zst[k][:, :], ps[pb][:, :512], [("ps", pb)], [("zst", k)])
                    P.dma("sp", o_z[r0:r0 + 128, col0:col0 + 512], zst[k][:, :], reads=[("zst", k)],
                          writes=[("A_out", "z", r0, half)])
                for c in range(8):
                    P.op("pe", lambda e, c=c, tok=tok: e.matmul(
                        ps[7][:, :512], lhsT=hT[:, c, tok], rhs=w_sb[:, c, 4112:4624],
                        start=(c == 0), stop=(c == 7)), reads=["win", hk], writes=[("ps", 7)])
                k = q % 4
                evac(P, "act", vst[k][:, :], ps[7][:, :512], [("ps", 7)], [("vst", k)])
                P.dma("sp", o_v[r0:r0 + 128, :], vst[k][:, :], reads=[("vst", k)], writes=[("A_out", "v", r0)])
            for c in range(8):
                P.op("pe", lambda e, c=c: e.matmul(ps[7][0:16, :T], lhsT=w_sb[:, c, 3072:3088], rhs=hT[:, c, :],
                                                    start=(c == 0), stop=(c == 7)), reads=["win", hk], writes=[("ps", 7)])
            evac(P, "dve", dst_[:, :], ps[7][0:16, :T], [("ps", 7)], ["dst"])
            P.dma("sp", o_dtT[:, tsl], dst_[:, :], reads=["dst"], writes=[("A_out", "dt", it)])

        proj_norm_tiles(C, cst, xT_in, W["mix_norm"], T, body)
    if upto == "A":
        return

    with C.scope():
        xcb = [C.sb("o_xcb%d" % i, [128, S], BF16) for i in range(2)]
        cw = C.sb("o_cw", [128, 16, 4], F32)
        cb = C.sb("o_cb", [128, 16], F32)
        P.dma("sp", cw.rearrange("p a b -> p (a b)"), W["ssd_conv_w"], writes=["cw"])
        P.dma("sp", cb[:, :], W["ssd_conv_b"], writes=["cb"])

        def sink_o(m, it, psap, bias, pkey):
            k = m % 2
            P.op("act", lambda e: e.activation(out=xcb[k][:, it * 512:(it + 1) * 512], in_=psap, func=AF.Silu, bias=bias),
                 reads=[pkey, "cb"], writes=[("xcb", k)])
            if it == S // 512 - 1:
                P.dma("sp", o_xc[m * 128:(m + 1) * 128, :], xcb[k][:, :], reads=[("xcb", k)], writes=[("xc", m)])

        conv_silu_pe(C, cst, "ocv", o_xbcT, 16, cw, cb, sink_o)
    if upto == "S0":
        return

    with C.scope():
        dtr = C.sb("o_dtr", [16, S], F32)
        ldt = C.sb("o_ldt", [16, S], F32)
        aa = C.sb("o_aa", [16, S], F32)
        te = C.sb("o_te", [16, S], F32)
        es = C.sb("o_es", [16, S], F32)
        dtb = C.sb("o_dtb", [16, 1], F32)
        Aneg = C.sb("o_Aneg", [16, 1], F32)
        cde = C.sb("o_cde", [16, 32], F32)
        oh16 = C.sb("o_oh16", [16, 16, 128], F32)
        one16 = cst["one"][0:16, 0:1]
        P.dma("sp", dtr[:, :], o_dtT[:, :], writes=["dtr"])
        P.dma("sp", dtb[:, :], W["ssd_dt_bias"], writes=["dtb"])
        P.dma("sp", Aneg[:, :], W["ssd_A_log"], writes=["Aneg"])
        P.dma("sp", oh16.rearrange("p a b -> p (a b)"), cin["onehot16"], writes=["oh16"])
        P.op("act", lambda e: e.activation(out=Aneg[:, :], in_=Aneg[:, :], func=AF.Exp), reads=["Aneg"], writes=["Aneg"])
        P.op("dve", lambda e: e.tensor_scalar(out=Aneg[:, :], in0=Aneg[:, :], scalar1=-1.0, scalar2=None, op0=ALU.mult),
             reads=["Aneg"], writes=["Aneg"])
        P.op("act", lambda e: e.activation(out=dtr[:, :], in_=dtr[:, :], func=AF.Exp, bias=dtb[:, 0:1]),
             reads=["dtr", "dtb"], writes=["dtr"])
        P.op("act", lambda e: e.activation(out=dtr[:, :], in_=dtr[:, :], func=AF.Ln, bias=one16),
             reads=["dtr", "one"], writes=["dtr"])
        P.op("act", lambda e: e.activation(out=ldt[:, :], in_=dtr[:, :], func=AF.Ln), reads=["dtr"], writes=["ldt"])
        P.op("dve", lambda e: e.tensor_scalar(out=aa[:, :], in0=dtr[:, :], scalar1=Aneg[:, 0:1], scalar2=None, op0=ALU.mult),
             reads=["dtr", "Aneg"], writes=["aa"])
        for c in range(32):
            blk = slice(c * 128, (c + 1) * 128)
            P.op("dve", lambda e, blk=blk: e.tensor_tensor_scan(
                out=aa[:, blk], data0=one16.to_broadcast([16, 128]), data1=aa[:, blk], initial=0.0,
                op0=ALU.mult, op1=ALU.add), reads=["aa", "one"], writes=["aa"])
        aa3 = aa.rearrange("h (c l) -> h c l", l=128)
        aend = aa3[:, :, 127:128]
        P.op("dve", lambda e: e.tensor_copy(out=cde[:, :].unsqueeze(2), in_=aend), reads=["aa"], writes=["cde"])
        P.op("act", lambda e: e.activation(out=cde[:, :], in_=cde[:, :], func=AF.Exp), reads=["cde"], writes=["cde"])
        P.op("act", lambda e: e.activation(out=es[:, :], in_=aa[:, :], func=AF.Exp), reads=["aa"], writes=["es"])
        te3 = te.rearrange("h (c l) -> h c l", l=128)
        P.op("dve", lambda e: e.tensor_tensor(out=te3, in0=aend.to_broadcast([16, 32, 128]), in1=aa3, op=ALU.subtract),
             reads=["aa"], writes=["te"])
        P.op("act", lambda e: e.activation(out=te[:, :], in_=te[:, :], func=AF.Exp), reads=["te"], writes=["te"])
        P.op("dve", lambda e: e.tensor_tensor(out=te[:, :], in0=te[:, :], in1=dtr[:, :], op=ALU.mult),
             reads=["te", "dtr"], writes=["te"])
        P.op("dve", lambda e: e.tensor_tensor(out=ldt[:, :], in0=ldt[:, :], in1=aa[:, :], op=ALU.subtract),
             reads=["ldt", "aa"], writes=["ldt"])
        for h in range(16):
            P.op("pe", lambda e, h=h: e.matmul(ps[1][:, h * 32:(h + 1) * 32], lhsT=oh16[:, h, :], rhs=cde[:, :],
                                                start=True, stop=True), reads=["oh16", "cde"], writes=[("ps", 1)])
        P.op("dve", lambda e: e.tensor_copy(out=cdb.rearrange("p c h -> p h c"),
                                            in_=ps[1][:, :].rearrange("p (h c) -> p h c", c=32)),
             reads=[("ps", 1)], writes=["cdb"])
        for src, sname, dst, dname, pb in ((ldt, "ldt", bias_tm, "bias_tm", 2), (es, "es", est_tm, "est_tm", 3),
                                           (te, "te", dtte_tm, "dtte_tm", 4)):
            for c in range(32):
                P.op("pe", lambda e, src=src, c=c, pb=pb: e.transpose(
                    out=ps[pb][:, c * 16:(c + 1) * 16], in_=src[:, c * 128:(c + 1) * 128],
                    identity=cst["identf"][0:16, 0:16]), reads=[sname, "identf"], writes=[("ps", pb)])
            P.op("dve", lambda e, dst=dst, pb=pb: e.tensor_copy(out=dst.rearrange("p a b -> p (a b)"), in_=ps[pb][:, :]),
                 reads=[("ps", pb)], writes=[dname])
        o_acum = C.dram("o_acum", [16, S], F32)
        P.dma("sp", o_acum[:, :], aa[:, :], reads=["aa"], writes=["o_acum"])
    if upto == "S1":
        return
    ssd_main(C, cst, cin, W, o_z, o_xc, mixT, bias_tm, est_tm, dtte_tm, cdb, negm, upto)
    if upto[0] == "S":
        return
    stick_breaking_phase(C, cst, o_qT, o_kT, o_v, mixT, upto)
    if upto[0] == "T":
        return
    with C.scope():
        wo = C.sb("o_wo", [128, 12, D], BF16)
        stg = Stager(C, "o_stgE")
        wov = W["w_out"].rearrange("(c p) f -> p c f", p=128)
        for c in range(12):
            stg.load(wo[:, c, :], wov[:, c, :], "wo")
        out_proj(C, wo, 12, mixT, xT_in, xT_out, T)


def ssd_main(C, cst, cin, W, o_z, o_xc, mixT, bias_tm, est_tm, dtte_tm, cdb, negm, upto):
    P = C.P
    ps = C.psum
    o_acum = C.dram("o_acum", [16, S], F32)
    with C.scope():
        acf = C.sb("s_acf", [16, S], F32)
        oh16 = C.sb("s_oh16", [16, 16, 128], F32)
        Dbc = C.sb("s_Dbc", [128, 16], F32)
        Did = C.sb("s_Did", [128, 16, 128], BF16)
        nwb = C.sb("s_nwb", [128, 1024], F32)
        xsup = [C.sb("s_xsup%d" % i, [128, 16, 512], BF16) for i in range(2)]
        xtm = C.sb("s_xtm", [128, 16, 64], BF16)
        xw = C.sb("s_xw", [128, 16, 64], BF16)
        Btm = C.sb("s_Btm", [128, 4, 128], BF16)
        dec = [C.sb("s_dec%d" % i, [128, 4, 128], F32) for i in range(2)]
        PTs = [C.sb("s_PT%d" % i, [128, 4, 128], BF16) for i in range(2)]
        Hst = C.sb("s_H", [128, 16, 64], F32)
        Hbf = C.sb("s_Hbf", [128, 16, 64], BF16)
        zch = [C.sb("s_z%d" % i, [128, 1024], F32) for i in range(2)]
        yoff = C.sb("s_yoff", [128, 16, 64], F32)
        ysb = C.sb("s_y", [128, 1024], F32)
        ssg = C.sb("s_ss", [128, 4], F32)
        junk = C.sb("s_junk", [128, 256], F32)
        cout = C.sb("s_cout", [128, 1024], BF16)
        coutT = [C.sb("s_coutT%d" % i, [128, 8, 128], BF16) for i in range(2)]
        P.dma("sp", acf[:, :], o_acum[:, :], writes=["acf"])
        P.dma("sp", oh16.rearrange("p a b -> p (a b)"), cin["onehot16"], writes=["oh16"])
        P.dma("sp", Dbc[:, :], W["ssd_D"].partition_broadcast(128), writes=["Dbc"])
        P.dma("sp", nwb[:, :], W["ssd_norm"].partition_broadcast(128), writes=["nwb"])
        for h in range(16):
            P.op("dve", lambda e, h=h: e.tensor_scalar(out=Did[:, h, :], in0=cst["identf"][:, :], scalar1=Dbc[:, h:h + 1],
                                                       scalar2=None, op0=ALU.mult), reads=["identf", "Dbc"], writes=["Did"])
        P.op("pool", lambda e: e.memset(Hst.rearrange("p a b -> p (a b)"), 0.0), writes=["H"])
        P.op("pool", lambda e: e.memset(Hbf.rearrange("p a b -> p (a b)"), 0.0), writes=["Hbf"])
        ps0b = ps[0][:, :].bitcast(BF16)
        ps1b = ps[1][:, :].bitcast(BF16)
        xcv = o_xc.rearrange("(m p) t -> p m t", p=128)
        nch = 1 if upto == "S2" else 32
        pending_tail = []
        for c in range(nch):
            b = c % 2
            sc, lc = c // 4, c % 4
            blk = slice(c * 128, (c + 1) * 128)
            tl = slice(lc * 128, (lc + 1) * 128)
            if lc == 0:
                P.dma("sp", xsup[sc % 2][:, :, :], xcv[:, :, sc * 512:(sc + 1) * 512], writes=[("xsup", sc % 2)])
            xs_ = xsup[sc % 2]
            xk = ("xsup", sc % 2)
            P.dma("sp", zch[b][:, :], o_z[blk, :], writes=[("zch", b)])
            for m in range(8):
                P.op("pe", lambda e, m=m, xs_=xs_, tl=tl: e.transpose(out=ps0b[:, m * 128:(m + 1) * 128], in_=xs_[:, m, tl],
                                                                      identity=cst["identb"][:, :]),
                     reads=[xk, "identb"], writes=[("ps", 0)])
            P.op("act", lambda e: e.activation(out=xtm.rearrange("p a b -> p (a b)"), in_=ps0b[:, :], func=AF.Copy),
                 reads=[("ps", 0)], writes=["xtm"])
            P.op("dve", lambda e, c=c: e.tensor_tensor(
                out=xw[:, :, :], in0=ps0b[:, :].rearrange("p (a b) -> p a b", b=64),
                in1=dtte_tm[:, c, :].unsqueeze(2).to_broadcast([128, 16, 64]), op=ALU.mult),
                reads=[("ps", 0), "dtte_tm"], writes=["xw"])
            for g in range(4):
                P.op("pe", lambda e, g=g, xs_=xs_, tl=tl: e.transpose(out=ps1b[:, g * 128:(g + 1) * 128], in_=xs_[:, 8 + g, tl],
                                                                      identity=cst["identb"][:, :]),
                     reads=[xk, "identb"], writes=[("ps", 1)])
            P.op("act", lambda e: e.activation(out=Btm.rearrange("p a b -> p (a b)"), in_=ps1b[:, 0:512], func=AF.Copy),
                 reads=[("ps", 1)], writes=["Btm"])
            for g in range(4):
                P.op("pe", lambda e, g=g, xs_=xs_, tl=tl: e.matmul(ps[2][:, g * 128:(g + 1) * 128], lhsT=xs_[:, 8 + g, tl],
                                                                   rhs=xs_[:, 12 + g, tl], start=True, stop=True),
                     reads=[xk], writes=[("ps", 2)])
            def emit_yoff(c=c, xs_=xs_, tl=tl, xk=xk):
                for g in range(4):
                    P.op("pe", lambda e, g=g, xs_=xs_, tl=tl: e.matmul(
                        ps[6 + g // 2][:, (g % 2) * 256:(g % 2 + 1) * 256], lhsT=xs_[:, 12 + g, tl],
                        rhs=Hbf[:, 4 * g:4 * g + 4, :], start=True, stop=True), reads=[xk, "Hbf"], writes=[("ps", 6 + g // 2)])
                for hb in range(2):
                    P.op("dve", lambda e, hb=hb, c=c: e.tensor_tensor(
                        out=yoff[:, 8 * hb:8 * hb + 8, :], in0=ps[6 + hb][:, :].rearrange("p (a b) -> p a b", b=64),
                        in1=est_tm[:, c, 8 * hb:8 * hb + 8].unsqueeze(2).to_broadcast([128, 8, 64]), op=ALU.mult),
                        reads=[("ps", 6 + hb), "est_tm"], writes=[("yoff", hb)])

            for g in range(4):
                k = g % 2
                if g == 2:
                    emit_yoff()
                for hh in range(4):
                    h = 4 * g + hh
                    P.op("pe", lambda e, hh=hh, h=h, blk=blk: e.matmul(
                        ps[3][:, hh * 128:(hh + 1) * 128], lhsT=oh16[:, h, :], rhs=acf[:, blk], start=True, stop=False),
                        reads=["oh16", "acf"], writes=[("ps", 3)])
                    P.op("pe", lambda e, hh=hh: e.matmul(
                        ps[3][:, hh * 128:(hh + 1) * 128], lhsT=cst["identb"][:, :], rhs=negm[:, :], start=False, stop=True),
                        reads=["identb", "negm"], writes=[("ps", 3)])
                for hh in range(4):
                    h = 4 * g + hh
                    P.op("act", lambda e, hh=hh, h=h, k=k, c=c: e.activation(
                        out=dec[k][:, hh, :], in_=ps[3][:, hh * 128:(hh + 1) * 128], func=AF.Exp,
                        bias=bias_tm[:, c, h:h + 1]), reads=[("ps", 3), "bias_tm"], writes=[("dec", k)])
                P.op("dve", lambda e, g=g, k=k: e.tensor_tensor(
                    out=PTs[k][:, :, :], in0=dec[k][:, :, :],
                    in1=ps[2][:, g * 128:(g + 1) * 128].unsqueeze(1).to_broadcast([128, 4, 128]), op=ALU.mult),
                    reads=[("dec", k), ("ps", 2)], writes=[("PTs", k)])
                for hh in range(4):
                    h = 4 * g + hh
                    yb = 4 + h // 8
                    yc = (h % 8) * 64
                    P.op("pe", lambda e, hh=hh, h=h, k=k, yb=yb, yc=yc: e.matmul(
                        ps[yb][:, yc:yc + 64], lhsT=PTs[k][:, hh, :], rhs=xtm[:, h, :], start=True, stop=False),
                        reads=[("PTs", k), "xtm"], writes=[("ps", yb)])
                    P.op("pe", lambda e, h=h, yb=yb, yc=yc: e.matmul(
                        ps[yb][:, yc:yc + 64], lhsT=Did[:, h, :], rhs=xtm[:, h, :], start=False, stop=True),
                        reads=["Did", "xtm"], writes=[("ps", yb)])
            while pending_tail:
                pending_tail.pop(0)()
            for hb in range(2):
                P.op("dve", lambda e, hb=hb: e.tensor_tensor(
                    out=ysb[:, hb * 512:(hb + 1) * 512], in0=ps[4 + hb][:, :],
                    in1=yoff[:, 8 * hb:8 * hb + 8, :].rearrange("p a b -> p (a b)"), op=ALU.add),
                    reads=[("ps", 4 + hb), ("yoff", hb)], writes=[("ysb", hb)])
            for g in range(4):
                P.op("pe", lambda e, g=g: e.matmul(
                    ps[6 + g // 2][:, (g % 2) * 256:(g % 2 + 1) * 256], lhsT=Btm[:, g, :],
                    rhs=xw[:, 4 * g:4 * g + 4, :], start=True, stop=True), reads=["Btm", "xw"], writes=[("ps", 6 + g // 2)])
            P.op("dve", lambda e, c=c: e.tensor_tensor(
                out=Hst[:, :, :], in0=Hst[:, :, :], in1=cdb[:, c, :].unsqueeze(2).to_broadcast([128, 16, 64]), op=ALU.mult),
                reads=["H", "cdb"], writes=["H"])
            for hb in range(2):
                P.op("dve", lambda e, hb=hb: e.tensor_tensor(
                    out=Hst[:, 8 * hb:8 * hb + 8, :], in0=Hst[:, 8 * hb:8 * hb + 8, :],
                    in1=ps[6 + hb][:, :].rearrange("p (a b) -> p a b", b=64), op=ALU.add),
                    reads=["H", ("ps", 6 + hb)], writes=["H"])
            P.op("act", lambda e: e.activation(out=Hbf.rearrange("p a b -> p (a b)"),
                                               in_=Hst.rearrange("p a b -> p (a b)"), func=AF.Copy),
                 reads=["H"], writes=["Hbf"])
            P.op("act", lambda e, b=b: e.activation(out=zch[b][:, :], in_=zch[b][:, :], func=AF.Silu),
                 reads=[("zch", b)], writes=[("zch", b)])
            P.op("dve", lambda e, b=b: e.tensor_tensor(out=ysb[:, :], in0=ysb[:, :], in1=zch[b][:, :], op=ALU.mult),
                 reads=[("ysb", 0), ("ysb", 1), ("zch", b)], writes=[("ysb", 0), ("ysb", 1)])
            for g in range(4):
                P.op("act", lambda e, g=g: e.activation(out=junk[:, :], in_=ysb[:, g * 256:(g + 1) * 256], func=AF.Square,
                                                        accum_out=ssg[:, g:g + 1]),
                     reads=[("ysb", 0), ("ysb", 1)], writes=["junk", ("ssg", g)])
            P.op("act", lambda e: e.activation(out=ssg[:, :], in_=ssg[:, :], func=AF.Sqrt, scale=1.0 / 256,
                                               bias=cst["eps"][:, 0:1]),
                 reads=[("ssg", g) for g in range(4)] + ["eps"], writes=[("ssg", g) for g in range(4)])
            P.op("dve", lambda e: e.reciprocal(out=ssg[:, :], in_=ssg[:, :]), reads=[("ssg", g) for g in range(4)],
                 writes=[("ssg", g) for g in range(4)])
            for g in range(4):
                P.op("dve", lambda e, g=g: e.scalar_tensor_tensor(
                    out=cout[:, g * 256:(g + 1) * 256], in0=ysb[:, g * 256:(g + 1) * 256], scalar=ssg[:, g:g + 1],
                    in1=nwb[:, g * 256:(g + 1) * 256], op0=ALU.mult, op1=ALU.mult),
                    reads=[("ysb", 0), ("ysb", 1), ("ssg", g), "nwb"], writes=["cout"])
            def emit_tail(c=c, b=b, blk=blk):
                for m in range(8):
                    P.op("pe", lambda e, m=m: e.transpose(out=ps0b[:, m * 128:(m + 1) * 128], in_=cout[:, m * 128:(m + 1) * 128],
                                                          identity=cst["identb"][:, :]),
                         reads=["cout", "identb"], writes=[("ps", 0)])
                P.op("act", lambda e, b=b: e.activation(out=coutT[b].rearrange("p a b -> p (a b)"), in_=ps0b[:, :], func=AF.Copy),
                     reads=[("ps", 0)], writes=[("coutT", b)])
                P.dma("sp", mixT[0:1024, blk].rearrange("(m p) t -> p m t", p=128), coutT[b][:, :, :],
                      reads=[("coutT", b)], writes=[("mixC", c)])
            pending_tail.append(emit_tail)
        for f_ in pending_tail:
            f_()


def stick_breaking_phase(C, cst, o_qT, o_kT, o_v, mixT, upto):
    P = C.P
    ps = C.psum
    with C.scope():
        kT = C.sb("t_kT", [128, 4, S], BF16)
        qT = C.sb("t_qT", [128, 4, S], BF16)
        vtm = C.sb("t_v", [128, 32, 512], BF16)
        tril = C.sb("t_tril", [128, 128], BF16)
        P.op("pool", lambda e: e.tensor_tensor(out=tril[:, :], in0=cst["ones"][:, :], in1=cst["triu"][:, :], op=ALU.subtract),
             reads=["ones", "triu"], writes=["tril"])
        P.dma("sp", kT[:, :, :], o_kT.rearrange("(m p) t -> p m t", p=128), writes=["kT"])
        P.dma("sp", qT[:, :, :], o_qT.rearrange("(m p) t -> p m t", p=128), writes=["qT"])
        ovv = o_v.rearrange("(b p) f -> p b f", p=128)
        for i in range(8):
            P.dma("sp", vtm[:, 4 * i:4 * i + 4, :], ovv[:, 4 * i:4 * i + 4, :], reads=["vtm"] if i else [], writes=["vtm"])
        NS = 2
        ee = [[C.sb("t_e%d_%d" % (s_, i), [128, 512], F32) for i in range(2)] for s_ in range(NS)]
        sp = [[C.sb("t_sp%d_%d" % (s_, i), [128, 512], F32) for i in range(2)] for s_ in range(NS)]
        l1m = [[C.sb("t_l1m%d_%d" % (s_, i), [128, 512], BF16) for i in range(2)] for s_ in range(NS)]
        E1 = [[C.sb("t_E1%d_%d" % (s_, i), [128, 512], F32) for i in range(2)] for s_ in range(NS)]
        PTb = [[C.sb("t_PT%d_%d" % (s_, i), [128, 512], BF16) for i in range(2)] for s_ in range(NS)]
        dout = C.sb("t_dout", [128, 4, 512], BF16)
        doutT = [C.sb("t_doutT%d" % i, [128, 4, 512], BF16) for i in range(2)]
        ps7b = ps[6][:, :].bitcast(BF16)
        nQ = {"T1": 1, "T2": 2}.get(upto, 8)
        for Q in range(nQ):
            for h0 in range(0, 8, NS):
                kbs = list(range(4 * Q + 3, -1, -1))
                n = len(kbs)

                def geo(i):
                    kb = kbs[i]
                    tb0 = max(0, kb - 4 * Q)
                    c0 = tb0 * 128
                    return kb, tb0, c0, slice(c0, 512), kb >= 4 * Q

                hs = [(h0 + s_, (h0 + s_) // 2, 64 * ((h0 + s_) % 2), 2 * s_, 4 + s_, 6 + s_) for s_ in range(NS)]
                def emit_z(i):
                    kb, tb0, c0, cs_, diag = geo(i)
                    kblk = slice(kb * 128, (kb + 1) * 128)
                    qsl = slice(Q * 512 + c0, (Q + 1) * 512)
                    for s_, (h, m, pb0, zb0, xb, ob) in enumerate(hs):
                        zb = zb0 + i % 2
                        P.op("pe", lambda e, m=m, pb0=pb0, kblk=kblk, qsl=qsl, cs_=cs_, zb=zb: e.matmul(
                            ps[zb][:, cs_], lhsT=kT[pb0:pb0 + 64, m, kblk], rhs=qT[pb0:pb0 + 64, m, qsl],
                            start=True, stop=True), reads=["kT", "qT"], writes=[("ps", zb)])

                for i in range(n + 1):
                    if i >= 1:
                        kb, tb0, c0, cs_, diag = geo(i - 1)
                        k = (i - 1) % 2
                        for s_, (h, m, pb0, zb, xb, ob) in enumerate(hs):
                            P.op("dve", lambda e, s_=s_, k=k, cs_=cs_, xb=xb: e.tensor_tensor(
                                out=E1[s_][k][:, cs_], in0=ps[xb][:, cs_], in1=sp[s_][k][:, cs_], op=ALU.subtract),
                                reads=[("ps", xb), ("sp", s_, k)], writes=[("E1", s_, k)])
                    if i < n:
                        kb, tb0, c0, cs_, diag = geo(i)
                        k = i % 2
                        if i == 0:
                            emit_z(0)
                        for s_, (h, m, pb0, zb0, xb, ob) in enumerate(hs):
                            zb = zb0 + k
                            P.op("act", lambda e, s_=s_, k=k, cs_=cs_, zb=zb: e.activation(
                                out=ee[s_][k][:, cs_], in_=ps[zb][:, cs_], func=AF.Exp, scale=-1.0),
                                reads=[("ps", zb)], writes=[("ee", s_, k)])
                        for s_, (h, m, pb0, zb, xb, ob) in enumerate(hs):
                            P.op("act", lambda e, s_=s_, k=k, cs_=cs_: e.activation(
                                out=sp[s_][k][:, cs_], in_=ee[s_][k][:, cs_], func=AF.Ln, bias=cst["one"][:, 0:1]),
                                reads=[("ee", s_, k), "one"], writes=[("sp", s_, k)])
                        for s_, (h, m, pb0, zb0, xb, ob) in enumerate(hs):
                            zb = zb0 + k
                            P.op("dve", lambda e, s_=s_, k=k, cs_=cs_, zb=zb: e.scalar_tensor_tensor(
                                out=l1m[s_][k][:, cs_], in0=ps[zb][:, cs_], scalar=-1.0, in1=sp[s_][k][:, cs_],
                                op0=ALU.mult, op1=ALU.subtract), reads=[("ps", zb), ("sp", s_, k)], writes=[("l1m", s_, k)])
                            if diag:
                                dsl = slice(c0, c0 + 128)
                                P.op("dve", lambda e, s_=s_, k=k, dsl=dsl: e.tensor_tensor(
                                    out=l1m[s_][k][:, dsl], in0=l1m[s_][k][:, dsl], in1=cst["maskTs"][:, :], op=ALU.mult),
                                    reads=[("l1m", s_, k), "maskTs"], writes=[("l1m", s_, k)])
                    if i + 1 < n:
                        emit_z(i + 1)
                    if i >= 1:
                        kb, tb0, c0, cs_, diag = geo(i - 1)
                        k = (i - 1) % 2
                        for s_, (h, m, pb0, zb, xb, ob) in enumerate(hs):
                            P.op("act", lambda e, s_=s_, k=k, cs_=cs_: e.activation(
                                out=PTb[s_][k][:, cs_], in_=E1[s_][k][:, cs_], func=AF.Exp),
                                reads=[("E1", s_, k)], writes=[("PTb", s_, k)])
                            if diag:
                                dsl = slice(c0, c0 + 128)
                                P.op("dve", lambda e, s_=s_, k=k, dsl=dsl: e.tensor_tensor(
                                    out=PTb[s_][k][:, dsl], in0=PTb[s_][k][:, dsl], in1=cst["maskTs"][:, :], op=ALU.mult),
                                    reads=[("PTb", s_, k), "maskTs"], writes=[("PTb", s_, k)])
                        for s_, (h, m, pb0, zb, xb, ob) in enumerate(hs):
                            for tb in range(tb0, 4):
                                P.op("pe", lambda e, s_=s_, k=k, tb=tb, kb=kb, h=h, ob=ob, st=(i == 1 and tb == tb0): e.matmul(
                                    ps[ob][:, tb * 64:(tb + 1) * 64], lhsT=PTb[s_][k][:, tb * 128:(tb + 1) * 128],
                                    rhs=vtm[:, kb, h * 64:(h + 1) * 64], start=st, stop=(kb == 0), skip_group_check=True),
                                    reads=[("PTb", s_, k), "vtm"], writes=[("ps", ob)])
                    if i < n:
                        kb, tb0, c0, cs_, diag = geo(i)
                        k = i % 2
                        for s_, (h, m, pb0, zb, xb, ob) in enumerate(hs):
                            if i >= 1:
                                pcs = geo(i - 1)[3]
                                pk = (i - 1) % 2
                                P.op("pe", lambda e, s_=s_, pk=pk, pcs=pcs, xb=xb: e.matmul(
                                    ps[xb][:, pcs], lhsT=tril[:, :], rhs=l1m[s_][pk][:, pcs], start=False, stop=False,
                                    skip_group_check=True), reads=["tril", ("l1m", s_, pk)], writes=[("ps", xb)])
                            P.op("pe", lambda e, s_=s_, k=k, cs_=cs_, xb=xb, st=(i == 0): e.matmul(
                                ps[xb][:, cs_], lhsT=cst["triu"][:, :], rhs=l1m[s_][k][:, cs_], start=st, stop=False,
                                skip_group_check=True), reads=["triu", ("l1m", s_, k)], writes=[("ps", xb)])
                for s_ in range(NS):
                    h = h0 + s_
                    ob = 6 + s_
                    P.op("act", lambda e, h=h, ob=ob: e.activation(
                        out=dout[:, :, h * 64:(h + 1) * 64], in_=ps[ob][:, 0:256].rearrange("p (a b) -> p a b", b=64),
                        func=AF.Copy), reads=[("ps", ob)], writes=["dout"])
            qb = Q % 2
            for mm in range(4):
                for tb in range(4):
                    P.op("pe", lambda e, mm=mm, tb=tb: e.transpose(
                        out=ps7b[:, tb * 128:(tb + 1) * 128], in_=dout[:, tb, mm * 128:(mm + 1) * 128],
                        identity=cst["identb"][:, :]), reads=["dout", "identb"], writes=[("ps", 6)])
                P.op("dve", lambda e, mm=mm, qb=qb: e.tensor_copy(out=doutT[qb][:, mm, :], in_=ps7b[:, 0:512]),
                     reads=[("ps", 6)], writes=[("doutT", qb)])
            P.dma("sp", mixT[1024:1536, Q * 512:(Q + 1) * 512].rearrange("(m p) t -> p m t", p=128), doutT[qb][:, :, :],
                  reads=[("doutT", qb)], writes=[("mixD", Q)])


def build_program():
    nc = bass.Bass("TRN2", target_bir_lowering=False)
    xT = nc.dram_tensor("xT", [D, S], F32, kind="ExternalInput").ap()
    yT = nc.dram_tensor("yT", [D, S], F32, kind="ExternalOutput").ap()
    Wd = {k: nc.dram_tensor(k, sh, F32, kind="ExternalInput").ap() for k, sh in PARAM_SHAPES.items()}
    cin = {k: nc.dram_tensor("c_" + k, sh, F32, kind="ExternalInput").ap() for k, sh in CONST_SHAPES.items()}
    with ExitStack() as stack:
        C = Ctx(nc, stack)
        cst = alloc_consts(C, cin)
        res = [C.dram("res%d" % i, [D, S], F32) for i in range(5)]

        def ffn(pre, src, dst):
            with C.scope():
                bufs = alloc_ffn_bufs(C, cst)
                ffn_phase(C, pre, src, dst, Wd[pre + "_norm"], Wd[pre + "_wg"], Wd[pre + "_wu"], Wd[pre + "_wd"], bufs)

        ffn("l0_ffn1", xT, res[0])
        with C.scope():
            even_mixer_phase(C, cst, cin, res[0], res[1], {k[3:]: v for k, v in Wd.items() if k.startswith("l0_")})
        ffn("l0_ffn2", res[1], res[2])
        ffn("l1_ffn1", res[2], res[3])
        with C.scope():
            odd_mixer_phase(C, cst, cin, res[3], res[4], {k[3:]: v for k, v in Wd.items() if k.startswith("l1_")})
        ffn("l1_ffn2", res[4], yT)
        C.P.emit()
    return nc


_NC_CACHE = {}


def kernel(**inputs):
    x = np.asarray(inputs["x"], np.float32)
    hp = host_params(inputs)
    hc = host_consts()
    shared = {k: hp[k] for k in PARAM_SHAPES}
    for k in CONST_SHAPES:
        shared["c_" + k] = hc[k]
    in_maps = []
    for b in range(NCORES):
        m = dict(shared)
        m["xT"] = np.ascontiguousarray(x[b].T)
        in_maps.append(m)
    if "nc" not in _NC_CACHE:
        _NC_CACHE["nc"] = build_program()
    res = run_bass_kernel_spmd(_NC_CACHE["nc"], in_maps, core_ids=list(range(NCORES)))
    out = np.stack([np.asarray(r["yT"], np.float32).T for r in res.results], axis=0)
    return np.ascontiguousarray(out)
```

```python
from contextlib import ExitStack

import numpy as np
import concourse.bass as bass
import concourse.mybir as mybir
from concourse.bass_utils import run_bass_kernel_spmd

F32 = mybir.dt.float32
BF16 = mybir.dt.bfloat16
ALU = mybir.AluOpType
AF = mybir.ActivationFunctionType
AX = mybir.AxisListType

S = 4096
D = 1024
DFF = 2816
NCORES = 8
EPS = 1e-6
CAST_DMA = True


class _Op:
    __slots__ = ("eng", "fn", "deps", "sig", "sem", "cnt", "is_dma", "pos")


def _bank_of(k):
    if isinstance(k, tuple):
        if k[0] == "ps":
            return k[1]
        if k[0] == "pso":
            return 3 + k[1] // 2
        if k[0] == "psd":
            return 5 + k[1] // 2
        if k[0] == "ps0":
            return 0
    return None


class Prog:
    ENGS = ("pe", "act", "dve", "pool", "sp")

    def __init__(self, nc, stack, n_dma_sems=8):
        self.nc = nc
        self.stack = stack
        self.streams = {e: [] for e in self.ENGS}
        self.lastw = {}
        self.readers = {}
        self.eng_sem = {e: stack.enter_context(nc.semaphore("s_" + e)) for e in ("pe", "act", "dve", "pool")}
        self.dma_pool = {}
        self.n_dma_sems = n_dma_sems
        self.dma_rr = {}
        self.nops = 0
        self.pending = {}
        self.bank_last = {}
        self.multi = {}

    def barrier(self):
        lasts = []
        for e in self.ENGS:
            for o in reversed(self.streams[e]):
                if not o.is_dma:
                    o.sig = True
                    lasts.append(o)
                    break
        for q, slots in self.dma_pool.items():
            for sl in slots:
                if sl[2] is not None:
                    lasts.append(sl[2])
        for e in self.ENGS:
            self.pending[e] = list(lasts) + self.pending.get(e, [])
        self.lastw.clear()
        self.readers.clear()
        self.multi.clear()

    def _dma_sem(self, q):
        if q not in self.dma_pool:
            self.dma_pool[q] = [[self.stack.enter_context(self.nc.semaphore("d_%s%d" % (q, i))), 0, None]
                                for i in range(self.n_dma_sems)]
            self.dma_rr[q] = 0
        i = self.dma_rr[q]
        self.dma_rr[q] = (i + 1) % self.n_dma_sems
        return self.dma_pool[q][i]

    def _deps(self, op, reads, writes):
        deps = set()
        for k in reads:
            w = self.lastw.get(k)
            if w is not None:
                deps.add(w)
            if k in self.multi:
                deps.update(self.multi[k])
        for k in writes:
            w = self.lastw.get(k)
            if w is not None:
                deps.add(w)
            for r in self.readers.get(k, ()):
                deps.add(r)
        deps.discard(op)
        for k in reads:
            self.readers.setdefault(k, []).append(op)
        for k in writes:
            self.lastw[k] = op
            self.readers[k] = []
        return deps

    def op(self, eng, fn, reads=(), writes=()):
        o = _Op()
        o.eng = eng
        o.fn = fn
        o.is_dma = False
        o.sig = False
        o.sem = None
        o.cnt = 0
        o.pos = self.nops
        self.nops += 1
        deps = self._deps(o, reads, writes)
        deps.update(self.pending.pop(eng, ()))
        for k in list(reads) + list(writes):
            bk = _bank_of(k)
            if bk is not None:
                prev = self.bank_last.get(bk)
                if prev is not None and prev is not o and prev.eng != eng:
                    deps.add(prev)
                self.bank_last[bk] = o
        keep = []
        for d in deps:
            if (not d.is_dma) and d.eng == eng and eng == "pe":
                continue
            keep.append(d)
            if not d.is_dma:
                d.sig = True
        o.deps = keep
        self.streams[eng].append(o)
        return o

    def dma(self, q, out, in_, reads=(), writes=(), multi=(), **kw):
        o = _Op()
        o.eng = q
        o.fn = lambda e: e.dma_start(out=out, in_=in_, **kw)
        o.is_dma = True
        o.sig = True
        o.pos = self.nops
        self.nops += 1
        slot = self._dma_sem(q)
        deps = self._deps(o, reads, writes)
        for k in multi:
            for r in self.readers.get(k, ()):
                deps.add(r)
            self.multi.setdefault(k, []).append(o)
        deps.update(self.pending.pop(q, ()))
        if slot[2] is not None:
            deps.add(slot[2])
        slot[1] += 16
        slot[2] = o
        o.sem = slot[0]
        o.cnt = slot[1]
        for d in deps:
            if not d.is_dma:
                d.sig = True
        o.deps = list(deps)
        self.streams[q].append(o)
        return o

    def emit(self):
        nc = self.nc
        for e in ("pe", "act", "dve", "pool"):
            c = 0
            for o in self.streams[e]:
                if o.is_dma:
                    continue
                o.sem = self.eng_sem[e]
                if o.sig:
                    c += 1
                    o.cnt = c
        streams = self.streams

        def run(e, h):
            known = {}
            for o in streams[e]:
                need = {}
                for d in o.deps:
                    key = id(d.sem)
                    if key not in need or need[key][1] < d.cnt:
                        need[key] = (d.sem, d.cnt)
                for key, (sem, v) in need.items():
                    if known.get(key, 0) < v:
                        h.wait_ge(sem, v)
                        known[key] = v
                ins = o.fn(h)
                if o.is_dma:
                    ins.then_inc(o.sem, 16)
                elif o.sig:
                    ins.then_inc(o.sem, 1)
            if e in self.dma_pool:
                for sem, cnt, _ in self.dma_pool[e]:
                    if cnt > 0:
                        h.wait_ge(sem, cnt)

        with nc.Block() as block:
            @block.tensor
            def _(h):
                run("pe", h)

            @block.scalar
            def _(h):
                run("act", h)

            @block.vector
            def _(h):
                run("dve", h)

            @block.gpsimd
            def _(h):
                run("pool", h)

            @block.sync
            def _(h):
                run("sp", h)


ARENA_WORDS = 53000


class Ctx:
    def __init__(self, nc, stack):
        self.nc = nc
        self.stack = stack
        self.P = Prog(nc, stack)
        self.psum_all = stack.enter_context(nc.psum_tensor("psall", [128, 4096], F32))
        self.psum = [self.psum_all[:, i * 512:(i + 1) * 512] for i in range(8)]
        self.arena = stack.enter_context(nc.sbuf_tensor("arena", [128, ARENA_WORDS], F32))
        self.top = 0
        self.scratch = {}

    def sb(self, name, shape, dt):
        esz = 4 if dt == F32 else 2
        n = 1
        for d_ in shape[1:]:
            n *= int(d_)
        words = (n * esz + 3) // 4
        words = (words + 15) // 16 * 16
        off = self.top
        self.top += words
        assert self.top <= ARENA_WORDS, "SBUF arena overflow at %s: %d words" % (name, self.top)
        ap = self.arena[0:shape[0], off:off + words]
        if dt != F32:
            ap = ap.bitcast(dt)
        ap = ap[:, 0:n]
        if len(shape) == 3:
            ap = ap.rearrange("p (a b) -> p a b", b=int(shape[2]))
        elif len(shape) == 4:
            ap = ap.rearrange("p (a b c) -> p a b c", b=int(shape[2]), c=int(shape[3]))
        return ap

    def scope(self):
        return _Scope(self)

    def dram(self, name, shape, dt):
        if name not in self.scratch:
            kind = "ExternalOutput" if name in getattr(self, "debug_out", ()) else "Internal"
            self.scratch[name] = self.nc.dram_tensor(name, list(shape), dt, kind=kind).ap()
        return self.scratch[name]


class _Scope:
    def __init__(self, C):
        self.C = C

    def __enter__(self):
        self.mark = self.C.top
        return self

    def __exit__(self, *a):
        self.C.P.barrier()
        self.C.top = self.mark
        return False


def ffn_phase(C, tag, xT_in, xT_out, nw, wg, wu, wd, bufs, T=512):
    P = C.P
    NT = S // T
    NF = DFF // 128
    wg_sb, wu_sb, wd_sb = bufs["wg"], bufs["wu"], bufs["wd"]
    nw_sb = bufs["nw"]
    ones = bufs["ones"]
    xin = xT_in.rearrange("(c p) t -> p c t", p=128)
    xout = xT_out.rearrange("(c p) t -> p c t", p=128)

    P.dma("sp", nw_sb[:, :], nw, writes=["nw"])
    wgv = wg.rearrange("(c p) f -> p c f", p=128)
    wuv = wu.rearrange("(c p) f -> p c f", p=128)
    wdv = wd.rearrange("(j p) d -> p j d", p=128)
    si = [0]

    def load_cast(dst, src, key):
        P.dma("pool", dst, src, writes=[key])

    wgk, wuk, wdk = ["wg"], ["wu"], ["wd"]
    if CAST_DMA:
        H = DFF // 2
        wgk, wuk, wdk = [], [], []
        for c in range(8):
            for hh in range(2):
                wgk.append(("wg", c, hh))
                load_cast(wg_sb[:, c, hh * H:(hh + 1) * H], wgv[:, c, hh * H:(hh + 1) * H], wgk[-1])
            for hh in range(2):
                wuk.append(("wu", c, hh))
                load_cast(wu_sb[:, c, hh * H:(hh + 1) * H], wuv[:, c, hh * H:(hh + 1) * H], wuk[-1])
        for j in range(0, NF, 2):
            wdk.append(("wd", j))
            load_cast(wd_sb[:, j:j + 2, :], wdv[:, j:j + 2, :], wdk[-1])
    else:
        st_ = Stager(C, tag + "_stg", cols=DFF // 8)
        for c in range(8):
            st_.load(wg_sb[:, c, :], wgv[:, c, :], "wg")
            st_.load(wu_sb[:, c, :], wuv[:, c, :], "wu")
        for j in range(NF):
            st_.load(wd_sb[:, j, :], wdv[:, j, :], "wd")

    xt, hT, aT, rs = bufs["xt"], bufs["hT"], bufs["aT"], bufs["rs"]
    sq2 = bufs["sq2"]
    sg = bufs["sg"]
    ps = C.psum

    def tsl_(it):
        return slice(it * T, (it + 1) * T)

    def load_x(it):
        b = it % 2
        P.dma("sp", xt[b][:, :, :], xin[:, :, tsl_(it)], writes=[("xt", b)])

    def sq_op(it, c):
        b = it % 2
        P.op("act", lambda e: e.activation(out=sq2[:, c % 2, :], in_=xt[b][:, c, :], func=AF.Square),
             reads=[("xt", b)], writes=[("sq2", c % 2)])

    def ones_mm(it, c):
        P.op("pe", lambda e: e.matmul(ps[0][:, :T], lhsT=ones[:, :], rhs=sq2[:, c % 2, :], start=(c == 0), stop=(c == 7)),
             reads=[("sq2", c % 2), "ones"], writes=[("ps", 0)])

    def norm_back(it):
        b = it % 2
        P.op("act", lambda e: e.activation(out=rs[:, :], in_=ps[0][:, :T], func=AF.Sqrt, scale=1.0 / D,
                                           bias=bufs["eps"][:, 0:1]), reads=[("ps", 0), "eps"], writes=["rs"])
        P.op("dve", lambda e: e.reciprocal(out=rs[:, :], in_=rs[:, :]), reads=["rs"], writes=["rs"])
        for c in range(8):
            P.op("dve", lambda e, c=c: e.scalar_tensor_tensor(
                out=hT[:, c, :], in0=xt[b][:, c, :], scalar=nw_sb[:, c:c + 1], in1=rs[:, :],
                op0=ALU.mult, op1=ALU.mult), reads=[("xt", b), "rs", "nw"], writes=["hT"])

    load_x(0)
    for c in range(8):
        sq_op(0, c)
        ones_mm(0, c)
    norm_back(0)
    for it in range(NT):
        b = it % 2
        nxt = it + 1 < NT
        if nxt:
            load_x(it + 1)
        for j in range(NF):
            pg = 1 + (j % 2) * 2
            pu = pg + 1
            fs = slice(j * 128, (j + 1) * 128)
            for c in range(8):
                P.op("pe", lambda e, c=c, fs=fs, pg=pg: e.matmul(ps[pg][:, :T], lhsT=wg_sb[:, c, fs], rhs=hT[:, c, :],
                                                                   start=(c == 0), stop=(c == 7)),
                     reads=wgk + ["hT"], writes=[("ps", pg)])
            for c in range(8):
                P.op("pe", lambda e, c=c, fs=fs, pu=pu: e.matmul(ps[pu][:, :T], lhsT=wu_sb[:, c, fs], rhs=hT[:, c, :],
                                                                   start=(c == 0), stop=(c == 7)),
                     reads=wuk + ["hT"], writes=[("ps", pu)])
            k = j % 2
            P.op("act", lambda e, pg=pg, k=k: e.activation(out=sg[k][:, :], in_=ps[pg][:, :T], func=AF.Silu),
                 reads=[("ps", pg)], writes=[("sg", k)])
            P.op("dve", lambda e, pu=pu, k=k, j=j: e.tensor_tensor(out=aT[:, j, :], in0=ps[pu][:, :T], in1=sg[k][:, :],
                                                                   op=ALU.mult),
                 reads=[("ps", pu), ("sg", k)], writes=[("aT", j)])
        if nxt:
            sq_op(it + 1, 0)
            sq_op(it + 1, 1)
        for i in range(8):
            py = 5 + (i % 2)
            ds = slice(i * 128, (i + 1) * 128)
            for j in range(NF):
                P.op("pe", lambda e, j=j, ds=ds, py=py: e.matmul(ps[py][:, :T], lhsT=wd_sb[:, j, ds], rhs=aT[:, j, :],
                                                                   start=(j == 0), stop=(j == NF - 1)),
                     reads=wdk + [("aT", j)], writes=[("ps", py)])
            P.op("dve", lambda e, i=i, py=py, b=b: e.scalar_tensor_tensor(
                out=xt[b][:, i, :], in0=ps[py][:, :T], scalar=0.5, in1=xt[b][:, i, :],
                op0=ALU.mult, op1=ALU.add),
                reads=[("ps", py), ("xt", b)], writes=[("xt", b)])
            if nxt and i < 4:
                ones_mm(it + 1, 2 * i)
                ones_mm(it + 1, 2 * i + 1)
                if i < 3:
                    sq_op(it + 1, 2 * i + 2)
                    sq_op(it + 1, 2 * i + 3)
                else:
                    norm_back(it + 1)
        P.dma("sp", xout[:, :, tsl_(it)], xt[b][:, :, :], reads=[("xt", b)], writes=[(tag, "out", it)])


def alloc_ffn_bufs(C, cst, T=512):
    b = {}
    b["wg"] = C.sb("wg_sb", [128, 8, DFF], BF16)
    b["wu"] = C.sb("wu_sb", [128, 8, DFF], BF16)
    b["wd"] = C.sb("wd_sb", [128, DFF // 128, D], BF16)
    b["nw"] = C.sb("nw_sb", [128, 8], F32)
    b["xt"] = [C.sb("xt%d" % i, [128, 8, T], F32) for i in range(2)]
    b["hT"] = C.sb("hT", [128, 8, T], BF16)
    b["aT"] = C.sb("aT", [128, DFF // 128, T], BF16)
    b["rs"] = C.sb("rs", [128, T], F32)
    b["sg"] = [C.sb("sg%d" % i, [128, T], F32) for i in range(2)]
    b["sq2"] = C.sb("sq2", [128, 2, T], BF16)
    b["ones"] = cst["ones"]
    b["eps"] = cst["eps"]
    return b


class Stager:
    def __init__(self, C, name, cols=704, n=2):
        self.C = C

    def load(self, dst, src, key, np_=128):
        P = self.C.P
        n = src.shape[-1]
        step = 1536
        for c0 in range(0, n, step):
            c1 = min(n, c0 + step)
            P.dma("pool", dst[:, c0:c1], src[:, c0:c1], multi=[key])


def emit_rmsnorm(C, xt, xkey, nw_sb, sq, sqkeys, hT, rs, cst, T, psb=0, hkey="hT"):
    P = C.P
    ps = C.psum
    P.op("act", lambda e: e.activation(out=sq[:, 0:8, :], in_=xt[:, :, :], func=AF.Square),
         reads=[xkey], writes=list(sqkeys))
    for c in range(8):
        P.op("pe", lambda e, c=c: e.matmul(ps[psb][:, :T], lhsT=cst["ones"][:, :], rhs=sq[:, c, :],
                                             start=(c == 0), stop=(c == 7)),
             reads=[sqkeys[c], "ones"], writes=[("ps", psb)])
    P.op("act", lambda e: e.activation(out=rs[:, :], in_=ps[psb][:, :T], func=AF.Sqrt,
                                       scale=1.0 / D, bias=cst["eps"][:, 0:1]),
         reads=[("ps", psb), "eps"], writes=["rs"])
    P.op("dve", lambda e: e.reciprocal(out=rs[:, :], in_=rs[:, :]), reads=["rs"], writes=["rs"])
    for c in range(8):
        P.op("dve", lambda e, c=c: e.scalar_tensor_tensor(
            out=hT[:, c, :], in0=xt[:, c, :], scalar=nw_sb[:, c:c + 1], in1=rs[:, :],
            op0=ALU.mult, op1=ALU.mult),
            reads=[xkey, "rs", "nw"], writes=[hkey])


def alloc_consts(C, cin):
    P = C.P
    cst = {}
    cst["ones"] = C.sb("ones", [128, 128], BF16)
    cst["eps"] = C.sb("epsc", [128, 1], F32)
    cst["one"] = C.sb("onec", [128, 1], F32)
    cst["identb"] = C.sb("identb", [128, 128], BF16)
    cst["identf"] = C.sb("identf", [128, 128], F32)
    cst["maskT"] = C.sb("maskT", [128, 128], F32)
    cst["maskTs"] = C.sb("maskTs", [128, 128], F32)
    cst["triu"] = C.sb("triu", [128, 128], BF16)
    P.op("pool", lambda e: e.memset(cst["ones"][:, :], 1.0), writes=["ones"])
    P.op("pool", lambda e: e.memset(cst["eps"][:, :], EPS), writes=["eps"])
    P.op("pool", lambda e: e.memset(cst["one"][:, :], 1.0), writes=["one"])
    P.dma("sp", cst["identf"][:, :], cin["identf"], writes=["identf"])
    P.dma("sp", cst["maskT"][:, :], cin["maskT"], writes=["maskT"])
    P.dma("sp", cst["maskTs"][:, :], cin["maskTs"], writes=["maskTs"])
    P.op("pool", lambda e: e.tensor_copy(out=cst["identb"][:, :], in_=cst["identf"][:, :]),
         reads=["identf"], writes=["identb"])
    cst["tmpf"] = C.sb("tmpf", [128, 128], F32)
    P.dma("sp", cst["tmpf"][:, :], cin["triu"], writes=["tmpf"])
    P.op("pool", lambda e: e.tensor_copy(out=cst["triu"][:, :], in_=cst["tmpf"][:, :]),
         reads=["tmpf"], writes=["triu"])
    return cst


def host_consts():
    i = np.arange(128)
    c = {}
    c["identf"] = np.eye(128, dtype=np.float32)
    c["maskT"] = (i[:, None] <= i[None, :]).astype(np.float32)
    c["maskTs"] = (i[:, None] < i[None, :]).astype(np.float32)
    c["triu"] = (i[:, None] > i[None, :]).astype(np.float32)
    c["invc"] = np.broadcast_to((1.0 / (np.arange(16) + 1.0)).astype(np.float32)[None, :], (128, 16)).copy()
    oh = np.zeros((4, 4, 128), np.float32)
    for h in range(4):
        oh[h, h, :] = 1.0
    c["onehot4"] = oh.reshape(4, 512)
    oh = np.zeros((16, 16, 128), np.float32)
    for h in range(16):
        oh[h, h, :] = 1.0
    c["onehot16"] = oh.reshape(16, 2048)
    c["blk1"] = ((i[:, None] // 64) == (i[None, :] // 64)).astype(np.float32)
    c["negm"] = np.where(i[:, None] > i[None, :], -30000.0, 0.0).astype(np.float32)
    return c


def conv_silu_pe(C, cst, tagp, src, nch, cw, cb, sink):
    P = C.P
    ps = C.psum
    xrow = [C.sb("%s_xrow%d" % (tagp, i), [128, 3 + S], BF16) for i in range(2)]
    dg = [C.sb("%s_dg%d" % (tagp, i), [128, 4, 128], BF16) for i in range(2)]
    for k in range(2):
        P.op("pool", lambda e, k=k: e.memset(xrow[k][:, 0:3], 0.0), writes=[(tagp, "xrow", k)])
    n = 0
    for m in range(nch):
        k = m % 2
        for h_ in range(4):
            P.dma("pool", xrow[k][:, 3 + h_ * 1024:3 + (h_ + 1) * 1024], src[m * 128:(m + 1) * 128, h_ * 1024:(h_ + 1) * 1024],
                  multi=[(tagp, "xrow", k)])
        for j in range(4):
            P.op("dve", lambda e, k=k, m=m, j=j: e.tensor_scalar(out=dg[k][:, j, :], in0=cst["identf"][:, :],
                                                                 scalar1=cw[:, m, j:j + 1], scalar2=None, op0=ALU.mult),
                 reads=["identf", "cw"], writes=[(tagp, "dg", k)])
        for it in range(S // 512):
            pb = 1 + n % 4
            n += 1
            for j in range(4):
                P.op("pe", lambda e, k=k, j=j, it=it, pb=pb: e.matmul(
                    ps[pb][:, :512], lhsT=dg[k][:, j, :], rhs=xrow[k][:, j + it * 512:j + it * 512 + 512],
                    start=(j == 0), stop=(j == 3)), reads=[(tagp, "dg", k), (tagp, "xrow", k)], writes=[("ps", pb)])
            sink(m, it, ps[pb][:, :512], cb[:, m:m + 1], ("ps", pb))


def evac(P, eng, out, in_, reads, writes):
    if eng == "act":
        return P.op("act", lambda e: e.activation(out=out, in_=in_, func=AF.Copy), reads=reads, writes=writes)
    return P.op(eng, lambda e: e.tensor_copy(out=out, in_=in_), reads=reads, writes=writes)


def proj_norm_tiles(C, cst, xT_in, nw_dram, T, body):
    P = C.P
    xin = xT_in.rearrange("(c p) t -> p c t", p=128)
    nw_sb = C.sb("pn_nw", [128, 8], F32)
    xt = [C.sb("pn_xt%d" % i, [128, 8, T], F32) for i in range(2)]
    sq = C.sb("pn_sq", [128, 8, T], BF16)
    hT = [C.sb("pn_hT%d" % i, [128, 8, T], BF16) for i in range(2)]
    rs = C.sb("pn_rs", [128, T], F32)
    P.dma("sp", nw_sb[:, :], nw_dram, writes=["nw"])
    NT = S // T

    def norm(it):
        b = it % 2
        P.dma("sp", xt[b][:, :, :], xin[:, :, it * T:(it + 1) * T], writes=[("pn_xt", b)])
        emit_rmsnorm(C, xt[b], ("pn_xt", b), nw_sb, sq, [("pn_sq", c) for c in range(8)], hT[b], rs, cst, T,
                     hkey=("hT", b))

    norm(0)
    for it in range(NT):
        if it + 1 < NT:
            norm(it + 1)
        body(it, hT[it % 2], ("hT", it % 2))


def even_mixer_phase(C, cst, cin, xT_in, xT_out, W, upto="E"):
    P = C.P
    ps = C.psum
    T = 512
    uT = C.dram("e_uT", [512, S], F32)
    qkT = C.dram("e_qkT", [1024, S], F32)
    v_tm = C.dram("e_v", [S, 512], BF16)
    o_tm = C.dram("e_o", [S, 512], F32)
    gT = C.dram("e_g", [8, S], F32)
    mixT = C.dram("e_mix", [1024, S], BF16)

    ws_tm = C.sb("e_ws", [128, 32, 4], F32)
    thr_tm = C.sb("e_thr", [128, 32, 4], F32)
    dcol = C.sb("e_dcol", [128, 4, 32], F32)

    with C.scope():
        w_sb = C.sb("e_win", [128, 8, 2568], BF16)
        stg = Stager(C, "e_stg")
        win = W["w_in"].rearrange("(c p) f -> p c f", p=128)
        for c in range(8):
            stg.load(w_sb[:, c, :], win[:, c, :], "win")
        fm = [C.sb("e_fm%d" % i, [128, T], F32) for i in range(6)]
        vst = [C.sb("e_vst%d" % i, [128, 512], BF16) for i in range(4)]
        ost = [C.sb("e_ost%d" % i, [128, 512], F32) for i in range(4)]
        gst = [C.sb("e_gst%d" % i, [4, T], F32) for i in range(2)]
        cnt = {"fm": 0, "tm": 0, "g": 0}

        def body(it, hT, hk):
            tsl = slice(it * T, (it + 1) * T)
            for m in range(12):
                pb = 1 + m % 4
                for c in range(8):
                    P.op("pe", lambda e, c=c, m=m, pb=pb: e.matmul(
                        ps[pb][:, :T], lhsT=w_sb[:, c, m * 128:(m + 1) * 128], rhs=hT[:, c, :],
                        start=(c == 0), stop=(c == 7)), reads=["win", hk], writes=[("ps", pb)])
                k = cnt["fm"] % 6
                cnt["fm"] += 1
                evac(P, "act" if m % 2 == 0 else "dve", fm[k][:, :], ps[pb][:, :T], [("ps", pb)], [("fm", k)])
                dst = uT[m * 128:(m + 1) * 128, tsl] if m < 4 else qkT[(m - 4) * 128:(m - 3) * 128, tsl]
                P.dma("sp", dst, fm[k][:, :], reads=[("fm", k)], writes=[("A_out", m, it)])
            for q in range(4):
                tok = slice(q * 128, (q + 1) * 128)
                r0 = it * T + q * 128
                for which, col0, pb, stb, dstT in (("v", 1536, 5, vst, v_tm), ("o", 2048, 6, ost, o_tm)):
                    for c in range(8):
                        P.op("pe", lambda e, c=c, tok=tok, col0=col0, pb=pb: e.matmul(
                            ps[pb][:, :512], lhsT=hT[:, c, tok], rhs=w_sb[:, c, col0:col0 + 512],
                            start=(c == 0), stop=(c == 7)), reads=["win", hk], writes=[("ps", pb)])
                    k = q % 4
                    evac(P, "act" if which == "v" else "dve", stb[k][:, :], ps[pb][:, :512],
                         [("ps", pb)], [(which + "st", k)])
                    P.dma("sp", dstT[r0:r0 + 128, :], stb[k][:, :], reads=[(which + "st", k)],
                          writes=[("A_out", which, r0)])
            for gi_ in range(2):
                col0 = 2560 + 4 * gi_
                for c in range(8):
                    P.op("pe", lambda e, c=c, col0=col0: e.matmul(
                        ps[7][0:4, :T], lhsT=w_sb[:, c, col0:col0 + 4], rhs=hT[:, c, :],
                        start=(c == 0), stop=(c == 7)), reads=["win", hk], writes=[("ps", 7)])
                evac(P, "dve", gst[gi_][:, :], ps[7][0:4, :T], [("ps", 7)], [("gst", gi_)])
                P.dma("sp", gT[4 * gi_:4 * gi_ + 4, tsl], gst[gi_][:, :], reads=[("gst", gi_)],
                      writes=[("A_out", "g", gi_, it)])

        proj_norm_tiles(C, cst, xT_in, W["mix_norm"], T, body)

    if upto == "A":
        return
    with C.scope():
        PADL = 16
        ub = C.sb("e_ub", [128, PADL + S], F32)
        sA = C.sb("e_sA", [128, PADL + S], F32)
        sB = C.sb("e_sB", [128, PADL + S], F32)
        pooled = C.sb("e_pooled", [128, S], BF16)
        aout = C.sb("e_aout", [128, S], BF16)
        pw_sb = C.sb("e_pw", [128, 4, 128], BF16)
        psc = C.sb("e_psc", [128, 4], F32)
        invc = C.sb("e_invc", [128, 16], F32)
        tmpc = C.sb("e_tmpc", [128, 16], F32)
        stg = Stager(C, "e_stgB", cols=512)
        stg.load(pw_sb.rearrange("p a b -> p (a b)"), W["pool_w"], "pw")
        P.dma("sp", psc[:, :], W["pool_scale"], writes=["psc"])
        P.dma("sp", invc[:, :], cin["invc"], writes=["invc"])
        for bname, buf in (("ub", ub), ("sA", sA), ("sB", sB)):
            P.op("pool", lambda e, buf=buf: e.memset(buf[:, 0:PADL], 0.0), writes=[bname])
        for g in range(4):
            win_ = 2 << g
            P.dma("sp", ub[:, PADL:], uT[g * 128:(g + 1) * 128, :], reads=["ub"], writes=["ub"])
            src, sname = ub, "ub"
            dsts = [(sA, "sA"), (sB, "sB")]
            for k in range(g + 1):
                sh = 1 << k
                dst, dname = dsts[k % 2]
                P.op("dve", lambda e, src=src, dst=dst, sh=sh: e.tensor_tensor(
                    out=dst[:, PADL:], in0=src[:, PADL:], in1=src[:, PADL - sh:PADL - sh + S], op=ALU.add),
                    reads=[sname], writes=[dname])
                src, sname = dst, dname
            P.op("dve", lambda e, src=src, win_=win_: e.scalar_tensor_tensor(
                out=pooled[:, :], in0=src[:, PADL:], scalar=1.0 / win_, in1=ub[:, PADL:],
                op0=ALU.mult, op1=ALU.subtract), reads=[sname, "ub"], writes=["pooled"])
            nfix = win_ - 1
            P.op("dve", lambda e, src=src, nfix=nfix: e.tensor_tensor(
                out=tmpc[:, 0:nfix], in0=src[:, PADL:PADL + nfix], in1=invc[:, 0:nfix], op=ALU.mult),
                reads=[sname, "invc"], writes=["tmpc"])
            P.op("dve", lambda e, nfix=nfix: e.tensor_tensor(
                out=pooled[:, 0:nfix], in0=tmpc[:, 0:nfix], in1=ub[:, PADL:PADL + nfix], op=ALU.subtract),
                reads=["tmpc", "ub", "pooled"], writes=["pooled"])
            for it in range(S // T):
                pb = 1 + it % 2
                P.op("pe", lambda e, g=g, it=it, pb=pb: e.matmul(
                    ps[pb][:, :T], lhsT=pw_sb[:, g, :], rhs=pooled[:, it * T:(it + 1) * T], start=True, stop=True),
                    reads=["pw", "pooled"], writes=[("ps", pb)])
                P.op("dve", lambda e, g=g, it=it, pb=pb: e.tensor_scalar(
                    out=aout[:, it * T:(it + 1) * T], in0=ps[pb][:, :T], scalar1=psc[:, g:g + 1], scalar2=None,
                    op0=ALU.mult), reads=[("ps", pb), "psc"], writes=["aout"])
            P.dma("sp", mixT[g * 128:(g + 1) * 128, :], aout[:, :], reads=["aout"], writes=[("mixA", g)])

    if upto == "B":
        return
    with C.scope():
        gi = C.sb("e_gi", [4, S], F32)
        gf = C.sb("e_gf", [4, S], F32)
        Bc = C.sb("e_Bc", [4, S], F32)
        Ac = C.sb("e_Ac", [4, S], F32)
        Gm = C.sb("e_Gm", [4, S], F32)
        gb = C.sb("e_gb", [4, 2], F32)
        nbf = C.sb("e_nbf", [4, 1], F32)
        mucol = C.sb("e_mucol", [4, 33], F32)
        dd = C.sb("e_dd", [4, 32], F32)
        oh4 = C.sb("e_oh4", [4, 4, 128], F32)
        P.dma("sp", gi[:, :], gT[0:4, :], writes=["gi"])
        P.dma("sp", gf[:, :], gT[4:8, :], writes=["gf"])
        P.dma("sp", gb[:, :], W["gate_bias"], writes=["gb"])
        P.dma("sp", oh4.rearrange("p a b -> p (a b)"), cin["onehot4"], writes=["oh4"])
        one4 = cst["one"][0:4, 0:1]
        P.op("dve", lambda e: e.tensor_scalar(out=gi[:, :], in0=gi[:, :], scalar1=gb[:, 0:1], scalar2=None, op0=ALU.add),
             reads=["gi", "gb"], writes=["gi"])
        P.op("dve", lambda e: e.tensor_scalar(out=nbf[:, :], in0=gb[:, 1:2], scalar1=-1.0, scalar2=None, op0=ALU.mult),
             reads=["gb"], writes=["nbf"])
        P.op("act", lambda e: e.activation(out=gf[:, :], in_=gf[:, :], func=AF.Exp, scale=-1.0, bias=nbf[:, 0:1]),
             reads=["gf", "nbf"], writes=["gf"])
        P.op("act", lambda e: e.activation(out=gf[:, :], in_=gf[:, :], func=AF.Ln, scale=1.0, bias=one4),
             reads=["gf", "one"], writes=["gf"])
        P.op("dve", lambda e: e.tensor_tensor_scan(out=Bc[:, :], data0=one4.to_broadcast([4, S]), data1=gf[:, :],
                                                   initial=0.0, op0=ALU.mult, op1=ALU.subtract),
             reads=["gf", "one"], writes=["Bc"])
        P.op("dve", lambda e: e.tensor_tensor(out=Ac[:, :], in0=gi[:, :], in1=Bc[:, :], op=ALU.subtract),
             reads=["gi", "Bc"], writes=["Ac"])
        P.op("dve", lambda e: e.tensor_tensor_scan(out=Gm[:, :], data0=Ac[:, :], data1=Ac[:, :],
                                                   initial=0.0, op0=ALU.max, op1=ALU.max),
             reads=["Ac"], writes=["Gm"])
        Gend = Gm.rearrange("h (c l) -> h c l", l=128)[:, :, 127:128]
        P.op("dve", lambda e: e.memset(mucol[:, 0:1], 0.0), writes=["mucol"])
        P.op("dve", lambda e: e.tensor_copy(out=mucol[:, 1:33].unsqueeze(2), in_=Gend), reads=["Gm", "mucol"],
             writes=["mucol"])
        P.op("dve", lambda e: e.tensor_tensor(out=dd[:, :], in0=mucol[:, 0:32], in1=mucol[:, 1:33], op=ALU.subtract),
             reads=["mucol"], writes=["dd"])
        P.op("act", lambda e: e.activation(out=dd[:, :], in_=dd[:, :], func=AF.Exp), reads=["dd"], writes=["dd"])
        A3 = Ac.rearrange("h (c l) -> h c l", l=128)
        B3 = Bc.rearrange("h (c l) -> h c l", l=128)
        P.op("dve", lambda e: e.tensor_tensor(out=A3, in0=A3, in1=Gend.to_broadcast([4, 32, 128]), op=ALU.subtract),
             reads=["Ac", "Gm"], writes=["Ac"])
        P.op("act", lambda e: e.activation(out=Ac[:, :], in_=Ac[:, :], func=AF.Exp), reads=["Ac"], writes=["Ac"])
        P.op("dve", lambda e: e.tensor_tensor(out=B3, in0=B3, in1=Gend.to_broadcast([4, 32, 128]), op=ALU.add),
             reads=["Bc", "Gm"], writes=["Bc"])
        P.op("act", lambda e: e.activation(out=Bc[:, :], in_=Bc[:, :], func=AF.Exp, scale=-1.0), reads=["Bc"],
             writes=["Bc"])
        for h in range(4):
            P.op("pe", lambda e, h=h: e.matmul(ps[1][:, h * 32:(h + 1) * 32], lhsT=oh4[:, h, :], rhs=dd[:, :],
                                                start=True, stop=True), reads=["oh4", "dd"], writes=[("ps", 1)])
        P.op("dve", lambda e: e.tensor_copy(out=dcol.rearrange("p a b -> p (a b)"), in_=ps[1][:, 0:128]),
             reads=[("ps", 1)], writes=["dcol"])
        for src, sname, dst, dname, pb in ((Ac, "Ac", ws_tm, "ws_tm", 2), (Bc, "Bc", thr_tm, "thr_tm", 3)):
            for c in range(32):
                P.op("pe", lambda e, src=src, c=c, pb=pb: e.transpose(
                    out=ps[pb][:, c * 4:(c + 1) * 4], in_=src[:, c * 128:(c + 1) * 128], identity=cst["identf"][0:4, 0:4]),
                    reads=[sname, "identf"], writes=[("ps", pb)])
            P.op("dve", lambda e, dst=dst, pb=pb: e.tensor_copy(out=dst.rearrange("p a b -> p (a b)"),
                                                               in_=ps[pb][:, 0:128]),
                 reads=[("ps", pb)], writes=[dname])

    if upto == "C":
        return
    with C.scope():
        qkb = C.sb("e_qkb", [128, 8, S], BF16)
        cw = C.sb("e_cw", [128, 8, 4], F32)
        cb = C.sb("e_cb", [128, 8], F32)
        qtmp = [C.sb("e_qtmp%d" % i, [128, 512], F32) for i in range(2)]
        P.dma("sp", cw.rearrange("p a b -> p (a b)"), W["qk_conv_w"], writes=["cw"])
        P.dma("sp", cb[:, :], W["qk_conv_b"], writes=["cb"])
        qcnt = [0]

        def sink_e(m, it, psap, bias, pkey):
            tsl = slice(it * 512, (it + 1) * 512)
            if m < 4:
                k = qcnt[0] % 2
                qcnt[0] += 1
                P.op("act", lambda e: e.activation(out=qtmp[k][:, :], in_=psap, func=AF.Silu, bias=bias),
                     reads=[pkey, "cb"], writes=[("qtmp", k)])
                P.op("dve", lambda e: e.tensor_scalar(out=qkb[:, m, tsl], in0=qtmp[k][:, :], scalar1=128.0 ** -0.5,
                                                       scalar2=None, op0=ALU.mult),
                     reads=[("qtmp", k)], writes=[("qkb", m)])
            else:
                P.op("act", lambda e: e.activation(out=qkb[:, m, tsl], in_=psap, func=AF.Silu, bias=bias),
                     reads=[pkey, "cb"], writes=[("qkb", m)])

        conv_silu_pe(C, cst, "ecv", qkT, 8, cw, cb, sink_e)

        if upto == "D0":
            return
        vch = [C.sb("e_vch%d" % i, [128, 4, 128], BF16) for i in range(2)]
        och = [C.sb("e_och%d" % i, [128, 512], F32) for i in range(2)]
        vw = [C.sb("e_vw%d" % i, [128, 4, 136], BF16) for i in range(2)]
        ktm = [C.sb("e_ktm%d" % i, [128, 4, 128], BF16) for i in range(2)]
        PT = C.sb("e_PT", [128, 4, 128], BF16)
        Sst = C.sb("e_S", [128, 4, 129], F32)
        Sbf = C.sb("e_Sbf", [128, 4, 136], BF16)
        nwb = C.sb("e_nwb", [128, 512], F32)
        nwo = C.sb("e_nwo", [128, 512], F32)
        den = C.sb("e_den", [128, 4], F32)
        ss = C.sb("e_ss", [128, 4], F32)
        junk = C.sb("e_junk", [128, 128], F32)
        bout = C.sb("e_bout", [128, 4, 128], BF16)
        boutT = [C.sb("e_boutT%d" % i, [128, 4, 128], BF16) for i in range(2)]
        P.dma("sp", nwb[:, :], W["mlstm_norm"].partition_broadcast(128), writes=["nwb"])
        P.op("pool", lambda e: e.memset(Sst.rearrange("p a b -> p (a b)"), 0.0), writes=["S0", "S1", "S2", "S3"])
        ps0b = ps[0][:, :].bitcast(BF16)
        pending_tail = []
        for c in range(1 if upto in ("D1", "D2", "D3") else 32):
            b = c % 2
            blk = slice(c * 128, (c + 1) * 128)
            P.dma("sp", vch[b].rearrange("p a b -> p (a b)"), v_tm[blk, :], writes=[("vch", b)])
            P.dma("sp", och[b][:, :], o_tm[blk, :], writes=[("och", b)])
            P.op("dve", lambda e, b=b, c=c: e.tensor_tensor(
                out=vw[b][:, :, 0:128], in0=vch[b][:, :, :],
                in1=ws_tm[:, c, :].unsqueeze(2).to_broadcast([128, 4, 128]), op=ALU.mult),
                reads=[("vch", b), "ws_tm"], writes=[("vw", b)])
            P.op("dve", lambda e, b=b, c=c: e.tensor_copy(out=vw[b][:, :, 128:129], in_=ws_tm[:, c, :].unsqueeze(2)),
                 reads=["ws_tm", ("vw", b)], writes=[("vw", b)])
            for h in range(4):
                P.op("pe", lambda e, h=h, blk=blk: e.transpose(out=ps0b[:, h * 128:(h + 1) * 128], in_=qkb[:, 4 + h, blk],
                                                                identity=cst["identb"][:, :]),
                     reads=[("qkb", 4 + h), "identb"], writes=[("ps0", "k")])
            P.op("act", lambda e, b=b: e.activation(out=ktm[b].rearrange("p a b -> p (a b)"), in_=ps0b[:, 0:512],
                                                    func=AF.Copy), reads=[("ps0", "k")], writes=[("ktm", b)])
            sb_ = 1 + b
            for h in range(4):
                P.op("pe", lambda e, h=h, blk=blk, sb_=sb_: e.matmul(
                    ps[sb_][:, h * 128:(h + 1) * 128], lhsT=qkb[:, 4 + h, blk], rhs=qkb[:, h, blk],
                    start=True, stop=True), reads=[("qkb", 4 + h), ("qkb", h)], writes=[("ps", sb_)])
            P.op("dve", lambda e, sb_=sb_: e.tensor_tensor(
                out=PT[:, :, :], in0=ps[sb_][:, :].rearrange("p (a b) -> p a b", b=128),
                in1=cst["maskT"][:, :].unsqueeze(1).to_broadcast([128, 4, 128]), op=ALU.mult),
                reads=[("ps", sb_), "maskT"], writes=["PT"])
            if upto == "D1":
                continue
            for h in range(4):
                ob = 3 + h // 2
                oc = (h % 2) * 129
                P.op("dve", lambda e, h=h, c=c: e.tensor_scalar(out=Sbf[:, h, 0:129], in0=Sst[:, h, :],
                                                                 scalar1=dcol[:, h, c:c + 1], scalar2=None, op0=ALU.mult),
                     reads=["S%d" % h, "dcol"], writes=["Sbf%d" % h])
                P.op("pe", lambda e, h=h, blk=blk, ob=ob, oc=oc: e.matmul(
                    ps[ob][:, oc:oc + 129], lhsT=qkb[:, h, blk], rhs=Sbf[:, h, 0:129], start=True, stop=False),
                    reads=[("qkb", h), "Sbf%d" % h], writes=[("pso", h)])
                P.op("pe", lambda e, h=h, b=b, ob=ob, oc=oc: e.matmul(
                    ps[ob][:, oc:oc + 129], lhsT=PT[:, h, :], rhs=vw[b][:, h, 0:129], start=False, stop=True),
                    reads=["PT", ("vw", b)], writes=[("pso", h)])
                db = 5 + h // 2
                P.op("pe", lambda e, h=h, b=b, db=db, oc=oc: e.matmul(
                    ps[db][:, oc:oc + 129], lhsT=ktm[b][:, h, :], rhs=vw[b][:, h, 0:129], start=True, stop=True),
                    reads=[("ktm", b), ("vw", b)], writes=[("psd", h)])
                P.op("dve", lambda e, h=h, c=c, db=db, oc=oc: e.scalar_tensor_tensor(
                    out=Sst[:, h, :], in0=Sst[:, h, :], scalar=dcol[:, h, c:c + 1], in1=ps[db][:, oc:oc + 129],
                    op0=ALU.mult, op1=ALU.add), reads=["S%d" % h, "dcol", ("psd", h), "Sbf%d" % h], writes=["S%d" % h])
            if upto == "D2":
                continue
            while pending_tail:
                pending_tail.pop(0)()
            P.op("act", lambda e, b=b: e.activation(out=nwo[:, :], in_=och[b][:, :], func=AF.Sigmoid),
                 reads=[("och", b)], writes=["nwo"])
            P.op("dve", lambda e: e.tensor_tensor(out=nwo[:, :], in0=nwo[:, :], in1=nwb[:, :], op=ALU.mult),
                 reads=["nwo", "nwb"], writes=["nwo"])
            for hp in range(2):
                ob = 3 + hp
                Dv = ps[ob][:, 0:258].rearrange("p (a b) -> p a b", b=129)[:, :, 128:129]
                P.op("act", lambda e, hp=hp, Dv=Dv: e.activation(
                    out=den[:, 2 * hp:2 * hp + 2].unsqueeze(2), in_=Dv, func=AF.Abs),
                    reads=[("pso", 2 * hp), ("pso", 2 * hp + 1)], writes=["den"])
            P.op("dve", lambda e, c=c: e.tensor_tensor(out=den[:, :], in0=den[:, :], in1=thr_tm[:, c, :], op=ALU.max),
                 reads=["den", "thr_tm"], writes=["den"])
            P.op("dve", lambda e: e.reciprocal(out=den[:, :], in_=den[:, :]), reads=["den"], writes=["den"])
            for h in range(4):
                ob = 3 + h // 2
                oc = (h % 2) * 129
                P.op("act", lambda e, h=h, ob=ob, oc=oc: e.activation(
                    out=junk[:, :], in_=ps[ob][:, oc:oc + 128], func=AF.Square, scale=den[:, h:h + 1],
                    accum_out=ss[:, h:h + 1]), reads=[("pso", h), "den"], writes=["junk", ("ss", h)])
            P.op("act", lambda e: e.activation(out=ss[:, :], in_=ss[:, :], func=AF.Sqrt, scale=1.0 / 128, bias=cst["eps"][:, 0:1]),
                 reads=[("ss", h) for h in range(4)] + ["eps"], writes=[("ss", h) for h in range(4)])
            P.op("dve", lambda e: e.reciprocal(out=ss[:, :], in_=ss[:, :]), reads=[("ss", h) for h in range(4)],
                 writes=[("ss", h) for h in range(4)])
            P.op("dve", lambda e: e.tensor_tensor(out=ss[:, :], in0=ss[:, :], in1=den[:, :], op=ALU.mult),
                 reads=[("ss", h) for h in range(4)] + ["den"], writes=[("ss", h) for h in range(4)])
            for h in range(4):
                ob = 3 + h // 2
                oc = (h % 2) * 129
                P.op("dve", lambda e, h=h, ob=ob, oc=oc: e.scalar_tensor_tensor(
                    out=bout[:, h, :], in0=ps[ob][:, oc:oc + 128], scalar=ss[:, h:h + 1], in1=nwo[:, h * 128:(h + 1) * 128],
                    op0=ALU.mult, op1=ALU.mult), reads=[("pso", h), ("ss", h), "nwo"], writes=[("bout", h)])
            def emit_tail(c=c, b=b, blk=blk):
                for h in range(4):
                    P.op("pe", lambda e, h=h: e.transpose(out=ps0b[:, 512 + h * 128:512 + (h + 1) * 128], in_=bout[:, h, :],
                                                          identity=cst["identb"][:, :]),
                         reads=[("bout", h), "identb"], writes=[("ps0", "b")])
                P.op("act", lambda e, b=b: e.activation(out=boutT[b].rearrange("p a b -> p (a b)"), in_=ps0b[:, 512:1024],
                                                        func=AF.Copy), reads=[("ps0", "b")], writes=[("boutT", b)])
                P.dma("sp", mixT[512:1024, blk].rearrange("(h p) t -> p h t", p=128), boutT[b][:, :, :],
                      reads=[("boutT", b)], writes=[("mixB", c)])
            pending_tail.append(emit_tail)
        for f_ in pending_tail:
            f_()

    if upto[0] == "D":
        return
    with C.scope():
        wo = C.sb("e_wo", [128, 8, D], BF16)
        stg = Stager(C, "e_stgE")
        wov = W["w_out"].rearrange("(c p) f -> p c f", p=128)
        for c in range(8):
            stg.load(wo[:, c, :], wov[:, c, :], "wo")
        out_proj(C, wo, 8, mixT, xT_in, xT_out, T)


def out_proj(C, wo, nk, mixT, xT_in, xT_out, T):
    P = C.P
    ps = C.psum
    xin = xT_in.rearrange("(c p) t -> p c t", p=128)
    xout = xT_out.rearrange("(c p) t -> p c t", p=128)
    mixv = mixT.rearrange("(c p) t -> p c t", p=128)
    xt = [C.sb("op_xt%d" % i, [128, 8, T], F32) for i in range(2)]
    mt = [C.sb("op_mt%d" % i, [128, nk, T], BF16) for i in range(2)]
    for it in range(S // T):
        b = it % 2
        tsl = slice(it * T, (it + 1) * T)
        P.dma("sp", xt[b][:, :, :], xin[:, :, tsl], writes=[("op_xt", b)])
        P.dma("act", mt[b][:, :, :], mixv[:, :, tsl], writes=[("op_mt", b)])
        for i in range(8):
            pb = 1 + i % 4
            for k in range(nk):
                P.op("pe", lambda e, i=i, k=k, pb=pb, b=b: e.matmul(
                    ps[pb][:, :T], lhsT=wo[:, k, i * 128:(i + 1) * 128], rhs=mt[b][:, k, :],
                    start=(k == 0), stop=(k == nk - 1)), reads=["wo", ("op_mt", b)], writes=[("ps", pb)])
            P.op("dve", lambda e, i=i, pb=pb, b=b: e.tensor_tensor(
                out=xt[b][:, i, :], in0=ps[pb][:, :T], in1=xt[b][:, i, :], op=ALU.add),
                reads=[("ps", pb), ("op_xt", b)], writes=[("op_xt", b)])
        P.dma("pool", xout[:, :, tsl], xt[b][:, :, :], reads=[("op_xt", b)], writes=[("op_out", it)])


def _pc(v, nchunk):
    return np.ascontiguousarray(np.asarray(v, np.float32).reshape(nchunk, 128).T)


def host_params(inp):
    f = lambda k: np.ascontiguousarray(np.asarray(inp[k], np.float32))
    out = {}
    for pre in ("l0_ffn1", "l0_ffn2", "l1_ffn1", "l1_ffn2"):
        out[pre + "_norm"] = _pc(inp[pre + "_norm"], 8)
        for w in ("wg", "wu", "wd"):
            out[pre + "_" + w] = f(pre + "_" + w)
    out["l0_mix_norm"] = _pc(inp["l0_mix_norm"], 8)
    out["l0_w_in"] = f("l0_w_in")
    out["l0_pool_w"] = np.ascontiguousarray(np.transpose(f("l0_pool_w"), (1, 0, 2)).reshape(128, 512))
    out["l0_pool_scale"] = _pc(inp["l0_pool_scale"], 4)
    out["l0_qk_conv_w"] = np.ascontiguousarray(f("l0_qk_conv_w").reshape(4, 8, 128).transpose(2, 1, 0).reshape(128, 32))
    out["l0_qk_conv_b"] = _pc(inp["l0_qk_conv_b"], 8)
    out["l0_gate_bias"] = np.ascontiguousarray(f("l0_gate_bias").reshape(2, 4).T)
    out["l0_mlstm_norm"] = f("l0_mlstm_norm")
    out["l0_w_out"] = f("l0_w_out")
    out["l1_mix_norm"] = _pc(inp["l1_mix_norm"], 8)
    out["l1_w_in"] = f("l1_w_in")
    out["l1_ssd_conv_w"] = np.ascontiguousarray(f("l1_ssd_conv_w").reshape(4, 16, 128).transpose(2, 1, 0).reshape(128, 64))
    out["l1_ssd_conv_b"] = _pc(inp["l1_ssd_conv_b"], 16)
    out["l1_ssd_dt_bias"] = f("l1_ssd_dt_bias").reshape(16, 1)
    out["l1_ssd_A_log"] = f("l1_ssd_A_log").reshape(16, 1)
    out["l1_ssd_D"] = f("l1_ssd_D")
    out["l1_ssd_norm"] = f("l1_ssd_norm")
    out["l1_sb_q_norm"] = np.ascontiguousarray(np.tile(f("l1_sb_q_norm"), 2).reshape(128, 1))
    out["l1_sb_k_norm"] = np.ascontiguousarray(np.tile(f("l1_sb_k_norm"), 2).reshape(128, 1))
    out["l1_w_out"] = f("l1_w_out")
    return out


PARAM_SHAPES = {
    "l0_mix_norm": [128, 8], "l0_w_in": [1024, 2568], "l0_pool_w": [128, 512], "l0_pool_scale": [128, 4],
    "l0_qk_conv_w": [128, 32], "l0_qk_conv_b": [128, 8], "l0_gate_bias": [4, 2], "l0_mlstm_norm": [512],
    "l0_w_out": [1024, 1024],
    "l1_mix_norm": [128, 8], "l1_w_in": [1024, 4624], "l1_ssd_conv_w": [128, 64], "l1_ssd_conv_b": [128, 16],
    "l1_ssd_dt_bias": [16, 1], "l1_ssd_A_log": [16, 1], "l1_ssd_D": [16], "l1_ssd_norm": [1024],
    "l1_sb_q_norm": [128, 1], "l1_sb_k_norm": [128, 1], "l1_w_out": [1536, 1024],
}
for _pre in ("l0_ffn1", "l0_ffn2", "l1_ffn1", "l1_ffn2"):
    PARAM_SHAPES[_pre + "_norm"] = [128, 8]
    PARAM_SHAPES[_pre + "_wg"] = [D, DFF]
    PARAM_SHAPES[_pre + "_wu"] = [D, DFF]
    PARAM_SHAPES[_pre + "_wd"] = [DFF, D]
CONST_SHAPES = {"identf": [128, 128], "maskT": [128, 128], "maskTs": [128, 128], "triu": [128, 128],
                "invc": [128, 16], "onehot4": [4, 512], "onehot16": [16, 2048], "blk1": [128, 128], "negm": [128, 128]}


def odd_mixer_phase(C, cst, cin, xT_in, xT_out, W, upto="E"):
    P = C.P
    ps = C.psum
    T = 512
    o_z = C.dram("o_z", [S, 1024], F32)
    o_xbcT = C.dram("o_xbcT", [2048, S], F32)
    o_dtT = C.dram("o_dtT", [16, S], F32)
    o_qT = C.dram("o_qT", [512, S], BF16)
    o_kT = C.dram("o_kT", [512, S], BF16)
    o_v = C.dram("o_v", [S, 512], BF16)
    o_xc = C.dram("o_xc", [2048, S], BF16)
    mixT = C.dram("o_mix", [1536, S], BF16)

    bias_tm = C.sb("o_bias_tm", [128, 32, 16], F32)
    est_tm = C.sb("o_est_tm", [128, 32, 16], F32)
    dtte_tm = C.sb("o_dtte_tm", [128, 32, 16], F32)
    cdb = C.sb("o_cdb", [128, 32, 16], F32)
    blk1 = C.sb("o_blk1", [128, 128], BF16)
    negm = C.sb("o_negm", [128, 128], BF16)
    eps64 = C.sb("o_eps64", [128, 1], F32)
    P.op("pool", lambda e: e.memset(eps64[:, :], 64.0 * EPS), writes=["eps64"])
    P.dma("sp", cst["tmpf"][:, :], cin["blk1"], reads=["tmpf"], writes=["tmpf"])
    P.op("pool", lambda e: e.tensor_copy(out=blk1[:, :], in_=cst["tmpf"][:, :]), reads=["tmpf"], writes=["blk1"])
    P.dma("sp", cst["tmpf"][:, :], cin["negm"], reads=["tmpf"], writes=["tmpf"])
    P.op("pool", lambda e: e.tensor_copy(out=negm[:, :], in_=cst["tmpf"][:, :]), reads=["tmpf"], writes=["negm"])

    with C.scope():
        w_sb = C.sb("o_win", [128, 8, 4624], BF16)
        stg = Stager(C, "o_stg", cols=1156)
        win = W["w_in"].rearrange("(c p) f -> p c f", p=128)
        for c in range(8):
            stg.load(w_sb[:, c, :], win[:, c, :], "win")
        fm = [C.sb("o_fm%d" % i, [128, T], F32) for i in range(6)]
        zst = [C.sb("o_zst%d" % i, [128, 512], F32) for i in range(6)]
        vst = [C.sb("o_vst%d" % i, [128, 512], BF16) for i in range(4)]
        dst_ = C.sb("o_dst", [16, T], F32)
        sqb = [C.sb("o_sqb%d" % i, [128, T], BF16) for i in range(4)]
        rr = [C.sb("o_rr%d" % i, [128, T], F32) for i in range(2)]
        qn = [C.sb("o_qn%d" % i, [128, T], BF16) for i in range(4)]
        qw = C.sb("o_qw", [128, 2], F32)
        P.dma("sp", qw[:, 0:1], W["sb_q_norm"], writes=["qw"])
        P.dma("sp", qw[:, 1:2], W["sb_k_norm"], reads=["qw"], writes=["qw"])
        cnt = {"fm": 0, "qn": 0, "z": 0}

        def body(it, hT, hk):
            tsl = slice(it * T, (it + 1) * T)
            for m in range(16):
                pb = 1 + m % 2
                for c in range(8):
                    P.op("pe", lambda e, c=c, m=m, pb=pb: e.matmul(
                        ps[pb][:, :T], lhsT=w_sb[:, c, 1024 + m * 128:1024 + (m + 1) * 128], rhs=hT[:, c, :],
                        start=(c == 0), stop=(c == 7)), reads=["win", hk], writes=[("ps", pb)])
                k = cnt["fm"] % 6
                cnt["fm"] += 1
                evac(P, "act" if m % 2 == 0 else "dve", fm[k][:, :], ps[pb][:, :T], [("ps", pb)], [("fm", k)])
                P.dma("sp", o_xbcT[m * 128:(m + 1) * 128, tsl], fm[k][:, :], reads=[("fm", k)],
                      writes=[("A_out", "xbc", m, it)])
            def qk_proj(m):
                isq = m < 4
                col0 = (3088 if isq else 3600) + (m % 4) * 128
                pbq = 3 + m % 4
                kk = m % 4
                for c in range(8):
                    P.op("pe", lambda e, c=c, col0=col0, pbq=pbq: e.matmul(
                        ps[pbq][:, :T], lhsT=w_sb[:, c, col0:col0 + 128], rhs=hT[:, c, :],
                        start=(c == 0), stop=(c == 7)), reads=["win", hk], writes=[("ps", pbq)])
                P.op("act", lambda e, pbq=pbq, kk=kk: e.activation(out=sqb[kk][:, :], in_=ps[pbq][:, :T], func=AF.Square),
                     reads=[("ps", pbq)], writes=[("sqb", kk)])

            qk_proj(0)
            qk_proj(1)
            for m in range(8):
                isq = m < 4
                pbq = 3 + m % 4
                pbs = 1 + m % 2
                kk = m % 4
                k2 = m % 2
                P.op("pe", lambda e, pbs=pbs, kk=kk: e.matmul(ps[pbs][:, :T], lhsT=blk1[:, :], rhs=sqb[kk][:, :],
                                                              start=True, stop=True),
                     reads=["blk1", ("sqb", kk)], writes=[("ps", pbs)])
                if isq:
                    P.op("act", lambda e, pbs=pbs, k2=k2: e.activation(out=rr[k2][:, :], in_=ps[pbs][:, :T], func=AF.Sqrt,
                                                                       scale=1.0, bias=eps64[:, 0:1]),
                         reads=[("ps", pbs), "eps64"], writes=[("rr", k2)])
                else:
                    P.op("act", lambda e, pbs=pbs, k2=k2: e.activation(out=rr[k2][:, :], in_=ps[pbs][:, :T], func=AF.Sqrt,
                                                                       scale=1.0 / 64, bias=cst["eps"][:, 0:1]),
                         reads=[("ps", pbs), "eps"], writes=[("rr", k2)])
                P.op("dve", lambda e, k2=k2: e.reciprocal(out=rr[k2][:, :], in_=rr[k2][:, :]), reads=[("rr", k2)],
                     writes=[("rr", k2)])
                k = cnt["qn"] % 4
                cnt["qn"] += 1
                wi = 0 if isq else 1
                P.op("dve", lambda e, k=k, wi=wi, pbq=pbq, k2=k2: e.scalar_tensor_tensor(
                    out=qn[k][:, :], in0=ps[pbq][:, :T], scalar=qw[:, wi:wi + 1], in1=rr[k2][:, :], op0=ALU.mult, op1=ALU.mult),
                    reads=[("ps", pbq), "qw", ("rr", k2)], writes=[("qn", k)])
                dd_ = (o_qT if isq else o_kT)[(m % 4) * 128:(m % 4 + 1) * 128, tsl]
                P.dma("sp", dd_, qn[k][:, :], reads=[("qn", k)], writes=[("A_out", "qk", m, it)])
                if m + 2 < 8:
                    qk_proj(m + 2)
            for q in range(4):
                tok = slice(q * 128, (q + 1) * 128)
                r0 = it * T + q * 128
                for half in range(2):
                    pb = 5 + half
                    col0 = half * 512
                    for c in range(8):
                        P.op("pe", lambda e, c=c, tok=tok, col0=col0, pb=pb: e.matmul(
                            ps[pb][:, :512], lhsT=hT[:, c, tok], rhs=w_sb[:, c, col0:col0 + 512],
                            start=(c == 0), stop=(c == 7)), reads=["win", hk], writes=[("ps", pb)])
                    k = cnt["z"] % 6
                    cnt["z"] += 1
                    evac(P, "act" if half == 0 else "dve", zst[k][:, :], ps[pb][:, :512], [("ps", pb)], [("zst", k)])
                    P.dma("sp", o_z[r0:r0 + 128, col0:col0 + 512], zst[k][:, :], reads=[("zst", k)],
                          writes=[("A_out", "z", r0, half)])
                for c in range(8):
                    P.op("pe", lambda e, c=c, tok=tok: e.matmul(
                        ps[7][:, :512], lhsT=hT[:, c, tok], rhs=w_sb[:, c, 4112:4624],
                        start=(c == 0), stop=(c == 7)), reads=["win", hk], writes=[("ps", 7)])
                k = q % 4
                evac(P, "act", vst[k][:, :], ps[7][:, :512], [("ps", 7)], [("vst", k)])
                P.dma("sp", o_v[r0:r0 + 128, :], vst[k][:, :], reads=[("vst", k)], writes=[("A_out", "v", r0)])
            for c in range(8):
                P.op("pe", lambda e, c=c: e.matmul(ps[7][0:16, :T], lhsT=w_sb[:, c, 3072:3088], rhs=hT[:, c, :],
                                                    start=(c == 0), stop=(c == 7)), reads=["win", hk], writes=[("ps", 7)])
            evac(P, "dve", dst_[:, :], ps[7][0:16, :T], [("ps", 7)], ["dst"])
            P.dma("sp", o_dtT[:, tsl], dst_[:, :], reads=["dst"], writes=[("A_out", "dt", it)])

        proj_norm_tiles(C, cst, xT_in, W["mix_norm"], T, body)
    if upto == "A":
        return

    with C.scope():
        xcb = [C.sb("o_xcb%d" % i, [128, S], BF16) for i in range(2)]
        cw = C.sb("o_cw", [128, 16, 4], F32)
        cb = C.sb("o_cb", [128, 16], F32)
        P.dma("sp", cw.rearrange("p a b -> p (a b)"), W["ssd_conv_w"], writes=["cw"])
        P.dma("sp", cb[:, :], W["ssd_conv_b"], writes=["cb"])

        def sink_o(m, it, psap, bias, pkey):
            k = m % 2
            P.op("act", lambda e: e.activation(out=xcb[k][:, it * 512:(it + 1) * 512], in_=psap, func=AF.Silu, bias=bias),
                 reads=[pkey, "cb"], writes=[("xcb", k)])
            if it == S // 512 - 1:
                P.dma("sp", o_xc[m * 128:(m + 1) * 128, :], xcb[k][:, :], reads=[("xcb", k)], writes=[("xc", m)])

        conv_silu_pe(C, cst, "ocv", o_xbcT, 16, cw, cb, sink_o)
    if upto == "S0":
        return

    with C.scope():
        dtr = C.sb("o_dtr", [16, S], F32)
        ldt = C.sb("o_ldt", [16, S], F32)
        aa = C.sb("o_aa", [16, S], F32)
        te = C.sb("o_te", [16, S], F32)
        es = C.sb("o_es", [16, S], F32)
        dtb = C.sb("o_dtb", [16, 1], F32)
        Aneg = C.sb("o_Aneg", [16, 1], F32)
        cde = C.sb("o_cde", [16, 32], F32)
        oh16 = C.sb("o_oh16", [16, 16, 128], F32)
        one16 = cst["one"][0:16, 0:1]
        P.dma("sp", dtr[:, :], o_dtT[:, :], writes=["dtr"])
        P.dma("sp", dtb[:, :], W["ssd_dt_bias"], writes=["dtb"])
        P.dma("sp", Aneg[:, :], W["ssd_A_log"], writes=["Aneg"])
        P.dma("sp", oh16.rearrange("p a b -> p (a b)"), cin["onehot16"], writes=["oh16"])
        P.op("act", lambda e: e.activation(out=Aneg[:, :], in_=Aneg[:, :], func=AF.Exp), reads=["Aneg"], writes=["Aneg"])
        P.op("dve", lambda e: e.tensor_scalar(out=Aneg[:, :], in0=Aneg[:, :], scalar1=-1.0, scalar2=None, op0=ALU.mult),
             reads=["Aneg"], writes=["Aneg"])
        P.op("act", lambda e: e.activation(out=dtr[:, :], in_=dtr[:, :], func=AF.Exp, bias=dtb[:, 0:1]),
             reads=["dtr", "dtb"], writes=["dtr"])
        P.op("act", lambda e: e.activation(out=dtr[:, :], in_=dtr[:, :], func=AF.Ln, bias=one16),
             reads=["dtr", "one"], writes=["dtr"])
        P.op("act", lambda e: e.activation(out=ldt[:, :], in_=dtr[:, :], func=AF.Ln), reads=["dtr"], writes=["ldt"])
        P.op("dve", lambda e: e.tensor_scalar(out=aa[:, :], in0=dtr[:, :], scalar1=Aneg[:, 0:1], scalar2=None, op0=ALU.mult),
             reads=["dtr", "Aneg"], writes=["aa"])
        for c in range(32):
            blk = slice(c * 128, (c + 1) * 128)
            P.op("dve", lambda e, blk=blk: e.tensor_tensor_scan(
                out=aa[:, blk], data0=one16.to_broadcast([16, 128]), data1=aa[:, blk], initial=0.0,
                op0=ALU.mult, op1=ALU.add), reads=["aa", "one"], writes=["aa"])
        aa3 = aa.rearrange("h (c l) -> h c l", l=128)
        aend = aa3[:, :, 127:128]
        P.op("dve", lambda e: e.tensor_copy(out=cde[:, :].unsqueeze(2), in_=aend), reads=["aa"], writes=["cde"])
        P.op("act", lambda e: e.activation(out=cde[:, :], in_=cde[:, :], func=AF.Exp), reads=["cde"], writes=["cde"])
        P.op("act", lambda e: e.activation(out=es[:, :], in_=aa[:, :], func=AF.Exp), reads=["aa"], writes=["es"])
        te3 = te.rearrange("h (c l) -> h c l", l=128)
        P.op("dve", lambda e: e.tensor_tensor(out=te3, in0=aend.to_broadcast([16, 32, 128]), in1=aa3, op=ALU.subtract),
             reads=["aa"], writes=["te"])
        P.op("act", lambda e: e.activation(out=te[:, :], in_=te[:, :], func=AF.Exp), reads=["te"], writes=["te"])
        P.op("dve", lambda e: e.tensor_tensor(out=te[:, :], in0=te[:, :], in1=dtr[:, :], op=ALU.mult),
             reads=["te", "dtr"], writes=["te"])
        P.op("dve", lambda e: e.tensor_tensor(out=ldt[:, :], in0=ldt[:, :], in1=aa[:, :], op=ALU.subtract),
             reads=["ldt", "aa"], writes=["ldt"])
        for h in range(16):
            P.op("pe", lambda e, h=h: e.matmul(ps[1][:, h * 32:(h + 1) * 32], lhsT=oh16[:, h, :], rhs=cde[:, :],
                                                start=True, stop=True), reads=["oh16", "cde"], writes=[("ps", 1)])
        P.op("dve", lambda e: e.tensor_copy(out=cdb.rearrange("p c h -> p h c"),
                                            in_=ps[1][:, :].rearrange("p (h c) -> p h c", c=32)),
             reads=[("ps", 1)], writes=["cdb"])
        for src, sname, dst, dname, pb in ((ldt, "ldt", bias_tm, "bias_tm", 2), (es, "es", est_tm, "est_tm", 3),
                                           (te, "te", dtte_tm, "dtte_tm", 4)):
            for c in range(32):
                P.op("pe", lambda e, src=src, c=c, pb=pb: e.transpose(
                    out=ps[pb][:, c * 16:(c + 1) * 16], in_=src[:, c * 128:(c + 1) * 128],
                    identity=cst["identf"][0:16, 0:16]), reads=[sname, "identf"], writes=[("ps", pb)])
            P.op("dve", lambda e, dst=dst, pb=pb: e.tensor_copy(out=dst.rearrange("p a b -> p (a b)"), in_=ps[pb][:, :]),
                 reads=[("ps", pb)], writes=[dname])
        o_acum = C.dram("o_acum", [16, S], F32)
        P.dma("sp", o_acum[:, :], aa[:, :], reads=["aa"], writes=["o_acum"])
    if upto == "S1":
        return
    ssd_main(C, cst, cin, W, o_z, o_xc, mixT, bias_tm, est_tm, dtte_tm, cdb, negm, upto)
    if upto[0] == "S":
        return
    stick_breaking_phase(C, cst, o_qT, o_kT, o_v, mixT, upto)
    if upto[0] == "T":
        return
    with C.scope():
        wo = C.sb("o_wo", [128, 12, D], BF16)
        stg = Stager(C, "o_stgE")
        wov = W["w_out"].rearrange("(c p) f -> p c f", p=128)
        for c in range(12):
            stg.load(wo[:, c, :], wov[:, c, :], "wo")
        out_proj(C, wo, 12, mixT, xT_in, xT_out, T)


def ssd_main(C, cst, cin, W, o_z, o_xc, mixT, bias_tm, est_tm, dtte_tm, cdb, negm, upto):
    P = C.P
    ps = C.psum
    o_acum = C.dram("o_acum", [16, S], F32)
    with C.scope():
        acf = C.sb("s_acf", [16, S], F32)
        oh16 = C.sb("s_oh16", [16, 16, 128], F32)
        Dbc = C.sb("s_Dbc", [128, 16], F32)
        Did = C.sb("s_Did", [128, 16, 128], BF16)
        nwb = C.sb("s_nwb", [128, 1024], F32)
        xsup = [C.sb("s_xsup%d" % i, [128, 16, 512], BF16) for i in range(2)]
        xtm = C.sb("s_xtm", [128, 16, 64], BF16)
        xw = C.sb("s_xw", [128, 16, 64], BF16)
        Btm = C.sb("s_Btm", [128, 4, 128], BF16)
        dec = [C.sb("s_dec%d" % i, [128, 4, 128], F32) for i in range(2)]
        PTs = [C.sb("s_PT%d" % i, [128, 4, 128], BF16) for i in range(2)]
        Hst = C.sb("s_H", [128, 16, 64], F32)
        Hbf = C.sb("s_Hbf", [128, 16, 64], BF16)
        zch = [C.sb("s_z%d" % i, [128, 1024], F32) for i in range(2)]
        yoff = C.sb("s_yoff", [128, 16, 64], F32)
        ysb = C.sb("s_y", [128, 1024], F32)
        ssg = C.sb("s_ss", [128, 4], F32)
        junk = C.sb("s_junk", [128, 256], F32)
        cout = C.sb("s_cout", [128, 1024], BF16)
        coutT = [C.sb("s_coutT%d" % i, [128, 8, 128], BF16) for i in range(2)]
        P.dma("sp", acf[:, :], o_acum[:, :], writes=["acf"])
        P.dma("sp", oh16.rearrange("p a b -> p (a b)"), cin["onehot16"], writes=["oh16"])
        P.dma("sp", Dbc[:, :], W["ssd_D"].partition_broadcast(128), writes=["Dbc"])
        P.dma("sp", nwb[:, :], W["ssd_norm"].partition_broadcast(128), writes=["nwb"])
        for h in range(16):
            P.op("dve", lambda e, h=h: e.tensor_scalar(out=Did[:, h, :], in0=cst["identf"][:, :], scalar1=Dbc[:, h:h + 1],
                                                       scalar2=None, op0=ALU.mult), reads=["identf", "Dbc"], writes=["Did"])
        P.op("pool", lambda e: e.memset(Hst.rearrange("p a b -> p (a b)"), 0.0), writes=["H"])
        P.op("pool", lambda e: e.memset(Hbf.rearrange("p a b -> p (a b)"), 0.0), writes=["Hbf"])
        ps0b = ps[0][:, :].bitcast(BF16)
        ps1b = ps[1][:, :].bitcast(BF16)
        xcv = o_xc.rearrange("(m p) t -> p m t", p=128)
        nch = 1 if upto == "S2" else 32
        pending_tail = []
        for c in range(nch):
            b = c % 2
            sc, lc = c // 4, c % 4
            blk = slice(c * 128, (c + 1) * 128)
            tl = slice(lc * 128, (lc + 1) * 128)
            if lc == 0:
                P.dma("sp", xsup[sc % 2][:, :, :], xcv[:, :, sc * 512:(sc + 1) * 512], writes=[("xsup", sc % 2)])
            xs_ = xsup[sc % 2]
            xk = ("xsup", sc % 2)
            P.dma("sp", zch[b][:, :], o_z[blk, :], writes=[("zch", b)])
            for m in range(8):
                P.op("pe", lambda e, m=m, xs_=xs_, tl=tl: e.transpose(out=ps0b[:, m * 128:(m + 1) * 128], in_=xs_[:, m, tl],
                                                                      identity=cst["identb"][:, :]),
                     reads=[xk, "identb"], writes=[("ps", 0)])
            P.op("act", lambda e: e.activation(out=xtm.rearrange("p a b -> p (a b)"), in_=ps0b[:, :], func=AF.Copy),
                 reads=[("ps", 0)], writes=["xtm"])
            P.op("dve", lambda e, c=c: e.tensor_tensor(
                out=xw[:, :, :], in0=ps0b[:, :].rearrange("p (a b) -> p a b", b=64),
                in1=dtte_tm[:, c, :].unsqueeze(2).to_broadcast([128, 16, 64]), op=ALU.mult),
                reads=[("ps", 0), "dtte_tm"], writes=["xw"])
            for g in range(4):
                P.op("pe", lambda e, g=g, xs_=xs_, tl=tl: e.transpose(out=ps1b[:, g * 128:(g + 1) * 128], in_=xs_[:, 8 + g, tl],
                                                                      identity=cst["identb"][:, :]),
                     reads=[xk, "identb"], writes=[("ps", 1)])
            P.op("act", lambda e: e.activation(out=Btm.rearrange("p a b -> p (a b)"), in_=ps1b[:, 0:512], func=AF.Copy),
                 reads=[("ps", 1)], writes=["Btm"])
            for g in range(4):
                P.op("pe", lambda e, g=g, xs_=xs_, tl=tl: e.matmul(ps[2][:, g * 128:(g + 1) * 128], lhsT=xs_[:, 8 + g, tl],
                                                                   rhs=xs_[:, 12 + g, tl], start=True, stop=True),
                     reads=[xk], writes=[("ps", 2)])
            def emit_yoff(c=c, xs_=xs_, tl=tl, xk=xk):
                for g in range(4):
                    P.op("pe", lambda e, g=g, xs_=xs_, tl=tl: e.matmul(
                        ps[6 + g // 2][:, (g % 2) * 256:(g % 2 + 1) * 256], lhsT=xs_[:, 12 + g, tl],
                        rhs=Hbf[:, 4 * g:4 * g + 4, :], start=True, stop=True), reads=[xk, "Hbf"], writes=[("ps", 6 + g // 2)])
                for hb in range(2):
                    P.op("dve", lambda e, hb=hb, c=c: e.tensor_tensor(
                        out=yoff[:, 8 * hb:8 * hb + 8, :], in0=ps[6 + hb][:, :].rearrange("p (a b) -> p a b", b=64),
                        in1=est_tm[:, c, 8 * hb:8 * hb + 8].unsqueeze(2).to_broadcast([128, 8, 64]), op=ALU.mult),
                        reads=[("ps", 6 + hb), "est_tm"], writes=[("yoff", hb)])

            for g in range(4):
                k = g % 2
                if g == 2:
                    emit_yoff()
                for hh in range(4):
                    h = 4 * g + hh
                    P.op("pe", lambda e, hh=hh, h=h, blk=blk: e.matmul(
                        ps[3][:, hh * 128:(hh + 1) * 128], lhsT=oh16[:, h, :], rhs=acf[:, blk], start=True, stop=False),
                        reads=["oh16", "acf"], writes=[("ps", 3)])
                    P.op("pe", lambda e, hh=hh: e.matmul(
                        ps[3][:, hh * 128:(hh + 1) * 128], lhsT=cst["identb"][:, :], rhs=negm[:, :], start=False, stop=True),
                        reads=["identb", "negm"], writes=[("ps", 3)])
                for hh in range(4):
                    h = 4 * g + hh
                    P.op("act", lambda e, hh=hh, h=h, k=k, c=c: e.activation(
                        out=dec[k][:, hh, :], in_=ps[3][:, hh * 128:(hh + 1) * 128], func=AF.Exp,
                        bias=bias_tm[:, c, h:h + 1]), reads=[("ps", 3), "bias_tm"], writes=[("dec", k)])
                P.op("dve", lambda e, g=g, k=k: e.tensor_tensor(
                    out=PTs[k][:, :, :], in0=dec[k][:, :, :],
                    in1=ps[2][:, g * 128:(g + 1) * 128].unsqueeze(1).to_broadcast([128, 4, 128]), op=ALU.mult),
                    reads=[("dec", k), ("ps", 2)], writes=[("PTs", k)])
                for hh in range(4):
                    h = 4 * g + hh
                    yb = 4 + h // 8
                    yc = (h % 8) * 64
                    P.op("pe", lambda e, hh=hh, h=h, k=k, yb=yb, yc=yc: e.matmul(
                        ps[yb][:, yc:yc + 64], lhsT=PTs[k][:, hh, :], rhs=xtm[:, h, :], start=True, stop=False),
                        reads=[("PTs", k), "xtm"], writes=[("ps", yb)])
                    P.op("pe", lambda e, h=h, yb=yb, yc=yc: e.matmul(
                        ps[yb][:, yc:yc + 64], lhsT=Did[:, h, :], rhs=xtm[:, h, :], start=False, stop=True),
                        reads=["Did", "xtm"], writes=[("ps", yb)])
            while pending_tail:
                pending_tail.pop(0)()
            for hb in range(2):
                P.op("dve", lambda e, hb=hb: e.tensor_tensor(
                    out=ysb[:, hb * 512:(hb + 1) * 512], in0=ps[4 + hb][:, :],
                    in1=yoff[:, 8 * hb:8 * hb + 8, :].rearrange("p a b -> p (a b)"), op=ALU.add),
                    reads=[("ps", 4 + hb), ("yoff", hb)], writes=[("ysb", hb)])
            for g in range(4):
                P.op("pe", lambda e, g=g: e.matmul(
                    ps[6 + g // 2][:, (g % 2) * 256:(g % 2 + 1) * 256], lhsT=Btm[:, g, :],
                    rhs=xw[:, 4 * g:4 * g + 4, :], start=True, stop=True), reads=["Btm", "xw"], writes=[("ps", 6 + g // 2)])
            P.op("dve", lambda e, c=c: e.tensor_tensor(
                out=Hst[:, :, :], in0=Hst[:, :, :], in1=cdb[:, c, :].unsqueeze(2).to_broadcast([128, 16, 64]), op=ALU.mult),
                reads=["H", "cdb"], writes=["H"])
            for hb in range(2):
                P.op("dve", lambda e, hb=hb: e.tensor_tensor(
                    out=Hst[:, 8 * hb:8 * hb + 8, :], in0=Hst[:, 8 * hb:8 * hb + 8, :],
                    in1=ps[6 + hb][:, :].rearrange("p (a b) -> p a b", b=64), op=ALU.add),
                    reads=["H", ("ps", 6 + hb)], writes=["H"])
            P.op("act", lambda e: e.activation(out=Hbf.rearrange("p a b -> p (a b)"),
                                               in_=Hst.rearrange("p a b -> p (a b)"), func=AF.Copy),
                 reads=["H"], writes=["Hbf"])
            P.op("act", lambda e, b=b: e.activation(out=zch[b][:, :], in_=zch[b][:, :], func=AF.Silu),
                 reads=[("zch", b)], writes=[("zch", b)])
            P.op("dve", lambda e, b=b: e.tensor_tensor(out=ysb[:, :], in0=ysb[:, :], in1=zch[b][:, :], op=ALU.mult),
                 reads=[("ysb", 0), ("ysb", 1), ("zch", b)], writes=[("ysb", 0), ("ysb", 1)])
            for g in range(4):
                P.op("act", lambda e, g=g: e.activation(out=junk[:, :], in_=ysb[:, g * 256:(g + 1) * 256], func=AF.Square,
                                                        accum_out=ssg[:, g:g + 1]),
                     reads=[("ysb", 0), ("ysb", 1)], writes=["junk", ("ssg", g)])
            P.op("act", lambda e: e.activation(out=ssg[:, :], in_=ssg[:, :], func=AF.Sqrt, scale=1.0 / 256,
                                               bias=cst["eps"][:, 0:1]),
                 reads=[("ssg", g) for g in range(4)] + ["eps"], writes=[("ssg", g) for g in range(4)])
            P.op("dve", lambda e: e.reciprocal(out=ssg[:, :], in_=ssg[:, :]), reads=[("ssg", g) for g in range(4)],
                 writes=[("ssg", g) for g in range(4)])
            for g in range(4):
                P.op("dve", lambda e, g=g: e.scalar_tensor_tensor(
                    out=cout[:, g * 256:(g + 1) * 256], in0=ysb[:, g * 256:(g + 1) * 256], scalar=ssg[:, g:g + 1],
                    in1=nwb[:, g * 256:(g + 1) * 256], op0=ALU.mult, op1=ALU.mult),
                    reads=[("ysb", 0), ("ysb", 1), ("ssg", g), "nwb"], writes=["cout"])
            def emit_tail(c=c, b=b, blk=blk):
                for m in range(8):
                    P.op("pe", lambda e, m=m: e.transpose(out=ps0b[:, m * 128:(m + 1) * 128], in_=cout[:, m * 128:(m + 1) * 128],
                                                          identity=cst["identb"][:, :]),
                         reads=["cout", "identb"], writes=[("ps", 0)])
                P.op("act", lambda e, b=b: e.activation(out=coutT[b].rearrange("p a b -> p (a b)"), in_=ps0b[:, :], func=AF.Copy),
                     reads=[("ps", 0)], writes=[("coutT", b)])
                P.dma("sp", mixT[0:1024, blk].rearrange("(m p) t -> p m t", p=128), coutT[b][:, :, :],
                      reads=[("coutT", b)], writes=[("mixC", c)])
            pending_tail.append(emit_tail)
        for f_ in pending_tail:
            f_()


def stick_breaking_phase(C, cst, o_qT, o_kT, o_v, mixT, upto):
    P = C.P
    ps = C.psum
    with C.scope():
        kT = C.sb("t_kT", [128, 4, S], BF16)
        qT = C.sb("t_qT", [128, 4, S], BF16)
        vtm = C.sb("t_v", [128, 32, 512], BF16)
        tril = C.sb("t_tril", [128, 128], BF16)
        P.op("pool", lambda e: e.tensor_tensor(out=tril[:, :], in0=cst["ones"][:, :], in1=cst["triu"][:, :], op=ALU.subtract),
             reads=["ones", "triu"], writes=["tril"])
        P.dma("sp", kT[:, :, :], o_kT.rearrange("(m p) t -> p m t", p=128), writes=["kT"])
        P.dma("sp", qT[:, :, :], o_qT.rearrange("(m p) t -> p m t", p=128), writes=["qT"])
        ovv = o_v.rearrange("(b p) f -> p b f", p=128)
        for i in range(8):
            P.dma("sp", vtm[:, 4 * i:4 * i + 4, :], ovv[:, 4 * i:4 * i + 4, :], reads=["vtm"] if i else [], writes=["vtm"])
        NS = 2
        ee = [[C.sb("t_e%d_%d" % (s_, i), [128, 512], F32) for i in range(2)] for s_ in range(NS)]
        sp = [[C.sb("t_sp%d_%d" % (s_, i), [128, 512], F32) for i in range(2)] for s_ in range(NS)]
        l1m = [[C.sb("t_l1m%d_%d" % (s_, i), [128, 512], BF16) for i in range(2)] for s_ in range(NS)]
        E1 = [[C.sb("t_E1%d_%d" % (s_, i), [128, 512], F32) for i in range(2)] for s_ in range(NS)]
        PTb = [[C.sb("t_PT%d_%d" % (s_, i), [128, 512], BF16) for i in range(2)] for s_ in range(NS)]
        dout = C.sb("t_dout", [128, 4, 512], BF16)
        doutT = [C.sb("t_doutT%d" % i, [128, 4, 512], BF16) for i in range(2)]
        ps7b = ps[6][:, :].bitcast(BF16)
        nQ = {"T1": 1, "T2": 2}.get(upto, 8)
        for Q in range(nQ):
            for h0 in range(0, 8, NS):
                kbs = list(range(4 * Q + 3, -1, -1))
                n = len(kbs)

                def geo(i):
                    kb = kbs[i]
                    tb0 = max(0, kb - 4 * Q)
                    c0 = tb0 * 128
                    return kb, tb0, c0, slice(c0, 512), kb >= 4 * Q

                hs = [(h0 + s_, (h0 + s_) // 2, 64 * ((h0 + s_) % 2), 2 * s_, 4 + s_, 6 + s_) for s_ in range(NS)]
                def emit_z(i):
                    kb, tb0, c0, cs_, diag = geo(i)
                    kblk = slice(kb * 128, (kb + 1) * 128)
                    qsl = slice(Q * 512 + c0, (Q + 1) * 512)
                    for s_, (h, m, pb0, zb0, xb, ob) in enumerate(hs):
                        zb = zb0 + i % 2
                        P.op("pe", lambda e, m=m, pb0=pb0, kblk=kblk, qsl=qsl, cs_=cs_, zb=zb: e.matmul(
                            ps[zb][:, cs_], lhsT=kT[pb0:pb0 + 64, m, kblk], rhs=qT[pb0:pb0 + 64, m, qsl],
                            start=True, stop=True), reads=["kT", "qT"], writes=[("ps", zb)])

                for i in range(n + 1):
                    if i >= 1:
                        kb, tb0, c0, cs_, diag = geo(i - 1)
                        k = (i - 1) % 2
                        for s_, (h, m, pb0, zb, xb, ob) in enumerate(hs):
                            P.op("dve", lambda e, s_=s_, k=k, cs_=cs_, xb=xb: e.tensor_tensor(
                                out=E1[s_][k][:, cs_], in0=ps[xb][:, cs_], in1=sp[s_][k][:, cs_], op=ALU.subtract),
                                reads=[("ps", xb), ("sp", s_, k)], writes=[("E1", s_, k)])
                    if i < n:
                        kb, tb0, c0, cs_, diag = geo(i)
                        k = i % 2
                        if i == 0:
                            emit_z(0)
                        for s_, (h, m, pb0, zb0, xb, ob) in enumerate(hs):
                            zb = zb0 + k
                            P.op("act", lambda e, s_=s_, k=k, cs_=cs_, zb=zb: e.activation(
                                out=ee[s_][k][:, cs_], in_=ps[zb][:, cs_], func=AF.Exp, scale=-1.0),
                                reads=[("ps", zb)], writes=[("ee", s_, k)])
                        for s_, (h, m, pb0, zb, xb, ob) in enumerate(hs):
                            P.op("act", lambda e, s_=s_, k=k, cs_=cs_: e.activation(
                                out=sp[s_][k][:, cs_], in_=ee[s_][k][:, cs_], func=AF.Ln, bias=cst["one"][:, 0:1]),
                                reads=[("ee", s_, k), "one"], writes=[("sp", s_, k)])
                        for s_, (h, m, pb0, zb0, xb, ob) in enumerate(hs):
                            zb = zb0 + k
                            P.op("dve", lambda e, s_=s_, k=k, cs_=cs_, zb=zb: e.scalar_tensor_tensor(
                                out=l1m[s_][k][:, cs_], in0=ps[zb][:, cs_], scalar=-1.0, in1=sp[s_][k][:, cs_],
                                op0=ALU.mult, op1=ALU.subtract), reads=[("ps", zb), ("sp", s_, k)], writes=[("l1m", s_, k)])
                            if diag:
                                dsl = slice(c0, c0 + 128)
                                P.op("dve", lambda e, s_=s_, k=k, dsl=dsl: e.tensor_tensor(
                                    out=l1m[s_][k][:, dsl], in0=l1m[s_][k][:, dsl], in1=cst["maskTs"][:, :], op=ALU.mult),
                                    reads=[("l1m", s_, k), "maskTs"], writes=[("l1m", s_, k)])
                    if i + 1 < n:
                        emit_z(i + 1)
                    if i >= 1:
                        kb, tb0, c0, cs_, diag = geo(i - 1)
                        k = (i - 1) % 2
                        for s_, (h, m, pb0, zb, xb, ob) in enumerate(hs):
                            P.op("act", lambda e, s_=s_, k=k, cs_=cs_: e.activation(
                                out=PTb[s_][k][:, cs_], in_=E1[s_][k][:, cs_], func=AF.Exp),
                                reads=[("E1", s_, k)], writes=[("PTb", s_, k)])
                            if diag:
                                dsl = slice(c0, c0 + 128)
                                P.op("dve", lambda e, s_=s_, k=k, dsl=dsl: e.tensor_tensor(
                                    out=PTb[s_][k][:, dsl], in0=PTb[s_][k][:, dsl], in1=cst["maskTs"][:, :], op=ALU.mult),
                                    reads=[("PTb", s_, k), "maskTs"], writes=[("PTb", s_, k)])
                        for s_, (h, m, pb0, zb, xb, ob) in enumerate(hs):
                            for tb in range(tb0, 4):
                                P.op("pe", lambda e, s_=s_, k=k, tb=tb, kb=kb, h=h, ob=ob, st=(i == 1 and tb == tb0): e.matmul(
                                    ps[ob][:, tb * 64:(tb + 1) * 64], lhsT=PTb[s_][k][:, tb * 128:(tb + 1) * 128],
                                    rhs=vtm[:, kb, h * 64:(h + 1) * 64], start=st, stop=(kb == 0), skip_group_check=True),
                                    reads=[("PTb", s_, k), "vtm"], writes=[("ps", ob)])
                    if i < n:
                        kb, tb0, c0, cs_, diag = geo(i)
                        k = i % 2
                        for s_, (h, m, pb0, zb, xb, ob) in enumerate(hs):
                            if i >= 1:
                                pcs = geo(i - 1)[3]
                                pk = (i - 1) % 2
                                P.op("pe", lambda e, s_=s_, pk=pk, pcs=pcs, xb=xb: e.matmul(
                                    ps[xb][:, pcs], lhsT=tril[:, :], rhs=l1m[s_][pk][:, pcs], start=False, stop=False,
                                    skip_group_check=True), reads=["tril", ("l1m", s_, pk)], writes=[("ps", xb)])
                            P.op("pe", lambda e, s_=s_, k=k, cs_=cs_, xb=xb, st=(i == 0): e.matmul(
                                ps[xb][:, cs_], lhsT=cst["triu"][:, :], rhs=l1m[s_][k][:, cs_], start=st, stop=False,
                                skip_group_check=True), reads=["triu", ("l1m", s_, k)], writes=[("ps", xb)])
                for s_ in range(NS):
                    h = h0 + s_
                    ob = 6 + s_
                    P.op("dve", lambda e, h=h, ob=ob: e.tensor_copy(
                        out=dout[:, :, h * 64:(h + 1) * 64], in_=ps[ob][:, 0:256].rearrange("p (a b) -> p a b", b=64)),
                        reads=[("ps", ob)], writes=["dout"])
            qb = Q % 2
            for mm in range(4):
                for tb in range(4):
                    P.op("pe", lambda e, mm=mm, tb=tb: e.transpose(
                        out=ps7b[:, tb * 128:(tb + 1) * 128], in_=dout[:, tb, mm * 128:(mm + 1) * 128],
                        identity=cst["identb"][:, :]), reads=["dout", "identb"], writes=[("ps", 6)])
                P.op("dve", lambda e, mm=mm, qb=qb: e.tensor_copy(out=doutT[qb][:, mm, :], in_=ps7b[:, 0:512]),
                     reads=[("ps", 6)], writes=[("doutT", qb)])
            P.dma("sp", mixT[1024:1536, Q * 512:(Q + 1) * 512].rearrange("(m p) t -> p m t", p=128), doutT[qb][:, :, :],
                  reads=[("doutT", qb)], writes=[("mixD", Q)])


def build_program():
    nc = bass.Bass("TRN2", target_bir_lowering=False)
    xT = nc.dram_tensor("xT", [D, S], F32, kind="ExternalInput").ap()
    yT = nc.dram_tensor("yT", [D, S], F32, kind="ExternalOutput").ap()
    Wd = {k: nc.dram_tensor(k, sh, F32, kind="ExternalInput").ap() for k, sh in PARAM_SHAPES.items()}
    cin = {k: nc.dram_tensor("c_" + k, sh, F32, kind="ExternalInput").ap() for k, sh in CONST_SHAPES.items()}
    with ExitStack() as stack:
        C = Ctx(nc, stack)
        cst = alloc_consts(C, cin)
        res = [C.dram("res%d" % i, [D, S], F32) for i in range(5)]

        def ffn(pre, src, dst):
            with C.scope():
                bufs = alloc_ffn_bufs(C, cst)
                ffn_phase(C, pre, src, dst, Wd[pre + "_norm"], Wd[pre + "_wg"], Wd[pre + "_wu"], Wd[pre + "_wd"], bufs)

        ffn("l0_ffn1", xT, res[0])
        with C.scope():
            even_mixer_phase(C, cst, cin, res[0], res[1], {k[3:]: v for k, v in Wd.items() if k.startswith("l0_")})
        ffn("l0_ffn2", res[1], res[2])
        ffn("l1_ffn1", res[2], res[3])
        with C.scope():
            odd_mixer_phase(C, cst, cin, res[3], res[4], {k[3:]: v for k, v in Wd.items() if k.startswith("l1_")})
        ffn("l1_ffn2", res[4], yT)
        C.P.emit()
    return nc


_NC_CACHE = {}


def kernel(**inputs):
    x = np.asarray(inputs["x"], np.float32)
    hp = host_params(inputs)
    hc = host_consts()
    shared = {k: hp[k] for k in PARAM_SHAPES}
    for k in CONST_SHAPES:
        shared["c_" + k] = hc[k]
    in_maps = []
    for b in range(NCORES):
        m = dict(shared)
        m["xT"] = np.ascontiguousarray(x[b].T)
        in_maps.append(m)
    if "nc" not in _NC_CACHE:
        _NC_CACHE["nc"] = build_program()
    res = run_bass_kernel_spmd(_NC_CACHE["nc"], in_maps, core_ids=list(range(NCORES)))
    out = np.stack([np.asarray(r["yT"], np.float32).T for r in res.results], axis=0)
    return np.ascontiguousarray(out)
```

```python
from contextlib import ExitStack

import numpy as np
import concourse.bass as bass
import concourse.mybir as mybir
from concourse.bass_utils import run_bass_kernel_spmd

F32 = mybir.dt.float32
BF16 = mybir.dt.bfloat16
ALU = mybir.AluOpType
AF = mybir.ActivationFunctionType
AX = mybir.AxisListType

S = 4096
D = 1024
DFF = 2816
NCORES = 8
EPS = 1e-6
CAST_DMA = True


class _Op:
    __slots__ = ("eng", "fn", "deps", "sig", "sem", "cnt", "is_dma", "pos")


def _bank_of(k):
    if isinstance(k, tuple):
        if k[0] == "ps":
            return k[1]
        if k[0] == "pso":
            return 3 + k[1] // 2
        if k[0] == "psd":
            return 5 + k[1] // 2
        if k[0] == "ps0":
            return 0
    return None


class Prog:
    ENGS = ("pe", "act", "dve", "pool", "sp")

    def __init__(self, nc, stack, n_dma_sems=8):
        self.nc = nc
        self.stack = stack
        self.streams = {e: [] for e in self.ENGS}
        self.lastw = {}
        self.readers = {}
        self.eng_sem = {e: stack.enter_context(nc.semaphore("s_" + e)) for e in ("pe", "act", "dve", "pool")}
        self.dma_pool = {}
        self.n_dma_sems = n_dma_sems
        self.dma_rr = {}
        self.nops = 0
        self.pending = {}
        self.bank_last = {}
        self.multi = {}

    def barrier(self):
        lasts = []
        for e in self.ENGS:
            for o in reversed(self.streams[e]):
                if not o.is_dma:
                    o.sig = True
                    lasts.append(o)
                    break
        for q, slots in self.dma_pool.items():
            for sl in slots:
                if sl[2] is not None:
                    lasts.append(sl[2])
        for e in self.ENGS:
            self.pending[e] = list(lasts) + self.pending.get(e, [])
        self.lastw.clear()
        self.readers.clear()
        self.multi.clear()

    def _dma_sem(self, q):
        if q not in self.dma_pool:
            self.dma_pool[q] = [[self.stack.enter_context(self.nc.semaphore("d_%s%d" % (q, i))), 0, None]
                                for i in range(self.n_dma_sems)]
            self.dma_rr[q] = 0
        i = self.dma_rr[q]
        self.dma_rr[q] = (i + 1) % self.n_dma_sems
        return self.dma_pool[q][i]

    def _deps(self, op, reads, writes):
        deps = set()
        for k in reads:
            w = self.lastw.get(k)
            if w is not None:
                deps.add(w)
            if k in self.multi:
                deps.update(self.multi[k])
        for k in writes:
            w = self.lastw.get(k)
            if w is not None:
                deps.add(w)
            for r in self.readers.get(k, ()):
                deps.add(r)
        deps.discard(op)
        for k in reads:
            self.readers.setdefault(k, []).append(op)
        for k in writes:
            self.lastw[k] = op
            self.readers[k] = []
        return deps

    def op(self, eng, fn, reads=(), writes=()):
        o = _Op()
        o.eng = eng
        o.fn = fn
        o.is_dma = False
        o.sig = False
        o.sem = None
        o.cnt = 0
        o.pos = self.nops
        self.nops += 1
        deps = self._deps(o, reads, writes)
        deps.update(self.pending.pop(eng, ()))
        for k in list(reads) + list(writes):
            bk = _bank_of(k)
            if bk is not None:
                prev = self.bank_last.get(bk)
                if prev is not None and prev is not o and prev.eng != eng:
                    deps.add(prev)
                self.bank_last[bk] = o
        keep = []
        for d in deps:
            if (not d.is_dma) and d.eng == eng and eng == "pe":
                continue
            keep.append(d)
            if not d.is_dma:
                d.sig = True
        o.deps = keep
        self.streams[eng].append(o)
        return o

    def dma(self, q, out, in_, reads=(), writes=(), multi=(), **kw):
        o = _Op()
        o.eng = q
        o.fn = lambda e: e.dma_start(out=out, in_=in_, **kw)
        o.is_dma = True
        o.sig = True
        o.pos = self.nops
        self.nops += 1
        slot = self._dma_sem(q)
        deps = self._deps(o, reads, writes)
        for k in multi:
            for r in self.readers.get(k, ()):
                deps.add(r)
            self.multi.setdefault(k, []).append(o)
        deps.update(self.pending.pop(q, ()))
        if slot[2] is not None:
            deps.add(slot[2])
        slot[1] += 16
        slot[2] = o
        o.sem = slot[0]
        o.cnt = slot[1]
        for d in deps:
            if not d.is_dma:
                d.sig = True
        o.deps = list(deps)
        self.streams[q].append(o)
        return o

    def emit(self):
        nc = self.nc
        for e in ("pe", "act", "dve", "pool"):
            c = 0
            for o in self.streams[e]:
                if o.is_dma:
                    continue
                o.sem = self.eng_sem[e]
                if o.sig:
                    c += 1
                    o.cnt = c
        streams = self.streams

        def run(e, h):
            known = {}
            for o in streams[e]:
                need = {}
                for d in o.deps:
                    key = id(d.sem)
                    if key not in need or need[key][1] < d.cnt:
                        need[key] = (d.sem, d.cnt)
                for key, (sem, v) in need.items():
                    if known.get(key, 0) < v:
                        h.wait_ge(sem, v)
                        known[key] = v
                ins = o.fn(h)
                if o.is_dma:
                    ins.then_inc(o.sem, 16)
                elif o.sig:
                    ins.then_inc(o.sem, 1)
            if e in self.dma_pool:
                for sem, cnt, _ in self.dma_pool[e]:
                    if cnt > 0:
                        h.wait_ge(sem, cnt)

        with nc.Block() as block:
            @block.tensor
            def _(h):
                run("pe", h)

            @block.scalar
            def _(h):
                run("act", h)

            @block.vector
            def _(h):
                run("dve", h)

            @block.gpsimd
            def _(h):
                run("pool", h)

            @block.sync
            def _(h):
                run("sp", h)


ARENA_WORDS = 53000


class Ctx:
    def __init__(self, nc, stack):
        self.nc = nc
        self.stack = stack
        self.P = Prog(nc, stack)
        self.psum_all = stack.enter_context(nc.psum_tensor("psall", [128, 4096], F32))
        self.psum = [self.psum_all[:, i * 512:(i + 1) * 512] for i in range(8)]
        self.arena = stack.enter_context(nc.sbuf_tensor("arena", [128, ARENA_WORDS], F32))
        self.top = 0
        self.scratch = {}

    def sb(self, name, shape, dt):
        esz = 4 if dt == F32 else 2
        n = 1
        for d_ in shape[1:]:
            n *= int(d_)
        words = (n * esz + 3) // 4
        words = (words + 15) // 16 * 16
        off = self.top
        self.top += words
        assert self.top <= ARENA_WORDS, "SBUF arena overflow at %s: %d words" % (name, self.top)
        ap = self.arena[0:shape[0], off:off + words]
        if dt != F32:
            ap = ap.bitcast(dt)
        ap = ap[:, 0:n]
        if len(shape) == 3:
            ap = ap.rearrange("p (a b) -> p a b", b=int(shape[2]))
        elif len(shape) == 4:
            ap = ap.rearrange("p (a b c) -> p a b c", b=int(shape[2]), c=int(shape[3]))
        return ap

    def scope(self):
        return _Scope(self)

    def dram(self, name, shape, dt):
        if name not in self.scratch:
            kind = "ExternalOutput" if name in getattr(self, "debug_out", ()) else "Internal"
            self.scratch[name] = self.nc.dram_tensor(name, list(shape), dt, kind=kind).ap()
        return self.scratch[name]


class _Scope:
    def __init__(self, C):
        self.C = C

    def __enter__(self):
        self.mark = self.C.top
        return self

    def __exit__(self, *a):
        self.C.P.barrier()
        self.C.top = self.mark
        return False


def ffn_phase(C, tag, xT_in, xT_out, nw, wg, wu, wd, bufs, T=512):
    P = C.P
    NT = S // T
    NF = DFF // 128
    wg_sb, wu_sb, wd_sb = bufs["wg"], bufs["wu"], bufs["wd"]
    nw_sb = bufs["nw"]
    ones = bufs["ones"]
    xin = xT_in.rearrange("(c p) t -> p c t", p=128)
    xout = xT_out.rearrange("(c p) t -> p c t", p=128)

    P.dma("sp", nw_sb[:, :], nw, writes=["nw"])
    wgv = wg.rearrange("(c p) f -> p c f", p=128)
    wuv = wu.rearrange("(c p) f -> p c f", p=128)
    wdv = wd.rearrange("(j p) d -> p j d", p=128)
    si = [0]

    def load_cast(dst, src, key):
        P.dma("pool", dst, src, writes=[key])

    wgk, wuk, wdk = ["wg"], ["wu"], ["wd"]
    if CAST_DMA:
        H = DFF // 2
        wgk, wuk, wdk = [], [], []
        for hh in range(2):
            for c in range(8):
                wgk.append(("wg", c, hh))
                load_cast(wg_sb[:, c, hh * H:(hh + 1) * H], wgv[:, c, hh * H:(hh + 1) * H], wgk[-1])
                wuk.append(("wu", c, hh))
                load_cast(wu_sb[:, c, hh * H:(hh + 1) * H], wuv[:, c, hh * H:(hh + 1) * H], wuk[-1])
        for j in range(0, NF, 2):
            wdk.append(("wd", j))
            load_cast(wd_sb[:, j:j + 2, :], wdv[:, j:j + 2, :], wdk[-1])
    else:
        st_ = Stager(C, tag + "_stg", cols=DFF // 8)
        for c in range(8):
            st_.load(wg_sb[:, c, :], wgv[:, c, :], "wg")
            st_.load(wu_sb[:, c, :], wuv[:, c, :], "wu")
        for j in range(NF):
            st_.load(wd_sb[:, j, :], wdv[:, j, :], "wd")

    xt, hT, aT, rs = bufs["xt"], bufs["hT"], bufs["aT"], bufs["rs"]
    sq2 = bufs["sq2"]
    sg = bufs["sg"]
    ps = C.psum

    def tsl_(it):
        return slice(it * T, (it + 1) * T)

    def load_x(it):
        b = it % 2
        P.dma("sp", xt[b][:, :, :], xin[:, :, tsl_(it)], writes=[("xt", b)])

    def sq_op(it, c):
        b = it % 2
        P.op("act", lambda e: e.activation(out=sq2[:, c % 2, :], in_=xt[b][:, c, :], func=AF.Square),
             reads=[("xt", b)], writes=[("sq2", c % 2)])

    def ones_mm(it, c):
        P.op("pe", lambda e: e.matmul(ps[0][:, :T], lhsT=ones[:, :], rhs=sq2[:, c % 2, :], start=(c == 0), stop=(c == 7)),
             reads=[("sq2", c % 2), "ones"], writes=[("ps", 0)])

    def norm_back(it):
        b = it % 2
        P.op("act", lambda e: e.activation(out=rs[:, :], in_=ps[0][:, :T], func=AF.Sqrt, scale=1.0 / D,
                                           bias=bufs["eps"][:, 0:1]), reads=[("ps", 0), "eps"], writes=["rs"])
        P.op("dve", lambda e: e.reciprocal(out=rs[:, :], in_=rs[:, :]), reads=["rs"], writes=["rs"])
        for c in range(8):
            P.op("dve", lambda e, c=c: e.scalar_tensor_tensor(
                out=hT[:, c, :], in0=xt[b][:, c, :], scalar=nw_sb[:, c:c + 1], in1=rs[:, :],
                op0=ALU.mult, op1=ALU.mult), reads=[("xt", b), "rs", "nw"], writes=["hT"])

    load_x(0)
    for c in range(8):
        sq_op(0, c)
        ones_mm(0, c)
    norm_back(0)
    for it in range(NT):
        b = it % 2
        nxt = it + 1 < NT
        if nxt:
            load_x(it + 1)
        for j in range(NF):
            pg = 1 + (j % 2) * 2
            pu = pg + 1
            fs = slice(j * 128, (j + 1) * 128)
            for c in range(8):
                P.op("pe", lambda e, c=c, fs=fs, pg=pg: e.matmul(ps[pg][:, :T], lhsT=wg_sb[:, c, fs], rhs=hT[:, c, :],
                                                                   start=(c == 0), stop=(c == 7)),
                     reads=([("wg", c_, j // 11) for c_ in range(8)] if CAST_DMA else wgk) + ["hT"], writes=[("ps", pg)])
            for c in range(8):
                P.op("pe", lambda e, c=c, fs=fs, pu=pu: e.matmul(ps[pu][:, :T], lhsT=wu_sb[:, c, fs], rhs=hT[:, c, :],
                                                                   start=(c == 0), stop=(c == 7)),
                     reads=([("wu", c_, j // 11) for c_ in range(8)] if CAST_DMA else wuk) + ["hT"], writes=[("ps", pu)])
            k = j % 2
            P.op("act", lambda e, pg=pg, k=k: e.activation(out=sg[k][:, :], in_=ps[pg][:, :T], func=AF.Silu),
                 reads=[("ps", pg)], writes=[("sg", k)])
            P.op("dve", lambda e, pu=pu, k=k, j=j: e.tensor_tensor(out=aT[:, j, :], in0=ps[pu][:, :T], in1=sg[k][:, :],
                                                                   op=ALU.mult),
                 reads=[("ps", pu), ("sg", k)], writes=[("aT", j)])
        if nxt:
            sq_op(it + 1, 0)
            sq_op(it + 1, 1)
        for i in range(8):
            py = 5 + (i % 2)
            ds = slice(i * 128, (i + 1) * 128)
            for j in range(NF):
                P.op("pe", lambda e, j=j, ds=ds, py=py: e.matmul(ps[py][:, :T], lhsT=wd_sb[:, j, ds], rhs=aT[:, j, :],
                                                                   start=(j == 0), stop=(j == NF - 1)),
                     reads=wdk + [("aT", j)], writes=[("ps", py)])
            P.op("dve", lambda e, i=i, py=py, b=b: e.scalar_tensor_tensor(
                out=xt[b][:, i, :], in0=ps[py][:, :T], scalar=0.5, in1=xt[b][:, i, :],
                op0=ALU.mult, op1=ALU.add),
                reads=[("ps", py), ("xt", b)], writes=[("xt", b)])
            if nxt and i < 4:
                ones_mm(it + 1, 2 * i)
                ones_mm(it + 1, 2 * i + 1)
                if i < 3:
                    sq_op(it + 1, 2 * i + 2)
                    sq_op(it + 1, 2 * i + 3)
                else:
                    norm_back(it + 1)
        P.dma("sp", xout[:, :, tsl_(it)], xt[b][:, :, :], reads=[("xt", b)], writes=[(tag, "out", it)])


def alloc_ffn_bufs(C, cst, T=512):
    b = {}
    b["wg"] = C.sb("wg_sb", [128, 8, DFF], BF16)
    b["wu"] = C.sb("wu_sb", [128, 8, DFF], BF16)
    b["wd"] = C.sb("wd_sb", [128, DFF // 128, D], BF16)
    b["nw"] = C.sb("nw_sb", [128, 8], F32)
    b["xt"] = [C.sb("xt%d" % i, [128, 8, T], F32) for i in range(2)]
    b["hT"] = C.sb("hT", [128, 8, T], BF16)
    b["aT"] = C.sb("aT", [128, DFF // 128, T], BF16)
    b["rs"] = C.sb("rs", [128, T], F32)
    b["sg"] = [C.sb("sg%d" % i, [128, T], F32) for i in range(2)]
    b["sq2"] = C.sb("sq2", [128, 2, T], BF16)
    b["ones"] = cst["ones"]
    b["eps"] = cst["eps"]
    return b


class Stager:
    def __init__(self, C, name, cols=704, n=2):
        self.C = C

    def load(self, dst, src, key, np_=128):
        P = self.C.P
        n = src.shape[-1]
        step = 1536
        for c0 in range(0, n, step):
            c1 = min(n, c0 + step)
            P.dma("pool", dst[:, c0:c1], src[:, c0:c1], multi=[key])


def emit_rmsnorm(C, xt, xkey, nw_sb, sq, sqkeys, hT, rs, cst, T, psb=0, hkey="hT"):
    P = C.P
    ps = C.psum
    P.op("act", lambda e: e.activation(out=sq[:, 0:8, :], in_=xt[:, :, :], func=AF.Square),
         reads=[xkey], writes=list(sqkeys))
    for c in range(8):
        P.op("pe", lambda e, c=c: e.matmul(ps[psb][:, :T], lhsT=cst["ones"][:, :], rhs=sq[:, c, :],
                                             start=(c == 0), stop=(c == 7)),
             reads=[sqkeys[c], "ones"], writes=[("ps", psb)])
    P.op("act", lambda e: e.activation(out=rs[:, :], in_=ps[psb][:, :T], func=AF.Sqrt,
                                       scale=1.0 / D, bias=cst["eps"][:, 0:1]),
         reads=[("ps", psb), "eps"], writes=["rs"])
    P.op("dve", lambda e: e.reciprocal(out=rs[:, :], in_=rs[:, :]), reads=["rs"], writes=["rs"])
    for c in range(8):
        P.op("dve", lambda e, c=c: e.scalar_tensor_tensor(
            out=hT[:, c, :], in0=xt[:, c, :], scalar=nw_sb[:, c:c + 1], in1=rs[:, :],
            op0=ALU.mult, op1=ALU.mult),
            reads=[xkey, "rs", "nw"], writes=[hkey])


def alloc_consts(C, cin):
    P = C.P
    cst = {}
    cst["ones"] = C.sb("ones", [128, 128], BF16)
    cst["eps"] = C.sb("epsc", [128, 1], F32)
    cst["one"] = C.sb("onec", [128, 1], F32)
    cst["identb"] = C.sb("identb", [128, 128], BF16)
    cst["identf"] = C.sb("identf", [128, 128], F32)
    cst["maskT"] = C.sb("maskT", [128, 128], F32)
    cst["maskTs"] = C.sb("maskTs", [128, 128], F32)
    cst["triu"] = C.sb("triu", [128, 128], BF16)
    P.op("pool", lambda e: e.memset(cst["ones"][:, :], 1.0), writes=["ones"])
    P.op("pool", lambda e: e.memset(cst["eps"][:, :], EPS), writes=["eps"])
    P.op("pool", lambda e: e.memset(cst["one"][:, :], 1.0), writes=["one"])
    P.dma("sp", cst["identf"][:, :], cin["identf"], writes=["identf"])
    P.dma("sp", cst["maskT"][:, :], cin["maskT"], writes=["maskT"])
    P.dma("sp", cst["maskTs"][:, :], cin["maskTs"], writes=["maskTs"])
    P.op("pool", lambda e: e.tensor_copy(out=cst["identb"][:, :], in_=cst["identf"][:, :]),
         reads=["identf"], writes=["identb"])
    cst["tmpf"] = C.sb("tmpf", [128, 128], F32)
    P.dma("sp", cst["tmpf"][:, :], cin["triu"], writes=["tmpf"])
    P.op("pool", lambda e: e.tensor_copy(out=cst["triu"][:, :], in_=cst["tmpf"][:, :]),
         reads=["tmpf"], writes=["triu"])
    return cst


def host_consts():
    i = np.arange(128)
    c = {}
    c["identf"] = np.eye(128, dtype=np.float32)
    c["maskT"] = (i[:, None] <= i[None, :]).astype(np.float32)
    c["maskTs"] = (i[:, None] < i[None, :]).astype(np.float32)
    c["triu"] = (i[:, None] > i[None, :]).astype(np.float32)
    c["invc"] = np.broadcast_to((1.0 / (np.arange(16) + 1.0)).astype(np.float32)[None, :], (128, 16)).copy()
    oh = np.zeros((4, 4, 128), np.float32)
    for h in range(4):
        oh[h, h, :] = 1.0
    c["onehot4"] = oh.reshape(4, 512)
    oh = np.zeros((16, 16, 128), np.float32)
    for h in range(16):
        oh[h, h, :] = 1.0
    c["onehot16"] = oh.reshape(16, 2048)
    c["blk1"] = ((i[:, None] // 64) == (i[None, :] // 64)).astype(np.float32)
    c["negm"] = np.where(i[:, None] > i[None, :], -30000.0, 0.0).astype(np.float32)
    return c


def conv_silu_pe(C, cst, tagp, src, nch, cw, cb, sink):
    P = C.P
    ps = C.psum
    xrow = [C.sb("%s_xrow%d" % (tagp, i), [128, 3 + S], BF16) for i in range(2)]
    dg = [C.sb("%s_dg%d" % (tagp, i), [128, 4, 128], BF16) for i in range(2)]
    for k in range(2):
        P.op("pool", lambda e, k=k: e.memset(xrow[k][:, 0:3], 0.0), writes=[(tagp, "xrow", k)])
    n = 0
    for m in range(nch):
        k = m % 2
        for h_ in range(4):
            P.dma("pool", xrow[k][:, 3 + h_ * 1024:3 + (h_ + 1) * 1024], src[m * 128:(m + 1) * 128, h_ * 1024:(h_ + 1) * 1024],
                  multi=[(tagp, "xrow", k)])
        for j in range(4):
            P.op("dve", lambda e, k=k, m=m, j=j: e.tensor_scalar(out=dg[k][:, j, :], in0=cst["identf"][:, :],
                                                                 scalar1=cw[:, m, j:j + 1], scalar2=None, op0=ALU.mult),
                 reads=["identf", "cw"], writes=[(tagp, "dg", k)])
        for it in range(S // 512):
            pb = 1 + n % 4
            n += 1
            for j in range(4):
                P.op("pe", lambda e, k=k, j=j, it=it, pb=pb: e.matmul(
                    ps[pb][:, :512], lhsT=dg[k][:, j, :], rhs=xrow[k][:, j + it * 512:j + it * 512 + 512],
                    start=(j == 0), stop=(j == 3)), reads=[(tagp, "dg", k), (tagp, "xrow", k)], writes=[("ps", pb)])
            sink(m, it, ps[pb][:, :512], cb[:, m:m + 1], ("ps", pb))


def evac(P, eng, out, in_, reads, writes):
    if eng == "act":
        return P.op("act", lambda e: e.activation(out=out, in_=in_, func=AF.Copy), reads=reads, writes=writes)
    return P.op(eng, lambda e: e.tensor_copy(out=out, in_=in_), reads=reads, writes=writes)


def proj_norm_tiles(C, cst, xT_in, nw_dram, T, body):
    P = C.P
    xin = xT_in.rearrange("(c p) t -> p c t", p=128)
    nw_sb = C.sb("pn_nw", [128, 8], F32)
    xt = [C.sb("pn_xt%d" % i, [128, 8, T], F32) for i in range(2)]
    sq = C.sb("pn_sq", [128, 8, T], BF16)
    hT = [C.sb("pn_hT%d" % i, [128, 8, T], BF16) for i in range(2)]
    rs = C.sb("pn_rs", [128, T], F32)
    P.dma("sp", nw_sb[:, :], nw_dram, writes=["nw"])
    NT = S // T

    def norm(it):
        b = it % 2
        P.dma("sp", xt[b][:, :, :], xin[:, :, it * T:(it + 1) * T], writes=[("pn_xt", b)])
        emit_rmsnorm(C, xt[b], ("pn_xt", b), nw_sb, sq, [("pn_sq", c) for c in range(8)], hT[b], rs, cst, T,
                     hkey=("hT", b))

    norm(0)
    for it in range(NT):
        if it + 1 < NT:
            norm(it + 1)
        body(it, hT[it % 2], ("hT", it % 2))


def even_mixer_phase(C, cst, cin, xT_in, xT_out, W, upto="E"):
    P = C.P
    ps = C.psum
    T = 512
    uT = C.dram("e_uT", [512, S], F32)
    qkT = C.dram("e_qkT", [1024, S], F32)
    v_tm = C.dram("e_v", [S, 512], BF16)
    o_tm = C.dram("e_o", [S, 512], F32)
    gT = C.dram("e_g", [8, S], F32)
    mixT = C.dram("e_mix", [1024, S], BF16)

    ws_tm = C.sb("e_ws", [128, 32, 4], F32)
    thr_tm = C.sb("e_thr", [128, 32, 4], F32)
    dcol = C.sb("e_dcol", [128, 4, 32], F32)

    with C.scope():
        w_sb = C.sb("e_win", [128, 8, 2568], BF16)
        stg = Stager(C, "e_stg")
        win = W["w_in"].rearrange("(c p) f -> p c f", p=128)
        for c in range(8):
            stg.load(w_sb[:, c, :], win[:, c, :], "win")
        fm = [C.sb("e_fm%d" % i, [128, T], F32) for i in range(6)]
        vst = [C.sb("e_vst%d" % i, [128, 512], BF16) for i in range(4)]
        ost = [C.sb("e_ost%d" % i, [128, 512], F32) for i in range(4)]
        gst = [C.sb("e_gst%d" % i, [4, T], F32) for i in range(2)]
        cnt = {"fm": 0, "tm": 0, "g": 0}

        def body(it, hT, hk):
            tsl = slice(it * T, (it + 1) * T)
            for m in range(12):
                pb = 1 + m % 4
                for c in range(8):
                    P.op("pe", lambda e, c=c, m=m, pb=pb: e.matmul(
                        ps[pb][:, :T], lhsT=w_sb[:, c, m * 128:(m + 1) * 128], rhs=hT[:, c, :],
                        start=(c == 0), stop=(c == 7)), reads=["win", hk], writes=[("ps", pb)])
                k = cnt["fm"] % 6
                cnt["fm"] += 1
                evac(P, "act" if m % 2 == 0 else "dve", fm[k][:, :], ps[pb][:, :T], [("ps", pb)], [("fm", k)])
                dst = uT[m * 128:(m + 1) * 128, tsl] if m < 4 else qkT[(m - 4) * 128:(m - 3) * 128, tsl]
                P.dma("sp", dst, fm[k][:, :], reads=[("fm", k)], writes=[("A_out", m, it)])
            for q in range(4):
                tok = slice(q * 128, (q + 1) * 128)
                r0 = it * T + q * 128
                for which, col0, pb, stb, dstT in (("v", 1536, 5, vst, v_tm), ("o", 2048, 6, ost, o_tm)):
                    for c in range(8):
                        P.op("pe", lambda e, c=c, tok=tok, col0=col0, pb=pb: e.matmul(
                            ps[pb][:, :512], lhsT=hT[:, c, tok], rhs=w_sb[:, c, col0:col0 + 512],
                            start=(c == 0), stop=(c == 7)), reads=["win", hk], writes=[("ps", pb)])
                    k = q % 4
                    evac(P, "act" if which == "v" else "dve", stb[k][:, :], ps[pb][:, :512],
                         [("ps", pb)], [(which + "st", k)])
                    P.dma("sp", dstT[r0:r0 + 128, :], stb[k][:, :], reads=[(which + "st", k)],
                          writes=[("A_out", which, r0)])
            for gi_ in range(2):
                col0 = 2560 + 4 * gi_
                for c in range(8):
                    P.op("pe", lambda e, c=c, col0=col0: e.matmul(
                        ps[7][0:4, :T], lhsT=w_sb[:, c, col0:col0 + 4], rhs=hT[:, c, :],
                        start=(c == 0), stop=(c == 7)), reads=["win", hk], writes=[("ps", 7)])
                evac(P, "dve", gst[gi_][:, :], ps[7][0:4, :T], [("ps", 7)], [("gst", gi_)])
                P.dma("sp", gT[4 * gi_:4 * gi_ + 4, tsl], gst[gi_][:, :], reads=[("gst", gi_)],
                      writes=[("A_out", "g", gi_, it)])

        proj_norm_tiles(C, cst, xT_in, W["mix_norm"], T, body)

    if upto == "A":
        return
    with C.scope():
        PADL = 16
        ub = C.sb("e_ub", [128, PADL + S], F32)
        sA = C.sb("e_sA", [128, PADL + S], F32)
        sB = C.sb("e_sB", [128, PADL + S], F32)
        pooled = C.sb("e_pooled", [128, S], BF16)
        aout = C.sb("e_aout", [128, S], BF16)
        pw_sb = C.sb("e_pw", [128, 4, 128], BF16)
        psc = C.sb("e_psc", [128, 4], F32)
        invc = C.sb("e_invc", [128, 16], F32)
        tmpc = C.sb("e_tmpc", [128, 16], F32)
        stg = Stager(C, "e_stgB", cols=512)
        stg.load(pw_sb.rearrange("p a b -> p (a b)"), W["pool_w"], "pw")
        P.dma("sp", psc[:, :], W["pool_scale"], writes=["psc"])
        P.dma("sp", invc[:, :], cin["invc"], writes=["invc"])
        for bname, buf in (("ub", ub), ("sA", sA), ("sB", sB)):
            P.op("pool", lambda e, buf=buf: e.memset(buf[:, 0:PADL], 0.0), writes=[bname])
        for g in range(4):
            win_ = 2 << g
            P.dma("sp", ub[:, PADL:], uT[g * 128:(g + 1) * 128, :], reads=["ub"], writes=["ub"])
            src, sname = ub, "ub"
            dsts = [(sA, "sA"), (sB, "sB")]
            for k in range(g + 1):
                sh = 1 << k
                dst, dname = dsts[k % 2]
                P.op("dve", lambda e, src=src, dst=dst, sh=sh: e.tensor_tensor(
                    out=dst[:, PADL:], in0=src[:, PADL:], in1=src[:, PADL - sh:PADL - sh + S], op=ALU.add),
                    reads=[sname], writes=[dname])
                src, sname = dst, dname
            P.op("dve", lambda e, src=src, win_=win_: e.scalar_tensor_tensor(
                out=pooled[:, :], in0=src[:, PADL:], scalar=1.0 / win_, in1=ub[:, PADL:],
                op0=ALU.mult, op1=ALU.subtract), reads=[sname, "ub"], writes=["pooled"])
            nfix = win_ - 1
            P.op("dve", lambda e, src=src, nfix=nfix: e.tensor_tensor(
                out=tmpc[:, 0:nfix], in0=src[:, PADL:PADL + nfix], in1=invc[:, 0:nfix], op=ALU.mult),
                reads=[sname, "invc"], writes=["tmpc"])
            P.op("dve", lambda e, nfix=nfix: e.tensor_tensor(
                out=pooled[:, 0:nfix], in0=tmpc[:, 0:nfix], in1=ub[:, PADL:PADL + nfix], op=ALU.subtract),
                reads=["tmpc", "ub", "pooled"], writes=["pooled"])
            for it in range(S // T):
                pb = 1 + it % 2
                P.op("pe", lambda e, g=g, it=it, pb=pb: e.matmul(
                    ps[pb][:, :T], lhsT=pw_sb[:, g, :], rhs=pooled[:, it * T:(it + 1) * T], start=True, stop=True),
                    reads=["pw", "pooled"], writes=[("ps", pb)])
                P.op("dve", lambda e, g=g, it=it, pb=pb: e.tensor_scalar(
                    out=aout[:, it * T:(it + 1) * T], in0=ps[pb][:, :T], scalar1=psc[:, g:g + 1], scalar2=None,
                    op0=ALU.mult), reads=[("ps", pb), "psc"], writes=["aout"])
            P.dma("sp", mixT[g * 128:(g + 1) * 128, :], aout[:, :], reads=["aout"], writes=[("mixA", g)])

    if upto == "B":
        return
    with C.scope():
        gi = C.sb("e_gi", [4, S], F32)
        gf = C.sb("e_gf", [4, S], F32)
        Bc = C.sb("e_Bc", [4, S], F32)
        Ac = C.sb("e_Ac", [4, S], F32)
        Gm = C.sb("e_Gm", [4, S], F32)
        gb = C.sb("e_gb", [4, 2], F32)
        nbf = C.sb("e_nbf", [4, 1], F32)
        mucol = C.sb("e_mucol", [4, 33], F32)
        dd = C.sb("e_dd", [4, 32], F32)
        oh4 = C.sb("e_oh4", [4, 4, 128], F32)
        P.dma("sp", gi[:, :], gT[0:4, :], writes=["gi"])
        P.dma("sp", gf[:, :], gT[4:8, :], writes=["gf"])
        P.dma("sp", gb[:, :], W["gate_bias"], writes=["gb"])
        P.dma("sp", oh4.rearrange("p a b -> p (a b)"), cin["onehot4"], writes=["oh4"])
        one4 = cst["one"][0:4, 0:1]
        P.op("dve", lambda e: e.tensor_scalar(out=gi[:, :], in0=gi[:, :], scalar1=gb[:, 0:1], scalar2=None, op0=ALU.add),
             reads=["gi", "gb"], writes=["gi"])
        P.op("dve", lambda e: e.tensor_scalar(out=nbf[:, :], in0=gb[:, 1:2], scalar1=-1.0, scalar2=None, op0=ALU.mult),
             reads=["gb"], writes=["nbf"])
        P.op("act", lambda e: e.activation(out=gf[:, :], in_=gf[:, :], func=AF.Exp, scale=-1.0, bias=nbf[:, 0:1]),
             reads=["gf", "nbf"], writes=["gf"])
        P.op("act", lambda e: e.activation(out=gf[:, :], in_=gf[:, :], func=AF.Ln, scale=1.0, bias=one4),
             reads=["gf", "one"], writes=["gf"])
        P.op("dve", lambda e: e.tensor_tensor_scan(out=Bc[:, :], data0=one4.to_broadcast([4, S]), data1=gf[:, :],
                                                   initial=0.0, op0=ALU.mult, op1=ALU.subtract),
             reads=["gf", "one"], writes=["Bc"])
        P.op("dve", lambda e: e.tensor_tensor(out=Ac[:, :], in0=gi[:, :], in1=Bc[:, :], op=ALU.subtract),
             reads=["gi", "Bc"], writes=["Ac"])
        P.op("dve", lambda e: e.tensor_tensor_scan(out=Gm[:, :], data0=Ac[:, :], data1=Ac[:, :],
                                                   initial=0.0, op0=ALU.max, op1=ALU.max),
             reads=["Ac"], writes=["Gm"])
        Gend = Gm.rearrange("h (c l) -> h c l", l=128)[:, :, 127:128]
        P.op("dve", lambda e: e.memset(mucol[:, 0:1], 0.0), writes=["mucol"])
        P.op("dve", lambda e: e.tensor_copy(out=mucol[:, 1:33].unsqueeze(2), in_=Gend), reads=["Gm", "mucol"],
             writes=["mucol"])
        P.op("dve", lambda e: e.tensor_tensor(out=dd[:, :], in0=mucol[:, 0:32], in1=mucol[:, 1:33], op=ALU.subtract),
             reads=["mucol"], writes=["dd"])
        P.op("act", lambda e: e.activation(out=dd[:, :], in_=dd[:, :], func=AF.Exp), reads=["dd"], writes=["dd"])
        A3 = Ac.rearrange("h (c l) -> h c l", l=128)
        B3 = Bc.rearrange("h (c l) -> h c l", l=128)
        P.op("dve", lambda e: e.tensor_tensor(out=A3, in0=A3, in1=Gend.to_broadcast([4, 32, 128]), op=ALU.subtract),
             reads=["Ac", "Gm"], writes=["Ac"])
        P.op("act", lambda e: e.activation(out=Ac[:, :], in_=Ac[:, :], func=AF.Exp), reads=["Ac"], writes=["Ac"])
        P.op("dve", lambda e: e.tensor_tensor(out=B3, in0=B3, in1=Gend.to_broadcast([4, 32, 128]), op=ALU.add),
             reads=["Bc", "Gm"], writes=["Bc"])
        P.op("act", lambda e: e.activation(out=Bc[:, :], in_=Bc[:, :], func=AF.Exp, scale=-1.0), reads=["Bc"],
             writes=["Bc"])
        for h in range(4):
            P.op("pe", lambda e, h=h: e.matmul(ps[1][:, h * 32:(h + 1) * 32], lhsT=oh4[:, h, :], rhs=dd[:, :],
                                                start=True, stop=True), reads=["oh4", "dd"], writes=[("ps", 1)])
        P.op("dve", lambda e: e.tensor_copy(out=dcol.rearrange("p a b -> p (a b)"), in_=ps[1][:, 0:128]),
             reads=[("ps", 1)], writes=["dcol"])
        for src, sname, dst, dname, pb in ((Ac, "Ac", ws_tm, "ws_tm", 2), (Bc, "Bc", thr_tm, "thr_tm", 3)):
            for c in range(32):
                P.op("pe", lambda e, src=src, c=c, pb=pb: e.transpose(
                    out=ps[pb][:, c * 4:(c + 1) * 4], in_=src[:, c * 128:(c + 1) * 128], identity=cst["identf"][0:4, 0:4]),
                    reads=[sname, "identf"], writes=[("ps", pb)])
            P.op("dve", lambda e, dst=dst, pb=pb: e.tensor_copy(out=dst.rearrange("p a b -> p (a b)"),
                                                               in_=ps[pb][:, 0:128]),
                 reads=[("ps", pb)], writes=[dname])

    if upto == "C":
        return
    with C.scope():
        qkb = C.sb("e_qkb", [128, 8, S], BF16)
        cw = C.sb("e_cw", [128, 8, 4], F32)
        cb = C.sb("e_cb", [128, 8], F32)
        qtmp = [C.sb("e_qtmp%d" % i, [128, 512], F32) for i in range(2)]
        P.dma("sp", cw.rearrange("p a b -> p (a b)"), W["qk_conv_w"], writes=["cw"])
        P.dma("sp", cb[:, :], W["qk_conv_b"], writes=["cb"])
        qcnt = [0]

        def sink_e(m, it, psap, bias, pkey):
            tsl = slice(it * 512, (it + 1) * 512)
            if m < 4:
                k = qcnt[0] % 2
                qcnt[0] += 1
                P.op("act", lambda e: e.activation(out=qtmp[k][:, :], in_=psap, func=AF.Silu, bias=bias),
                     reads=[pkey, "cb"], writes=[("qtmp", k)])
                P.op("dve", lambda e: e.tensor_scalar(out=qkb[:, m, tsl], in0=qtmp[k][:, :], scalar1=128.0 ** -0.5,
                                                       scalar2=None, op0=ALU.mult),
                     reads=[("qtmp", k)], writes=[("qkb", m)])
            else:
                P.op("act", lambda e: e.activation(out=qkb[:, m, tsl], in_=psap, func=AF.Silu, bias=bias),
                     reads=[pkey, "cb"], writes=[("qkb", m)])

        conv_silu_pe(C, cst, "ecv", qkT, 8, cw, cb, sink_e)

        if upto == "D0":
            return
        vch = [C.sb("e_vch%d" % i, [128, 4, 128], BF16) for i in range(2)]
        och = [C.sb("e_och%d" % i, [128, 512], F32) for i in range(2)]
        vw = [C.sb("e_vw%d" % i, [128, 4, 136], BF16) for i in range(2)]
        ktm = [C.sb("e_ktm%d" % i, [128, 4, 128], BF16) for i in range(2)]
        PT = C.sb("e_PT", [128, 4, 128], BF16)
        Sst = C.sb("e_S", [128, 4, 129], F32)
        Sbf = C.sb("e_Sbf", [128, 4, 136], BF16)
        nwb = C.sb("e_nwb", [128, 512], F32)
        nwo = C.sb("e_nwo", [128, 512], F32)
        den = C.sb("e_den", [128, 4], F32)
        ss = C.sb("e_ss", [128, 4], F32)
        junk = C.sb("e_junk", [128, 128], F32)
        bout = C.sb("e_bout", [128, 4, 128], BF16)
        boutT = [C.sb("e_boutT%d" % i, [128, 4, 128], BF16) for i in range(2)]
        P.dma("sp", nwb[:, :], W["mlstm_norm"].partition_broadcast(128), writes=["nwb"])
        P.op("pool", lambda e: e.memset(Sst.rearrange("p a b -> p (a b)"), 0.0), writes=["S0", "S1", "S2", "S3"])
        ps0b = ps[0][:, :].bitcast(BF16)
        pending_tail = []
        for c in range(1 if upto in ("D1", "D2", "D3") else 32):
            b = c % 2
            blk = slice(c * 128, (c + 1) * 128)
            P.dma("sp", vch[b].rearrange("p a b -> p (a b)"), v_tm[blk, :], writes=[("vch", b)])
            P.dma("sp", och[b][:, :], o_tm[blk, :], writes=[("och", b)])
            P.op("dve", lambda e, b=b, c=c: e.tensor_tensor(
                out=vw[b][:, :, 0:128], in0=vch[b][:, :, :],
                in1=ws_tm[:, c, :].unsqueeze(2).to_broadcast([128, 4, 128]), op=ALU.mult),
                reads=[("vch", b), "ws_tm"], writes=[("vw", b)])
            P.op("dve", lambda e, b=b, c=c: e.tensor_copy(out=vw[b][:, :, 128:129], in_=ws_tm[:, c, :].unsqueeze(2)),
                 reads=["ws_tm", ("vw", b)], writes=[("vw", b)])
            for h in range(4):
                P.op("pe", lambda e, h=h, blk=blk: e.transpose(out=ps0b[:, h * 128:(h + 1) * 128], in_=qkb[:, 4 + h, blk],
                                                                identity=cst["identb"][:, :]),
                     reads=[("qkb", 4 + h), "identb"], writes=[("ps0", "k")])
            P.op("act", lambda e, b=b: e.activation(out=ktm[b].rearrange("p a b -> p (a b)"), in_=ps0b[:, 0:512],
                                                    func=AF.Copy), reads=[("ps0", "k")], writes=[("ktm", b)])
            sb_ = 1 + b
            for h in range(4):
                P.op("pe", lambda e, h=h, blk=blk, sb_=sb_: e.matmul(
                    ps[sb_][:, h * 128:(h + 1) * 128], lhsT=qkb[:, 4 + h, blk], rhs=qkb[:, h, blk],
                    start=True, stop=True), reads=[("qkb", 4 + h), ("qkb", h)], writes=[("ps", sb_)])
            P.op("dve", lambda e, sb_=sb_: e.tensor_tensor(
                out=PT[:, :, :], in0=ps[sb_][:, :].rearrange("p (a b) -> p a b", b=128),
                in1=cst["maskT"][:, :].unsqueeze(1).to_broadcast([128, 4, 128]), op=ALU.mult),
                reads=[("ps", sb_), "maskT"], writes=["PT"])
            if upto == "D1":
                continue
            for h in range(4):
                ob = 3 + h // 2
                oc = (h % 2) * 129
                P.op("dve", lambda e, h=h, c=c: e.tensor_scalar(out=Sbf[:, h, 0:129], in0=Sst[:, h, :],
                                                                 scalar1=dcol[:, h, c:c + 1], scalar2=None, op0=ALU.mult),
                     reads=["S%d" % h, "dcol"], writes=["Sbf%d" % h])
                P.op("pe", lambda e, h=h, blk=blk, ob=ob, oc=oc: e.matmul(
                    ps[ob][:, oc:oc + 129], lhsT=qkb[:, h, blk], rhs=Sbf[:, h, 0:129], start=True, stop=False),
                    reads=[("qkb", h), "Sbf%d" % h], writes=[("pso", h)])
                P.op("pe", lambda e, h=h, b=b, ob=ob, oc=oc: e.matmul(
                    ps[ob][:, oc:oc + 129], lhsT=PT[:, h, :], rhs=vw[b][:, h, 0:129], start=False, stop=True),
                    reads=["PT", ("vw", b)], writes=[("pso", h)])
                db = 5 + h // 2
                P.op("pe", lambda e, h=h, b=b, db=db, oc=oc: e.matmul(
                    ps[db][:, oc:oc + 129], lhsT=ktm[b][:, h, :], rhs=vw[b][:, h, 0:129], start=True, stop=True),
                    reads=[("ktm", b), ("vw", b)], writes=[("psd", h)])
                P.op("dve", lambda e, h=h, c=c, db=db, oc=oc: e.scalar_tensor_tensor(
                    out=Sst[:, h, :], in0=Sst[:, h, :], scalar=dcol[:, h, c:c + 1], in1=ps[db][:, oc:oc + 129],
                    op0=ALU.mult, op1=ALU.add), reads=["S%d" % h, "dcol", ("psd", h), "Sbf%d" % h], writes=["S%d" % h])
            if upto == "D2":
                continue
            while pending_tail:
                pending_tail.pop(0)()
            P.op("act", lambda e, b=b: e.activation(out=nwo[:, :], in_=och[b][:, :], func=AF.Sigmoid),
                 reads=[("och", b)], writes=["nwo"])
            P.op("dve", lambda e: e.tensor_tensor(out=nwo[:, :], in0=nwo[:, :], in1=nwb[:, :], op=ALU.mult),
                 reads=["nwo", "nwb"], writes=["nwo"])
            for hp in range(2):
                ob = 3 + hp
                Dv = ps[ob][:, 0:258].rearrange("p (a b) -> p a b", b=129)[:, :, 128:129]
                P.op("act", lambda e, hp=hp, Dv=Dv: e.activation(
                    out=den[:, 2 * hp:2 * hp + 2].unsqueeze(2), in_=Dv, func=AF.Abs),
                    reads=[("pso", 2 * hp), ("pso", 2 * hp + 1)], writes=["den"])
            P.op("dve", lambda e, c=c: e.tensor_tensor(out=den[:, :], in0=den[:, :], in1=thr_tm[:, c, :], op=ALU.max),
                 reads=["den", "thr_tm"], writes=["den"])
            P.op("dve", lambda e: e.reciprocal(out=den[:, :], in_=den[:, :]), reads=["den"], writes=["den"])
            for h in range(4):
                ob = 3 + h // 2
                oc = (h % 2) * 129
                P.op("act", lambda e, h=h, ob=ob, oc=oc: e.activation(
                    out=junk[:, :], in_=ps[ob][:, oc:oc + 128], func=AF.Square, scale=den[:, h:h + 1],
                    accum_out=ss[:, h:h + 1]), reads=[("pso", h), "den"], writes=["junk", ("ss", h)])
            P.op("act", lambda e: e.activation(out=ss[:, :], in_=ss[:, :], func=AF.Sqrt, scale=1.0 / 128, bias=cst["eps"][:, 0:1]),
                 reads=[("ss", h) for h in range(4)] + ["eps"], writes=[("ss", h) for h in range(4)])
            P.op("dve", lambda e: e.reciprocal(out=ss[:, :], in_=ss[:, :]), reads=[("ss", h) for h in range(4)],
                 writes=[("ss", h) for h in range(4)])
            P.op("dve", lambda e: e.tensor_tensor(out=ss[:, :], in0=ss[:, :], in1=den[:, :], op=ALU.mult),
                 reads=[("ss", h) for h in range(4)] + ["den"], writes=[("ss", h) for h in range(4)])
            for h in range(4):
                ob = 3 + h // 2
                oc = (h % 2) * 129
                P.op("dve", lambda e, h=h, ob=ob, oc=oc: e.scalar_tensor_tensor(
                    out=bout[:, h, :], in0=ps[ob][:, oc:oc + 128], scalar=ss[:, h:h + 1], in1=nwo[:, h * 128:(h + 1) * 128],
                    op0=ALU.mult, op1=ALU.mult), reads=[("pso", h), ("ss", h), "nwo"], writes=[("bout", h)])
            def emit_tail(c=c, b=b, blk=blk):
                for h in range(4):
                    P.op("pe", lambda e, h=h: e.transpose(out=ps0b[:, 512 + h * 128:512 + (h + 1) * 128], in_=bout[:, h, :],
                                                          identity=cst["identb"][:, :]),
                         reads=[("bout", h), "identb"], writes=[("ps0", "b")])
                P.op("act", lambda e, b=b: e.activation(out=boutT[b].rearrange("p a b -> p (a b)"), in_=ps0b[:, 512:1024],
                                                        func=AF.Copy), reads=[("ps0", "b")], writes=[("boutT", b)])
                P.dma("sp", mixT[512:1024, blk].rearrange("(h p) t -> p h t", p=128), boutT[b][:, :, :],
                      reads=[("boutT", b)], writes=[("mixB", c)])
            pending_tail.append(emit_tail)
        for f_ in pending_tail:
            f_()

    if upto[0] == "D":
        return
    with C.scope():
        wo = C.sb("e_wo", [128, 8, D], BF16)
        stg = Stager(C, "e_stgE")
        wov = W["w_out"].rearrange("(c p) f -> p c f", p=128)
        for c in range(8):
            stg.load(wo[:, c, :], wov[:, c, :], "wo")
        out_proj(C, wo, 8, mixT, xT_in, xT_out, T)


def out_proj(C, wo, nk, mixT, xT_in, xT_out, T):
    P = C.P
    ps = C.psum
    xin = xT_in.rearrange("(c p) t -> p c t", p=128)
    xout = xT_out.rearrange("(c p) t -> p c t", p=128)
    mixv = mixT.rearrange("(c p) t -> p c t", p=128)
    xt = [C.sb("op_xt%d" % i, [128, 8, T], F32) for i in range(2)]
    mt = [C.sb("op_mt%d" % i, [128, nk, T], BF16) for i in range(2)]
    for it in range(S // T):
        b = it % 2
        tsl = slice(it * T, (it + 1) * T)
        P.dma("sp", xt[b][:, :, :], xin[:, :, tsl], writes=[("op_xt", b)])
        P.dma("act", mt[b][:, :, :], mixv[:, :, tsl], writes=[("op_mt", b)])
        for i in range(8):
            pb = 1 + i % 4
            for k in range(nk):
                P.op("pe", lambda e, i=i, k=k, pb=pb, b=b: e.matmul(
                    ps[pb][:, :T], lhsT=wo[:, k, i * 128:(i + 1) * 128], rhs=mt[b][:, k, :],
                    start=(k == 0), stop=(k == nk - 1)), reads=["wo", ("op_mt", b)], writes=[("ps", pb)])
            P.op("dve", lambda e, i=i, pb=pb, b=b: e.tensor_tensor(
                out=xt[b][:, i, :], in0=ps[pb][:, :T], in1=xt[b][:, i, :], op=ALU.add),
                reads=[("ps", pb), ("op_xt", b)], writes=[("op_xt", b)])
        P.dma("pool", xout[:, :, tsl], xt[b][:, :, :], reads=[("op_xt", b)], writes=[("op_out", it)])


def _pc(v, nchunk):
    return np.ascontiguousarray(np.asarray(v, np.float32).reshape(nchunk, 128).T)


def host_params(inp):
    f = lambda k: np.ascontiguousarray(np.asarray(inp[k], np.float32))
    out = {}
    for pre in ("l0_ffn1", "l0_ffn2", "l1_ffn1", "l1_ffn2"):
        out[pre + "_norm"] = _pc(inp[pre + "_norm"], 8)
        for w in ("wg", "wu", "wd"):
            out[pre + "_" + w] = f(pre + "_" + w)
    out["l0_mix_norm"] = _pc(inp["l0_mix_norm"], 8)
    out["l0_w_in"] = f("l0_w_in")
    out["l0_pool_w"] = np.ascontiguousarray(np.transpose(f("l0_pool_w"), (1, 0, 2)).reshape(128, 512))
    out["l0_pool_scale"] = _pc(inp["l0_pool_scale"], 4)
    out["l0_qk_conv_w"] = np.ascontiguousarray(f("l0_qk_conv_w").reshape(4, 8, 128).transpose(2, 1, 0).reshape(128, 32))
    out["l0_qk_conv_b"] = _pc(inp["l0_qk_conv_b"], 8)
    out["l0_gate_bias"] = np.ascontiguousarray(f("l0_gate_bias").reshape(2, 4).T)
    out["l0_mlstm_norm"] = f("l0_mlstm_norm")
    out["l0_w_out"] = f("l0_w_out")
    out["l1_mix_norm"] = _pc(inp["l1_mix_norm"], 8)
    out["l1_w_in"] = f("l1_w_in")
    out["l1_ssd_conv_w"] = np.ascontiguousarray(f("l1_ssd_conv_w").reshape(4, 16, 128).transpose(2, 1, 0).reshape(128, 64))
    out["l1_ssd_conv_b"] = _pc(inp["l1_ssd_conv_b"], 16)
    out["l1_ssd_dt_bias"] = f("l1_ssd_dt_bias").reshape(16, 1)
    out["l1_ssd_A_log"] = f("l1_ssd_A_log").reshape(16, 1)
    out["l1_ssd_D"] = f("l1_ssd_D")
    out["l1_ssd_norm"] = f("l1_ssd_norm")
    out["l1_sb_q_norm"] = np.ascontiguousarray(np.tile(f("l1_sb_q_norm"), 2).reshape(128, 1))
    out["l1_sb_k_norm"] = np.ascontiguousarray(np.tile(f("l1_sb_k_norm"), 2).reshape(128, 1))
    out["l1_w_out"] = f("l1_w_out")
    return out


PARAM_SHAPES = {
    "l0_mix_norm": [128, 8], "l0_w_in": [1024, 2568], "l0_pool_w": [128, 512], "l0_pool_scale": [128, 4],
    "l0_qk_conv_w": [128, 32], "l0_qk_conv_b": [128, 8], "l0_gate_bias": [4, 2], "l0_mlstm_norm": [512],
    "l0_w_out": [1024, 1024],
    "l1_mix_norm": [128, 8], "l1_w_in": [1024, 4624], "l1_ssd_conv_w": [128, 64], "l1_ssd_conv_b": [128, 16],
    "l1_ssd_dt_bias": [16, 1], "l1_ssd_A_log": [16, 1], "l1_ssd_D": [16], "l1_ssd_norm": [1024],
    "l1_sb_q_norm": [128, 1], "l1_sb_k_norm": [128, 1], "l1_w_out": [1536, 1024],
}
for _pre in ("l0_ffn1", "l0_ffn2", "l1_ffn1", "l1_ffn2"):
    PARAM_SHAPES[_pre + "_norm"] = [128, 8]
    PARAM_SHAPES[_pre + "_wg"] = [D, DFF]
    PARAM_SHAPES[_pre + "_wu"] = [D, DFF]
    PARAM_SHAPES[_pre + "_wd"] = [DFF, D]
CONST_SHAPES = {"identf": [128, 128], "maskT": [128, 128], "maskTs": [128, 128], "triu": [128, 128],
                "invc": [128, 16], "onehot4": [4, 512], "onehot16": [16, 2048], "blk1": [128, 128], "negm": [128, 128]}


def odd_mixer_phase(C, cst, cin, xT_in, xT_out, W, upto="E"):
    P = C.P
    ps = C.psum
    T = 512
    o_z = C.dram("o_z", [S, 1024], F32)
    o_xbcT = C.dram("o_xbcT", [2048, S], F32)
    o_dtT = C.dram("o_dtT", [16, S], F32)
    o_qT = C.dram("o_qT", [512, S], BF16)
    o_kT = C.dram("o_kT", [512, S], BF16)
    o_v = C.dram("o_v", [S, 512], BF16)
    o_xc = C.dram("o_xc", [2048, S], BF16)
    mixT = C.dram("o_mix", [1536, S], BF16)

    bias_tm = C.sb("o_bias_tm", [128, 32, 16], F32)
    est_tm = C.sb("o_est_tm", [128, 32, 16], F32)
    dtte_tm = C.sb("o_dtte_tm", [128, 32, 16], F32)
    cdb = C.sb("o_cdb", [128, 32, 16], F32)
    blk1 = C.sb("o_blk1", [128, 128], BF16)
    negm = C.sb("o_negm", [128, 128], BF16)
    eps64 = C.sb("o_eps64", [128, 1], F32)
    P.op("pool", lambda e: e.memset(eps64[:, :], 64.0 * EPS), writes=["eps64"])
    P.dma("sp", cst["tmpf"][:, :], cin["blk1"], reads=["tmpf"], writes=["tmpf"])
    P.op("pool", lambda e: e.tensor_copy(out=blk1[:, :], in_=cst["tmpf"][:, :]), reads=["tmpf"], writes=["blk1"])
    P.dma("sp", cst["tmpf"][:, :], cin["negm"], reads=["tmpf"], writes=["tmpf"])
    P.op("pool", lambda e: e.tensor_copy(out=negm[:, :], in_=cst["tmpf"][:, :]), reads=["tmpf"], writes=["negm"])

    with C.scope():
        w_sb = C.sb("o_win", [128, 8, 4624], BF16)
        stg = Stager(C, "o_stg", cols=1156)
        win = W["w_in"].rearrange("(c p) f -> p c f", p=128)
        for c in range(8):
            stg.load(w_sb[:, c, :], win[:, c, :], "win")
        fm = [C.sb("o_fm%d" % i, [128, T], F32) for i in range(6)]
        zst = [C.sb("o_zst%d" % i, [128, 512], F32) for i in range(6)]
        vst = [C.sb("o_vst%d" % i, [128, 512], BF16) for i in range(4)]
        dst_ = C.sb("o_dst", [16, T], F32)
        sqb = [C.sb("o_sqb%d" % i, [128, T], BF16) for i in range(4)]
        rr = [C.sb("o_rr%d" % i, [128, T], F32) for i in range(2)]
        qn = [C.sb("o_qn%d" % i, [128, T], BF16) for i in range(4)]
        qw = C.sb("o_qw", [128, 2], F32)
        P.dma("sp", qw[:, 0:1], W["sb_q_norm"], writes=["qw"])
        P.dma("sp", qw[:, 1:2], W["sb_k_norm"], reads=["qw"], writes=["qw"])
        cnt = {"fm": 0, "qn": 0, "z": 0}

        def body(it, hT, hk):
            tsl = slice(it * T, (it + 1) * T)
            for m in range(16):
                pb = 1 + m % 2
                for c in range(8):
                    P.op("pe", lambda e, c=c, m=m, pb=pb: e.matmul(
                        ps[pb][:, :T], lhsT=w_sb[:, c, 1024 + m * 128:1024 + (m + 1) * 128], rhs=hT[:, c, :],
                        start=(c == 0), stop=(c == 7)), reads=["win", hk], writes=[("ps", pb)])
                k = cnt["fm"] % 6
                cnt["fm"] += 1
                evac(P, "act" if m % 2 == 0 else "dve", fm[k][:, :], ps[pb][:, :T], [("ps", pb)], [("fm", k)])
                P.dma("sp", o_xbcT[m * 128:(m + 1) * 128, tsl], fm[k][:, :], reads=[("fm", k)],
                      writes=[("A_out", "xbc", m, it)])
            def qk_proj(m):
                isq = m < 4
                col0 = (3088 if isq else 3600) + (m % 4) * 128
                pbq = 3 + m % 4
                kk = m % 4
                for c in range(8):
                    P.op("pe", lambda e, c=c, col0=col0, pbq=pbq: e.matmul(
                        ps[pbq][:, :T], lhsT=w_sb[:, c, col0:col0 + 128], rhs=hT[:, c, :],
                        start=(c == 0), stop=(c == 7)), reads=["win", hk], writes=[("ps", pbq)])
                P.op("act", lambda e, pbq=pbq, kk=kk: e.activation(out=sqb[kk][:, :], in_=ps[pbq][:, :T], func=AF.Square),
                     reads=[("ps", pbq)], writes=[("sqb", kk)])

            qk_proj(0)
            qk_proj(1)
            for m in range(8):
                isq = m < 4
                pbq = 3 + m % 4
                pbs = 1 + m % 2
                kk = m % 4
                k2 = m % 2
                P.op("pe", lambda e, pbs=pbs, kk=kk: e.matmul(ps[pbs][:, :T], lhsT=blk1[:, :], rhs=sqb[kk][:, :],
                                                              start=True, stop=True),
                     reads=["blk1", ("sqb", kk)], writes=[("ps", pbs)])
                if isq:
                    P.op("act", lambda e, pbs=pbs, k2=k2: e.activation(out=rr[k2][:, :], in_=ps[pbs][:, :T], func=AF.Sqrt,
                                                                       scale=1.0, bias=eps64[:, 0:1]),
                         reads=[("ps", pbs), "eps64"], writes=[("rr", k2)])
                else:
                    P.op("act", lambda e, pbs=pbs, k2=k2: e.activation(out=rr[k2][:, :], in_=ps[pbs][:, :T], func=AF.Sqrt,
                                                                       scale=1.0 / 64, bias=cst["eps"][:, 0:1]),
                         reads=[("ps", pbs), "eps"], writes=[("rr", k2)])
                P.op("dve", lambda e, k2=k2: e.reciprocal(out=rr[k2][:, :], in_=rr[k2][:, :]), reads=[("rr", k2)],
                     writes=[("rr", k2)])
                k = cnt["qn"] % 4
                cnt["qn"] += 1
                wi = 0 if isq else 1
                P.op("dve", lambda e, k=k, wi=wi, pbq=pbq, k2=k2: e.scalar_tensor_tensor(
                    out=qn[k][:, :], in0=ps[pbq][:, :T], scalar=qw[:, wi:wi + 1], in1=rr[k2][:, :], op0=ALU.mult, op1=ALU.mult),
                    reads=[("ps", pbq), "qw", ("rr", k2)], writes=[("qn", k)])
                dd_ = (o_qT if isq else o_kT)[(m % 4) * 128:(m % 4 + 1) * 128, tsl]
                P.dma("sp", dd_, qn[k][:, :], reads=[("qn", k)], writes=[("A_out", "qk", m, it)])
                if m + 2 < 8:
                    qk_proj(m + 2)
            for q in range(4):
                tok = slice(q * 128, (q + 1) * 128)
                r0 = it * T + q * 128
                for half in range(2):
                    pb = 5 + half
                    col0 = half * 512
                    for c in range(8):
                        P.op("pe", lambda e, c=c, tok=tok, col0=col0, pb=pb: e.matmul(
                            ps[pb][:, :512], lhsT=hT[:, c, tok], rhs=w_sb[:, c, col0:col0 + 512],
                            start=(c == 0), stop=(c == 7)), reads=["win", hk], writes=[("ps", pb)])
                    k = cnt["z"] % 6
                    cnt["z"] += 1
                    evac(P, "act" if half == 0 else "dve", zst[k][:, :], ps[pb][:, :512], [("ps", pb)], [("zst", k)])
                    P.dma("sp", o_z[r0:r0 + 128, col0:col0 + 512], zst[k][:, :], reads=[("zst", k)],
                          writes=[("A_out", "z", r0, half)])
                for c in range(8):
                    P.op("pe", lambda e, c=c, tok=tok: e.matmul(
                        ps[7][:, :512], lhsT=hT[:, c, tok], rhs=w_sb[:, c, 4112:4624],
                        start=(c == 0), stop=(c == 7)), reads=["win", hk], writes=[("ps", 7)])
                k = q % 4
                evac(P, "act", vst[k][:, :], ps[7][:, :512], [("ps", 7)], [("vst", k)])
                P.dma("sp", o_v[r0:r0 + 128, :], vst[k][:, :], reads=[("vst", k)], writes=[("A_out", "v", r0)])
            for c in range(8):
                P.op("pe", lambda e, c=c: e.matmul(ps[7][0:16, :T], lhsT=w_sb[:, c, 3072:3088], rhs=hT[:, c, :],
                                                    start=(c == 0), stop=(c == 7)), reads=["win", hk], writes=[("ps", 7)])
            evac(P, "dve", dst_[:, :], ps[7][0:16, :T], [("ps", 7)], ["dst"])
            P.dma("sp", o_dtT[:, tsl], dst_[:, :], reads=["dst"], writes=[("A_out", "dt", it)])

        proj_norm_tiles(C, cst, xT_in, W["mix_norm"], T, body)
    if upto == "A":
        return

    with C.scope():
        xcb = [C.sb("o_xcb%d" % i, [128, S], BF16) for i in range(2)]
        cw = C.sb("o_cw", [128, 16, 4], F32)
        cb = C.sb("o_cb", [128, 16], F32)
        P.dma("sp", cw.rearrange("p a b -> p (a b)"), W["ssd_conv_w"], writes=["cw"])
        P.dma("sp", cb[:, :], W["ssd_conv_b"], writes=["cb"])

        def sink_o(m, it, psap, bias, pkey):
            k = m % 2
            P.op("act", lambda e: e.activation(out=xcb[k][:, it * 512:(it + 1) * 512], in_=psap, func=AF.Silu, bias=bias),
                 reads=[pkey, "cb"], writes=[("xcb", k)])
            if it == S // 512 - 1:
                P.dma("sp", o_xc[m * 128:(m + 1) * 128, :], xcb[k][:, :], reads=[("xcb", k)], writes=[("xc", m)])

        conv_silu_pe(C, cst, "ocv", o_xbcT, 16, cw, cb, sink_o)
    if upto == "S0":
        return

    with C.scope():
        dtr = C.sb("o_dtr", [16, S], F32)
        ldt = C.sb("o_ldt", [16, S], F32)
        aa = C.sb("o_aa", [16, S], F32)
        te = C.sb("o_te", [16, S], F32)
        es = C.sb("o_es", [16, S], F32)
        dtb = C.sb("o_dtb", [16, 1], F32)
        Aneg = C.sb("o_Aneg", [16, 1], F32)
        cde = C.sb("o_cde", [16, 32], F32)
        oh16 = C.sb("o_oh16", [16, 16, 128], F32)
        one16 = cst["one"][0:16, 0:1]
        P.dma("sp", dtr[:, :], o_dtT[:, :], writes=["dtr"])
        P.dma("sp", dtb[:, :], W["ssd_dt_bias"], writes=["dtb"])
        P.dma("sp", Aneg[:, :], W["ssd_A_log"], writes=["Aneg"])
        P.dma("sp", oh16.rearrange("p a b -> p (a b)"), cin["onehot16"], writes=["oh16"])
        P.op("act", lambda e: e.activation(out=Aneg[:, :], in_=Aneg[:, :], func=AF.Exp), reads=["Aneg"], writes=["Aneg"])
        P.op("dve", lambda e: e.tensor_scalar(out=Aneg[:, :], in0=Aneg[:, :], scalar1=-1.0, scalar2=None, op0=ALU.mult),
             reads=["Aneg"], writes=["Aneg"])
        P.op("act", lambda e: e.activation(out=dtr[:, :], in_=dtr[:, :], func=AF.Exp, bias=dtb[:, 0:1]),
             reads=["dtr", "dtb"], writes=["dtr"])
        P.op("act", lambda e: e.activation(out=dtr[:, :], in_=dtr[:, :], func=AF.Ln, bias=one16),
             reads=["dtr", "one"], writes=["dtr"])
        P.op("act", lambda e: e.activation(out=ldt[:, :], in_=dtr[:, :], func=AF.Ln), reads=["dtr"], writes=["ldt"])
        P.op("dve", lambda e: e.tensor_scalar(out=aa[:, :], in0=dtr[:, :], scalar1=Aneg[:, 0:1], scalar2=None, op0=ALU.mult),
             reads=["dtr", "Aneg"], writes=["aa"])
        for c in range(32):
            blk = slice(c * 128, (c + 1) * 128)
            P.op("dve", lambda e, blk=blk: e.tensor_tensor_scan(
                out=aa[:, blk], data0=one16.to_broadcast([16, 128]), data1=aa[:, blk], initial=0.0,
                op0=ALU.mult, op1=ALU.add), reads=["aa", "one"], writes=["aa"])
        aa3 = aa.rearrange("h (c l) -> h c l", l=128)
        aend = aa3[:, :, 127:128]
        P.op("dve", lambda e: e.tensor_copy(out=cde[:, :].unsqueeze(2), in_=aend), reads=["aa"], writes=["cde"])
        P.op("act", lambda e: e.activation(out=cde[:, :], in_=cde[:, :], func=AF.Exp), reads=["cde"], writes=["cde"])
        P.op("act", lambda e: e.activation(out=es[:, :], in_=aa[:, :], func=AF.Exp), reads=["aa"], writes=["es"])
        te3 = te.rearrange("h (c l) -> h c l", l=128)
        P.op("dve", lambda e: e.tensor_tensor(out=te3, in0=aend.to_broadcast([16, 32, 128]), in1=aa3, op=ALU.subtract),
             reads=["aa"], writes=["te"])
        P.op("act", lambda e: e.activation(out=te[:, :], in_=te[:, :], func=AF.Exp), reads=["te"], writes=["te"])
        P.op("dve", lambda e: e.tensor_tensor(out=te[:, :], in0=te[:, :], in1=dtr[:, :], op=ALU.mult),
             reads=["te", "dtr"], writes=["te"])
        P.op("dve", lambda e: e.tensor_tensor(out=ldt[:, :], in0=ldt[:, :], in1=aa[:, :], op=ALU.subtract),
             reads=["ldt", "aa"], writes=["ldt"])
        for h in range(16):
            P.op("pe", lambda e, h=h: e.matmul(ps[1][:, h * 32:(h + 1) * 32], lhsT=oh16[:, h, :], rhs=cde[:, :],
                                                start=True, stop=True), reads=["oh16", "cde"], writes=[("ps", 1)])
        P.op("dve", lambda e: e.tensor_copy(out=cdb.rearrange("p c h -> p h c"),
                                            in_=ps[1][:, :].rearrange("p (h c) -> p h c", c=32)),
             reads=[("ps", 1)], writes=["cdb"])
        for src, sname, dst, dname, pb in ((ldt, "ldt", bias_tm, "bias_tm", 2), (es, "es", est_tm, "est_tm", 3),
                                           (te, "te", dtte_tm, "dtte_tm", 4)):
            for c in range(32):
                P.op("pe", lambda e, src=src, c=c, pb=pb: e.transpose(
                    out=ps[pb][:, c * 16:(c + 1) * 16], in_=src[:, c * 128:(c + 1) * 128],
                    identity=cst["identf"][0:16, 0:16]), reads=[sname, "identf"], writes=[("ps", pb)])
            P.op("dve", lambda e, dst=dst, pb=pb: e.tensor_copy(out=dst.rearrange("p a b -> p (a b)"), in_=ps[pb][:, :]),
                 reads=[("ps", pb)], writes=[dname])
        o_acum = C.dram("o_acum", [16, S], F32)
        P.dma("sp", o_acum[:, :], aa[:, :], reads=["aa"], writes=["o_acum"])
    if upto == "S1":
        return
    ssd_main(C, cst, cin, W, o_z, o_xc, mixT, bias_tm, est_tm, dtte_tm, cdb, negm, upto)
    if upto[0] == "S":
        return
    stick_breaking_phase(C, cst, o_qT, o_kT, o_v, mixT, upto)
    if upto[0] == "T":
        return
    with C.scope():
        wo = C.sb("o_wo", [128, 12, D], BF16)
        stg = Stager(C, "o_stgE")
        wov = W["w_out"].rearrange("(c p) f -> p c f", p=128)
        for c in range(12):
            stg.load(wo[:, c, :], wov[:, c, :], "wo")
        out_proj(C, wo, 12, mixT, xT_in, xT_out, T)


def ssd_main(C, cst, cin, W, o_z, o_xc, mixT, bias_tm, est_tm, dtte_tm, cdb, negm, upto):
    P = C.P
    ps = C.psum
    o_acum = C.dram("o_acum", [16, S], F32)
    with C.scope():
        acf = C.sb("s_acf", [16, S], F32)
        oh16 = C.sb("s_oh16", [16, 16, 128], F32)
        Dbc = C.sb("s_Dbc", [128, 16], F32)
        Did = C.sb("s_Did", [128, 16, 128], BF16)
        nwb = C.sb("s_nwb", [128, 1024], F32)
        xsup = [C.sb("s_xsup%d" % i, [128, 16, 512], BF16) for i in range(2)]
        xtm = C.sb("s_xtm", [128, 16, 64], BF16)
        xw = C.sb("s_xw", [128, 16, 64], BF16)
        Btm = C.sb("s_Btm", [128, 4, 128], BF16)
        dec = [C.sb("s_dec%d" % i, [128, 4, 128], F32) for i in range(2)]
        PTs = [C.sb("s_PT%d" % i, [128, 4, 128], BF16) for i in range(2)]
        Hst = C.sb("s_H", [128, 16, 64], F32)
        Hbf = C.sb("s_Hbf", [128, 16, 64], BF16)
        zch = [C.sb("s_z%d" % i, [128, 1024], F32) for i in range(2)]
        yoff = C.sb("s_yoff", [128, 16, 64], F32)
        ysb = C.sb("s_y", [128, 1024], F32)
        ssg = C.sb("s_ss", [128, 4], F32)
        junk = C.sb("s_junk", [128, 256], F32)
        cout = C.sb("s_cout", [128, 1024], BF16)
        coutT = [C.sb("s_coutT%d" % i, [128, 8, 128], BF16) for i in range(2)]
        P.dma("sp", acf[:, :], o_acum[:, :], writes=["acf"])
        P.dma("sp", oh16.rearrange("p a b -> p (a b)"), cin["onehot16"], writes=["oh16"])
        P.dma("sp", Dbc[:, :], W["ssd_D"].partition_broadcast(128), writes=["Dbc"])
        P.dma("sp", nwb[:, :], W["ssd_norm"].partition_broadcast(128), writes=["nwb"])
        for h in range(16):
            P.op("dve", lambda e, h=h: e.tensor_scalar(out=Did[:, h, :], in0=cst["identf"][:, :], scalar1=Dbc[:, h:h + 1],
                                                       scalar2=None, op0=ALU.mult), reads=["identf", "Dbc"], writes=["Did"])
        P.op("pool", lambda e: e.memset(Hst.rearrange("p a b -> p (a b)"), 0.0), writes=["H"])
        P.op("pool", lambda e: e.memset(Hbf.rearrange("p a b -> p (a b)"), 0.0), writes=["Hbf"])
        ps0b = ps[0][:, :].bitcast(BF16)
        ps1b = ps[1][:, :].bitcast(BF16)
        xcv = o_xc.rearrange("(m p) t -> p m t", p=128)
        nch = 1 if upto == "S2" else 32
        pending_tail = []
        for c in range(nch):
            b = c % 2
            sc, lc = c // 4, c % 4
            blk = slice(c * 128, (c + 1) * 128)
            tl = slice(lc * 128, (lc + 1) * 128)
            if lc == 0:
                P.dma("sp", xsup[sc % 2][:, :, :], xcv[:, :, sc * 512:(sc + 1) * 512], writes=[("xsup", sc % 2)])
            xs_ = xsup[sc % 2]
            xk = ("xsup", sc % 2)
            P.dma("sp", zch[b][:, :], o_z[blk, :], writes=[("zch", b)])
            for m in range(8):
                P.op("pe", lambda e, m=m, xs_=xs_, tl=tl: e.transpose(out=ps0b[:, m * 128:(m + 1) * 128], in_=xs_[:, m, tl],
                                                                      identity=cst["identb"][:, :]),
                     reads=[xk, "identb"], writes=[("ps", 0)])
            P.op("act", lambda e: e.activation(out=xtm.rearrange("p a b -> p (a b)"), in_=ps0b[:, :], func=AF.Copy),
                 reads=[("ps", 0)], writes=["xtm"])
            P.op("dve", lambda e, c=c: e.tensor_tensor(
                out=xw[:, :, :], in0=ps0b[:, :].rearrange("p (a b) -> p a b", b=64),
                in1=dtte_tm[:, c, :].unsqueeze(2).to_broadcast([128, 16, 64]), op=ALU.mult),
                reads=[("ps", 0), "dtte_tm"], writes=["xw"])
            for g in range(4):
                P.op("pe", lambda e, g=g, xs_=xs_, tl=tl: e.transpose(out=ps1b[:, g * 128:(g + 1) * 128], in_=xs_[:, 8 + g, tl],
                                                                      identity=cst["identb"][:, :]),
                     reads=[xk, "identb"], writes=[("ps", 1)])
            P.op("act", lambda e: e.activation(out=Btm.rearrange("p a b -> p (a b)"), in_=ps1b[:, 0:512], func=AF.Copy),
                 reads=[("ps", 1)], writes=["Btm"])
            for g in range(4):
                P.op("pe", lambda e, g=g, xs_=xs_, tl=tl: e.matmul(ps[2][:, g * 128:(g + 1) * 128], lhsT=xs_[:, 8 + g, tl],
                                                                   rhs=xs_[:, 12 + g, tl], start=True, stop=True),
                     reads=[xk], writes=[("ps", 2)])
            def emit_yoff(c=c, xs_=xs_, tl=tl, xk=xk):
                for g in range(4):
                    P.op("pe", lambda e, g=g, xs_=xs_, tl=tl: e.matmul(
                        ps[6 + g // 2][:, (g % 2) * 256:(g % 2 + 1) * 256], lhsT=xs_[:, 12 + g, tl],
                        rhs=Hbf[:, 4 * g:4 * g + 4, :], start=True, stop=True), reads=[xk, "Hbf"], writes=[("ps", 6 + g // 2)])
                for hb in range(2):
                    P.op("dve", lambda e, hb=hb, c=c: e.tensor_tensor(
                        out=yoff[:, 8 * hb:8 * hb + 8, :], in0=ps[6 + hb][:, :].rearrange("p (a b) -> p a b", b=64),
                        in1=est_tm[:, c, 8 * hb:8 * hb + 8].unsqueeze(2).to_broadcast([128, 8, 64]), op=ALU.mult),
                        reads=[("ps", 6 + hb), "est_tm"], writes=[("yoff", hb)])

            for g in range(4):
                k = g % 2
                if g == 2:
                    emit_yoff()
                for hh in range(4):
                    h = 4 * g + hh
                    P.op("pe", lambda e, hh=hh, h=h, blk=blk: e.matmul(
                        ps[3][:, hh * 128:(hh + 1) * 128], lhsT=oh16[:, h, :], rhs=acf[:, blk], start=True, stop=False),
                        reads=["oh16", "acf"], writes=[("ps", 3)])
                    P.op("pe", lambda e, hh=hh: e.matmul(
                        ps[3][:, hh * 128:(hh + 1) * 128], lhsT=cst["identb"][:, :], rhs=negm[:, :], start=False, stop=True),
                        reads=["identb", "negm"], writes=[("ps", 3)])
                for hh in range(4):
                    h = 4 * g + hh
                    P.op("act", lambda e, hh=hh, h=h, k=k, c=c: e.activation(
                        out=dec[k][:, hh, :], in_=ps[3][:, hh * 128:(hh + 1) * 128], func=AF.Exp,
                        bias=bias_tm[:, c, h:h + 1]), reads=[("ps", 3), "bias_tm"], writes=[("dec", k)])
                P.op("dve", lambda e, g=g, k=k: e.tensor_tensor(
                    out=PTs[k][:, :, :], in0=dec[k][:, :, :],
                    in1=ps[2][:, g * 128:(g + 1) * 128].unsqueeze(1).to_broadcast([128, 4, 128]), op=ALU.mult),
                    reads=[("dec", k), ("ps", 2)], writes=[("PTs", k)])
                for hh in range(4):
                    h = 4 * g + hh
                    yb = 4 + h // 8
                    yc = (h % 8) * 64
                    P.op("pe", lambda e, hh=hh, h=h, k=k, yb=yb, yc=yc: e.matmul(
                        ps[yb][:, yc:yc + 64], lhsT=PTs[k][:, hh, :], rhs=xtm[:, h, :], start=True, stop=False),
                        reads=[("PTs", k), "xtm"], writes=[("ps", yb)])
                    P.op("pe", lambda e, h=h, yb=yb, yc=yc: e.matmul(
                        ps[yb][:, yc:yc + 64], lhsT=Did[:, h, :], rhs=xtm[:, h, :], start=False, stop=True),
                        reads=["Did", "xtm"], writes=[("ps", yb)])
            while pending_tail:
                pending_tail.pop(0)()
            for hb in range(2):
                P.op("dve", lambda e, hb=hb: e.tensor_tensor(
                    out=ysb[:, hb * 512:(hb + 1) * 512], in0=ps[4 + hb][:, :],
                    in1=yoff[:, 8 * hb:8 * hb + 8, :].rearrange("p a b -> p (a b)"), op=ALU.add),
                    reads=[("ps", 4 + hb), ("yoff", hb)], writes=[("ysb", hb)])
            for g in range(4):
                P.op("pe", lambda e, g=g: e.matmul(
                    ps[6 + g // 2][:, (g % 2) * 256:(g % 2 + 1) * 256], lhsT=Btm[:, g, :],
                    rhs=xw[:, 4 * g:4 * g + 4, :], start=True, stop=True), reads=["Btm", "xw"], writes=[("ps", 6 + g // 2)])
            P.op("dve", lambda e, c=c: e.tensor_tensor(
                out=Hst[:, :, :], in0=Hst[:, :, :], in1=cdb[:, c, :].unsqueeze(2).to_broadcast([128, 16, 64]), op=ALU.mult),
                reads=["H", "cdb"], writes=["H"])
            for hb in range(2):
                P.op("dve", lambda e, hb=hb: e.tensor_tensor(
                    out=Hst[:, 8 * hb:8 * hb + 8, :], in0=Hst[:, 8 * hb:8 * hb + 8, :],
                    in1=ps[6 + hb][:, :].rearrange("p (a b) -> p a b", b=64), op=ALU.add),
                    reads=["H", ("ps", 6 + hb)], writes=["H"])
            P.op("act", lambda e: e.activation(out=Hbf.rearrange("p a b -> p (a b)"),
                                               in_=Hst.rearrange("p a b -> p (a b)"), func=AF.Copy),
                 reads=["H"], writes=["Hbf"])
            P.op("act", lambda e, b=b: e.activation(out=zch[b][:, :], in_=zch[b][:, :], func=AF.Silu),
                 reads=[("zch", b)], writes=[("zch", b)])
            P.op("dve", lambda e, b=b: e.tensor_tensor(out=ysb[:, :], in0=ysb[:, :], in1=zch[b][:, :], op=ALU.mult),
                 reads=[("ysb", 0), ("ysb", 1), ("zch", b)], writes=[("ysb", 0), ("ysb", 1)])
            for g in range(4):
                P.op("act", lambda e, g=g: e.activation(out=junk[:, :], in_=ysb[:, g * 256:(g + 1) * 256], func=AF.Square,
                                                        accum_out=ssg[:, g:g + 1]),
                     reads=[("ysb", 0), ("ysb", 1)], writes=["junk", ("ssg", g)])
            P.op("act", lambda e: e.activation(out=ssg[:, :], in_=ssg[:, :], func=AF.Sqrt, scale=1.0 / 256,
                                               bias=cst["eps"][:, 0:1]),
                 reads=[("ssg", g) for g in range(4)] + ["eps"], writes=[("ssg", g) for g in range(4)])
            P.op("dve", lambda e: e.reciprocal(out=ssg[:, :], in_=ssg[:, :]), reads=[("ssg", g) for g in range(4)],
                 writes=[("ssg", g) for g in range(4)])
            for g in range(4):
                P.op("dve", lambda e, g=g: e.scalar_tensor_tensor(
                    out=cout[:, g * 256:(g + 1) * 256], in0=ysb[:, g * 256:(g + 1) * 256], scalar=ssg[:, g:g + 1],
                    in1=nwb[:, g * 256:(g + 1) * 256], op0=ALU.mult, op1=ALU.mult),
                    reads=[("ysb", 0), ("ysb", 1), ("ssg", g), "nwb"], writes=["cout"])
            def emit_tail(c=c, b=b, blk=blk):
                for m in range(8):
                    P.op("pe", lambda e, m=m: e.transpose(out=ps0b[:, m * 128:(m + 1) * 128], in_=cout[:, m * 128:(m + 1) * 128],
                                                          identity=cst["identb"][:, :]),
                         reads=["cout", "identb"], writes=[("ps", 0)])
                P.op("act", lambda e, b=b: e.activation(out=coutT[b].rearrange("p a b -> p (a b)"), in_=ps0b[:, :], func=AF.Copy),
                     reads=[("ps", 0)], writes=[("coutT", b)])
                P.dma("sp", mixT[0:1024, blk].rearrange("(m p) t -> p m t", p=128), coutT[b][:, :, :],
                      reads=[("coutT", b)], writes=[("mixC", c)])
            pending_tail.append(emit_tail)
        for f_ in pending_tail:
            f_()


def stick_breaking_phase(C, cst, o_qT, o_kT, o_v, mixT, upto):
    P = C.P
    ps = C.psum
    with C.scope():
        kT = C.sb("t_kT", [128, 4, S], BF16)
        qT = C.sb("t_qT", [128, 4, S], BF16)
        vtm = C.sb("t_v", [128, 32, 512], BF16)
        tril = C.sb("t_tril", [128, 128], BF16)
        P.op("pool", lambda e: e.tensor_tensor(out=tril[:, :], in0=cst["ones"][:, :], in1=cst["triu"][:, :], op=ALU.subtract),
             reads=["ones", "triu"], writes=["tril"])
        P.dma("sp", kT[:, :, :], o_kT.rearrange("(m p) t -> p m t", p=128), writes=["kT"])
        P.dma("sp", qT[:, :, :], o_qT.rearrange("(m p) t -> p m t", p=128), writes=["qT"])
        ovv = o_v.rearrange("(b p) f -> p b f", p=128)
        for i in range(8):
            P.dma("sp", vtm[:, 4 * i:4 * i + 4, :], ovv[:, 4 * i:4 * i + 4, :], reads=["vtm"] if i else [], writes=["vtm"])
        NS = 2
        ee = [[C.sb("t_e%d_%d" % (s_, i), [128, 512], F32) for i in range(2)] for s_ in range(NS)]
        sp = [[C.sb("t_sp%d_%d" % (s_, i), [128, 512], F32) for i in range(2)] for s_ in range(NS)]
        l1m = [[C.sb("t_l1m%d_%d" % (s_, i), [128, 512], BF16) for i in range(2)] for s_ in range(NS)]
        E1 = [[C.sb("t_E1%d_%d" % (s_, i), [128, 512], F32) for i in range(2)] for s_ in range(NS)]
        PTb = [[C.sb("t_PT%d_%d" % (s_, i), [128, 512], BF16) for i in range(2)] for s_ in range(NS)]
        dout = C.sb("t_dout", [128, 4, 512], BF16)
        doutT = [C.sb("t_doutT%d" % i, [128, 4, 512], BF16) for i in range(2)]
        ps7b = ps[6][:, :].bitcast(BF16)
        nQ = {"T1": 1, "T2": 2}.get(upto, 8)
        for Q in range(nQ):
            for h0 in range(0, 8, NS):
                kbs = list(range(4 * Q + 3, -1, -1))
                n = len(kbs)

                def geo(i):
                    kb = kbs[i]
                    tb0 = max(0, kb - 4 * Q)
                    c0 = tb0 * 128
                    return kb, tb0, c0, slice(c0, 512), kb >= 4 * Q

                hs = [(h0 + s_, (h0 + s_) // 2, 64 * ((h0 + s_) % 2), 2 * s_, 4 + s_, 6 + s_) for s_ in range(NS)]
                def emit_z(i):
                    kb, tb0, c0, cs_, diag = geo(i)
                    kblk = slice(kb * 128, (kb + 1) * 128)
                    qsl = slice(Q * 512 + c0, (Q + 1) * 512)
                    for s_, (h, m, pb0, zb0, xb, ob) in enumerate(hs):
                        zb = zb0 + i % 2
                        P.op("pe", lambda e, m=m, pb0=pb0, kblk=kblk, qsl=qsl, cs_=cs_, zb=zb: e.matmul(
                            ps[zb][:, cs_], lhsT=kT[pb0:pb0 + 64, m, kblk], rhs=qT[pb0:pb0 + 64, m, qsl],
                            start=True, stop=True), reads=["kT", "qT"], writes=[("ps", zb)])

                for i in range(n + 1):
                    if i >= 1:
                        kb, tb0, c0, cs_, diag = geo(i - 1)
                        k = (i - 1) % 2
                        for s_, (h, m, pb0, zb, xb, ob) in enumerate(hs):
                            P.op("dve", lambda e, s_=s_, k=k, cs_=cs_, xb=xb: e.tensor_tensor(
                                out=E1[s_][k][:, cs_], in0=ps[xb][:, cs_], in1=sp[s_][k][:, cs_], op=ALU.subtract),
                                reads=[("ps", xb), ("sp", s_, k)], writes=[("E1", s_, k)])
                    if i < n:
                        kb, tb0, c0, cs_, diag = geo(i)
                        k = i % 2
                        if i == 0:
                            emit_z(0)
                        for s_, (h, m, pb0, zb0, xb, ob) in enumerate(hs):
                            zb = zb0 + k
                            P.op("act", lambda e, s_=s_, k=k, cs_=cs_, zb=zb: e.activation(
                                out=ee[s_][k][:, cs_], in_=ps[zb][:, cs_], func=AF.Exp, scale=-1.0),
                                reads=[("ps", zb)], writes=[("ee", s_, k)])
                        for s_, (h, m, pb0, zb, xb, ob) in enumerate(hs):
                            P.op("act", lambda e, s_=s_, k=k, cs_=cs_: e.activation(
                                out=sp[s_][k][:, cs_], in_=ee[s_][k][:, cs_], func=AF.Ln, bias=cst["one"][:, 0:1]),
                                reads=[("ee", s_, k), "one"], writes=[("sp", s_, k)])
                        for s_, (h, m, pb0, zb0, xb, ob) in enumerate(hs):
                            zb = zb0 + k
                            P.op("dve", lambda e, s_=s_, k=k, cs_=cs_, zb=zb: e.scalar_tensor_tensor(
                                out=l1m[s_][k][:, cs_], in0=ps[zb][:, cs_], scalar=-1.0, in1=sp[s_][k][:, cs_],
                                op0=ALU.mult, op1=ALU.subtract), reads=[("ps", zb), ("sp", s_, k)], writes=[("l1m", s_, k)])
                            if diag:
                                dsl = slice(c0, c0 + 128)
                                P.op("dve", lambda e, s_=s_, k=k, dsl=dsl: e.tensor_tensor(
                                    out=l1m[s_][k][:, dsl], in0=l1m[s_][k][:, dsl], in1=cst["maskTs"][:, :], op=ALU.mult),
                                    reads=[("l1m", s_, k), "maskTs"], writes=[("l1m", s_, k)])
                    if i + 1 < n:
                        emit_z(i + 1)
                    if i >= 1:
                        kb, tb0, c0, cs_, diag = geo(i - 1)
                        k = (i - 1) % 2
                        for s_, (h, m, pb0, zb, xb, ob) in enumerate(hs):
                            P.op("act", lambda e, s_=s_, k=k, cs_=cs_: e.activation(
                                out=PTb[s_][k][:, cs_], in_=E1[s_][k][:, cs_], func=AF.Exp),
                                reads=[("E1", s_, k)], writes=[("PTb", s_, k)])
                            if diag:
                                dsl = slice(c0, c0 + 128)
                                P.op("dve", lambda e, s_=s_, k=k, dsl=dsl: e.tensor_tensor(
                                    out=PTb[s_][k][:, dsl], in0=PTb[s_][k][:, dsl], in1=cst["maskTs"][:, :], op=ALU.mult),
                                    reads=[("PTb", s_, k), "maskTs"], writes=[("PTb", s_, k)])
                        for s_, (h, m, pb0, zb, xb, ob) in enumerate(hs):
                            for tb in range(tb0, 4):
                                P.op("pe", lambda e, s_=s_, k=k, tb=tb, kb=kb, h=h, ob=ob, st=(i == 1 and tb == tb0): e.matmul(
                                    ps[ob][:, tb * 64:(tb + 1) * 64], lhsT=PTb[s_][k][:, tb * 128:(tb + 1) * 128],
                                    rhs=vtm[:, kb, h * 64:(h + 1) * 64], start=st, stop=(kb == 0), skip_group_check=True),
                                    reads=[("PTb", s_, k), "vtm"], writes=[("ps", ob)])
                    if i < n:
                        kb, tb0, c0, cs_, diag = geo(i)
                        k = i % 2
                        for s_, (h, m, pb0, zb, xb, ob) in enumerate(hs):
                            if i >= 1:
                                pcs = geo(i - 1)[3]
                                pk = (i - 1) % 2
                                P.op("pe", lambda e, s_=s_, pk=pk, pcs=pcs, xb=xb: e.matmul(
                                    ps[xb][:, pcs], lhsT=tril[:, :], rhs=l1m[s_][pk][:, pcs], start=False, stop=False,
                                    skip_group_check=True), reads=["tril", ("l1m", s_, pk)], writes=[("ps", xb)])
                            P.op("pe", lambda e, s_=s_, k=k, cs_=cs_, xb=xb, st=(i == 0): e.matmul(
                                ps[xb][:, cs_], lhsT=cst["triu"][:, :], rhs=l1m[s_][k][:, cs_], start=st, stop=False,
                                skip_group_check=True), reads=["triu", ("l1m", s_, k)], writes=[("ps", xb)])
                for s_ in range(NS):
                    h = h0 + s_
                    ob = 6 + s_
                    P.op("dve", lambda e, h=h, ob=ob: e.tensor_copy(
                        out=dout[:, :, h * 64:(h + 1) * 64], in_=ps[ob][:, 0:256].rearrange("p (a b) -> p a b", b=64)),
                        reads=[("ps", ob)], writes=["dout"])
            qb = Q % 2
            for mm in range(4):
                for tb in range(4):
                    P.op("pe", lambda e, mm=mm, tb=tb: e.transpose(
                        out=ps7b[:, tb * 128:(tb + 1) * 128], in_=dout[:, tb, mm * 128:(mm + 1) * 128],
                        identity=cst["identb"][:, :]), reads=["dout", "identb"], writes=[("ps", 6)])
                P.op("dve", lambda e, mm=mm, qb=qb: e.tensor_copy(out=doutT[qb][:, mm, :], in_=ps7b[:, 0:512]),
                     reads=[("ps", 6)], writes=[("doutT", qb)])
            P.dma("sp", mixT[1024:1536, Q * 512:(Q + 1) * 512].rearrange("(m p) t -> p m t", p=128), doutT[qb][:, :, :],
                  reads=[("doutT", qb)], writes=[("mixD", Q)])


def build_program():
    nc = bass.Bass("TRN2", target_bir_lowering=False)
    xT = nc.dram_tensor("xT", [D, S], F32, kind="ExternalInput").ap()
    yT = nc.dram_tensor("yT", [D, S], F32, kind="ExternalOutput").ap()
    Wd = {k: nc.dram_tensor(k, sh, F32, kind="ExternalInput").ap() for k, sh in PARAM_SHAPES.items()}
    cin = {k: nc.dram_tensor("c_" + k, sh, F32, kind="ExternalInput").ap() for k, sh in CONST_SHAPES.items()}
    with ExitStack() as stack:
        C = Ctx(nc, stack)
        cst = alloc_consts(C, cin)
        res = [C.dram("res%d" % i, [D, S], F32) for i in range(5)]

        def ffn(pre, src, dst):
            with C.scope():
                bufs = alloc_ffn_bufs(C, cst)
                ffn_phase(C, pre, src, dst, Wd[pre + "_norm"], Wd[pre + "_wg"], Wd[pre + "_wu"], Wd[pre + "_wd"], bufs)

        ffn("l0_ffn1", xT, res[0])
        with C.scope():
            even_mixer_phase(C, cst, cin, res[0], res[1], {k[3:]: v for k, v in Wd.items() if k.startswith("l0_")})
        ffn("l0_ffn2", res[1], res[2])
        ffn("l1_ffn1", res[2], res[3])
        with C.scope():
            odd_mixer_phase(C, cst, cin, res[3], res[4], {k[3:]: v for k, v in Wd.items() if k.startswith("l1_")})
        ffn("l1_ffn2", res[4], yT)
        C.P.emit()
    return nc


_NC_CACHE = {}


def kernel(**inputs):
    x = np.asarray(inputs["x"], np.float32)
    hp = host_params(inputs)
    hc = host_consts()
    shared = {k: hp[k] for k in PARAM_SHAPES}
    for k in CONST_SHAPES:
        shared["c_" + k] = hc[k]
    in_maps = []
    for b in range(NCORES):
        m = dict(shared)
        m["xT"] = np.ascontiguousarray(x[b].T)
        in_maps.append(m)
    if "nc" not in _NC_CACHE:
        _NC_CACHE["nc"] = build_program()
    res = run_bass_kernel_spmd(_NC_CACHE["nc"], in_maps, core_ids=list(range(NCORES)))
    out = np.stack([np.asarray(r["yT"], np.float32).T for r in res.results], axis=0)
    return np.ascontiguousarray(out)
```

```python
from contextlib import ExitStack

import numpy as np
import concourse.bass as bass
import concourse.mybir as mybir
from concourse.bass_utils import run_bass_kernel_spmd

F32 = mybir.dt.float32
BF16 = mybir.dt.bfloat16
ALU = mybir.AluOpType
AF = mybir.ActivationFunctionType
AX = mybir.AxisListType

S = 4096
D = 1024
DFF = 2816
NCORES = 8
EPS = 1e-6
CAST_DMA = True


class _Op:
    __slots__ = ("eng", "fn", "deps", "sig", "sem", "cnt", "is_dma", "pos")


def _bank_of(k):
    if isinstance(k, tuple):
        if k[0] == "ps":
            return k[1]
        if k[0] == "pso":
            return 3 + k[1] // 2
        if k[0] == "psd":
            return 5 + k[1] // 2
        if k[0] == "ps0":
            return 0
    return None


class Prog:
    ENGS = ("pe", "act", "dve", "pool", "sp")

    def __init__(self, nc, stack, n_dma_sems=8):
        self.nc = nc
        self.stack = stack
        self.streams = {e: [] for e in self.ENGS}
        self.lastw = {}
        self.readers = {}
        self.eng_sem = {e: stack.enter_context(nc.semaphore("s_" + e)) for e in ("pe", "act", "dve", "pool")}
        self.dma_pool = {}
        self.n_dma_sems = n_dma_sems
        self.dma_rr = {}
        self.nops = 0
        self.pending = {}
        self.bank_last = {}
        self.multi = {}

    def barrier(self):
        lasts = []
        for e in self.ENGS:
            for o in reversed(self.streams[e]):
                if not o.is_dma:
                    o.sig = True
                    lasts.append(o)
                    break
        for q, slots in self.dma_pool.items():
            for sl in slots:
                if sl[2] is not None:
                    lasts.append(sl[2])
        for e in self.ENGS:
            self.pending[e] = list(lasts) + self.pending.get(e, [])
        self.lastw.clear()
        self.readers.clear()
        self.multi.clear()

    def _dma_sem(self, q):
        if q not in self.dma_pool:
            self.dma_pool[q] = [[self.stack.enter_context(self.nc.semaphore("d_%s%d" % (q, i))), 0, None]
                                for i in range(self.n_dma_sems)]
            self.dma_rr[q] = 0
        i = self.dma_rr[q]
        self.dma_rr[q] = (i + 1) % self.n_dma_sems
        return self.dma_pool[q][i]

    def _deps(self, op, reads, writes):
        deps = set()
        for k in reads:
            w = self.lastw.get(k)
            if w is not None:
                deps.add(w)
            if k in self.multi:
                deps.update(self.multi[k])
        for k in writes:
            w = self.lastw.get(k)
            if w is not None:
                deps.add(w)
            for r in self.readers.get(k, ()):
                deps.add(r)
        deps.discard(op)
        for k in reads:
            self.readers.setdefault(k, []).append(op)
        for k in writes:
            self.lastw[k] = op
            self.readers[k] = []
        return deps

    def op(self, eng, fn, reads=(), writes=()):
        o = _Op()
        o.eng = eng
        o.fn = fn
        o.is_dma = False
        o.sig = False
        o.sem = None
        o.cnt = 0
        o.pos = self.nops
        self.nops += 1
        deps = self._deps(o, reads, writes)
        deps.update(self.pending.pop(eng, ()))
        for k in list(reads) + list(writes):
            bk = _bank_of(k)
            if bk is not None:
                prev = self.bank_last.get(bk)
                if prev is not None and prev is not o and prev.eng != eng:
                    deps.add(prev)
                self.bank_last[bk] = o
        keep = []
        for d in deps:
            if (not d.is_dma) and d.eng == eng and eng == "pe":
                continue
            keep.append(d)
            if not d.is_dma:
                d.sig = True
        o.deps = keep
        self.streams[eng].append(o)
        return o

    def dma(self, q, out, in_, reads=(), writes=(), multi=(), **kw):
        o = _Op()
        o.eng = q
        o.fn = lambda e: e.dma_start(out=out, in_=in_, **kw)
        o.is_dma = True
        o.sig = True
        o.pos = self.nops
        self.nops += 1
        slot = self._dma_sem(q)
        deps = self._deps(o, reads, writes)
        for k in multi:
            for r in self.readers.get(k, ()):
                deps.add(r)
            self.multi.setdefault(k, []).append(o)
        deps.update(self.pending.pop(q, ()))
        if slot[2] is not None:
            deps.add(slot[2])
        slot[1] += 16
        slot[2] = o
        o.sem = slot[0]
        o.cnt = slot[1]
        for d in deps:
            if not d.is_dma:
                d.sig = True
        o.deps = list(deps)
        self.streams[q].append(o)
        return o

    def emit(self):
        nc = self.nc
        for e in ("pe", "act", "dve", "pool"):
            c = 0
            for o in self.streams[e]:
                if o.is_dma:
                    continue
                o.sem = self.eng_sem[e]
                if o.sig:
                    c += 1
                    o.cnt = c
        streams = self.streams

        def run(e, h):
            known = {}
            for o in streams[e]:
                need = {}
                for d in o.deps:
                    key = id(d.sem)
                    if key not in need or need[key][1] < d.cnt:
                        need[key] = (d.sem, d.cnt)
                for key, (sem, v) in need.items():
                    if known.get(key, 0) < v:
                        h.wait_ge(sem, v)
                        known[key] = v
                ins = o.fn(h)
                if o.is_dma:
                    ins.then_inc(o.sem, 16)
                elif o.sig:
                    ins.then_inc(o.sem, 1)
            if e in self.dma_pool:
                for sem, cnt, _ in self.dma_pool[e]:
                    if cnt > 0:
                        h.wait_ge(sem, cnt)

        with nc.Block() as block:
            @block.tensor
            def _(h):
                run("pe", h)

            @block.scalar
            def _(h):
                run("act", h)

            @block.vector
            def _(h):
                run("dve", h)

            @block.gpsimd
            def _(h):
                run("pool", h)

            @block.sync
            def _(h):
                run("sp", h)


ARENA_WORDS = 53000


class Ctx:
    def __init__(self, nc, stack):
        self.nc = nc
        self.stack = stack
        self.P = Prog(nc, stack)
        self.psum_all = stack.enter_context(nc.psum_tensor("psall", [128, 4096], F32))
        self.psum = [self.psum_all[:, i * 512:(i + 1) * 512] for i in range(8)]
        self.arena = stack.enter_context(nc.sbuf_tensor("arena", [128, ARENA_WORDS], F32))
        self.top = 0
        self.scratch = {}

    def sb(self, name, shape, dt):
        esz = 4 if dt == F32 else 2
        n = 1
        for d_ in shape[1:]:
            n *= int(d_)
        words = (n * esz + 3) // 4
        words = (words + 15) // 16 * 16
        off = self.top
        self.top += words
        assert self.top <= ARENA_WORDS, "SBUF arena overflow at %s: %d words" % (name, self.top)
        ap = self.arena[0:shape[0], off:off + words]
        if dt != F32:
            ap = ap.bitcast(dt)
        ap = ap[:, 0:n]
        if len(shape) == 3:
            ap = ap.rearrange("p (a b) -> p a b", b=int(shape[2]))
        elif len(shape) == 4:
            ap = ap.rearrange("p (a b c) -> p a b c", b=int(shape[2]), c=int(shape[3]))
        return ap

    def scope(self):
        return _Scope(self)

    def dram(self, name, shape, dt):
        if name not in self.scratch:
            kind = "ExternalOutput" if name in getattr(self, "debug_out", ()) else "Internal"
            self.scratch[name] = self.nc.dram_tensor(name, list(shape), dt, kind=kind).ap()
        return self.scratch[name]


class _Scope:
    def __init__(self, C):
        self.C = C

    def __enter__(self):
        self.mark = self.C.top
        return self

    def __exit__(self, *a):
        self.C.P.barrier()
        self.C.top = self.mark
        return False


def ffn_phase(C, tag, xT_in, xT_out, nw, wg, wu, wd, bufs, T=512):
    P = C.P
    NT = S // T
    NF = DFF // 128
    wg_sb, wu_sb, wd_sb = bufs["wg"], bufs["wu"], bufs["wd"]
    nw_sb = bufs["nw"]
    ones = bufs["ones"]
    xin = xT_in.rearrange("(c p) t -> p c t", p=128)
    xout = xT_out.rearrange("(c p) t -> p c t", p=128)

    P.dma("sp", nw_sb[:, :], nw, writes=["nw"])
    wgv = wg.rearrange("(c p) f -> p c f", p=128)
    wuv = wu.rearrange("(c p) f -> p c f", p=128)
    wdv = wd.rearrange("(j p) d -> p j d", p=128)
    si = [0]

    def load_cast(dst, src, key):
        P.dma("pool", dst, src, writes=[key])

    wgk, wuk, wdk = ["wg"], ["wu"], ["wd"]
    if CAST_DMA:
        H = DFF // 2
        wgk, wuk, wdk = [], [], []
        for hh in range(2):
            for c in range(8):
                wgk.append(("wg", c, hh))
                load_cast(wg_sb[:, c, hh * H:(hh + 1) * H], wgv[:, c, hh * H:(hh + 1) * H], wgk[-1])
                wuk.append(("wu", c, hh))
                load_cast(wu_sb[:, c, hh * H:(hh + 1) * H], wuv[:, c, hh * H:(hh + 1) * H], wuk[-1])
        for j in range(0, NF, 2):
            wdk.append(("wd", j))
            load_cast(wd_sb[:, j:j + 2, :], wdv[:, j:j + 2, :], wdk[-1])
    else:
        st_ = Stager(C, tag + "_stg", cols=DFF // 8)
        for c in range(8):
            st_.load(wg_sb[:, c, :], wgv[:, c, :], "wg")
            st_.load(wu_sb[:, c, :], wuv[:, c, :], "wu")
        for j in range(NF):
            st_.load(wd_sb[:, j, :], wdv[:, j, :], "wd")

    xt, hT, aT, rs = bufs["xt"], bufs["hT"], bufs["aT"], bufs["rs"]
    sq2 = bufs["sq2"]
    sg = bufs["sg"]
    ps = C.psum

    def tsl_(it):
        return slice(it * T, (it + 1) * T)

    def load_x(it):
        b = it % 2
        P.dma("sp", xt[b][:, :, :], xin[:, :, tsl_(it)], writes=[("xt", b)])

    def sq_op(it, c):
        b = it % 2
        P.op("act", lambda e: e.activation(out=sq2[:, c % 2, :], in_=xt[b][:, c, :], func=AF.Square),
             reads=[("xt", b)], writes=[("sq2", c % 2)])

    def ones_mm(it, c):
        P.op("pe", lambda e: e.matmul(ps[0][:, :T], lhsT=ones[:, :], rhs=sq2[:, c % 2, :], start=(c == 0), stop=(c == 7)),
             reads=[("sq2", c % 2), "ones"], writes=[("ps", 0)])

    def norm_back(it):
        b = it % 2
        P.op("act", lambda e: e.activation(out=rs[:, :], in_=ps[0][:, :T], func=AF.Sqrt, scale=1.0 / D,
                                           bias=bufs["eps"][:, 0:1]), reads=[("ps", 0), "eps"], writes=["rs"])
        P.op("dve", lambda e: e.reciprocal(out=rs[:, :], in_=rs[:, :]), reads=["rs"], writes=["rs"])
        for c in range(8):
            P.op("dve", lambda e, c=c: e.scalar_tensor_tensor(
                out=hT[:, c, :], in0=xt[b][:, c, :], scalar=nw_sb[:, c:c + 1], in1=rs[:, :],
                op0=ALU.mult, op1=ALU.mult), reads=[("xt", b), "rs", "nw"], writes=["hT"])

    load_x(0)
    for c in range(8):
        sq_op(0, c)
        ones_mm(0, c)
    norm_back(0)
    for it in range(NT):
        b = it % 2
        nxt = it + 1 < NT
        if nxt:
            load_x(it + 1)
        for j in range(NF):
            pg = 1 + (j % 2) * 2
            pu = pg + 1
            fs = slice(j * 128, (j + 1) * 128)
            for c in range(8):
                P.op("pe", lambda e, c=c, fs=fs, pg=pg: e.matmul(ps[pg][:, :T], lhsT=wg_sb[:, c, fs], rhs=hT[:, c, :],
                                                                   start=(c == 0), stop=(c == 7)),
                     reads=([("wg", c_, j // 11) for c_ in range(8)] if CAST_DMA else wgk) + ["hT"], writes=[("ps", pg)])
            for c in range(8):
                P.op("pe", lambda e, c=c, fs=fs, pu=pu: e.matmul(ps[pu][:, :T], lhsT=wu_sb[:, c, fs], rhs=hT[:, c, :],
                                                                   start=(c == 0), stop=(c == 7)),
                     reads=([("wu", c_, j // 11) for c_ in range(8)] if CAST_DMA else wuk) + ["hT"], writes=[("ps", pu)])
            k = j % 2
            P.op("act", lambda e, pg=pg, k=k: e.activation(out=sg[k][:, :], in_=ps[pg][:, :T], func=AF.Silu),
                 reads=[("ps", pg)], writes=[("sg", k)])
            P.op("dve", lambda e, pu=pu, k=k, j=j: e.tensor_tensor(out=aT[:, j, :], in0=ps[pu][:, :T], in1=sg[k][:, :],
                                                                   op=ALU.mult),
                 reads=[("ps", pu), ("sg", k)], writes=[("aT", j)])
        if nxt:
            sq_op(it + 1, 0)
            sq_op(it + 1, 1)
        for i in range(8):
            py = 5 + (i % 2)
            ds = slice(i * 128, (i + 1) * 128)
            for j in range(NF):
                P.op("pe", lambda e, j=j, ds=ds, py=py: e.matmul(ps[py][:, :T], lhsT=wd_sb[:, j, ds], rhs=aT[:, j, :],
                                                                   start=(j == 0), stop=(j == NF - 1)),
                     reads=wdk + [("aT", j)], writes=[("ps", py)])
            P.op("dve", lambda e, i=i, py=py, b=b: e.scalar_tensor_tensor(
                out=xt[b][:, i, :], in0=ps[py][:, :T], scalar=0.5, in1=xt[b][:, i, :],
                op0=ALU.mult, op1=ALU.add),
                reads=[("ps", py), ("xt", b)], writes=[("xt", b)])
            if nxt and i < 4:
                ones_mm(it + 1, 2 * i)
                ones_mm(it + 1, 2 * i + 1)
                if i < 3:
                    sq_op(it + 1, 2 * i + 2)
                    sq_op(it + 1, 2 * i + 3)
                else:
                    norm_back(it + 1)
        P.dma("sp", xout[:, :, tsl_(it)], xt[b][:, :, :], reads=[("xt", b)], writes=[(tag, "out", it)])


def alloc_ffn_bufs(C, cst, T=512):
    b = {}
    b["wg"] = C.sb("wg_sb", [128, 8, DFF], BF16)
    b["wu"] = C.sb("wu_sb", [128, 8, DFF], BF16)
    b["wd"] = C.sb("wd_sb", [128, DFF // 128, D], BF16)
    b["nw"] = C.sb("nw_sb", [128, 8], F32)
    b["xt"] = [C.sb("xt%d" % i, [128, 8, T], F32) for i in range(2)]
    b["hT"] = C.sb("hT", [128, 8, T], BF16)
    b["aT"] = C.sb("aT", [128, DFF // 128, T], BF16)
    b["rs"] = C.sb("rs", [128, T], F32)
    b["sg"] = [C.sb("sg%d" % i, [128, T], F32) for i in range(2)]
    b["sq2"] = C.sb("sq2", [128, 2, T], BF16)
    b["ones"] = cst["ones"]
    b["eps"] = cst["eps"]
    return b


class Stager:
    def __init__(self, C, name, cols=704, n=2):
        self.C = C

    def load(self, dst, src, key, np_=128):
        P = self.C.P
        n = src.shape[-1]
        step = 1536
        for c0 in range(0, n, step):
            c1 = min(n, c0 + step)
            P.dma("pool", dst[:, c0:c1], src[:, c0:c1], multi=[key])


def emit_rmsnorm(C, xt, xkey, nw_sb, sq, sqkeys, hT, rs, cst, T, psb=0, hkey="hT"):
    P = C.P
    ps = C.psum
    P.op("act", lambda e: e.activation(out=sq[:, 0:8, :], in_=xt[:, :, :], func=AF.Square),
         reads=[xkey], writes=list(sqkeys))
    for c in range(8):
        P.op("pe", lambda e, c=c: e.matmul(ps[psb][:, :T], lhsT=cst["ones"][:, :], rhs=sq[:, c, :],
                                             start=(c == 0), stop=(c == 7)),
             reads=[sqkeys[c], "ones"], writes=[("ps", psb)])
    P.op("act", lambda e: e.activation(out=rs[:, :], in_=ps[psb][:, :T], func=AF.Sqrt,
                                       scale=1.0 / D, bias=cst["eps"][:, 0:1]),
         reads=[("ps", psb), "eps"], writes=["rs"])
    P.op("dve", lambda e: e.reciprocal(out=rs[:, :], in_=rs[:, :]), reads=["rs"], writes=["rs"])
    for c in range(8):
        P.op("dve", lambda e, c=c: e.scalar_tensor_tensor(
            out=hT[:, c, :], in0=xt[:, c, :], scalar=nw_sb[:, c:c + 1], in1=rs[:, :],
            op0=ALU.mult, op1=ALU.mult),
            reads=[xkey, "rs", "nw"], writes=[hkey])


def alloc_consts(C, cin):
    P = C.P
    cst = {}
    cst["ones"] = C.sb("ones", [128, 128], BF16)
    cst["eps"] = C.sb("epsc", [128, 1], F32)
    cst["one"] = C.sb("onec", [128, 1], F32)
    cst["identb"] = C.sb("identb", [128, 128], BF16)
    cst["identf"] = C.sb("identf", [128, 128], F32)
    cst["maskT"] = C.sb("maskT", [128, 128], F32)
    cst["maskTs"] = C.sb("maskTs", [128, 128], F32)
    cst["triu"] = C.sb("triu", [128, 128], BF16)
    P.op("pool", lambda e: e.memset(cst["ones"][:, :], 1.0), writes=["ones"])
    P.op("pool", lambda e: e.memset(cst["eps"][:, :], EPS), writes=["eps"])
    P.op("pool", lambda e: e.memset(cst["one"][:, :], 1.0), writes=["one"])
    P.dma("sp", cst["identf"][:, :], cin["identf"], writes=["identf"])
    P.dma("sp", cst["maskT"][:, :], cin["maskT"], writes=["maskT"])
    P.dma("sp", cst["maskTs"][:, :], cin["maskTs"], writes=["maskTs"])
    P.op("pool", lambda e: e.tensor_copy(out=cst["identb"][:, :], in_=cst["identf"][:, :]),
         reads=["identf"], writes=["identb"])
    cst["tmpf"] = C.sb("tmpf", [128, 128], F32)
    P.dma("sp", cst["tmpf"][:, :], cin["triu"], writes=["tmpf"])
    P.op("pool", lambda e: e.tensor_copy(out=cst["triu"][:, :], in_=cst["tmpf"][:, :]),
         reads=["tmpf"], writes=["triu"])
    return cst


def host_consts():
    i = np.arange(128)
    c = {}
    c["identf"] = np.eye(128, dtype=np.float32)
    c["maskT"] = (i[:, None] <= i[None, :]).astype(np.float32)
    c["maskTs"] = (i[:, None] < i[None, :]).astype(np.float32)
    c["triu"] = (i[:, None] > i[None, :]).astype(np.float32)
    c["invc"] = np.broadcast_to((1.0 / (np.arange(16) + 1.0)).astype(np.float32)[None, :], (128, 16)).copy()
    oh = np.zeros((4, 4, 128), np.float32)
    for h in range(4):
        oh[h, h, :] = 1.0
    c["onehot4"] = oh.reshape(4, 512)
    oh = np.zeros((16, 16, 128), np.float32)
    for h in range(16):
        oh[h, h, :] = 1.0
    c["onehot16"] = oh.reshape(16, 2048)
    c["blk1"] = ((i[:, None] // 64) == (i[None, :] // 64)).astype(np.float32)
    c["negm"] = np.where(i[:, None] > i[None, :], -30000.0, 0.0).astype(np.float32)
    return c


def conv_silu_pe(C, cst, tagp, src, nch, cw, cb, sink):
    P = C.P
    ps = C.psum
    xrow = [C.sb("%s_xrow%d" % (tagp, i), [128, 3 + S], BF16) for i in range(2)]
    dg = [C.sb("%s_dg%d" % (tagp, i), [128, 4, 128], BF16) for i in range(2)]
    for k in range(2):
        P.op("pool", lambda e, k=k: e.memset(xrow[k][:, 0:3], 0.0), writes=[(tagp, "xrow", k)])
    n = 0
    for m in range(nch):
        k = m % 2
        for h_ in range(4):
            P.dma("pool", xrow[k][:, 3 + h_ * 1024:3 + (h_ + 1) * 1024], src[m * 128:(m + 1) * 128, h_ * 1024:(h_ + 1) * 1024],
                  multi=[(tagp, "xrow", k)])
        for j in range(4):
            P.op("dve", lambda e, k=k, m=m, j=j: e.tensor_scalar(out=dg[k][:, j, :], in0=cst["identf"][:, :],
                                                                 scalar1=cw[:, m, j:j + 1], scalar2=None, op0=ALU.mult),
                 reads=["identf", "cw"], writes=[(tagp, "dg", k)])
        for it in range(S // 512):
            pb = 1 + n % 4
            n += 1
            for j in range(4):
                P.op("pe", lambda e, k=k, j=j, it=it, pb=pb: e.matmul(
                    ps[pb][:, :512], lhsT=dg[k][:, j, :], rhs=xrow[k][:, j + it * 512:j + it * 512 + 512],
                    start=(j == 0), stop=(j == 3)), reads=[(tagp, "dg", k), (tagp, "xrow", k)], writes=[("ps", pb)])
            sink(m, it, ps[pb][:, :512], cb[:, m:m + 1], ("ps", pb))


def evac(P, eng, out, in_, reads, writes):
    if eng == "act":
        return P.op("act", lambda e: e.activation(out=out, in_=in_, func=AF.Copy), reads=reads, writes=writes)
    return P.op(eng, lambda e: e.tensor_copy(out=out, in_=in_), reads=reads, writes=writes)


def proj_norm_tiles(C, cst, xT_in, nw_dram, T, body):
    P = C.P
    xin = xT_in.rearrange("(c p) t -> p c t", p=128)
    nw_sb = C.sb("pn_nw", [128, 8], F32)
    xt = [C.sb("pn_xt%d" % i, [128, 8, T], F32) for i in range(2)]
    sq = C.sb("pn_sq", [128, 8, T], BF16)
    hT = [C.sb("pn_hT%d" % i, [128, 8, T], BF16) for i in range(2)]
    rs = C.sb("pn_rs", [128, T], F32)
    P.dma("sp", nw_sb[:, :], nw_dram, writes=["nw"])
    NT = S // T

    def norm(it):
        b = it % 2
        P.dma("sp", xt[b][:, :, :], xin[:, :, it * T:(it + 1) * T], writes=[("pn_xt", b)])
        emit_rmsnorm(C, xt[b], ("pn_xt", b), nw_sb, sq, [("pn_sq", c) for c in range(8)], hT[b], rs, cst, T,
                     hkey=("hT", b))

    norm(0)
    for it in range(NT):
        if it + 1 < NT:
            norm(it + 1)
        body(it, hT[it % 2], ("hT", it % 2))


def even_mixer_phase(C, cst, cin, xT_in, xT_out, W, upto="E"):
    P = C.P
    ps = C.psum
    T = 512
    uT = C.dram("e_uT", [512, S], F32)
    qkT = C.dram("e_qkT", [1024, S], F32)
    v_tm = C.dram("e_v", [S, 512], BF16)
    o_tm = C.dram("e_o", [S, 512], F32)
    gT = C.dram("e_g", [8, S], F32)
    mixT = C.dram("e_mix", [1024, S], BF16)

    ws_tm = C.sb("e_ws", [128, 32, 4], F32)
    thr_tm = C.sb("e_thr", [128, 32, 4], F32)
    dcol = C.sb("e_dcol", [128, 4, 32], F32)

    with C.scope():
        w_sb = C.sb("e_win", [128, 8, 2568], BF16)
        stg = Stager(C, "e_stg")
        win = W["w_in"].rearrange("(c p) f -> p c f", p=128)
        for c in range(8):
            stg.load(w_sb[:, c, :], win[:, c, :], "win")
        fm = [C.sb("e_fm%d" % i, [128, T], F32) for i in range(6)]
        vst = [C.sb("e_vst%d" % i, [128, 512], BF16) for i in range(4)]
        ost = [C.sb("e_ost%d" % i, [128, 512], F32) for i in range(4)]
        gst = [C.sb("e_gst%d" % i, [4, T], F32) for i in range(2)]
        cnt = {"fm": 0, "tm": 0, "g": 0}

        def body(it, hT, hk):
            tsl = slice(it * T, (it + 1) * T)
            for m in range(12):
                pb = 1 + m % 4
                for c in range(8):
                    P.op("pe", lambda e, c=c, m=m, pb=pb: e.matmul(
                        ps[pb][:, :T], lhsT=w_sb[:, c, m * 128:(m + 1) * 128], rhs=hT[:, c, :],
                        start=(c == 0), stop=(c == 7)), reads=["win", hk], writes=[("ps", pb)])
                k = cnt["fm"] % 6
                cnt["fm"] += 1
                evac(P, "act" if m % 2 == 0 else "dve", fm[k][:, :], ps[pb][:, :T], [("ps", pb)], [("fm", k)])
                dst = uT[m * 128:(m + 1) * 128, tsl] if m < 4 else qkT[(m - 4) * 128:(m - 3) * 128, tsl]
                P.dma("sp", dst, fm[k][:, :], reads=[("fm", k)], writes=[("A_out", m, it)])
            for q in range(4):
                tok = slice(q * 128, (q + 1) * 128)
                r0 = it * T + q * 128
                for which, col0, pb, stb, dstT in (("v", 1536, 5, vst, v_tm), ("o", 2048, 6, ost, o_tm)):
                    for c in range(8):
                        P.op("pe", lambda e, c=c, tok=tok, col0=col0, pb=pb: e.matmul(
                            ps[pb][:, :512], lhsT=hT[:, c, tok], rhs=w_sb[:, c, col0:col0 + 512],
                            start=(c == 0), stop=(c == 7)), reads=["win", hk], writes=[("ps", pb)])
                    k = q % 4
                    evac(P, "act" if which == "v" else "dve", stb[k][:, :], ps[pb][:, :512],
                         [("ps", pb)], [(which + "st", k)])
                    P.dma("sp", dstT[r0:r0 + 128, :], stb[k][:, :], reads=[(which + "st", k)],
                          writes=[("A_out", which, r0)])
            for gi_ in range(2):
                col0 = 2560 + 4 * gi_
                for c in range(8):
                    P.op("pe", lambda e, c=c, col0=col0: e.matmul(
                        ps[7][0:4, :T], lhsT=w_sb[:, c, col0:col0 + 4], rhs=hT[:, c, :],
                        start=(c == 0), stop=(c == 7)), reads=["win", hk], writes=[("ps", 7)])
                evac(P, "dve", gst[gi_][:, :], ps[7][0:4, :T], [("ps", 7)], [("gst", gi_)])
                P.dma("sp", gT[4 * gi_:4 * gi_ + 4, tsl], gst[gi_][:, :], reads=[("gst", gi_)],
                      writes=[("A_out", "g", gi_, it)])

        proj_norm_tiles(C, cst, xT_in, W["mix_norm"], T, body)

    if upto == "A":
        return
    with C.scope():
        PADL = 16
        ub = C.sb("e_ub", [128, PADL + S], F32)
        sA = C.sb("e_sA", [128, PADL + S], F32)
        sB = C.sb("e_sB", [128, PADL + S], F32)
        pooled = C.sb("e_pooled", [128, S], BF16)
        aout = C.sb("e_aout", [128, S], BF16)
        pw_sb = C.sb("e_pw", [128, 4, 128], BF16)
        psc = C.sb("e_psc", [128, 4], F32)
        invc = C.sb("e_invc", [128, 16], F32)
        tmpc = C.sb("e_tmpc", [128, 16], F32)
        stg = Stager(C, "e_stgB", cols=512)
        stg.load(pw_sb.rearrange("p a b -> p (a b)"), W["pool_w"], "pw")
        P.dma("sp", psc[:, :], W["pool_scale"], writes=["psc"])
        P.dma("sp", invc[:, :], cin["invc"], writes=["invc"])
        for bname, buf in (("ub", ub), ("sA", sA), ("sB", sB)):
            P.op("pool", lambda e, buf=buf: e.memset(buf[:, 0:PADL], 0.0), writes=[bname])
        for g in range(4):
            win_ = 2 << g
            P.dma("sp", ub[:, PADL:], uT[g * 128:(g + 1) * 128, :], reads=["ub"], writes=["ub"])
            src, sname = ub, "ub"
            dsts = [(sA, "sA"), (sB, "sB")]
            for k in range(g + 1):
                sh = 1 << k
                dst, dname = dsts[k % 2]
                P.op("dve", lambda e, src=src, dst=dst, sh=sh: e.tensor_tensor(
                    out=dst[:, PADL:], in0=src[:, PADL:], in1=src[:, PADL - sh:PADL - sh + S], op=ALU.add),
                    reads=[sname], writes=[dname])
                src, sname = dst, dname
            P.op("dve", lambda e, src=src, win_=win_: e.scalar_tensor_tensor(
                out=pooled[:, :], in0=src[:, PADL:], scalar=1.0 / win_, in1=ub[:, PADL:],
                op0=ALU.mult, op1=ALU.subtract), reads=[sname, "ub"], writes=["pooled"])
            nfix = win_ - 1
            P.op("dve", lambda e, src=src, nfix=nfix: e.tensor_tensor(
                out=tmpc[:, 0:nfix], in0=src[:, PADL:PADL + nfix], in1=invc[:, 0:nfix], op=ALU.mult),
                reads=[sname, "invc"], writes=["tmpc"])
            P.op("dve", lambda e, nfix=nfix: e.tensor_tensor(
                out=pooled[:, 0:nfix], in0=tmpc[:, 0:nfix], in1=ub[:, PADL:PADL + nfix], op=ALU.subtract),
                reads=["tmpc", "ub", "pooled"], writes=["pooled"])
            for it in range(S // T):
                pb = 1 + it % 2
                P.op("pe", lambda e, g=g, it=it, pb=pb: e.matmul(
                    ps[pb][:, :T], lhsT=pw_sb[:, g, :], rhs=pooled[:, it * T:(it + 1) * T], start=True, stop=True),
                    reads=["pw", "pooled"], writes=[("ps", pb)])
                P.op("dve", lambda e, g=g, it=it, pb=pb: e.tensor_scalar(
                    out=aout[:, it * T:(it + 1) * T], in0=ps[pb][:, :T], scalar1=psc[:, g:g + 1], scalar2=None,
                    op0=ALU.mult), reads=[("ps", pb), "psc"], writes=["aout"])
            P.dma("sp", mixT[g * 128:(g + 1) * 128, :], aout[:, :], reads=["aout"], writes=[("mixA", g)])

    if upto == "B":
        return
    with C.scope():
        gi = C.sb("e_gi", [4, S], F32)
        gf = C.sb("e_gf", [4, S], F32)
        Bc = C.sb("e_Bc", [4, S], F32)
        Ac = C.sb("e_Ac", [4, S], F32)
        Gm = C.sb("e_Gm", [4, S], F32)
        gb = C.sb("e_gb", [4, 2], F32)
        nbf = C.sb("e_nbf", [4, 1], F32)
        mucol = C.sb("e_mucol", [4, 33], F32)
        dd = C.sb("e_dd", [4, 32], F32)
        oh4 = C.sb("e_oh4", [4, 4, 128], F32)
        P.dma("sp", gi[:, :], gT[0:4, :], writes=["gi"])
        P.dma("sp", gf[:, :], gT[4:8, :], writes=["gf"])
        P.dma("sp", gb[:, :], W["gate_bias"], writes=["gb"])
        P.dma("sp", oh4.rearrange("p a b -> p (a b)"), cin["onehot4"], writes=["oh4"])
        one4 = cst["one"][0:4, 0:1]
        P.op("dve", lambda e: e.tensor_scalar(out=gi[:, :], in0=gi[:, :], scalar1=gb[:, 0:1], scalar2=None, op0=ALU.add),
             reads=["gi", "gb"], writes=["gi"])
        P.op("dve", lambda e: e.tensor_scalar(out=nbf[:, :], in0=gb[:, 1:2], scalar1=-1.0, scalar2=None, op0=ALU.mult),
             reads=["gb"], writes=["nbf"])
        P.op("act", lambda e: e.activation(out=gf[:, :], in_=gf[:, :], func=AF.Exp, scale=-1.0, bias=nbf[:, 0:1]),
             reads=["gf", "nbf"], writes=["gf"])
        P.op("act", lambda e: e.activation(out=gf[:, :], in_=gf[:, :], func=AF.Ln, scale=1.0, bias=one4),
             reads=["gf", "one"], writes=["gf"])
        P.op("dve", lambda e: e.tensor_tensor_scan(out=Bc[:, :], data0=one4.to_broadcast([4, S]), data1=gf[:, :],
                                                   initial=0.0, op0=ALU.mult, op1=ALU.subtract),
             reads=["gf", "one"], writes=["Bc"])
        P.op("dve", lambda e: e.tensor_tensor(out=Ac[:, :], in0=gi[:, :], in1=Bc[:, :], op=ALU.subtract),
             reads=["gi", "Bc"], writes=["Ac"])
        P.op("dve", lambda e: e.tensor_tensor_scan(out=Gm[:, :], data0=Ac[:, :], data1=Ac[:, :],
                                                   initial=0.0, op0=ALU.max, op1=ALU.max),
             reads=["Ac"], writes=["Gm"])
        Gend = Gm.rearrange("h (c l) -> h c l", l=128)[:, :, 127:128]
        P.op("dve", lambda e: e.memset(mucol[:, 0:1], 0.0), writes=["mucol"])
        P.op("dve", lambda e: e.tensor_copy(out=mucol[:, 1:33].unsqueeze(2), in_=Gend), reads=["Gm", "mucol"],
             writes=["mucol"])
        P.op("dve", lambda e: e.tensor_tensor(out=dd[:, :], in0=mucol[:, 0:32], in1=mucol[:, 1:33], op=ALU.subtract),
             reads=["mucol"], writes=["dd"])
        P.op("act", lambda e: e.activation(out=dd[:, :], in_=dd[:, :], func=AF.Exp), reads=["dd"], writes=["dd"])
        A3 = Ac.rearrange("h (c l) -> h c l", l=128)
        B3 = Bc.rearrange("h (c l) -> h c l", l=128)
        P.op("dve", lambda e: e.tensor_tensor(out=A3, in0=A3, in1=Gend.to_broadcast([4, 32, 128]), op=ALU.subtract),
             reads=["Ac", "Gm"], writes=["Ac"])
        P.op("act", lambda e: e.activation(out=Ac[:, :], in_=Ac[:, :], func=AF.Exp), reads=["Ac"], writes=["Ac"])
        P.op("dve", lambda e: e.tensor_tensor(out=B3, in0=B3, in1=Gend.to_broadcast([4, 32, 128]), op=ALU.add),
             reads=["Bc", "Gm"], writes=["Bc"])
        P.op("act", lambda e: e.activation(out=Bc[:, :], in_=Bc[:, :], func=AF.Exp, scale=-1.0), reads=["Bc"],
             writes=["Bc"])
        for h in range(4):
            P.op("pe", lambda e, h=h: e.matmul(ps[1][:, h * 32:(h + 1) * 32], lhsT=oh4[:, h, :], rhs=dd[:, :],
                                                start=True, stop=True), reads=["oh4", "dd"], writes=[("ps", 1)])
        P.op("dve", lambda e: e.tensor_copy(out=dcol.rearrange("p a b -> p (a b)"), in_=ps[1][:, 0:128]),
             reads=[("ps", 1)], writes=["dcol"])
        for src, sname, dst, dname, pb in ((Ac, "Ac", ws_tm, "ws_tm", 2), (Bc, "Bc", thr_tm, "thr_tm", 3)):
            for c in range(32):
                P.op("pe", lambda e, src=src, c=c, pb=pb: e.transpose(
                    out=ps[pb][:, c * 4:(c + 1) * 4], in_=src[:, c * 128:(c + 1) * 128], identity=cst["identf"][0:4, 0:4]),
                    reads=[sname, "identf"], writes=[("ps", pb)])
            P.op("dve", lambda e, dst=dst, pb=pb: e.tensor_copy(out=dst.rearrange("p a b -> p (a b)"),
                                                               in_=ps[pb][:, 0:128]),
                 reads=[("ps", pb)], writes=[dname])

    if upto == "C":
        return
    with C.scope():
        qkb = C.sb("e_qkb", [128, 8, S], BF16)
        cw = C.sb("e_cw", [128, 8, 4], F32)
        cb = C.sb("e_cb", [128, 8], F32)
        qtmp = [C.sb("e_qtmp%d" % i, [128, 512], F32) for i in range(2)]
        P.dma("sp", cw.rearrange("p a b -> p (a b)"), W["qk_conv_w"], writes=["cw"])
        P.dma("sp", cb[:, :], W["qk_conv_b"], writes=["cb"])
        qcnt = [0]

        def sink_e(m, it, psap, bias, pkey):
            tsl = slice(it * 512, (it + 1) * 512)
            if m < 4:
                k = qcnt[0] % 2
                qcnt[0] += 1
                P.op("act", lambda e: e.activation(out=qtmp[k][:, :], in_=psap, func=AF.Silu, bias=bias),
                     reads=[pkey, "cb"], writes=[("qtmp", k)])
                P.op("dve", lambda e: e.tensor_scalar(out=qkb[:, m, tsl], in0=qtmp[k][:, :], scalar1=128.0 ** -0.5,
                                                       scalar2=None, op0=ALU.mult),
                     reads=[("qtmp", k)], writes=[("qkb", m)])
            else:
                P.op("act", lambda e: e.activation(out=qkb[:, m, tsl], in_=psap, func=AF.Silu, bias=bias),
                     reads=[pkey, "cb"], writes=[("qkb", m)])

        conv_silu_pe(C, cst, "ecv", qkT, 8, cw, cb, sink_e)

        if upto == "D0":
            return
        vch = [C.sb("e_vch%d" % i, [128, 4, 128], BF16) for i in range(2)]
        och = [C.sb("e_och%d" % i, [128, 512], F32) for i in range(2)]
        vw = [C.sb("e_vw%d" % i, [128, 4, 136], BF16) for i in range(2)]
        ktm = [C.sb("e_ktm%d" % i, [128, 4, 128], BF16) for i in range(2)]
        PT = C.sb("e_PT", [128, 4, 128], BF16)
        Sst = C.sb("e_S", [128, 4, 129], F32)
        Sbf = C.sb("e_Sbf", [128, 4, 136], BF16)
        nwb = C.sb("e_nwb", [128, 512], F32)
        nwo = C.sb("e_nwo", [128, 512], F32)
        den = C.sb("e_den", [128, 4], F32)
        ss = C.sb("e_ss", [128, 4], F32)
        junk = C.sb("e_junk", [128, 128], F32)
        bout = C.sb("e_bout", [128, 4, 128], BF16)
        boutT = [C.sb("e_boutT%d" % i, [128, 4, 128], BF16) for i in range(2)]
        P.dma("sp", nwb[:, :], W["mlstm_norm"].partition_broadcast(128), writes=["nwb"])
        P.op("pool", lambda e: e.memset(Sst.rearrange("p a b -> p (a b)"), 0.0), writes=["S0", "S1", "S2", "S3"])
        ps0b = ps[0][:, :].bitcast(BF16)
        pending_tail = []
        for c in range(1 if upto in ("D1", "D2", "D3") else 32):
            b = c % 2
            blk = slice(c * 128, (c + 1) * 128)
            P.dma("sp", vch[b].rearrange("p a b -> p (a b)"), v_tm[blk, :], writes=[("vch", b)])
            P.dma("sp", och[b][:, :], o_tm[blk, :], writes=[("och", b)])
            P.op("dve", lambda e, b=b, c=c: e.tensor_tensor(
                out=vw[b][:, :, 0:128], in0=vch[b][:, :, :],
                in1=ws_tm[:, c, :].unsqueeze(2).to_broadcast([128, 4, 128]), op=ALU.mult),
                reads=[("vch", b), "ws_tm"], writes=[("vw", b)])
            P.op("dve", lambda e, b=b, c=c: e.tensor_copy(out=vw[b][:, :, 128:129], in_=ws_tm[:, c, :].unsqueeze(2)),
                 reads=["ws_tm", ("vw", b)], writes=[("vw", b)])
            for h in range(4):
                P.op("pe", lambda e, h=h, blk=blk: e.transpose(out=ps0b[:, h * 128:(h + 1) * 128], in_=qkb[:, 4 + h, blk],
                                                                identity=cst["identb"][:, :]),
                     reads=[("qkb", 4 + h), "identb"], writes=[("ps0", "k")])
            P.op("act", lambda e, b=b: e.activation(out=ktm[b].rearrange("p a b -> p (a b)"), in_=ps0b[:, 0:512],
                                                    func=AF.Copy), reads=[("ps0", "k")], writes=[("ktm", b)])
            sb_ = 1 + b
            for h in range(4):
                P.op("pe", lambda e, h=h, blk=blk, sb_=sb_: e.matmul(
                    ps[sb_][:, h * 128:(h + 1) * 128], lhsT=qkb[:, 4 + h, blk], rhs=qkb[:, h, blk],
                    start=True, stop=True), reads=[("qkb", 4 + h), ("qkb", h)], writes=[("ps", sb_)])
            P.op("dve", lambda e, sb_=sb_: e.tensor_tensor(
                out=PT[:, :, :], in0=ps[sb_][:, :].rearrange("p (a b) -> p a b", b=128),
                in1=cst["maskT"][:, :].unsqueeze(1).to_broadcast([128, 4, 128]), op=ALU.mult),
                reads=[("ps", sb_), "maskT"], writes=["PT"])
            if upto == "D1":
                continue
            for h in range(4):
                ob = 3 + h // 2
                oc = (h % 2) * 129
                P.op("dve", lambda e, h=h, c=c: e.tensor_scalar(out=Sbf[:, h, 0:129], in0=Sst[:, h, :],
                                                                 scalar1=dcol[:, h, c:c + 1], scalar2=None, op0=ALU.mult),
                     reads=["S%d" % h, "dcol"], writes=["Sbf%d" % h])
                P.op("pe", lambda e, h=h, blk=blk, ob=ob, oc=oc: e.matmul(
                    ps[ob][:, oc:oc + 129], lhsT=qkb[:, h, blk], rhs=Sbf[:, h, 0:129], start=True, stop=False),
                    reads=[("qkb", h), "Sbf%d" % h], writes=[("pso", h)])
                P.op("pe", lambda e, h=h, b=b, ob=ob, oc=oc: e.matmul(
                    ps[ob][:, oc:oc + 129], lhsT=PT[:, h, :], rhs=vw[b][:, h, 0:129], start=False, stop=True),
                    reads=["PT", ("vw", b)], writes=[("pso", h)])
                db = 5 + h // 2
                P.op("pe", lambda e, h=h, b=b, db=db, oc=oc: e.matmul(
                    ps[db][:, oc:oc + 129], lhsT=ktm[b][:, h, :], rhs=vw[b][:, h, 0:129], start=True, stop=True),
                    reads=[("ktm", b), ("vw", b)], writes=[("psd", h)])
                P.op("dve", lambda e, h=h, c=c, db=db, oc=oc: e.scalar_tensor_tensor(
                    out=Sst[:, h, :], in0=Sst[:, h, :], scalar=dcol[:, h, c:c + 1], in1=ps[db][:, oc:oc + 129],
                    op0=ALU.mult, op1=ALU.add), reads=["S%d" % h, "dcol", ("psd", h), "Sbf%d" % h], writes=["S%d" % h])
            if upto == "D2":
                continue
            while pending_tail:
                pending_tail.pop(0)()
            P.op("act", lambda e, b=b: e.activation(out=nwo[:, :], in_=och[b][:, :], func=AF.Sigmoid),
                 reads=[("och", b)], writes=["nwo"])
            P.op("dve", lambda e: e.tensor_tensor(out=nwo[:, :], in0=nwo[:, :], in1=nwb[:, :], op=ALU.mult),
                 reads=["nwo", "nwb"], writes=["nwo"])
            for hp in range(2):
                ob = 3 + hp
                Dv = ps[ob][:, 0:258].rearrange("p (a b) -> p a b", b=129)[:, :, 128:129]
                P.op("act", lambda e, hp=hp, Dv=Dv: e.activation(
                    out=den[:, 2 * hp:2 * hp + 2].unsqueeze(2), in_=Dv, func=AF.Abs),
                    reads=[("pso", 2 * hp), ("pso", 2 * hp + 1)], writes=["den"])
            P.op("dve", lambda e, c=c: e.tensor_tensor(out=den[:, :], in0=den[:, :], in1=thr_tm[:, c, :], op=ALU.max),
                 reads=["den", "thr_tm"], writes=["den"])
            P.op("dve", lambda e: e.reciprocal(out=den[:, :], in_=den[:, :]), reads=["den"], writes=["den"])
            for h in range(4):
                ob = 3 + h // 2
                oc = (h % 2) * 129
                P.op("act", lambda e, h=h, ob=ob, oc=oc: e.activation(
                    out=junk[:, :], in_=ps[ob][:, oc:oc + 128], func=AF.Square, scale=den[:, h:h + 1],
                    accum_out=ss[:, h:h + 1]), reads=[("pso", h), "den"], writes=["junk", ("ss", h)])
            P.op("act", lambda e: e.activation(out=ss[:, :], in_=ss[:, :], func=AF.Sqrt, scale=1.0 / 128, bias=cst["eps"][:, 0:1]),
                 reads=[("ss", h) for h in range(4)] + ["eps"], writes=[("ss", h) for h in range(4)])
            P.op("dve", lambda e: e.reciprocal(out=ss[:, :], in_=ss[:, :]), reads=[("ss", h) for h in range(4)],
                 writes=[("ss", h) for h in range(4)])
            P.op("dve", lambda e: e.tensor_tensor(out=ss[:, :], in0=ss[:, :], in1=den[:, :], op=ALU.mult),
                 reads=[("ss", h) for h in range(4)] + ["den"], writes=[("ss", h) for h in range(4)])
            for h in range(4):
                ob = 3 + h // 2
                oc = (h % 2) * 129
                P.op("dve", lambda e, h=h, ob=ob, oc=oc: e.scalar_tensor_tensor(
                    out=bout[:, h, :], in0=ps[ob][:, oc:oc + 128], scalar=ss[:, h:h + 1], in1=nwo[:, h * 128:(h + 1) * 128],
                    op0=ALU.mult, op1=ALU.mult), reads=[("pso", h), ("ss", h), "nwo"], writes=[("bout", h)])
            def emit_tail(c=c, b=b, blk=blk):
                for h in range(4):
                    P.op("pe", lambda e, h=h: e.transpose(out=ps0b[:, 512 + h * 128:512 + (h + 1) * 128], in_=bout[:, h, :],
                                                          identity=cst["identb"][:, :]),
                         reads=[("bout", h), "identb"], writes=[("ps0", "b")])
                P.op("act", lambda e, b=b: e.activation(out=boutT[b].rearrange("p a b -> p (a b)"), in_=ps0b[:, 512:1024],
                                                        func=AF.Copy), reads=[("ps0", "b")], writes=[("boutT", b)])
                P.dma("sp", mixT[512:1024, blk].rearrange("(h p) t -> p h t", p=128), boutT[b][:, :, :],
                      reads=[("boutT", b)], writes=[("mixB", c)])
            pending_tail.append(emit_tail)
        for f_ in pending_tail:
            f_()

    if upto[0] == "D":
        return
    with C.scope():
        wo = C.sb("e_wo", [128, 8, D], BF16)
        stg = Stager(C, "e_stgE")
        wov = W["w_out"].rearrange("(c p) f -> p c f", p=128)
        for c in range(8):
            stg.load(wo[:, c, :], wov[:, c, :], "wo")
        out_proj(C, wo, 8, mixT, xT_in, xT_out, T)


def out_proj(C, wo, nk, mixT, xT_in, xT_out, T):
    P = C.P
    ps = C.psum
    xin = xT_in.rearrange("(c p) t -> p c t", p=128)
    xout = xT_out.rearrange("(c p) t -> p c t", p=128)
    mixv = mixT.rearrange("(c p) t -> p c t", p=128)
    xt = [C.sb("op_xt%d" % i, [128, 8, T], F32) for i in range(2)]
    mt = [C.sb("op_mt%d" % i, [128, nk, T], BF16) for i in range(2)]
    for it in range(S // T):
        b = it % 2
        tsl = slice(it * T, (it + 1) * T)
        P.dma("sp", xt[b][:, :, :], xin[:, :, tsl], writes=[("op_xt", b)])
        P.dma("act", mt[b][:, :, :], mixv[:, :, tsl], writes=[("op_mt", b)])
        for i in range(8):
            pb = 1 + i % 4
            for k in range(nk):
                P.op("pe", lambda e, i=i, k=k, pb=pb, b=b: e.matmul(
                    ps[pb][:, :T], lhsT=wo[:, k, i * 128:(i + 1) * 128], rhs=mt[b][:, k, :],
                    start=(k == 0), stop=(k == nk - 1)), reads=["wo", ("op_mt", b)], writes=[("ps", pb)])
            P.op("dve", lambda e, i=i, pb=pb, b=b: e.tensor_tensor(
                out=xt[b][:, i, :], in0=ps[pb][:, :T], in1=xt[b][:, i, :], op=ALU.add),
                reads=[("ps", pb), ("op_xt", b)], writes=[("op_xt", b)])
        P.dma("pool", xout[:, :, tsl], xt[b][:, :, :], reads=[("op_xt", b)], writes=[("op_out", it)])


def _pc(v, nchunk):
    return np.ascontiguousarray(np.asarray(v, np.float32).reshape(nchunk, 128).T)


def host_params(inp):
    f = lambda k: np.ascontiguousarray(np.asarray(inp[k], np.float32))
    out = {}
    for pre in ("l0_ffn1", "l0_ffn2", "l1_ffn1", "l1_ffn2"):
        out[pre + "_norm"] = _pc(inp[pre + "_norm"], 8)
        for w in ("wg", "wu", "wd"):
            out[pre + "_" + w] = f(pre + "_" + w)
    out["l0_mix_norm"] = _pc(inp["l0_mix_norm"], 8)
    out["l0_w_in"] = f("l0_w_in")
    out["l0_pool_w"] = np.ascontiguousarray(np.transpose(f("l0_pool_w"), (1, 0, 2)).reshape(128, 512))
    out["l0_pool_scale"] = _pc(inp["l0_pool_scale"], 4)
    out["l0_qk_conv_w"] = np.ascontiguousarray(f("l0_qk_conv_w").reshape(4, 8, 128).transpose(2, 1, 0).reshape(128, 32))
    out["l0_qk_conv_b"] = _pc(inp["l0_qk_conv_b"], 8)
    out["l0_gate_bias"] = np.ascontiguousarray(f("l0_gate_bias").reshape(2, 4).T)
    out["l0_mlstm_norm"] = f("l0_mlstm_norm")
    out["l0_w_out"] = f("l0_w_out")
    out["l1_mix_norm"] = _pc(inp["l1_mix_norm"], 8)
    out["l1_w_in"] = f("l1_w_in")
    out["l1_ssd_conv_w"] = np.ascontiguousarray(f("l1_ssd_conv_w").reshape(4, 16, 128).transpose(2, 1, 0).reshape(128, 64))
    out["l1_ssd_conv_b"] = _pc(inp["l1_ssd_conv_b"], 16)
    out["l1_ssd_dt_bias"] = f("l1_ssd_dt_bias").reshape(16, 1)
    out["l1_ssd_A_log"] = f("l1_ssd_A_log").reshape(16, 1)
    out["l1_ssd_D"] = f("l1_ssd_D")
    out["l1_ssd_norm"] = f("l1_ssd_norm")
    out["l1_sb_q_norm"] = np.ascontiguousarray(np.tile(f("l1_sb_q_norm"), 2).reshape(128, 1))
    out["l1_sb_k_norm"] = np.ascontiguousarray(np.tile(f("l1_sb_k_norm"), 2).reshape(128, 1))
    out["l1_w_out"] = f("l1_w_out")
    return out


PARAM_SHAPES = {
    "l0_mix_norm": [128, 8], "l0_w_in": [1024, 2568], "l0_pool_w": [128, 512], "l0_pool_scale": [128, 4],
    "l0_qk_conv_w": [128, 32], "l0_qk_conv_b": [128, 8], "l0_gate_bias": [4, 2], "l0_mlstm_norm": [512],
    "l0_w_out": [1024, 1024],
    "l1_mix_norm": [128, 8], "l1_w_in": [1024, 4624], "l1_ssd_conv_w": [128, 64], "l1_ssd_conv_b": [128, 16],
    "l1_ssd_dt_bias": [16, 1], "l1_ssd_A_log": [16, 1], "l1_ssd_D": [16], "l1_ssd_norm": [1024],
    "l1_sb_q_norm": [128, 1], "l1_sb_k_norm": [128, 1], "l1_w_out": [1536, 1024],
}
for _pre in ("l0_ffn1", "l0_ffn2", "l1_ffn1", "l1_ffn2"):
    PARAM_SHAPES[_pre + "_norm"] = [128, 8]
    PARAM_SHAPES[_pre + "_wg"] = [D, DFF]
    PARAM_SHAPES[_pre + "_wu"] = [D, DFF]
    PARAM_SHAPES[_pre + "_wd"] = [DFF, D]
CONST_SHAPES = {"identf": [128, 128], "maskT": [128, 128], "maskTs": [128, 128], "triu": [128, 128],
                "invc": [128, 16], "onehot4": [4, 512], "onehot16": [16, 2048], "blk1": [128, 128], "negm": [128, 128]}


def odd_mixer_phase(C, cst, cin, xT_in, xT_out, W, upto="E"):
    P = C.P
    ps = C.psum
    T = 512
    o_z = C.dram("o_z", [S, 1024], F32)
    o_xbcT = C.dram("o_xbcT", [2048, S], F32)
    o_dtT = C.dram("o_dtT", [16, S], F32)
    o_qT = C.dram("o_qT", [512, S], BF16)
    o_kT = C.dram("o_kT", [512, S], BF16)
    o_v = C.dram("o_v", [S, 512], BF16)
    o_xc = C.dram("o_xc", [2048, S], BF16)
    mixT = C.dram("o_mix", [1536, S], BF16)

    bias_tm = C.sb("o_bias_tm", [128, 32, 16], F32)
    est_tm = C.sb("o_est_tm", [128, 32, 16], F32)
    dtte_tm = C.sb("o_dtte_tm", [128, 32, 16], F32)
    cdb = C.sb("o_cdb", [128, 32, 16], F32)
    blk1 = C.sb("o_blk1", [128, 128], BF16)
    negm = C.sb("o_negm", [128, 128], BF16)
    eps64 = C.sb("o_eps64", [128, 1], F32)
    P.op("pool", lambda e: e.memset(eps64[:, :], 64.0 * EPS), writes=["eps64"])
    P.dma("sp", cst["tmpf"][:, :], cin["blk1"], reads=["tmpf"], writes=["tmpf"])
    P.op("pool", lambda e: e.tensor_copy(out=blk1[:, :], in_=cst["tmpf"][:, :]), reads=["tmpf"], writes=["blk1"])
    P.dma("sp", cst["tmpf"][:, :], cin["negm"], reads=["tmpf"], writes=["tmpf"])
    P.op("pool", lambda e: e.tensor_copy(out=negm[:, :], in_=cst["tmpf"][:, :]), reads=["tmpf"], writes=["negm"])

    with C.scope():
        w_sb = C.sb("o_win", [128, 8, 4624], BF16)
        stg = Stager(C, "o_stg", cols=1156)
        win = W["w_in"].rearrange("(c p) f -> p c f", p=128)
        for c in range(8):
            stg.load(w_sb[:, c, :], win[:, c, :], "win")
        fm = [C.sb("o_fm%d" % i, [128, T], F32) for i in range(6)]
        zst = [C.sb("o_zst%d" % i, [128, 512], F32) for i in range(6)]
        vst = [C.sb("o_vst%d" % i, [128, 512], BF16) for i in range(4)]
        dst_ = C.sb("o_dst", [16, T], F32)
        sqb = [C.sb("o_sqb%d" % i, [128, T], BF16) for i in range(4)]
        rr = [C.sb("o_rr%d" % i, [128, T], F32) for i in range(2)]
        qn = [C.sb("o_qn%d" % i, [128, T], BF16) for i in range(4)]
        qw = C.sb("o_qw", [128, 2], F32)
        P.dma("sp", qw[:, 0:1], W["sb_q_norm"], writes=["qw"])
        P.dma("sp", qw[:, 1:2], W["sb_k_norm"], reads=["qw"], writes=["qw"])
        cnt = {"fm": 0, "qn": 0, "z": 0}

        def body(it, hT, hk):
            tsl = slice(it * T, (it + 1) * T)
            for m in range(16):
                pb = 1 + m % 2
                for c in range(8):
                    P.op("pe", lambda e, c=c, m=m, pb=pb: e.matmul(
                        ps[pb][:, :T], lhsT=w_sb[:, c, 1024 + m * 128:1024 + (m + 1) * 128], rhs=hT[:, c, :],
                        start=(c == 0), stop=(c == 7)), reads=["win", hk], writes=[("ps", pb)])
                k = cnt["fm"] % 6
                cnt["fm"] += 1
                evac(P, "act" if m % 2 == 0 else "dve", fm[k][:, :], ps[pb][:, :T], [("ps", pb)], [("fm", k)])
                P.dma("sp", o_xbcT[m * 128:(m + 1) * 128, tsl], fm[k][:, :], reads=[("fm", k)],
                      writes=[("A_out", "xbc", m, it)])
            def qk_proj(m):
                isq = m < 4
                col0 = (3088 if isq else 3600) + (m % 4) * 128
                pbq = 3 + m % 4
                kk = m % 4
                for c in range(8):
                    P.op("pe", lambda e, c=c, col0=col0, pbq=pbq: e.matmul(
                        ps[pbq][:, :T], lhsT=w_sb[:, c, col0:col0 + 128], rhs=hT[:, c, :],
                        start=(c == 0), stop=(c == 7)), reads=["win", hk], writes=[("ps", pbq)])
                P.op("act", lambda e, pbq=pbq, kk=kk: e.activation(out=sqb[kk][:, :], in_=ps[pbq][:, :T], func=AF.Square),
                     reads=[("ps", pbq)], writes=[("sqb", kk)])

            qk_proj(0)
            qk_proj(1)
            for m in range(8):
                isq = m < 4
                pbq = 3 + m % 4
                pbs = 1 + m % 2
                kk = m % 4
                k2 = m % 2
                P.op("pe", lambda e, pbs=pbs, kk=kk: e.matmul(ps[pbs][:, :T], lhsT=blk1[:, :], rhs=sqb[kk][:, :],
                                                              start=True, stop=True),
                     reads=["blk1", ("sqb", kk)], writes=[("ps", pbs)])
                if isq:
                    P.op("act", lambda e, pbs=pbs, k2=k2: e.activation(out=rr[k2][:, :], in_=ps[pbs][:, :T], func=AF.Sqrt,
                                                                       scale=1.0, bias=eps64[:, 0:1]),
                         reads=[("ps", pbs), "eps64"], writes=[("rr", k2)])
                else:
                    P.op("act", lambda e, pbs=pbs, k2=k2: e.activation(out=rr[k2][:, :], in_=ps[pbs][:, :T], func=AF.Sqrt,
                                                                       scale=1.0 / 64, bias=cst["eps"][:, 0:1]),
                         reads=[("ps", pbs), "eps"], writes=[("rr", k2)])
                P.op("dve", lambda e, k2=k2: e.reciprocal(out=rr[k2][:, :], in_=rr[k2][:, :]), reads=[("rr", k2)],
                     writes=[("rr", k2)])
                k = cnt["qn"] % 4
                cnt["qn"] += 1
                wi = 0 if isq else 1
                P.op("dve", lambda e, k=k, wi=wi, pbq=pbq, k2=k2: e.scalar_tensor_tensor(
                    out=qn[k][:, :], in0=ps[pbq][:, :T], scalar=qw[:, wi:wi + 1], in1=rr[k2][:, :], op0=ALU.mult, op1=ALU.mult),
                    reads=[("ps", pbq), "qw", ("rr", k2)], writes=[("qn", k)])
                dd_ = (o_qT if isq else o_kT)[(m % 4) * 128:(m % 4 + 1) * 128, tsl]
                P.dma("sp", dd_, qn[k][:, :], reads=[("qn", k)], writes=[("A_out", "qk", m, it)])
                if m + 2 < 8:
                    qk_proj(m + 2)
            for q in range(4):
                tok = slice(q * 128, (q + 1) * 128)
                r0 = it * T + q * 128
                for half in range(2):
                    pb = 5 + half
                    col0 = half * 512
                    for c in range(8):
                        P.op("pe", lambda e, c=c, tok=tok, col0=col0, pb=pb: e.matmul(
                            ps[pb][:, :512], lhsT=hT[:, c, tok], rhs=w_sb[:, c, col0:col0 + 512],
                            start=(c == 0), stop=(c == 7)), reads=["win", hk], writes=[("ps", pb)])
                    k = cnt["z"] % 6
                    cnt["z"] += 1
                    evac(P, "act" if half == 0 else "dve", zst[k][:, :], ps[pb][:, :512], [("ps", pb)], [("zst", k)])
                    P.dma("sp", o_z[r0:r0 + 128, col0:col0 + 512], zst[k][:, :], reads=[("zst", k)],
                          writes=[("A_out", "z", r0, half)])
                for c in range(8):
                    P.op("pe", lambda e, c=c, tok=tok: e.matmul(
                        ps[7][:, :512], lhsT=hT[:, c, tok], rhs=w_sb[:, c, 4112:4624],
                        start=(c == 0), stop=(c == 7)), reads=["win", hk], writes=[("ps", 7)])
                k = q % 4
                evac(P, "act", vst[k][:, :], ps[7][:, :512], [("ps", 7)], [("vst", k)])
                P.dma("sp", o_v[r0:r0 + 128, :], vst[k][:, :], reads=[("vst", k)], writes=[("A_out", "v", r0)])
            for c in range(8):
                P.op("pe", lambda e, c=c: e.matmul(ps[7][0:16, :T], lhsT=w_sb[:, c, 3072:3088], rhs=hT[:, c, :],
                                                    start=(c == 0), stop=(c == 7)), reads=["win", hk], writes=[("ps", 7)])
            evac(P, "dve", dst_[:, :], ps[7][0:16, :T], [("ps", 7)], ["dst"])
            P.dma("sp", o_dtT[:, tsl], dst_[:, :], reads=["dst"], writes=[("A_out", "dt", it)])

        proj_norm_tiles(C, cst, xT_in, W["mix_norm"], T, body)
    if upto == "A":
        return

    with C.scope():
        xcb = [C.sb("o_xcb%d" % i, [128, S], BF16) for i in range(2)]
        cw = C.sb("o_cw", [128, 16, 4], F32)
        cb = C.sb("o_cb", [128, 16], F32)
        P.dma("sp", cw.rearrange("p a b -> p (a b)"), W["ssd_conv_w"], writes=["cw"])
        P.dma("sp", cb[:, :], W["ssd_conv_b"], writes=["cb"])

        def sink_o(m, it, psap, bias, pkey):
            k = m % 2
            P.op("act", lambda e: e.activation(out=xcb[k][:, it * 512:(it + 1) * 512], in_=psap, func=AF.Silu, bias=bias),
                 reads=[pkey, "cb"], writes=[("xcb", k)])
            if it == S // 512 - 1:
                P.dma("sp", o_xc[m * 128:(m + 1) * 128, :], xcb[k][:, :], reads=[("xcb", k)], writes=[("xc", m)])

        conv_silu_pe(C, cst, "ocv", o_xbcT, 16, cw, cb, sink_o)
    if upto == "S0":
        return

    with C.scope():
        dtr = C.sb("o_dtr", [16, S], F32)
        ldt = C.sb("o_ldt", [16, S], F32)
        aa = C.sb("o_aa", [16, S], F32)
        te = C.sb("o_te", [16, S], F32)
        es = C.sb("o_es", [16, S], F32)
        dtb = C.sb("o_dtb", [16, 1], F32)
        Aneg = C.sb("o_Aneg", [16, 1], F32)
        cde = C.sb("o_cde", [16, 32], F32)
        oh16 = C.sb("o_oh16", [16, 16, 128], F32)
        one16 = cst["one"][0:16, 0:1]
        P.dma("sp", dtr[:, :], o_dtT[:, :], writes=["dtr"])
        P.dma("sp", dtb[:, :], W["ssd_dt_bias"], writes=["dtb"])
        P.dma("sp", Aneg[:, :], W["ssd_A_log"], writes=["Aneg"])
        P.dma("sp", oh16.rearrange("p a b -> p (a b)"), cin["onehot16"], writes=["oh16"])
        P.op("act", lambda e: e.activation(out=Aneg[:, :], in_=Aneg[:, :], func=AF.Exp), reads=["Aneg"], writes=["Aneg"])
        P.op("dve", lambda e: e.tensor_scalar(out=Aneg[:, :], in0=Aneg[:, :], scalar1=-1.0, scalar2=None, op0=ALU.mult),
             reads=["Aneg"], writes=["Aneg"])
        P.op("act", lambda e: e.activation(out=dtr[:, :], in_=dtr[:, :], func=AF.Exp, bias=dtb[:, 0:1]),
             reads=["dtr", "dtb"], writes=["dtr"])
        P.op("act", lambda e: e.activation(out=dtr[:, :], in_=dtr[:, :], func=AF.Ln, bias=one16),
             reads=["dtr", "one"], writes=["dtr"])
        P.op("act", lambda e: e.activation(out=ldt[:, :], in_=dtr[:, :], func=AF.Ln), reads=["dtr"], writes=["ldt"])
        P.op("dve", lambda e: e.tensor_scalar(out=aa[:, :], in0=dtr[:, :], scalar1=Aneg[:, 0:1], scalar2=None, op0=ALU.mult),
             reads=["dtr", "Aneg"], writes=["aa"])
        for c in range(32):
            blk = slice(c * 128, (c + 1) * 128)
            P.op("dve", lambda e, blk=blk: e.tensor_tensor_scan(
                out=aa[:, blk], data0=one16.to_broadcast([16, 128]), data1=aa[:, blk], initial=0.0,
                op0=ALU.mult, op1=ALU.add), reads=["aa", "one"], writes=["aa"])
        aa3 = aa.rearrange("h (c l) -> h c l", l=128)
        aend = aa3[:, :, 127:128]
        P.op("dve", lambda e: e.tensor_copy(out=cde[:, :].unsqueeze(2), in_=aend), reads=["aa"], writes=["cde"])
        P.op("act", lambda e: e.activation(out=cde[:, :], in_=cde[:, :], func=AF.Exp), reads=["cde"], writes=["cde"])
        P.op("act", lambda e: e.activation(out=es[:, :], in_=aa[:, :], func=AF.Exp), reads=["aa"], writes=["es"])
        te3 = te.rearrange("h (c l) -> h c l", l=128)
        P.op("dve", lambda e: e.tensor_tensor(out=te3, in0=aend.to_broadcast([16, 32, 128]), in1=aa3, op=ALU.subtract),
             reads=["aa"], writes=["te"])
        P.op("act", lambda e: e.activation(out=te[:, :], in_=te[:, :], func=AF.Exp), reads=["te"], writes=["te"])
        P.op("dve", lambda e: e.tensor_tensor(out=te[:, :], in0=te[:, :], in1=dtr[:, :], op=ALU.mult),
             reads=["te", "dtr"], writes=["te"])
        P.op("dve", lambda e: e.tensor_tensor(out=ldt[:, :], in0=ldt[:, :], in1=aa[:, :], op=ALU.subtract),
             reads=["ldt", "aa"], writes=["ldt"])
        for h in range(16):
            P.op("pe", lambda e, h=h: e.matmul(ps[1][:, h * 32:(h + 1) * 32], lhsT=oh16[:, h, :], rhs=cde[:, :],
                                                start=True, stop=True), reads=["oh16", "cde"], writes=[("ps", 1)])
        P.op("dve", lambda e: e.tensor_copy(out=cdb.rearrange("p c h -> p h c"),
                                            in_=ps[1][:, :].rearrange("p (h c) -> p h c", c=32)),
             reads=[("ps", 1)], writes=["cdb"])
        for src, sname, dst, dname, pb in ((ldt, "ldt", bias_tm, "bias_tm", 2), (es, "es", est_tm, "est_tm", 3),
                                           (te, "te", dtte_tm, "dtte_tm", 4)):
            for c in range(32):
                P.op("pe", lambda e, src=src, c=c, pb=pb: e.transpose(
                    out=ps[pb][:, c * 16:(c + 1) * 16], in_=src[:, c * 128:(c + 1) * 128],
                    identity=cst["identf"][0:16, 0:16]), reads=[sname, "identf"], writes=[("ps", pb)])
            P.op("dve", lambda e, dst=dst, pb=pb: e.tensor_copy(out=dst.rearrange("p a b -> p (a b)"), in_=ps[pb][:, :]),
                 reads=[("ps", pb)], writes=[dname])
        o_acum = C.dram("o_acum", [16, S], F32)
        P.dma("sp", o_acum[:, :], aa[:, :], reads=["aa"], writes=["o_acum"])
    if upto == "S1":
        return
    ssd_main(C, cst, cin, W, o_z, o_xc, mixT, bias_tm, est_tm, dtte_tm, cdb, negm, upto)
    if upto[0] == "S":
        return
    stick_breaking_phase(C, cst, o_qT, o_kT, o_v, mixT, upto)
    if upto[0] == "T":
        return
    with C.scope():
        wo = C.sb("o_wo", [128, 12, D], BF16)
        stg = Stager(C, "o_stgE")
        wov = W["w_out"].rearrange("(c p) f -> p c f", p=128)
        for c in range(12):
            stg.load(wo[:, c, :], wov[:, c, :], "wo")
        out_proj(C, wo, 12, mixT, xT_in, xT_out, T)


def ssd_main(C, cst, cin, W, o_z, o_xc, mixT, bias_tm, est_tm, dtte_tm, cdb, negm, upto):
    P = C.P
    ps = C.psum
    o_acum = C.dram("o_acum", [16, S], F32)
    with C.scope():
        acf = C.sb("s_acf", [16, S], F32)
        oh16 = C.sb("s_oh16", [16, 16, 128], F32)
        Dbc = C.sb("s_Dbc", [128, 16], F32)
        Did = C.sb("s_Did", [128, 16, 128], BF16)
        nwb = C.sb("s_nwb", [128, 1024], F32)
        xsup = [C.sb("s_xsup%d" % i, [128, 16, 512], BF16) for i in range(2)]
        xtm = C.sb("s_xtm", [128, 16, 64], BF16)
        xw = C.sb("s_xw", [128, 16, 64], BF16)
        Btm = C.sb("s_Btm", [128, 4, 128], BF16)
        dec = [C.sb("s_dec%d" % i, [128, 4, 128], F32) for i in range(2)]
        PTs = [C.sb("s_PT%d" % i, [128, 4, 128], BF16) for i in range(2)]
        Hst = C.sb("s_H", [128, 16, 64], F32)
        Hbf = C.sb("s_Hbf", [128, 16, 64], BF16)
        zch = [C.sb("s_z%d" % i, [128, 1024], F32) for i in range(2)]
        yoff = C.sb("s_yoff", [128, 16, 64], F32)
        ysb = C.sb("s_y", [128, 1024], F32)
        ssg = C.sb("s_ss", [128, 4], F32)
        junk = C.sb("s_junk", [128, 256], F32)
        cout = C.sb("s_cout", [128, 1024], BF16)
        coutT = [C.sb("s_coutT%d" % i, [128, 8, 128], BF16) for i in range(2)]
        P.dma("sp", acf[:, :], o_acum[:, :], writes=["acf"])
        P.dma("sp", oh16.rearrange("p a b -> p (a b)"), cin["onehot16"], writes=["oh16"])
        P.dma("sp", Dbc[:, :], W["ssd_D"].partition_broadcast(128), writes=["Dbc"])
        P.dma("sp", nwb[:, :], W["ssd_norm"].partition_broadcast(128), writes=["nwb"])
        for h in range(16):
            P.op("dve", lambda e, h=h: e.tensor_scalar(out=Did[:, h, :], in0=cst["identf"][:, :], scalar1=Dbc[:, h:h + 1],
                                                       scalar2=None, op0=ALU.mult), reads=["identf", "Dbc"], writes=["Did"])
        P.op("pool", lambda e: e.memset(Hst.rearrange("p a b -> p (a b)"), 0.0), writes=["H"])
        P.op("pool", lambda e: e.memset(Hbf.rearrange("p a b -> p (a b)"), 0.0), writes=["Hbf"])
        ps0b = ps[0][:, :].bitcast(BF16)
        ps1b = ps[1][:, :].bitcast(BF16)
        xcv = o_xc.rearrange("(m p) t -> p m t", p=128)
        nch = 1 if upto == "S2" else 32
        pending_tail = []
        for c in range(nch):
            b = c % 2
            sc, lc = c // 4, c % 4
            blk = slice(c * 128, (c + 1) * 128)
            tl = slice(lc * 128, (lc + 1) * 128)
            if lc == 0:
                P.dma("sp", xsup[sc % 2][:, :, :], xcv[:, :, sc * 512:(sc + 1) * 512], writes=[("xsup", sc % 2)])
            xs_ = xsup[sc % 2]
            xk = ("xsup", sc % 2)
            P.dma("sp", zch[b][:, :], o_z[blk, :], writes=[("zch", b)])
            for m in range(8):
                P.op("pe", lambda e, m=m, xs_=xs_, tl=tl: e.transpose(out=ps0b[:, m * 128:(m + 1) * 128], in_=xs_[:, m, tl],
                                                                      identity=cst["identb"][:, :]),
                     reads=[xk, "identb"], writes=[("ps", 0)])
            P.op("act", lambda e: e.activation(out=xtm.rearrange("p a b -> p (a b)"), in_=ps0b[:, :], func=AF.Copy),
                 reads=[("ps", 0)], writes=["xtm"])
            P.op("dve", lambda e, c=c: e.tensor_tensor(
                out=xw[:, :, :], in0=ps0b[:, :].rearrange("p (a b) -> p a b", b=64),
                in1=dtte_tm[:, c, :].unsqueeze(2).to_broadcast([128, 16, 64]), op=ALU.mult),
                reads=[("ps", 0), "dtte_tm"], writes=["xw"])
            for g in range(4):
                P.op("pe", lambda e, g=g, xs_=xs_, tl=tl: e.transpose(out=ps1b[:, g * 128:(g + 1) * 128], in_=xs_[:, 8 + g, tl],
                                                                      identity=cst["identb"][:, :]),
                     reads=[xk, "identb"], writes=[("ps", 1)])
            P.op("act", lambda e: e.activation(out=Btm.rearrange("p a b -> p (a b)"), in_=ps1b[:, 0:512], func=AF.Copy),
                 reads=[("ps", 1)], writes=["Btm"])
            for g in range(4):
                P.op("pe", lambda e, g=g, xs_=xs_, tl=tl: e.matmul(ps[2][:, g * 128:(g + 1) * 128], lhsT=xs_[:, 8 + g, tl],
                                                                   rhs=xs_[:, 12 + g, tl], start=True, stop=True),
                     reads=[xk], writes=[("ps", 2)])
            def emit_yoff(c=c, xs_=xs_, tl=tl, xk=xk):
                for g in range(4):
                    P.op("pe", lambda e, g=g, xs_=xs_, tl=tl: e.matmul(
                        ps[6 + g // 2][:, (g % 2) * 256:(g % 2 + 1) * 256], lhsT=xs_[:, 12 + g, tl],
                        rhs=Hbf[:, 4 * g:4 * g + 4, :], start=True, stop=True), reads=[xk, "Hbf"], writes=[("ps", 6 + g // 2)])
                for hb in range(2):
                    P.op("dve", lambda e, hb=hb, c=c: e.tensor_tensor(
                        out=yoff[:, 8 * hb:8 * hb + 8, :], in0=ps[6 + hb][:, :].rearrange("p (a b) -> p a b", b=64),
                        in1=est_tm[:, c, 8 * hb:8 * hb + 8].unsqueeze(2).to_broadcast([128, 8, 64]), op=ALU.mult),
                        reads=[("ps", 6 + hb), "est_tm"], writes=[("yoff", hb)])

            for g in range(4):
                k = g % 2
                if g == 2:
                    emit_yoff()
                for hh in range(4):
                    h = 4 * g + hh
                    P.op("pe", lambda e, hh=hh, h=h, blk=blk: e.matmul(
                        ps[3][:, hh * 128:(hh + 1) * 128], lhsT=oh16[:, h, :], rhs=acf[:, blk], start=True, stop=False),
                        reads=["oh16", "acf"], writes=[("ps", 3)])
                    P.op("pe", lambda e, hh=hh: e.matmul(
                        ps[3][:, hh * 128:(hh + 1) * 128], lhsT=cst["identb"][:, :], rhs=negm[:, :], start=False, stop=True),
                        reads=["identb", "negm"], writes=[("ps", 3)])
                for hh in range(4):
                    h = 4 * g + hh
                    P.op("act", lambda e, hh=hh, h=h, k=k, c=c: e.activation(
                        out=dec[k][:, hh, :], in_=ps[3][:, hh * 128:(hh + 1) * 128], func=AF.Exp,
                        bias=bias_tm[:, c, h:h + 1]), reads=[("ps", 3), "bias_tm"], writes=[("dec", k)])
                P.op("dve", lambda e, g=g, k=k: e.tensor_tensor(
                    out=PTs[k][:, :, :], in0=dec[k][:, :, :],
                    in1=ps[2][:, g * 128:(g + 1) * 128].unsqueeze(1).to_broadcast([128, 4, 128]), op=ALU.mult),
                    reads=[("dec", k), ("ps", 2)], writes=[("PTs", k)])
                for hh in range(4):
                    h = 4 * g + hh
                    yb = 4 + h // 8
                    yc = (h % 8) * 64
                    P.op("pe", lambda e, hh=hh, h=h, k=k, yb=yb, yc=yc: e.matmul(
                        ps[yb][:, yc:yc + 64], lhsT=PTs[k][:, hh, :], rhs=xtm[:, h, :], start=True, stop=False),
                        reads=[("PTs", k), "xtm"], writes=[("ps", yb)])
                    P.op("pe", lambda e, h=h, yb=yb, yc=yc: e.matmul(
                        ps[yb][:, yc:yc + 64], lhsT=Did[:, h, :], rhs=xtm[:, h, :], start=False, stop=True),
                        reads=["Did", "xtm"], writes=[("ps", yb)])
            while pending_tail:
                pending_tail.pop(0)()
            for hb in range(2):
                P.op("dve", lambda e, hb=hb: e.tensor_tensor(
                    out=ysb[:, hb * 512:(hb + 1) * 512], in0=ps[4 + hb][:, :],
                    in1=yoff[:, 8 * hb:8 * hb + 8, :].rearrange("p a b -> p (a b)"), op=ALU.add),
                    reads=[("ps", 4 + hb), ("yoff", hb)], writes=[("ysb", hb)])
            for g in range(4):
                P.op("pe", lambda e, g=g: e.matmul(
                    ps[6 + g // 2][:, (g % 2) * 256:(g % 2 + 1) * 256], lhsT=Btm[:, g, :],
                    rhs=xw[:, 4 * g:4 * g + 4, :], start=True, stop=True), reads=["Btm", "xw"], writes=[("ps", 6 + g // 2)])
            P.op("dve", lambda e, c=c: e.tensor_tensor(
                out=Hst[:, :, :], in0=Hst[:, :, :], in1=cdb[:, c, :].unsqueeze(2).to_broadcast([128, 16, 64]), op=ALU.mult),
                reads=["H", "cdb"], writes=["H"])
            for hb in range(2):
                P.op("dve", lambda e, hb=hb: e.tensor_tensor(
                    out=Hst[:, 8 * hb:8 * hb + 8, :], in0=Hst[:, 8 * hb:8 * hb + 8, :],
                    in1=ps[6 + hb][:, :].rearrange("p (a b) -> p a b", b=64), op=ALU.add),
                    reads=["H", ("ps", 6 + hb)], writes=["H"])
            P.op("act", lambda e: e.activation(out=Hbf.rearrange("p a b -> p (a b)"),
                                               in_=Hst.rearrange("p a b -> p (a b)"), func=AF.Copy),
                 reads=["H"], writes=["Hbf"])
            P.op("act", lambda e, b=b: e.activation(out=zch[b][:, :], in_=zch[b][:, :], func=AF.Silu),
                 reads=[("zch", b)], writes=[("zch", b)])
            P.op("dve", lambda e, b=b: e.tensor_tensor(out=ysb[:, :], in0=ysb[:, :], in1=zch[b][:, :], op=ALU.mult),
                 reads=[("ysb", 0), ("ysb", 1), ("zch", b)], writes=[("ysb", 0), ("ysb", 1)])
            for g in range(4):
                P.op("act", lambda e, g=g: e.activation(out=junk[:, :], in_=ysb[:, g * 256:(g + 1) * 256], func=AF.Square,
                                                        accum_out=ssg[:, g:g + 1]),
                     reads=[("ysb", 0), ("ysb", 1)], writes=["junk", ("ssg", g)])
            P.op("act", lambda e: e.activation(out=ssg[:, :], in_=ssg[:, :], func=AF.Sqrt, scale=1.0 / 256,
                                               bias=cst["eps"][:, 0:1]),
                 reads=[("ssg", g) for g in range(4)] + ["eps"], writes=[("ssg", g) for g in range(4)])
            P.op("dve", lambda e: e.reciprocal(out=ssg[:, :], in_=ssg[:, :]), reads=[("ssg", g) for g in range(4)],
                 writes=[("ssg", g) for g in range(4)])
            for g in range(4):
                P.op("dve", lambda e, g=g: e.scalar_tensor_tensor(
                    out=cout[:, g * 256:(g + 1) * 256], in0=ysb[:, g * 256:(g + 1) * 256], scalar=ssg[:, g:g + 1],
                    in1=nwb[:, g * 256:(g + 1) * 256], op0=ALU.mult, op1=ALU.mult),
                    reads=[("ysb", 0), ("ysb", 1), ("ssg", g), "nwb"], writes=["cout"])
            def emit_tail(c=c, b=b, blk=blk):
                for m in range(8):
                    P.op("pe", lambda e, m=m: e.transpose(out=ps0b[:, m * 128:(m + 1) * 128], in_=cout[:, m * 128:(m + 1) * 128],
                                                          identity=cst["identb"][:, :]),
                         reads=["cout", "identb"], writes=[("ps", 0)])
                P.op("act", lambda e, b=b: e.activation(out=coutT[b].rearrange("p a b -> p (a b)"), in_=ps0b[:, :], func=AF.Copy),
                     reads=[("ps", 0)], writes=[("coutT", b)])
                P.dma("sp", mixT[0:1024, blk].rearrange("(m p) t -> p m t", p=128), coutT[b][:, :, :],
                      reads=[("coutT", b)], writes=[("mixC", c)])
            pending_tail.append(emit_tail)
        for f_ in pending_tail:
            f_()


def stick_breaking_phase(C, cst, o_qT, o_kT, o_v, mixT, upto):
    P = C.P
    ps = C.psum
    with C.scope():
        kT = C.sb("t_kT", [128, 4, S], BF16)
        qT = C.sb("t_qT", [128, 4, S], BF16)
        vtm = C.sb("t_v", [128, 32, 512], BF16)
        tril = C.sb("t_tril", [128, 128], BF16)
        P.op("pool", lambda e: e.tensor_tensor(out=tril[:, :], in0=cst["ones"][:, :], in1=cst["triu"][:, :], op=ALU.subtract),
             reads=["ones", "triu"], writes=["tril"])
        P.dma("sp", kT[:, :, :], o_kT.rearrange("(m p) t -> p m t", p=128), writes=["kT"])
        P.dma("sp", qT[:, :, :], o_qT.rearrange("(m p) t -> p m t", p=128), writes=["qT"])
        ovv = o_v.rearrange("(b p) f -> p b f", p=128)
        for i in range(8):
            P.dma("sp", vtm[:, 4 * i:4 * i + 4, :], ovv[:, 4 * i:4 * i + 4, :], reads=["vtm"] if i else [], writes=["vtm"])
        NS = 2
        ee = [[C.sb("t_e%d_%d" % (s_, i), [128, 512], F32) for i in range(2)] for s_ in range(NS)]
        sp = [[C.sb("t_sp%d_%d" % (s_, i), [128, 512], F32) for i in range(2)] for s_ in range(NS)]
        l1m = [[C.sb("t_l1m%d_%d" % (s_, i), [128, 512], BF16) for i in range(2)] for s_ in range(NS)]
        E1 = [[C.sb("t_E1%d_%d" % (s_, i), [128, 512], F32) for i in range(2)] for s_ in range(NS)]
        PTb = [[C.sb("t_PT%d_%d" % (s_, i), [128, 512], BF16) for i in range(2)] for s_ in range(NS)]
        dout = C.sb("t_dout", [128, 4, 512], BF16)
        doutT = [C.sb("t_doutT%d" % i, [128, 4, 512], BF16) for i in range(2)]
        ps7b = ps[6][:, :].bitcast(BF16)
        nQ = {"T1": 1, "T2": 2}.get(upto, 8)
        for Q in range(nQ):
            for h0 in range(0, 8, NS):
                kbs = list(range(4 * Q + 3, -1, -1))
                n = len(kbs)

                def geo(i):
                    kb = kbs[i]
                    tb0 = max(0, kb - 4 * Q)
                    c0 = tb0 * 128
                    return kb, tb0, c0, slice(c0, 512), kb >= 4 * Q

                hs = [(h0 + s_, (h0 + s_) // 2, 64 * ((h0 + s_) % 2), 2 * s_, 4 + s_, 6 + s_) for s_ in range(NS)]
                def emit_z(i):
                    kb, tb0, c0, cs_, diag = geo(i)
                    kblk = slice(kb * 128, (kb + 1) * 128)
                    qsl = slice(Q * 512 + c0, (Q + 1) * 512)
                    for s_, (h, m, pb0, zb0, xb, ob) in enumerate(hs):
                        zb = zb0 + i % 2
                        P.op("pe", lambda e, m=m, pb0=pb0, kblk=kblk, qsl=qsl, cs_=cs_, zb=zb: e.matmul(
                            ps[zb][:, cs_], lhsT=kT[pb0:pb0 + 64, m, kblk], rhs=qT[pb0:pb0 + 64, m, qsl],
                            start=True, stop=True), reads=["kT", "qT"], writes=[("ps", zb)])

                for i in range(n + 1):
                    if i >= 1:
                        kb, tb0, c0, cs_, diag = geo(i - 1)
                        k = (i - 1) % 2
                        for s_, (h, m, pb0, zb, xb, ob) in enumerate(hs):
                            P.op("dve", lambda e, s_=s_, k=k, cs_=cs_, xb=xb: e.tensor_tensor(
                                out=E1[s_][k][:, cs_], in0=ps[xb][:, cs_], in1=sp[s_][k][:, cs_], op=ALU.subtract),
                                reads=[("ps", xb), ("sp", s_, k)], writes=[("E1", s_, k)])
                    if i < n:
                        kb, tb0, c0, cs_, diag = geo(i)
                        k = i % 2
                        if i == 0:
                            emit_z(0)
                        for s_, (h, m, pb0, zb0, xb, ob) in enumerate(hs):
                            zb = zb0 + k
                            P.op("act", lambda e, s_=s_, k=k, cs_=cs_, zb=zb: e.activation(
                                out=ee[s_][k][:, cs_], in_=ps[zb][:, cs_], func=AF.Exp, scale=-1.0),
                                reads=[("ps", zb)], writes=[("ee", s_, k)])
                        for s_, (h, m, pb0, zb, xb, ob) in enumerate(hs):
                            P.op("act", lambda e, s_=s_, k=k, cs_=cs_: e.activation(
                                out=sp[s_][k][:, cs_], in_=ee[s_][k][:, cs_], func=AF.Ln, bias=1.0),
                                reads=[("ee", s_, k)], writes=[("sp", s_, k)])
                        for s_, (h, m, pb0, zb0, xb, ob) in enumerate(hs):
                            zb = zb0 + k
                            P.op("dve", lambda e, s_=s_, k=k, cs_=cs_, zb=zb: e.scalar_tensor_tensor(
                                out=l1m[s_][k][:, cs_], in0=ps[zb][:, cs_], scalar=-1.0, in1=sp[s_][k][:, cs_],
                                op0=ALU.mult, op1=ALU.subtract), reads=[("ps", zb), ("sp", s_, k)], writes=[("l1m", s_, k)])
                            if diag:
                                dsl = slice(c0, c0 + 128)
                                P.op("dve", lambda e, s_=s_, k=k, dsl=dsl: e.tensor_tensor(
                                    out=l1m[s_][k][:, dsl], in0=l1m[s_][k][:, dsl], in1=cst["maskTs"][:, :], op=ALU.mult),
                                    reads=[("l1m", s_, k), "maskTs"], writes=[("l1m", s_, k)])
                    if i + 1 < n:
                        emit_z(i + 1)
                    if i >= 1:
                        kb, tb0, c0, cs_, diag = geo(i - 1)
                        k = (i - 1) % 2
                        for s_, (h, m, pb0, zb, xb, ob) in enumerate(hs):
                            P.op("act", lambda e, s_=s_, k=k, cs_=cs_: e.activation(
                                out=PTb[s_][k][:, cs_], in_=E1[s_][k][:, cs_], func=AF.Exp),
                                reads=[("E1", s_, k)], writes=[("PTb", s_, k)])
                            if diag:
                                dsl = slice(c0, c0 + 128)
                                P.op("dve", lambda e, s_=s_, k=k, dsl=dsl: e.tensor_tensor(
                                    out=PTb[s_][k][:, dsl], in0=PTb[s_][k][:, dsl], in1=cst["maskTs"][:, :], op=ALU.mult),
                                    reads=[("PTb", s_, k), "maskTs"], writes=[("PTb", s_, k)])
                        for s_, (h, m, pb0, zb, xb, ob) in enumerate(hs):
                            for tb in range(tb0, 4):
                                P.op("pe", lambda e, s_=s_, k=k, tb=tb, kb=kb, h=h, ob=ob, st=(i == 1 and tb == tb0): e.matmul(
                                    ps[ob][:, tb * 64:(tb + 1) * 64], lhsT=PTb[s_][k][:, tb * 128:(tb + 1) * 128],
                                    rhs=vtm[:, kb, h * 64:(h + 1) * 64], start=st, stop=(kb == 0), skip_group_check=True),
                                    reads=[("PTb", s_, k), "vtm"], writes=[("ps", ob)])
                    if i < n:
                        kb, tb0, c0, cs_, diag = geo(i)
                        k = i % 2
                        for s_, (h, m, pb0, zb, xb, ob) in enumerate(hs):
                            if i >= 1:
                                pcs = geo(i - 1)[3]
                                pk = (i - 1) % 2
                                P.op("pe", lambda e, s_=s_, pk=pk, pcs=pcs, xb=xb: e.matmul(
                                    ps[xb][:, pcs], lhsT=tril[:, :], rhs=l1m[s_][pk][:, pcs], start=False, stop=False,
                                    skip_group_check=True), reads=["tril", ("l1m", s_, pk)], writes=[("ps", xb)])
                            P.op("pe", lambda e, s_=s_, k=k, cs_=cs_, xb=xb, st=(i == 0): e.matmul(
                                ps[xb][:, cs_], lhsT=cst["triu"][:, :], rhs=l1m[s_][k][:, cs_], start=st, stop=False,
                                skip_group_check=True), reads=["triu", ("l1m", s_, k)], writes=[("ps", xb)])
                for s_ in range(NS):
                    h = h0 + s_
                    ob = 6 + s_
                    P.op("dve", lambda e, h=h, ob=ob: e.tensor_copy(
                        out=dout[:, :, h * 64:(h + 1) * 64], in_=ps[ob][:, 0:256].rearrange("p (a b) -> p a b", b=64)),
                        reads=[("ps", ob)], writes=["dout"])
            qb = Q % 2
            for mm in range(4):
                for tb in range(4):
                    P.op("pe", lambda e, mm=mm, tb=tb: e.transpose(
                        out=ps7b[:, tb * 128:(tb + 1) * 128], in_=dout[:, tb, mm * 128:(mm + 1) * 128],
                        identity=cst["identb"][:, :]), reads=["dout", "identb"], writes=[("ps", 6)])
                P.op("dve", lambda e, mm=mm, qb=qb: e.tensor_copy(out=doutT[qb][:, mm, :], in_=ps7b[:, 0:512]),
                     reads=[("ps", 6)], writes=[("doutT", qb)])
            P.dma("sp", mixT[1024:1536, Q * 512:(Q + 1) * 512].rearrange("(m p) t -> p m t", p=128), doutT[qb][:, :, :],
                  reads=[("doutT", qb)], writes=[("mixD", Q)])


def build_program():
    nc = bass.Bass("TRN2", target_bir_lowering=False)
    xT = nc.dram_tensor("xT", [D, S], F32, kind="ExternalInput").ap()
    yT = nc.dram_tensor("yT", [D, S], F32, kind="ExternalOutput").ap()
    Wd = {k: nc.dram_tensor(k, sh, F32, kind="ExternalInput").ap() for k, sh in PARAM_SHAPES.items()}
    cin = {k: nc.dram_tensor("c_" + k, sh, F32, kind="ExternalInput").ap() for k, sh in CONST_SHAPES.items()}
    with ExitStack() as stack:
        C = Ctx(nc, stack)
        cst = alloc_consts(C, cin)
        res = [C.dram("res%d" % i, [D, S], F32) for i in range(5)]

        def ffn(pre, src, dst):
            with C.scope():
                bufs = alloc_ffn_bufs(C, cst)
                ffn_phase(C, pre, src, dst, Wd[pre + "_norm"], Wd[pre + "_wg"], Wd[pre + "_wu"], Wd[pre + "_wd"], bufs)

        ffn("l0_ffn1", xT, res[0])
        with C.scope():
            even_mixer_phase(C, cst, cin, res[0], res[1], {k[3:]: v for k, v in Wd.items() if k.startswith("l0_")})
        ffn("l0_ffn2", res[1], res[2])
        ffn("l1_ffn1", res[2], res[3])
        with C.scope():
            odd_mixer_phase(C, cst, cin, res[3], res[4], {k[3:]: v for k, v in Wd.items() if k.startswith("l1_")})
        ffn("l1_ffn2", res[4], yT)
        C.P.emit()
    return nc


_NC_CACHE = {}


def kernel(**inputs):
    x = np.asarray(inputs["x"], np.float32)
    hp = host_params(inputs)
    hc = host_consts()
    shared = {k: hp[k] for k in PARAM_SHAPES}
    for k in CONST_SHAPES:
        shared["c_" + k] = hc[k]
    in_maps = []
    for b in range(NCORES):
        m = dict(shared)
        m["xT"] = np.ascontiguousarray(x[b].T)
        in_maps.append(m)
    if "nc" not in _NC_CACHE:
        _NC_CACHE["nc"] = build_program()
    res = run_bass_kernel_spmd(_NC_CACHE["nc"], in_maps, core_ids=list(range(NCORES)))
    out = np.stack([np.asarray(r["yT"], np.float32).T for r in res.results], axis=0)
    return np.ascontiguousarray(out)
```
